# Optimizing a Trainium2 kernel written in Bass

```python
import math
import jax, jax.numpy as jnp
from jax import lax
import numpy as np

D_MODEL = 4096
BATCH = 4
SEQ = 2048
DEPTH = 1

EPS = 1e-6

MLA_HEADS = 16
QK_NOPE_DIM = 128
QK_ROPE_DIM = 64
QK_HEAD_DIM = QK_NOPE_DIM + QK_ROPE_DIM
V_HEAD_DIM = 128
Q_LORA_RANK = 1024
KV_LORA_RANK = 512
ROPE_BASE = 10000.0
Q_BLOCK = 128
MLA_WIDTH = MLA_HEADS * V_HEAD_DIM

SSM_WIDTH = D_MODEL - MLA_WIDTH
SSM_HEAD_DIM = 64
SSM_HEADS = SSM_WIDTH // SSM_HEAD_DIM
SSM_GROUPS = 8
SSM_HEADS_PER_GROUP = SSM_HEADS // SSM_GROUPS
D_STATE = 128
D_CONV = 5
CHUNK = 128
CONV_DIM = SSM_WIDTH + 2 * SSM_GROUPS * D_STATE

MIX_WIDTH = MLA_WIDTH + SSM_WIDTH

IN_SIZES = (Q_LORA_RANK, KV_LORA_RANK, QK_ROPE_DIM, SSM_WIDTH, CONV_DIM, SSM_HEADS, SSM_HEADS)
IN_WIDTH = sum(IN_SIZES)
IN_SPLITS = tuple(int(v) for v in np.cumsum(IN_SIZES)[:-1])

PEER_HEADS = 8
N_KEYS = 128
N_EXPERTS = N_KEYS * N_KEYS
PEER_TOPK = 16
PEER_QDIM = 256
PEER_TOKEN_BLOCK = 128

kernel_name = "hymba_mla_ssd_peer_encoder_layer"


def rms_norm(x, gain):
    xf = x.astype(jnp.float32)
    y = xf * lax.rsqrt(jnp.mean(xf * xf, axis=-1, keepdims=True) + EPS)
    return (y * gain.astype(jnp.float32)).astype(x.dtype)


def apply_rope(t, positions):
    half = QK_ROPE_DIM // 2
    inv_freq = ROPE_BASE ** (-jnp.arange(half, dtype=jnp.float32) / half)
    ang = positions.astype(jnp.float32)[..., None] * inv_freq
    cos = jnp.cos(ang)[:, :, None, :]
    sin = jnp.sin(ang)[:, :, None, :]
    tf = t.astype(jnp.float32)
    t1, t2 = tf[..., :half], tf[..., half:]
    return jnp.concatenate([t1 * cos - t2 * sin, t1 * sin + t2 * cos], axis=-1).astype(t.dtype)


def mla_mixer(c_q, c_kv, k_rope, positions, q_a_norm, w_uq, kv_a_norm, w_ukv, q_norm, k_norm, attn_out_norm):
    b, s, _ = c_q.shape
    q = (rms_norm(c_q, q_a_norm) @ w_uq).reshape(b, s, MLA_HEADS, QK_HEAD_DIM)
    kv = (rms_norm(c_kv, kv_a_norm) @ w_ukv).reshape(b, s, MLA_HEADS, QK_NOPE_DIM + V_HEAD_DIM)
    k_nope, v = kv[..., :QK_NOPE_DIM], kv[..., QK_NOPE_DIM:]
    k = jnp.concatenate(
        [k_nope, jnp.broadcast_to(k_rope[:, :, None, :], (b, s, MLA_HEADS, QK_ROPE_DIM))], axis=-1)
    q = rms_norm(q, q_norm)
    k = rms_norm(k, k_norm)
    q = jnp.concatenate([q[..., :QK_NOPE_DIM], apply_rope(q[..., QK_NOPE_DIM:], positions)], axis=-1)
    k = jnp.concatenate([k[..., :QK_NOPE_DIM], apply_rope(k[..., QK_NOPE_DIM:], positions)], axis=-1)
    scale = QK_HEAD_DIM ** -0.5
    n_blk = s // Q_BLOCK
    q_blocks = jnp.moveaxis(q.reshape(b, n_blk, Q_BLOCK, MLA_HEADS, QK_HEAD_DIM), 1, 0)

    def attend(q_blk):
        sc = jnp.einsum("bqhd,bkhd->bhqk", q_blk, k, preferred_element_type=jnp.float32) * scale
        p = jax.nn.softmax(sc, axis=-1)
        return jnp.einsum("bhqk,bkhd->bqhd", p.astype(v.dtype), v)

    o = lax.map(attend, q_blocks)
    o = jnp.moveaxis(o, 0, 1).reshape(b, s, MLA_HEADS, V_HEAD_DIM)
    o = rms_norm(o, attn_out_norm)
    return o.reshape(b, s, MLA_WIDTH)


def ssd_scan(xs, dt_raw, a_log, dt_bias, bs, cs):
    b, s, g, r, p = xs.shape
    nc = s // CHUNK
    dt = jax.nn.softplus(dt_raw.astype(jnp.float32) + dt_bias.astype(jnp.float32)).reshape(b, s, g, r)
    a = -jnp.exp(a_log.astype(jnp.float32)).reshape(g, r)
    a_dt = (dt * a).reshape(b, nc, CHUNK, g, r).transpose(0, 3, 4, 1, 2)
    xdt = (xs.astype(jnp.float32) * dt[..., None]).reshape(b, nc, CHUNK, g, r, p)
    bc = bs.astype(jnp.float32).reshape(b, nc, CHUNK, g, D_STATE)
    cc = cs.astype(jnp.float32).reshape(b, nc, CHUNK, g, D_STATE)
    a_cum = jnp.cumsum(a_dt, axis=-1)
    seg = a_cum[..., :, None] - a_cum[..., None, :]
    mask = jnp.tril(jnp.ones((CHUNK, CHUNK), dtype=bool))
    l_mat = jnp.exp(jnp.where(mask, seg, -jnp.inf))
    y_diag = jnp.einsum("bclgn,bcsgn,bgrcls,bcsgrp->bclgrp", cc, bc, l_mat, xdt)
    decay_states = jnp.exp(a_cum[..., -1:] - a_cum)
    states = jnp.einsum("bclgn,bgrcl,bclgrp->bcgrpn", bc, decay_states, xdt)
    chunk_decay = jnp.exp(a_cum[..., -1])

    def carry_state(h, inp):
        s_c, d_c = inp
        return h * d_c[..., None, None] + s_c, h

    h0 = jnp.zeros((b, g, r, p, D_STATE), jnp.float32)
    _, states_in = lax.scan(carry_state, h0, (jnp.moveaxis(states, 1, 0), jnp.moveaxis(chunk_decay, -1, 0)))
    states_in = jnp.moveaxis(states_in, 0, 1)
    y_off = jnp.einsum("bclgn,bcgrpn,bgrcl->bclgrp", cc, states_in, jnp.exp(a_cum))
    return (y_diag + y_off).reshape(b, s, g, r, p)


def ssd_mixer(z, xbc, dt_f, dt_b, conv_w, conv_b, a_log_fwd, a_log_bwd, dt_bias_fwd, dt_bias_bwd,
              d_skip, ssm_out_norm):
    b, s, _ = z.shape
    xbc = lax.conv_general_dilated(
        xbc, conv_w, window_strides=(1,), padding=((D_CONV // 2, D_CONV // 2),),
        dimension_numbers=("NWC", "WIO", "NWC"), feature_group_count=CONV_DIM)
    xbc = jax.nn.silu(xbc + conv_b)
    gn = SSM_GROUPS * D_STATE
    xs = xbc[..., :SSM_WIDTH].reshape(b, s, SSM_GROUPS, SSM_HEADS_PER_GROUP, SSM_HEAD_DIM)
    bs = xbc[..., SSM_WIDTH:SSM_WIDTH + gn].reshape(b, s, SSM_GROUPS, D_STATE)
    cs = xbc[..., SSM_WIDTH + gn:].reshape(b, s, SSM_GROUPS, D_STATE)
    y_fwd = ssd_scan(xs, dt_f, a_log_fwd, dt_bias_fwd, bs, cs)
    y_bwd = jnp.flip(ssd_scan(jnp.flip(xs, 1), jnp.flip(dt_b, 1), a_log_bwd, dt_bias_bwd,
                              jnp.flip(bs, 1), jnp.flip(cs, 1)), 1)
    d = d_skip.astype(jnp.float32).reshape(SSM_GROUPS, SSM_HEADS_PER_GROUP)[..., None]
    y = y_fwd + y_bwd + d * xs.astype(jnp.float32)
    y = y.reshape(b, s, SSM_WIDTH) * jax.nn.silu(z.astype(jnp.float32))
    y = rms_norm(y.reshape(b, s, SSM_GROUPS, SSM_WIDTH // SSM_GROUPS),
                 ssm_out_norm.reshape(SSM_GROUPS, SSM_WIDTH // SSM_GROUPS))
    return y.reshape(b, s, SSM_WIDTH).astype(z.dtype)


def peer_ffn(xn, w_query, sub_keys, expert_u, expert_v):
    b, s, d = xn.shape
    q = (xn @ w_query).reshape(b, s, PEER_HEADS, 2, PEER_QDIM // 2)
    sc = jnp.einsum("bshcd,hckd->bshck", q, sub_keys, preferred_element_type=jnp.float32)
    v1, i1 = lax.top_k(sc[..., 0, :], PEER_TOPK)
    v2, i2 = lax.top_k(sc[..., 1, :], PEER_TOPK)
    cand = (v1[..., :, None] + v2[..., None, :]).reshape(b, s, PEER_HEADS, PEER_TOPK * PEER_TOPK)
    top_s, top_c = lax.top_k(cand, PEER_TOPK)
    e1 = jnp.take_along_axis(i1, top_c // PEER_TOPK, axis=-1)
    e2 = jnp.take_along_axis(i2, top_c % PEER_TOPK, axis=-1)
    idx = (e1 * N_KEYS + e2).reshape(b * s, PEER_HEADS * PEER_TOPK)
    gates = jax.nn.softmax(top_s, axis=-1).reshape(b * s, PEER_HEADS * PEER_TOPK)
    n_blk = (b * s) // PEER_TOKEN_BLOCK
    xb = xn.reshape(n_blk, PEER_TOKEN_BLOCK, d)
    idx_b = idx.reshape(n_blk, PEER_TOKEN_BLOCK, PEER_HEADS * PEER_TOPK)
    g_b = gates.reshape(n_blk, PEER_TOKEN_BLOCK, PEER_HEADS * PEER_TOPK)

    def block(args):
        xt, it, gt = args
        u = expert_u[it]
        v = expert_v[it]
        act = jax.nn.gelu(jnp.einsum("td,tkd->tk", xt, u, preferred_element_type=jnp.float32))
        return jnp.einsum("tk,tkd->td", (gt * act).astype(v.dtype), v)

    out = lax.map(block, (xb, idx_b, g_b))
    return out.reshape(b, s, d)


def setup_inputs(seed: int = 0) -> dict:
    key = jax.random.key(seed)
    ks = jax.random.split(key, 26)
    L = DEPTH

    def normal(k, shape, scale):
        return jax.random.normal(k, shape, jnp.float32) * scale

    def gain(k, shape):
        return 1.0 + 0.01 * jax.random.normal(k, shape, jnp.float32)

    def dt_bias(k):
        dt = jnp.exp(jax.random.uniform(k, (L, SSM_HEADS), jnp.float32,
                                        minval=math.log(1e-3), maxval=math.log(1e-1)))
        return dt + jnp.log(-jnp.expm1(-dt))

    x = normal(ks[0], (BATCH, SEQ, D_MODEL), 1.0)
    positions = (jax.random.randint(ks[1], (BATCH, 1), 0, 4096, dtype=jnp.int32)
                 + jnp.arange(SEQ, dtype=jnp.int32)[None, :])
    return {
        "x": x,
        "positions": positions,
        "norm_mix": gain(ks[2], (L, D_MODEL)),
        "w_in": normal(ks[3], (L, D_MODEL, IN_WIDTH), D_MODEL ** -0.5),
        "q_a_norm": gain(ks[4], (L, Q_LORA_RANK)),
        "w_uq": normal(ks[5], (L, Q_LORA_RANK, MLA_HEADS * QK_HEAD_DIM), Q_LORA_RANK ** -0.5),
        "kv_a_norm": gain(ks[6], (L, KV_LORA_RANK)),
        "w_ukv": normal(ks[7], (L, KV_LORA_RANK, MLA_HEADS * (QK_NOPE_DIM + V_HEAD_DIM)), KV_LORA_RANK ** -0.5),
        "q_norm": gain(ks[8], (L, QK_HEAD_DIM)),
        "k_norm": gain(ks[9], (L, QK_HEAD_DIM)),
        "attn_out_norm": gain(ks[10], (L, MLA_HEADS, V_HEAD_DIM)),
        "conv_w": normal(ks[11], (L, D_CONV, 1, CONV_DIM), D_CONV ** -0.5),
        "conv_b": normal(ks[12], (L, CONV_DIM), 0.01),
        "a_log_fwd": jnp.log(jax.random.uniform(ks[13], (L, SSM_HEADS), jnp.float32, minval=1.0, maxval=16.0)),
        "a_log_bwd": jnp.log(jax.random.uniform(ks[14], (L, SSM_HEADS), jnp.float32, minval=1.0, maxval=16.0)),
        "dt_bias_fwd": dt_bias(ks[15]),
        "dt_bias_bwd": dt_bias(ks[16]),
        "d_skip": gain(ks[17], (L, SSM_HEADS)),
        "ssm_out_norm": gain(ks[18], (L, SSM_WIDTH)),
        "w_out": normal(ks[19], (L, MIX_WIDTH, D_MODEL), MIX_WIDTH ** -0.5),
        "norm_ffn": gain(ks[20], (L, D_MODEL)),
        "w_query": normal(ks[21], (L, D_MODEL, PEER_HEADS * PEER_QDIM), D_MODEL ** -0.5),
        "sub_keys": normal(ks[22], (L, PEER_HEADS, 2, N_KEYS, PEER_QDIM // 2), (PEER_QDIM // 2) ** -0.5),
        "expert_u": normal(ks[23], (L, N_EXPERTS, D_MODEL), D_MODEL ** -0.5),
        "expert_v": normal(ks[24], (L, N_EXPERTS, D_MODEL), PEER_TOPK ** -0.5),
    }


def reference(x, positions, norm_mix, w_in, q_a_norm, w_uq, kv_a_norm, w_ukv, q_norm, k_norm,
              attn_out_norm, conv_w, conv_b, a_log_fwd, a_log_bwd, dt_bias_fwd, dt_bias_bwd, d_skip,
              ssm_out_norm, w_out, norm_ffn, w_query, sub_keys, expert_u, expert_v):
    h = x
    for l in range(DEPTH):
        xn = rms_norm(h, norm_mix[l])
        proj = xn @ w_in[l]
        c_q, c_kv, k_rope, z, xbc, dt_f, dt_b = jnp.split(proj, IN_SPLITS, axis=-1)
        attn = mla_mixer(c_q, c_kv, k_rope, positions, q_a_norm[l], w_uq[l], kv_a_norm[l], w_ukv[l],
                         q_norm[l], k_norm[l], attn_out_norm[l])
        ssm = ssd_mixer(z, xbc, dt_f, dt_b, conv_w[l], conv_b[l], a_log_fwd[l], a_log_bwd[l],
                        dt_bias_fwd[l], dt_bias_bwd[l], d_skip[l], ssm_out_norm[l])
        h = h + jnp.concatenate([attn, ssm.astype(attn.dtype)], axis=-1) @ w_out[l]
        hn = rms_norm(h, norm_ffn[l])
        h = h + peer_ffn(hn, w_query[l], sub_keys[l], expert_u[l], expert_v[l]).astype(h.dtype)
    return h.astype(x.dtype)
```

```python
from contextlib import ExitStack
import numpy as np
import concourse.bass as bass
import concourse.mybir as mybir
from concourse.bass_utils import run_bass_kernel_spmd

F32 = mybir.dt.float32
BF16 = mybir.dt.bfloat16
I32 = mybir.dt.int32
AF = mybir.ActivationFunctionType
ALU = mybir.AluOpType
AX = mybir.AxisListType

EPS = 1e-6
D = 4096
T = 2048
TO = 1024
NH = 16
O_KV, O_ROPE, O_DTF, O_DTB, O_X, O_B, O_C, O_Q, O_Z, O_END = 0, 512, 576, 608, 640, 2688, 3712, 4736, 5760, 7808


import types


def _snap(fn):
    if fn.__closure__ is None:
        return fn
    cells = []
    for cl in fn.__closure__:
        try:
            cells.append(types.CellType(cl.cell_contents))
        except ValueError:
            cells.append(cl)
    g = types.FunctionType(fn.__code__, fn.__globals__, fn.__name__, fn.__defaults__, tuple(cells))
    g.__kwdefaults__ = fn.__kwdefaults__
    return g


class Buf:
    __slots__ = ("name", "w", "r", "excl")

    def __init__(self, name="", excl=False):
        self.name = name
        self.w = None
        self.r = {}
        self.excl = excl


def PB():
    return Buf(excl=True)


class Sched:
    NDMA = 24

    def __init__(self, nc, ctx):
        self.nc = nc
        self.engs = ["pe", "dve", "act", "pool", "sp"]
        self.sem = {}
        self.cnt = {}
        for e in self.engs:
            self.sem[e] = ctx.enter_context(nc.semaphore("s_" + e))
            self.cnt[e] = 0
        self.dsem = [ctx.enter_context(nc.semaphore("d%d" % i)) for i in range(self.NDMA)]
        self.dcnt = [0] * self.NDMA
        self.dnext = {"sp": 0, "pool": 0, "act": 0}
        self.drange = {"sp": (0, 14), "pool": (14, 22), "act": (22, 24)}
        self.seen = {e: {} for e in self.engs}
        self.prog = {e: [] for e in self.engs}

    def _semobj(self, key):
        return self.sem[key] if isinstance(key, str) else self.dsem[key]

    def _wait(self, e, key, val):
        if key == "pe" and e == "pe":
            return
        if self.seen[e].get(key, 0) >= val:
            return
        so = self._semobj(key)
        self.prog[e].append(lambda E, so=so, val=val: E.wait_ge(so, val))
        self.seen[e][key] = val

    def _deps(self, e, reads, writes):
        for b in reads:
            if b.w is not None:
                self._wait(e, *b.w)
        for b in writes:
            if b.w is not None:
                self._wait(e, *b.w)
            for k, v in b.r.items():
                self._wait(e, k, v)

    def _commit(self, ev, reads, writes):
        for b in reads:
            if b.r.get(ev[0], 0) < ev[1]:
                b.r[ev[0]] = ev[1]
        for b in writes:
            b.w = ev
            b.r = {}

    def op(self, e, fn, reads=(), writes=()):
        fn = _snap(fn)
        if any(b.excl for b in reads):
            writes = list(writes) + [b for b in reads if b.excl]
            reads = [b for b in reads if not b.excl]
        self._deps(e, reads, writes)
        self.cnt[e] += 1
        so = self.sem[e]
        self.prog[e].append(lambda E, fn=fn, so=so: fn(E).then_inc(so, 1))
        ev = (e, self.cnt[e])
        self._commit(ev, reads, writes)
        return ev

    def dma(self, q, out, in_, reads=(), writes=(), **kw):
        lo, hi = self.drange[q]
        k = lo + self.dnext[q]
        self.dnext[q] = (self.dnext[q] + 1) % (hi - lo)
        if self.dcnt[k] > 0:
            self._wait(q, k, self.dcnt[k])
        self._deps(q, reads, writes)
        self.dcnt[k] += 16
        so = self.dsem[k]
        self.prog[q].append(
            lambda E, out=out, in_=in_, kw=kw, so=so: E.dma_start(out=out, in_=in_, **kw).then_inc(so, 16))
        ev = (k, self.dcnt[k])
        self._commit(ev, reads, writes)
        return ev

    def barrier(self):
        for e in self.engs:
            for o in self.engs:
                if o != e and self.cnt[o] > 0:
                    self._wait(e, o, self.cnt[o])
            for k in range(self.NDMA):
                if self.dcnt[k] > 0:
                    self._wait(e, k, self.dcnt[k])

    def emit(self, block):
        reg = {"pe": block.tensor, "dve": block.vector, "act": block.scalar, "pool": block.gpsimd, "sp": block.sync}
        for e in self.engs:
            lst = self.prog[e]
            if not lst:
                continue

            def body(E, lst=lst):
                for f in lst:
                    f(E)
            reg[e](body)


def bcast_rows(ap_1d, nparts):
    return ap_1d.partition_broadcast(nparts)


_rope_id = [0]


def rope_ops(S, src, B_src, dst, B_dst, cs, B_cs, tile, sb, c, tag, nheads):
    _rope_id[0] += 1
    nm = "rp%d_" % _rope_id[0]
    if nheads == 1:
        shp = [128, 32]
        t1, t2 = src[:, 0:32], src[:, 32:64]
        d1, d2 = dst[:, 0:32], dst[:, 32:64]
        co, si = cs[:, tile, 0:32], cs[:, tile, 32:64]
    else:
        shp = [128, nheads, 32]
        t1, t2 = src[:, :, 0:32], src[:, :, 32:64]
        d1, d2 = dst[:, :, 0:32], dst[:, :, 32:64]
        co = cs[:, tile, 0:32].unsqueeze(1).broadcast_to(shp)
        si = cs[:, tile, 32:64].unsqueeze(1).broadcast_to(shp)
    a = sb(c, nm + "a", shp)
    b = sb(c, nm + "b", shp)
    Ba, Bb = Buf(), Buf()
    S.op("dve", lambda E: E.tensor_tensor(out=a[:], in0=t1, in1=co, op=ALU.mult), reads=[B_src, B_cs], writes=[Ba])
    S.op("dve", lambda E: E.tensor_tensor(out=b[:], in0=t2, in1=si, op=ALU.mult), reads=[B_src, B_cs], writes=[Bb])
    S.op("dve", lambda E: E.tensor_tensor(out=d1, in0=a[:], in1=b[:], op=ALU.subtract), reads=[Ba, Bb], writes=[B_dst])
    S.op("dve", lambda E: E.tensor_tensor(out=a[:], in0=t1, in1=si, op=ALU.mult), reads=[B_src, B_cs, B_dst], writes=[Ba])
    S.op("dve", lambda E: E.tensor_tensor(out=b[:], in0=t2, in1=co, op=ALU.mult), reads=[B_src, B_cs, B_dst], writes=[Bb])
    S.op("dve", lambda E: E.tensor_tensor(out=d2, in0=a[:], in1=b[:], op=ALU.add), reads=[Ba, Bb], writes=[B_dst])

class K:
    pass


def build(stage=99):
    nc = bass.Bass("TRN2", target_bir_lowering=False)
    k = K()
    k.nc = nc

    def din(name, shape, dt=F32):
        return nc.dram_tensor(name, list(shape), dt, kind="ExternalInput").ap()

    def dscr(name, shape, dt=F32):
        return nc.dram_tensor(name, list(shape), dt, kind="Internal").ap()

    x = din("x", [T, D])
    norm_mix = din("norm_mix", [D])
    w_in = din("w_in", [D, O_END])
    ident_in = din("ident", [128, 128])
    pos_in = din("pos", [T], I32)
    invf_in = din("invf", [32])
    q_a_norm = din("q_a_norm", [1024])
    w_uq = din("w_uq", [1024, 3072])
    kv_a_norm = din("kv_a_norm", [512])
    w_ukv = din("w_ukv", [512, 4096])
    q_norm = din("q_norm", [192])
    k_norm = din("k_norm", [192])
    aon = din("aon", [2048])
    s_mix = dscr("s_mix", [TO, D], BF16)
    conv_w = din("conv_w", [4096, 5])
    conv_b = din("conv_b", [4096])
    a_log = din("a_log", [64])
    dt_bias = din("dt_bias", [64])
    d_skip = din("d_skip", [32])
    son = din("son", [2048])
    tri_in = din("tri", [4, 128, 128])
    w_out = din("w_out", [D, D])
    norm_ffn = din("norm_ffn", [D])
    w_query = din("w_query", [D, 2048])
    skT_in = din("skT", [16, 128, 128])
    UTt = din("UTt", [128, 128, 32, 128])
    Vexp = din("Vexp", [16384, D])
    s_hnT = dscr("s_hnT", [D, TO], BF16)
    s_WT = dscr("s_WT", [128, 128, TO], BF16)
    out = nc.dram_tensor("out", [TO, D], F32, kind="ExternalOutput").ap()

    s_kv = dscr("s_kv", [T, 640])
    s_xT = dscr("s_xT", [4096, T], BF16)
    s_q = dscr("s_q", [TO, 1024])
    s_z = dscr("s_z", [TO, 2048])

    dbg = {}
    if stage == 1:
        dbg["d_kv"] = nc.dram_tensor("d_kv", [T, 640], F32, kind="ExternalOutput").ap()
        dbg["d_xT"] = nc.dram_tensor("d_xT", [4096, T], BF16, kind="ExternalOutput").ap()
        dbg["d_q"] = nc.dram_tensor("d_q", [TO, 1024], F32, kind="ExternalOutput").ap()
        dbg["d_z"] = nc.dram_tensor("d_z", [TO, 2048], F32, kind="ExternalOutput").ap()
        s_kv, s_xT, s_q, s_z = dbg["d_kv"], dbg["d_xT"], dbg["d_q"], dbg["d_z"]

    if stage == 4:
        dbg["d_h"] = nc.dram_tensor("d_h", [TO, D], F32, kind="ExternalOutput").ap()
    if stage in (2, 3):
        dbg["d_mix"] = nc.dram_tensor("d_mix", [TO, D], BF16, kind="ExternalOutput").ap()
        s_mix = dbg["d_mix"]

    with ExitStack() as ctx:
        S = Sched(nc, ctx)
        uid = [0]

        def sb(c, name, shape, dt=F32):
            uid[0] += 1
            return c.enter_context(nc.sbuf_tensor("%s_%d" % (name, uid[0]), list(shape), dt))

        def ps(c, name, shape, dt=F32):
            uid[0] += 1
            return c.enter_context(nc.psum_tensor("%s_%d" % (name, uid[0]), list(shape), dt))

        ident = sb(ctx, "ident_sb", [128, 128], BF16)
        B_ident = Buf("ident")
        S.dma("pool", ident[:], ident_in, writes=[B_ident])
        outbufs = []
        B_mix = Buf()
        B_swt = Buf()
        B_fin = Buf()

        B_scr = Buf()
        import os as _os0
        SKIP = _os0.environ.get("KSKIP", "")
        with ExitStack() as c:
          if "A" not in SKIP:
              gain = sb(c, "gain", [128, D])
              B_gain = Buf()
              S.dma("sp", gain[:], bcast_rows(norm_mix, 128), writes=[B_gain])
              xt = [sb(c, "xt%d" % i, [128, D]) for i in range(2)]
              B_xt = [Buf() for _ in range(2)]
              junk = sb(c, "junk", [128, D], BF16)
              B_junk = Buf()
              xb = sb(c, "xb", [128, D], BF16)
              B_xb = Buf()
              st = sb(c, "stat", [128, 8])
              B_st = Buf()
              xnT = sb(c, "xnT", [128, 32, 512], BF16)
              B_xnT = [Buf() for _ in range(4)]
              wblk = [sb(c, "wblk%d" % i, [128, 32, 512], BF16) for i in range(2)]
              B_w = [Buf() for _ in range(2)]
              ost = [sb(c, "ost%d" % i, [128, 512]) for i in range(3)]
              B_ost = [Buf() for _ in range(3)]
              ostb = [sb(c, "ostb%d" % i, [128, 512], BF16) for i in range(3)]
              B_ostb = [Buf() for _ in range(3)]
              tp = [ps(c, "tp%d" % i, [128, 8, 128], BF16) for i in range(2)]
              B_tp = [PB() for _ in range(2)]
              acc = [ps(c, "acc%d" % i, [128, 512]) for i in range(4)]
              B_acc = [PB() for _ in range(4)]
              w_v = w_in.rearrange("(k p) c -> p k c", p=128)
              x_v = x.rearrange("(n p) d -> n p d", p=128)

              blocks = [(O_KV, 512, "kv"), (O_ROPE, 128, "kv")]
              blocks += [(O_X + 512 * i, 512, "feat") for i in range(8)]
              nb_other = len(blocks)
              blocks += [(O_Q + 512 * i, 512, "q") for i in range(2)]
              blocks += [(O_Z + 512 * i, 512, "z") for i in range(4)]
              wi = 0
              ai = 0
              oi = 0
              for tb in range(4):
                  for tt in range(4):
                      tile = tb * 4 + tt
                      xi = tile % 2
                      S.dma("sp", xt[xi][:], x_v[tile], writes=[B_xt[xi]])
                      S.op("act", lambda E, xi=xi: E.activation(out=junk[:], in_=xt[xi][:], func=AF.Square,
                                                                accum_out=st[:, 0:1]),
                           reads=[B_xt[xi]], writes=[B_junk, B_st])
                      S.op("dve", lambda E: E.tensor_scalar(out=st[:, 1:2], in0=st[:, 0:1], scalar1=1.0 / D,
                                                            scalar2=EPS, op0=ALU.mult, op1=ALU.add),
                           reads=[B_st], writes=[B_st])
                      S.op("act", lambda E: E.activation(out=st[:, 2:3], in_=st[:, 1:2], func=AF.Sqrt),
                           reads=[B_st], writes=[B_st])
                      S.op("dve", lambda E: E.reciprocal(out=st[:, 3:4], in_=st[:, 2:3]), reads=[B_st], writes=[B_st])
                      S.op("dve", lambda E, xi=xi: E.scalar_tensor_tensor(out=xb[:], in0=xt[xi][:], scalar=st[:, 3:4],
                                                                          in1=gain[:], op0=ALU.mult, op1=ALU.mult),
                           reads=[B_xt[xi], B_st, B_gain], writes=[B_xb])
                      for g in range(4):
                          ti = g % 2
                          for j in range(8):
                              kk = g * 8 + j
                              S.op("pe", lambda E, ti=ti, j=j, kk=kk: E.transpose(out=tp[ti][:, j, :],
                                                                                 in_=xb[:, kk * 128:(kk + 1) * 128],
                                                                                 identity=ident[:]),
                                   reads=[B_xb, B_ident], writes=[B_tp[ti]])
                          eng = "act" if g % 2 == 0 else "dve"
                          if eng == "act":
                              S.op("act", lambda E, ti=ti, g=g, tt=tt: E.copy(
                                  out=xnT[:, g * 8:(g + 1) * 8, tt * 128:(tt + 1) * 128], in_=tp[ti][:]),
                                  reads=[B_tp[ti]], writes=[B_xnT[tt]])
                          else:
                              S.op("dve", lambda E, ti=ti, g=g, tt=tt: E.tensor_copy(
                                  out=xnT[:, g * 8:(g + 1) * 8, tt * 128:(tt + 1) * 128], in_=tp[ti][:]),
                                  reads=[B_tp[ti]], writes=[B_xnT[tt]])
                  blks = blocks if tb < 2 else blocks[:nb_other]
                  for (c0, cw, kind) in blks:
                      wb = wblk[wi % 2]
                      Bw = B_w[wi % 2]
                      wi += 1
                      S.dma("pool", wb[:, :, 0:cw], w_v[:, :, c0:c0 + cw], writes=[Bw])
                      if kind == "feat":
                          for cc in range(cw // 128):
                              a = acc[ai % 4]
                              Ba = B_acc[ai % 4]
                              ai += 1
                              for kk in range(32):
                                  S.op("pe", lambda E, a=a, wb=wb, kk=kk, cc=cc: E.matmul(
                                      a[:], lhsT=wb[:, kk, cc * 128:(cc + 1) * 128], rhs=xnT[:, kk, :],
                                      start=(kk == 0), stop=(kk == 31)),
                                      reads=[Bw] + B_xnT, writes=[Ba])
                              o = ostb[oi % 3]
                              Bo = B_ostb[oi % 3]
                              if oi % 2 == 0:
                                  S.op("act", lambda E, o=o, a=a: E.copy(out=o[:], in_=a[:]), reads=[Ba], writes=[Bo])
                              else:
                                  S.op("dve", lambda E, o=o, a=a: E.tensor_copy(out=o[:], in_=a[:]), reads=[Ba], writes=[Bo])
                              oi += 1
                              r0 = c0 - O_X + cc * 128
                              S.dma("sp", s_xT[r0:r0 + 128, tb * 512:(tb + 1) * 512], o[:], reads=[Bo], writes=[B_scr])
                      else:
                          for tt in range(4):
                              a = acc[ai % 4]
                              Ba = B_acc[ai % 4]
                              ai += 1
                              for kk in range(32):
                                  S.op("pe", lambda E, a=a, wb=wb, kk=kk, tt=tt, cw=cw: E.matmul(
                                      a[:, 0:cw], lhsT=xnT[:, kk, tt * 128:(tt + 1) * 128], rhs=wb[:, kk, 0:cw],
                                      start=(kk == 0), stop=(kk == 31)),
                                      reads=[Bw, B_xnT[tt]], writes=[Ba])
                              o = ost[oi % 3]
                              Bo = B_ost[oi % 3]
                              if oi % 2 == 0:
                                  S.op("act", lambda E, o=o, a=a, cw=cw: E.copy(out=o[:, 0:cw], in_=a[:, 0:cw]),
                                       reads=[Ba], writes=[Bo])
                              else:
                                  S.op("dve", lambda E, o=o, a=a, cw=cw: E.tensor_copy(out=o[:, 0:cw], in_=a[:, 0:cw]),
                                       reads=[Ba], writes=[Bo])
                              oi += 1
                              t0 = tb * 512 + tt * 128
                              if kind == "kv":
                                  dst = s_kv[t0:t0 + 128, c0:c0 + cw]
                              elif kind == "q":
                                  dst = s_q[t0:t0 + 128, c0 - O_Q:c0 - O_Q + cw]
                              else:
                                  dst = s_z[t0:t0 + 128, c0 - O_Z:c0 - O_Z + cw]
                              S.dma("sp", dst, o[:, 0:cw], reads=[Bo], writes=[B_scr])
          outbufs.append(B_scr)
          S.barrier()

        if stage >= 2 and "B" not in SKIP:
          with ExitStack() as c:
            TWO_PI = 2.0 * np.pi
            C1 = 6.28125
            C2 = float(np.float32(TWO_PI - C1).view(np.uint32) & np.uint32(0xFFFFF000)) if False else 0.0019350051879882812
            C3 = float(TWO_PI - C1 - C2)
            MAGIC = 12582912.0
            SCALE = 192.0 ** -0.5
            qan = sb(c, "qan", [128, 1024]); kvan = sb(c, "kvan", [128, 512])
            qn_bc = sb(c, "qn_bc", [128, 192]); kn_bc = sb(c, "kn_bc", [128, 192])
            aon_bc = sb(c, "aon_bc", [128, 2048]); invf = sb(c, "invf_sb", [128, 32])
            B_const = Buf()
            for dst, src in ((qan, q_a_norm), (kvan, kv_a_norm), (qn_bc, q_norm), (kn_bc, k_norm), (aon_bc, aon), (invf, invf_in)):
                S.dma("sp", dst[:], src.partition_broadcast(128), writes=[B_const])
            posi = sb(c, "posi", [128, 16], I32)
            S.dma("sp", posi[:], pos_in.rearrange("(n p) -> p n", p=128), writes=[B_const], allow_slow_non_contiguous=True) if False else None
            posf = sb(c, "posf", [128, 16])
            ang = sb(c, "ang", [128, 16, 32]); kk_t = sb(c, "kk_t", [128, 16, 32]); rr = sb(c, "rr", [128, 16, 32])
            cs = sb(c, "cs", [128, 16, 64])
            B_cs = Buf()
            pos_t = sb(c, "pos_t", [16, 128], I32)
            for n in range(16):
                S.dma("sp", posi[:, n:n + 1], pos_in[n * 128:(n + 1) * 128].rearrange("(p o) -> p o", o=1), writes=[B_const])
            S.op("dve", lambda E: E.tensor_copy(out=posf[:], in_=posi[:]), reads=[B_const], writes=[B_cs])
            S.op("dve", lambda E: E.tensor_tensor(out=ang[:], in0=posf[:].unsqueeze(2).broadcast_to([128, 16, 32]),
                                                  in1=invf[:].unsqueeze(1).broadcast_to([128, 16, 32]), op=ALU.mult),
                 reads=[B_const, B_cs], writes=[B_cs])
            for which, shift in ((1, 0.0), (0, 0.25)):
                S.op("dve", lambda E, shift=shift: E.tensor_scalar(out=kk_t[:], in0=ang[:], scalar1=1.0 / TWO_PI, scalar2=shift,
                                                                   op0=ALU.mult, op1=ALU.add), reads=[B_cs], writes=[B_cs])
                S.op("dve", lambda E: E.tensor_scalar(out=kk_t[:], in0=kk_t[:], scalar1=MAGIC, scalar2=None, op0=ALU.add),
                     reads=[B_cs], writes=[B_cs])
                S.op("dve", lambda E: E.tensor_scalar(out=kk_t[:], in0=kk_t[:], scalar1=-MAGIC, scalar2=None, op0=ALU.add),
                     reads=[B_cs], writes=[B_cs])
                S.op("dve", lambda E: E.scalar_tensor_tensor(out=rr[:], in0=kk_t[:], scalar=-C1, in1=ang[:], op0=ALU.mult, op1=ALU.add),
                     reads=[B_cs], writes=[B_cs])
                S.op("dve", lambda E: E.scalar_tensor_tensor(out=rr[:], in0=kk_t[:], scalar=-C2, in1=rr[:], op0=ALU.mult, op1=ALU.add),
                     reads=[B_cs], writes=[B_cs])
                S.op("dve", lambda E: E.scalar_tensor_tensor(out=rr[:], in0=kk_t[:], scalar=-C3, in1=rr[:], op0=ALU.mult, op1=ALU.add),
                     reads=[B_cs], writes=[B_cs])
                if shift != 0.0:
                    S.op("dve", lambda E: E.tensor_scalar(out=rr[:], in0=rr[:], scalar1=float(np.pi / 2), scalar2=None, op0=ALU.add),
                         reads=[B_cs], writes=[B_cs])
                S.op("dve", lambda E: E.tensor_scalar(out=rr[:], in0=rr[:], scalar1=3.1415925, scalar2=-3.1415925,
                                                      op0=ALU.min, op1=ALU.max), reads=[B_cs], writes=[B_cs])
                S.op("act", lambda E, which=which: E.activation(out=cs[:, :, which * 32:(which + 1) * 32], in_=rr[:], func=AF.Sin),
                     reads=[B_cs], writes=[B_cs])

            ckvT = sb(c, "ckvT", [128, 4, T], BF16); B_ckvT = Buf()
            cqT = sb(c, "cqT", [128, 8, TO], BF16); B_cqT = Buf()
            krr = sb(c, "krr", [128, 16, 64]); B_krr = Buf()
            ssr = sb(c, "ssr", [128, 16]); B_ssr = Buf()
            stB = sb(c, "stB", [128, 16]); B_stB = Buf()
            with ExitStack() as c2:
                lt = [sb(c2, "lt%d" % i, [128, 1024]) for i in range(2)]; B_lt = [Buf(), Buf()]
                jk = sb(c2, "jkB", [128, 1024], BF16); B_jk = Buf()
                nb = sb(c2, "nbB", [128, 1024], BF16); B_nb = Buf()
                tq = sb(c2, "tqB", [128, 64]); B_tq = Buf()
                tpB = [ps(c2, "tpB%d" % i, [128, 8, 128], BF16) for i in range(2)]; B_tpB = [PB(), PB()]
                for tile in range(16):
                    li = tile % 2
                    S.dma("sp", lt[li][:, 0:576], s_kv[tile * 128:(tile + 1) * 128, 0:576], reads=[B_scr], writes=[B_lt[li]])
                    S.op("act", lambda E, li=li: E.activation(out=jk[:, 0:512], in_=lt[li][:, 0:512], func=AF.Square,
                                                              accum_out=stB[:, 0:1]), reads=[B_lt[li]], writes=[B_jk, B_stB])
                    S.op("act", lambda E, li=li, tile=tile: E.activation(out=jk[:, 512:576], in_=lt[li][:, 512:576], func=AF.Square,
                                                                         accum_out=ssr[:, tile:tile + 1]),
                         reads=[B_lt[li]], writes=[B_jk, B_ssr])
                    S.op("dve", lambda E: E.tensor_scalar(out=stB[:, 1:2], in0=stB[:, 0:1], scalar1=1.0 / 512, scalar2=EPS,
                                                          op0=ALU.mult, op1=ALU.add), reads=[B_stB], writes=[B_stB])
                    S.op("act", lambda E: E.activation(out=stB[:, 2:3], in_=stB[:, 1:2], func=AF.Sqrt), reads=[B_stB], writes=[B_stB])
                    S.op("dve", lambda E: E.reciprocal(out=stB[:, 3:4], in_=stB[:, 2:3]), reads=[B_stB], writes=[B_stB])
                    S.op("dve", lambda E, li=li: E.scalar_tensor_tensor(out=nb[:, 0:512], in0=lt[li][:, 0:512], scalar=stB[:, 3:4],
                                                                        in1=kvan[:], op0=ALU.mult, op1=ALU.mult),
                         reads=[B_lt[li], B_stB, B_const], writes=[B_nb])
                    ti = tile % 2
                    for j in range(4):
                        S.op("pe", lambda E, ti=ti, j=j: E.transpose(out=tpB[ti][:, j, :], in_=nb[:, j * 128:(j + 1) * 128],
                                                                     identity=ident[:]), reads=[B_nb, B_ident], writes=[B_tpB[ti]])
                    S.op("act", lambda E, ti=ti, tile=tile: E.copy(out=ckvT[:, :, tile * 128:(tile + 1) * 128], in_=tpB[ti][:, 0:4, :]),
                         reads=[B_tpB[ti]], writes=[B_ckvT])
                    S.op("dve", lambda E, li=li: E.tensor_tensor(out=tq[:], in0=lt[li][:, 512:576], in1=kn_bc[:, 128:192], op=ALU.mult),
                         reads=[B_lt[li], B_const], writes=[B_tq])
                    rope_ops(S, tq, B_tq, krr[:, tile, :], B_krr, cs, B_cs, tile, sb, c2, "k%d" % tile, nheads=1)
                for tile in range(8):
                    li = tile % 2
                    S.dma("sp", lt[li][:], s_q[tile * 128:(tile + 1) * 128, :], reads=[B_scr], writes=[B_lt[li]])
                    S.op("act", lambda E, li=li: E.activation(out=jk[:], in_=lt[li][:], func=AF.Square, accum_out=stB[:, 0:1]),
                         reads=[B_lt[li]], writes=[B_jk, B_stB])
                    S.op("dve", lambda E: E.tensor_scalar(out=stB[:, 1:2], in0=stB[:, 0:1], scalar1=1.0 / 1024, scalar2=EPS,
                                                          op0=ALU.mult, op1=ALU.add), reads=[B_stB], writes=[B_stB])
                    S.op("act", lambda E: E.activation(out=stB[:, 2:3], in_=stB[:, 1:2], func=AF.Sqrt), reads=[B_stB], writes=[B_stB])
                    S.op("dve", lambda E: E.reciprocal(out=stB[:, 3:4], in_=stB[:, 2:3]), reads=[B_stB], writes=[B_stB])
                    S.op("dve", lambda E, li=li: E.scalar_tensor_tensor(out=nb[:], in0=lt[li][:], scalar=stB[:, 3:4], in1=qan[:],
                                                                        op0=ALU.mult, op1=ALU.mult),
                         reads=[B_lt[li], B_stB, B_const], writes=[B_nb])
                    ti = tile % 2
                    for j in range(8):
                        S.op("pe", lambda E, ti=ti, j=j: E.transpose(out=tpB[ti][:, j, :], in_=nb[:, j * 128:(j + 1) * 128],
                                                                     identity=ident[:]), reads=[B_nb, B_ident], writes=[B_tpB[ti]])
                    S.op("act", lambda E, ti=ti, tile=tile: E.copy(out=cqT[:, :, tile * 128:(tile + 1) * 128], in_=tpB[ti][:]),
                         reads=[B_tpB[ti]], writes=[B_cqT])
                S.barrier()

            HG = 4
            KT = sb(c, "KT", [128, HG, T], BF16); B_KT = Buf()
            KTr = sb(c, "KTr", [64, HG, T], BF16); B_KTr = Buf()
            vext = sb(c, "vext", [128, 16, HG, 130], BF16); B_vext = Buf()
            QT = sb(c, "QT", [128, HG, TO], BF16); B_QT = Buf()
            QTr = sb(c, "QTr", [64, HG, TO], BF16); B_QTr = Buf()
            wkv = sb(c, "wkv", [128, 4, HG * 256], BF16); B_wkv = Buf()
            wq = sb(c, "wq", [128, 8, HG * 192], BF16); B_wq = Buf()
            S.op("pool", lambda E: E.memset(vext[:], 1.0), writes=[B_vext])
            for hg in range(NH // HG):
                S.dma("pool", wkv[:], w_ukv.rearrange("(k p) c -> p k c", p=128)[:, :, hg * HG * 256:(hg + 1) * HG * 256],
                      writes=[B_wkv])
                S.dma("pool", wq[:], w_uq.rearrange("(k p) c -> p k c", p=128)[:, :, hg * HG * 192:(hg + 1) * HG * 192],
                      writes=[B_wq])
                with ExitStack() as c2:
                    pk = [ps(c2, "pk%d" % i, [128, 512]) for i in range(4)]; B_pk = [PB() for _ in range(4)]
                    tpk = [ps(c2, "tpk%d" % i, [128, 8, 128], BF16) for i in range(2)]; B_tpk = [PB(), PB()]
                    tpr = [ps(c2, "tpr%d" % i, [128, 8, 128], BF16) for i in range(2)]; B_tpr = [PB(), PB()]
                    jk = sb(c2, "jkK", [128, 192], BF16); B_jk = Buf()
                    sk = [sb(c2, "sk%d" % i, [128, 16]) for i in range(2)]; B_sk = [Buf(), Buf()]
                    kn = [sb(c2, "kn%d" % i, [128, HG, 192], BF16) for i in range(2)]; B_kn = [Buf(), Buf()]
                    for tile in range(16):
                        pi = tile % 2
                        for b in range(2):
                            for kc in range(4):
                                S.op("pe", lambda E, pi=pi, b=b, kc=kc, tile=tile: E.matmul(
                                    pk[pi * 2 + b][:], lhsT=ckvT[:, kc, tile * 128:(tile + 1) * 128],
                                    rhs=wkv[:, kc, b * 512:(b + 1) * 512], start=(kc == 0), stop=(kc == 3)),
                                    reads=[B_ckvT, B_wkv], writes=[B_pk[pi * 2 + b]])
                        st_ = sk[pi]; Bs = B_sk[pi]
                        for hl in range(HG):
                            p_ = pk[pi * 2 + hl // 2]; Bp = B_pk[pi * 2 + hl // 2]; off = (hl % 2) * 256
                            S.op("act", lambda E, p_=p_, off=off, st_=st_, hl=hl: E.activation(
                                out=jk[:, 0:128], in_=p_[:, off:off + 128], func=AF.Square, accum_out=st_[:, hl:hl + 1]),
                                reads=[Bp], writes=[B_jk, Bs])
                        S.op("dve", lambda E, st_=st_, tile=tile: E.tensor_scalar(
                            out=st_[:, 4:8], in0=st_[:, 0:4], scalar1=ssr[:, tile:tile + 1], scalar2=1.0 / 192,
                            op0=ALU.add, op1=ALU.mult), reads=[Bs, B_ssr], writes=[Bs])
                        S.op("dve", lambda E, st_=st_: E.tensor_scalar(out=st_[:, 4:8], in0=st_[:, 4:8], scalar1=EPS, scalar2=None,
                                                                       op0=ALU.add), reads=[Bs], writes=[Bs])
                        S.op("act", lambda E, st_=st_: E.activation(out=st_[:, 8:12], in_=st_[:, 4:8], func=AF.Sqrt), reads=[Bs], writes=[Bs])
                        S.op("dve", lambda E, st_=st_: E.reciprocal(out=st_[:, 12:16], in_=st_[:, 8:12]), reads=[Bs], writes=[Bs])
                        kn_ = kn[pi]; Bk = B_kn[pi]
                        for hl in range(HG):
                            p_ = pk[pi * 2 + hl // 2]; Bp = B_pk[pi * 2 + hl // 2]; off = (hl % 2) * 256
                            S.op("dve", lambda E, p_=p_, off=off, st_=st_, hl=hl, kn_=kn_: E.scalar_tensor_tensor(
                                out=kn_[:, hl, 0:128], in0=p_[:, off:off + 128], scalar=st_[:, 12 + hl:13 + hl], in1=kn_bc[:, 0:128],
                                op0=ALU.mult, op1=ALU.mult), reads=[Bp, Bs, B_const], writes=[Bk])
                            S.op("dve", lambda E, st_=st_, hl=hl, kn_=kn_, tile=tile: E.tensor_scalar(
                                out=kn_[:, hl, 128:192], in0=krr[:, tile, :], scalar1=st_[:, 12 + hl:13 + hl], scalar2=None, op0=ALU.mult),
                                reads=[B_krr, Bs], writes=[Bk])
                            S.op("act", lambda E, p_=p_, off=off, hl=hl, tile=tile: E.copy(
                                out=vext[:, tile, hl, 0:128], in_=p_[:, off + 128:off + 256]), reads=[Bp], writes=[B_vext])
                        for hl in range(HG):
                            S.op("pe", lambda E, pi=pi, hl=hl, kn_=kn_: E.transpose(out=tpk[pi][:, hl, :], in_=kn_[:, hl, 0:128],
                                                                                   identity=ident[:]), reads=[Bk, B_ident], writes=[B_tpk[pi]])
                            S.op("pe", lambda E, pi=pi, hl=hl, kn_=kn_: E.transpose(out=tpr[pi][0:64, hl, :], in_=kn_[:, hl, 128:192],
                                                                                   identity=ident[:]), reads=[Bk, B_ident], writes=[B_tpr[pi]])
                        S.op("dve", lambda E, pi=pi, tile=tile: E.tensor_copy(out=KT[:, :, tile * 128:(tile + 1) * 128], in_=tpk[pi][:, 0:4, :]),
                             reads=[B_tpk[pi]], writes=[B_KT])
                        S.op("act", lambda E, pi=pi, tile=tile: E.copy(out=KTr[:, :, tile * 128:(tile + 1) * 128], in_=tpr[pi][0:64, 0:4, :]),
                             reads=[B_tpr[pi]], writes=[B_KTr])
                    S.barrier()
                with ExitStack() as c2:
                    pq = [ps(c2, "pq%d" % i, [128, 512]) for i in range(4)]; B_pq = [PB() for _ in range(4)]
                    tpk = [ps(c2, "tpq%d" % i, [128, 8, 128], BF16) for i in range(2)]; B_tpk = [PB(), PB()]
                    tpr = [ps(c2, "tpqr%d" % i, [128, 8, 128], BF16) for i in range(2)]; B_tpr = [PB(), PB()]
                    jk = sb(c2, "jkQ", [128, 192], BF16); B_jk = Buf()
                    sk = [sb(c2, "sq%d" % i, [128, 16]) for i in range(2)]; B_sk = [Buf(), Buf()]
                    qg = [sb(c2, "qg%d" % i, [128, HG, 192]) for i in range(2)]; B_qg = [Buf(), Buf()]
                    qn_ = [sb(c2, "qn%d" % i, [128, HG, 192], BF16) for i in range(2)]; B_qn = [Buf(), Buf()]
                    for tile in range(8):
                        pi = tile % 2
                        for b in range(2):
                            for kc in range(8):
                                S.op("pe", lambda E, pi=pi, b=b, kc=kc, tile=tile: E.matmul(
                                    pq[pi * 2 + b][:, 0:384], lhsT=cqT[:, kc, tile * 128:(tile + 1) * 128],
                                    rhs=wq[:, kc, b * 384:(b + 1) * 384], start=(kc == 0), stop=(kc == 7)),
                                    reads=[B_cqT, B_wq], writes=[B_pq[pi * 2 + b]])
                        st_ = sk[pi]; Bs = B_sk[pi]
                        for hl in range(HG):
                            p_ = pq[pi * 2 + hl // 2]; Bp = B_pq[pi * 2 + hl // 2]; off = (hl % 2) * 192
                            S.op("act", lambda E, p_=p_, off=off, st_=st_, hl=hl: E.activation(
                                out=jk[:], in_=p_[:, off:off + 192], func=AF.Square, accum_out=st_[:, hl:hl + 1]),
                                reads=[Bp], writes=[B_jk, Bs])
                        S.op("dve", lambda E, st_=st_: E.tensor_scalar(out=st_[:, 4:8], in0=st_[:, 0:4], scalar1=1.0 / 192, scalar2=EPS,
                                                                       op0=ALU.mult, op1=ALU.add), reads=[Bs], writes=[Bs])
                        S.op("act", lambda E, st_=st_: E.activation(out=st_[:, 8:12], in_=st_[:, 4:8], func=AF.Sqrt), reads=[Bs], writes=[Bs])
                        S.op("dve", lambda E, st_=st_: E.reciprocal(out=st_[:, 12:16], in_=st_[:, 8:12]), reads=[Bs], writes=[Bs])
                        g_ = qg[pi]; Bg = B_qg[pi]; n_ = qn_[pi]; Bn = B_qn[pi]
                        for hl in range(HG):
                            p_ = pq[pi * 2 + hl // 2]; Bp = B_pq[pi * 2 + hl // 2]; off = (hl % 2) * 192
                            S.op("dve", lambda E, p_=p_, off=off, st_=st_, hl=hl, g_=g_: E.scalar_tensor_tensor(
                                out=g_[:, hl, :], in0=p_[:, off:off + 192], scalar=st_[:, 12 + hl:13 + hl], in1=qn_bc[:],
                                op0=ALU.mult, op1=ALU.mult), reads=[Bp, Bs, B_const], writes=[Bg])
                        S.op("dve", lambda E, g_=g_, n_=n_: E.tensor_copy(out=n_[:, :, 0:128], in_=g_[:, :, 0:128]), reads=[Bg], writes=[Bn])
                        rope_ops(S, g_[:, :, 128:192], Bg, n_[:, :, 128:192], Bn, cs, B_cs, tile, sb, c2, "q%d_%d" % (hg, tile), nheads=HG)
                        for hl in range(HG):
                            S.op("pe", lambda E, pi=pi, hl=hl, n_=n_: E.transpose(out=tpk[pi][:, hl, :], in_=n_[:, hl, 0:128],
                                                                                  identity=ident[:]), reads=[Bn, B_ident], writes=[B_tpk[pi]])
                            S.op("pe", lambda E, pi=pi, hl=hl, n_=n_: E.transpose(out=tpr[pi][0:64, hl, :], in_=n_[:, hl, 128:192],
                                                                                  identity=ident[:]), reads=[Bn, B_ident], writes=[B_tpr[pi]])
                        S.op("dve", lambda E, pi=pi, tile=tile: E.tensor_copy(out=QT[:, :, tile * 128:(tile + 1) * 128], in_=tpk[pi][:, 0:4, :]),
                             reads=[B_tpk[pi]], writes=[B_QT])
                        S.op("act", lambda E, pi=pi, tile=tile: E.copy(out=QTr[:, :, tile * 128:(tile + 1) * 128], in_=tpr[pi][0:64, 0:4, :]),
                             reads=[B_tpr[pi]], writes=[B_QTr])
                    S.barrier()
                with ExitStack() as c2:
                    pS = [ps(c2, "pS%d" % i, [128, 512]) for i in range(3)]; B_pS = [PB() for _ in range(3)]
                    pO = [ps(c2, "pO%d" % i, [128, 512]) for i in range(4)]; B_pO = [PB() for _ in range(4)]
                    PT = [sb(c2, "PT%d" % i, [128, 512], BF16) for i in range(3)]; B_PT = [Buf() for _ in range(3)]
                    of = [sb(c2, "of%d" % i, [128, 128]) for i in range(2)]; B_of = [Buf(), Buf()]
                    jk = sb(c2, "jkA", [128, 128], BF16); B_jk = Buf()
                    sa = [sb(c2, "sa%d" % i, [128, 8]) for i in range(2)]; B_sa = [Buf(), Buf()]
                    ob = [sb(c2, "ob%d" % i, [128, 128], BF16) for i in range(2)]; B_ob = [Buf(), Buf()]
                    si = 0
                    oi2 = 0
                    for hl in range(HG):
                        h = hg * HG + hl
                        for tg in range(2):
                            for tk in range(16):
                                p_ = pS[si % 3]; Bp = B_pS[si % 3]; pt = PT[si % 3]; Bpt = B_PT[si % 3]; si += 1
                                S.op("pe", lambda E, p_=p_, hl=hl, tk=tk, tg=tg: E.matmul(
                                    p_[:], lhsT=KT[:, hl, tk * 128:(tk + 1) * 128], rhs=QT[:, hl, tg * 512:(tg + 1) * 512],
                                    start=True, stop=False), reads=[B_KT, B_QT], writes=[Bp])
                                S.op("pe", lambda E, p_=p_, hl=hl, tk=tk, tg=tg: E.matmul(
                                    p_[:], lhsT=KTr[:, hl, tk * 128:(tk + 1) * 128], rhs=QTr[:, hl, tg * 512:(tg + 1) * 512],
                                    start=False, stop=True), reads=[B_KTr, B_QTr], writes=[Bp])
                                S.op("act", lambda E, p_=p_, pt=pt: E.activation(out=pt[:], in_=p_[:], func=AF.Exp, scale=SCALE),
                                     reads=[Bp], writes=[Bpt])
                                for tqt in range(4):
                                    S.op("pe", lambda E, pt=pt, tqt=tqt, tk=tk, hl=hl: E.matmul(
                                        pO[tqt][:, 0:129], lhsT=pt[:, tqt * 128:(tqt + 1) * 128], rhs=vext[:, tk, hl, 0:129],
                                        start=(tk == 0), stop=(tk == 15)), reads=[Bpt, B_vext], writes=[B_pO[tqt]])
                            for tqt in range(4):
                                o_ = of[oi2 % 2]; Bo = B_of[oi2 % 2]; s_ = sa[oi2 % 2]; Bs = B_sa[oi2 % 2]
                                b_ = ob[oi2 % 2]; Bb = B_ob[oi2 % 2]; oi2 += 1
                                S.op("dve", lambda E, s_=s_, tqt=tqt: E.reciprocal(out=s_[:, 0:1], in_=pO[tqt][:, 128:129]),
                                     reads=[B_pO[tqt]], writes=[Bs])
                                S.op("dve", lambda E, s_=s_, tqt=tqt, o_=o_: E.tensor_scalar(
                                    out=o_[:], in0=pO[tqt][:, 0:128], scalar1=s_[:, 0:1], scalar2=None, op0=ALU.mult),
                                    reads=[B_pO[tqt], Bs], writes=[Bo])
                                S.op("act", lambda E, o_=o_, s_=s_: E.activation(out=jk[:], in_=o_[:], func=AF.Square, accum_out=s_[:, 1:2]),
                                     reads=[Bo], writes=[B_jk, Bs])
                                S.op("dve", lambda E, s_=s_: E.tensor_scalar(out=s_[:, 2:3], in0=s_[:, 1:2], scalar1=1.0 / 128, scalar2=EPS,
                                                                             op0=ALU.mult, op1=ALU.add), reads=[Bs], writes=[Bs])
                                S.op("act", lambda E, s_=s_: E.activation(out=s_[:, 3:4], in_=s_[:, 2:3], func=AF.Sqrt), reads=[Bs], writes=[Bs])
                                S.op("dve", lambda E, s_=s_: E.reciprocal(out=s_[:, 4:5], in_=s_[:, 3:4]), reads=[Bs], writes=[Bs])
                                S.op("dve", lambda E, o_=o_, s_=s_, b_=b_, h=h: E.scalar_tensor_tensor(
                                    out=b_[:], in0=o_[:], scalar=s_[:, 4:5], in1=aon_bc[:, h * 128:(h + 1) * 128], op0=ALU.mult, op1=ALU.mult),
                                    reads=[Bo, Bs, B_const], writes=[Bb])
                                t0 = tg * 512 + tqt * 128
                                S.dma("sp", s_mix[t0:t0 + 128, h * 128:(h + 1) * 128], b_[:], reads=[Bb], writes=[B_mix])
                    S.barrier()
            S.barrier()

        if stage >= 3 and "C" not in SKIP:
          with ExitStack() as c:
            B_cc = Buf()
            tri = sb(c, "tri", [128, 4, 128])
            S.dma("sp", tri[:], tri_in.rearrange("f k l -> k f l"), writes=[B_cc])
            identF = sb(c, "identF", [128, 128])
            S.dma("sp", identF[:], ident_in, writes=[B_cc])
            alog_bc = sb(c, "alog_bc", [128, 64]); dtb_bc = sb(c, "dtb_bc", [128, 64]); dsk_bc = sb(c, "dsk_bc", [128, 32])
            son_bc = sb(c, "son_bc", [128, 2048])
            for dst, src in ((alog_bc, a_log), (dtb_bc, dt_bias), (dsk_bc, d_skip), (son_bc, son)):
                S.dma("sp", dst[:], src.partition_broadcast(128), writes=[B_cc])
            A_bc = sb(c, "A_bc", [128, 64])
            S.op("act", lambda E: E.activation(out=A_bc[:], in_=alog_bc[:], func=AF.Exp), reads=[B_cc], writes=[B_cc])
            S.op("dve", lambda E: E.tensor_scalar(out=A_bc[:], in0=A_bc[:], scalar1=-1.0, scalar2=None, op0=ALU.mult),
                 reads=[B_cc], writes=[B_cc])
            dtr = sb(c, "dtr", [128, 16, 64]); dtv = sb(c, "dtv", [128, 16, 64]); adt = sb(c, "adt", [128, 16, 64])
            tmpd = sb(c, "tmpd", [128, 16, 64])
            B_dt = Buf()
            for tile in range(16):
                S.dma("sp", dtr[:, tile, :], s_kv[tile * 128:(tile + 1) * 128, 576:640], reads=[B_scr], writes=[B_dt])
            bc64 = lambda t_: t_[:].unsqueeze(1).broadcast_to([128, 16, 64])
            S.op("dve", lambda E: E.tensor_tensor(out=dtr[:], in0=dtr[:], in1=bc64(dtb_bc), op=ALU.add), reads=[B_dt, B_cc], writes=[B_dt])
            S.op("act", lambda E: E.activation(out=tmpd[:], in_=dtr[:], func=AF.Abs), reads=[B_dt], writes=[B_dt])
            S.op("act", lambda E: E.activation(out=tmpd[:], in_=tmpd[:], func=AF.Exp, scale=-1.0), reads=[B_dt], writes=[B_dt])
            S.op("act", lambda E: E.activation(out=tmpd[:], in_=tmpd[:], func=AF.Ln, bias=1.0), reads=[B_dt], writes=[B_dt])
            S.op("dve", lambda E: E.scalar_tensor_tensor(out=dtv[:], in0=dtr[:], scalar=0.0, in1=tmpd[:], op0=ALU.max, op1=ALU.add),
                 reads=[B_dt], writes=[B_dt])
            S.op("dve", lambda E: E.tensor_tensor(out=adt[:], in0=dtv[:], in1=bc64(A_bc), op=ALU.mult), reads=[B_dt, B_cc], writes=[B_dt])

            import os as _os
            SUB = int(_os.environ.get("KSUB", "99"))
            cw_v = conv_w.rearrange("(n p) j -> n p j", p=128)
            cb_v = conv_b.rearrange("(n p o) -> n p o", p=128, o=1)
            cin = [sb(c, "cin%d" % i, [128, T + 4], BF16) for i in range(2)]; B_cin = [Buf(), Buf()]
            for i in range(2):
                S.op("pool", lambda E, i=i: E.memset(cin[i][:], 0.0), writes=[B_cin[i]])
            cacc = sb(c, "cacc", [128, T]); B_cacc = Buf()
            cwt = [sb(c, "cwt%d" % i, [128, 8]) for i in range(2)]; B_cwt = [Buf(), Buf()]
            cT = [sb(c, "cT%d" % i, [128, T], BF16) for i in range(4)]; B_cT = [Buf() for _ in range(4)]
            xtok = sb(c, "xtok", [128, 16, 256], BF16); B_xtok = Buf()
            Btok = sb(c, "Btok", [128, 16, 128], BF16); B_Btok = Buf()
            CBT = sb(c, "CBT", [128, 8, 128], BF16); B_CBT = Buf()
            yacc = sb(c, "yacc", [128, 8, 256]); B_yacc = Buf()
            state = sb(c, "state", [128, 256]); B_state = Buf()
            state_bf = sb(c, "state_bf", [128, 256], BF16); B_stbf = Buf()
            P_tp = ps(c, "P_tp", [128, 8, 128], BF16); B_Ptp = PB()
            P_ct = ps(c, "P_ct", [128, 512]); B_Pct = PB()
            P_cb = [ps(c, "P_cb%d" % i, [128, 4, 128]) for i in range(2)]; B_Pcb = [PB(), PB()]
            P_cbt = ps(c, "P_cbt", [128, 512]); B_Pcbt = PB()
            P_y = ps(c, "P_y", [128, 512]); B_Py = PB()
            P_yo = ps(c, "P_yo", [128, 512]); B_Pyo = PB()
            P_st = ps(c, "P_st", [128, 512]); B_Pst = PB()
            ci_n = 0
            sm = [sb(c, "sm%d" % i, [128, 32]) for i in range(2)]; B_sm = [Buf(), Buf()]
            arep = [sb(c, "arep%d" % i, [128, 4, 128]) for i in range(2)]; B_arep = [Buf(), Buf()]
            LT = [sb(c, "LT%d" % i, [128, 4, 128]) for i in range(2)]; B_LT = [Buf(), Buf()]
            MT = [sb(c, "MT%d" % i, [128, 4, 128], BF16) for i in range(2)]; B_MT = [Buf(), Buf()]
            xdt = [sb(c, "xdt%d" % i, [128, 4, 64], BF16) for i in range(2)]; B_xdt = [Buf(), Buf()]
            xdd = [sb(c, "xdd%d" % i, [128, 4, 64], BF16) for i in range(2)]; B_xdd = [Buf(), Buf()]
            zt = [sb(c, "zt%d" % i, [128, 256]) for i in range(2)]; B_zt = [Buf(), Buf()]
            yf = [sb(c, "yf%d" % i, [128, 256]) for i in range(2)]; B_yf = [Buf(), Buf()]
            yb = [sb(c, "yb%d" % i, [128, 256], BF16) for i in range(2)]; B_yb = [Buf(), Buf()]
            jkC = sb(c, "jkC", [128, 256], BF16); B_jkC = Buf()
            it = 0
            for g in range(8 if SUB > 0 else 0):
                chans = [g * 256, g * 256 + 128, 2048 + g * 128, 3072 + g * 128]
                for qi, ch0 in enumerate(chans):
                    ci = ci_n % 2; ci_n += 1
                    S.dma("sp", cin[ci][:, 2:2 + T], s_xT[ch0:ch0 + 128, :], reads=[B_scr], writes=[B_cin[ci]])
                    S.dma("sp", cwt[ci][:, 0:5], cw_v[ch0 // 128], writes=[B_cwt[ci]])
                    S.dma("sp", cwt[ci][:, 5:6], cb_v[ch0 // 128], writes=[B_cwt[ci]])
                    S.op("dve", lambda E, ci=ci: E.tensor_scalar(out=cacc[:], in0=cin[ci][:, 0:T], scalar1=cwt[ci][:, 0:1], scalar2=None,
                                                                 op0=ALU.mult), reads=[B_cin[ci], B_cwt[ci]], writes=[B_cacc])
                    for j in range(1, 5):
                        S.op("dve", lambda E, ci=ci, j=j: E.scalar_tensor_tensor(out=cacc[:], in0=cin[ci][:, j:j + T], scalar=cwt[ci][:, j:j + 1],
                                                                                in1=cacc[:], op0=ALU.mult, op1=ALU.add),
                             reads=[B_cin[ci], B_cwt[ci]], writes=[B_cacc])
                    S.op("act", lambda E, ci=ci, qi=qi: E.activation(out=cT[qi][:], in_=cacc[:], func=AF.Silu, bias=cwt[ci][:, 5:6]),
                         reads=[B_cacc, B_cwt[ci]], writes=[B_cT[qi]])
                if SUB < 2:
                    continue
                for tile in range(16):
                    for qi in range(3):
                        S.op("pe", lambda E, qi=qi, tile=tile: E.transpose(out=P_tp[:, qi, :], in_=cT[qi][:, tile * 128:(tile + 1) * 128],
                                                                           identity=ident[:]), reads=[B_cT[qi], B_ident], writes=[B_Ptp])
                    S.op("dve", lambda E, tile=tile: E.tensor_copy(out=xtok[:, tile, :], in_=P_tp[:, 0:2, :]), reads=[B_Ptp], writes=[B_xtok])
                    S.op("act", lambda E, tile=tile: E.copy(out=Btok[:, tile, :], in_=P_tp[:, 2, :]), reads=[B_Ptp], writes=[B_Btok])
                if SUB < 3:
                    continue
                for ch in range(8):
                    S.op("pe", lambda E, ch=ch: E.matmul(P_cbt[:, 0:128], lhsT=cT[2][:, ch * 128:(ch + 1) * 128],
                                                         rhs=cT[3][:, ch * 128:(ch + 1) * 128], start=True, stop=True),
                         reads=[B_cT[2], B_cT[3]], writes=[B_Pcbt])
                    S.op("act", lambda E, ch=ch: E.copy(out=CBT[:, ch, :], in_=P_cbt[:, 0:128]), reads=[B_Pcbt], writes=[B_CBT])
                S.op("dve", lambda E, g=g: E.tensor_tensor(
                    out=yacc[:].rearrange("p c (r q) -> p c r q", r=4),
                    in0=xtok[:, 0:8, :].rearrange("p c (r q) -> p c r q", r=4),
                    in1=dsk_bc[:, g * 4:(g + 1) * 4].unsqueeze(1).unsqueeze(3).broadcast_to([128, 8, 4, 64]), op=ALU.mult),
                    reads=[B_xtok, B_cc], writes=[B_yacc])
                for di in range(2 if SUB > 3 else 0):
                    colX = 127 if di == 0 else 0
                    triX = tri[:, di, :]
                    negX = tri[:, 2 + di, :]
                    order = list(range(8)) if di == 0 else list(range(15, -1, -1))
                    S.op("dve", lambda E: E.memset(state[:], 0.0), writes=[B_state])
                    S.op("dve", lambda E: E.memset(state_bf[:], 0.0), writes=[B_stbf])
                    for ch in order:
                        own = ch < 8
                        k_ = it % 2; it += 1
                        s_ = sm[k_]; Bs = B_sm[k_]
                        h0 = di * 32 + g * 4
                        adt4 = adt[:, ch, h0:h0 + 4]
                        S.op("pe", lambda E, triX=triX, adt4=adt4: E.matmul(P_ct[:, 0:4], lhsT=triX, rhs=adt4, start=True, stop=True),
                             reads=[B_cc, B_dt], writes=[B_Pct])
                        S.op("pool", lambda E, k_=k_, adt4=adt4: E.tensor_copy(out=arep[k_][:], in_=adt4.unsqueeze(2).broadcast_to([128, 4, 128])),
                             reads=[B_dt], writes=[B_arep[k_]])
                        pc = P_cb[k_]; Bpc = B_Pcb[k_]
                        for r in range(4):
                            S.op("pe", lambda E, pc=pc, r=r, k_=k_, triX=triX: E.matmul(pc[:, r, :], lhsT=arep[k_][:, r, :], rhs=triX,
                                                                                       start=True, stop=False),
                                 reads=[B_arep[k_], B_cc], writes=[Bpc])
                            S.op("pe", lambda E, pc=pc, r=r, negX=negX: E.matmul(pc[:, r, :], lhsT=identF[:], rhs=negX, start=False, stop=True),
                                 reads=[B_cc], writes=[Bpc])
                        S.op("dve", lambda E, s_=s_: E.tensor_scalar(out=s_[:, 0:4], in0=P_ct[:, 0:4], scalar1=-1.0, scalar2=None, op0=ALU.mult),
                             reads=[B_Pct], writes=[Bs])
                        S.op("act", lambda E, s_=s_: E.activation(out=s_[:, 4:8], in_=P_ct[:, 0:4], func=AF.Exp), reads=[B_Pct], writes=[Bs])
                        S.op("dve", lambda E, s_=s_, pc=pc, colX=colX: E.tensor_tensor(out=s_[:, 16:20], in0=pc[:, :, colX], in1=s_[:, 0:4], op=ALU.add),
                             reads=[Bpc, Bs], writes=[Bs])
                        S.op("act", lambda E, s_=s_: E.activation(out=s_[:, 8:12], in_=s_[:, 16:20], func=AF.Exp), reads=[Bs], writes=[Bs])
                        S.op("act", lambda E, s_=s_, pc=pc, colX=colX: E.activation(out=s_[:, 12:16], in_=pc[:, :, colX], func=AF.Exp),
                             reads=[Bpc], writes=[Bs])
                        dtc = dtv[:, ch, h0:h0 + 4]
                        S.op("dve", lambda E, k_=k_, ch=ch, dtc=dtc: E.tensor_tensor(
                            out=xdt[k_][:], in0=xtok[:, ch, :].rearrange("p (r q) -> p r q", r=4),
                            in1=dtc.unsqueeze(2).broadcast_to([128, 4, 64]), op=ALU.mult), reads=[B_xtok, B_dt], writes=[B_xdt[k_]])
                        if own:
                            for r in range(4):
                                S.op("act", lambda E, k_=k_, r=r, pc=pc, s_=s_: E.activation(out=LT[k_][:, r, :], in_=pc[:, r, :], func=AF.Exp,
                                                                                             bias=s_[:, r:r + 1]), reads=[Bpc, Bs], writes=[B_LT[k_]])
                            S.op("dve", lambda E, k_=k_, ch=ch: E.tensor_tensor(out=MT[k_][:], in0=LT[k_][:],
                                                                                in1=CBT[:, ch, :].unsqueeze(1).broadcast_to([128, 4, 128]), op=ALU.mult),
                                 reads=[B_LT[k_], B_CBT], writes=[B_MT[k_]])
                            for r in range(4):
                                S.op("pe", lambda E, k_=k_, r=r: E.matmul(P_y[:, r * 64:(r + 1) * 64], lhsT=MT[k_][:, r, :], rhs=xdt[k_][:, r, :],
                                                                          start=True, stop=True), reads=[B_MT[k_], B_xdt[k_]], writes=[B_Py])
                            S.op("pe", lambda E, ch=ch: E.matmul(P_yo[:, 0:256], lhsT=cT[3][:, ch * 128:(ch + 1) * 128], rhs=state_bf[:],
                                                                 start=True, stop=True), reads=[B_cT[3], B_stbf], writes=[B_Pyo])
                            S.op("dve", lambda E, ch=ch: E.tensor_tensor(out=yacc[:, ch, :], in0=yacc[:, ch, :], in1=P_y[:, 0:256], op=ALU.add),
                                 reads=[B_Py], writes=[B_yacc])
                            for r in range(4):
                                S.op("dve", lambda E, ch=ch, r=r, s_=s_: E.scalar_tensor_tensor(
                                    out=yacc[:, ch, r * 64:(r + 1) * 64], in0=P_yo[:, r * 64:(r + 1) * 64], scalar=s_[:, 4 + r:5 + r],
                                    in1=yacc[:, ch, r * 64:(r + 1) * 64], op0=ALU.mult, op1=ALU.add), reads=[B_Pyo, Bs], writes=[B_yacc])
                        S.op("dve", lambda E, k_=k_, s_=s_: E.tensor_tensor(out=xdd[k_][:], in0=xdt[k_][:],
                                                                            in1=s_[:, 8:12].unsqueeze(2).broadcast_to([128, 4, 64]), op=ALU.mult),
                             reads=[B_xdt[k_], Bs], writes=[B_xdd[k_]])
                        S.op("pe", lambda E, k_=k_, ch=ch: E.matmul(P_st[:, 0:256], lhsT=Btok[:, ch, :],
                                                                    rhs=xdd[k_][:].rearrange("p r q -> p (r q)"), start=True, stop=True),
                             reads=[B_Btok, B_xdd[k_]], writes=[B_Pst])
                        for r in range(4):
                            S.op("dve", lambda E, r=r, s_=s_: E.scalar_tensor_tensor(
                                out=state[:, r * 64:(r + 1) * 64], in0=state[:, r * 64:(r + 1) * 64], scalar=s_[:, 12 + r:13 + r],
                                in1=P_st[:, r * 64:(r + 1) * 64], op0=ALU.mult, op1=ALU.add), reads=[B_Pst, Bs, B_Pyo], writes=[B_state])
                        S.op("dve", lambda E: E.tensor_copy(out=state_bf[:], in_=state[:]), reads=[B_state, B_Pyo], writes=[B_stbf])
                for ch in range(8 if SUB > 4 else 0):
                    k_ = ch % 2
                    S.dma("sp", zt[k_][:], s_z[ch * 128:(ch + 1) * 128, g * 256:(g + 1) * 256], reads=[B_scr], writes=[B_zt[k_]])
                    S.op("act", lambda E, k_=k_: E.activation(out=zt[k_][:], in_=zt[k_][:], func=AF.Silu), reads=[B_zt[k_]], writes=[B_zt[k_]])
                    S.op("dve", lambda E, k_=k_, ch=ch: E.tensor_tensor(out=yf[k_][:], in0=yacc[:, ch, :], in1=zt[k_][:], op=ALU.mult),
                         reads=[B_yacc, B_zt[k_]], writes=[B_yf[k_]])
                    s_ = sm[k_]; Bs = B_sm[k_]
                    S.op("act", lambda E, k_=k_, s_=s_: E.activation(out=jkC[:], in_=yf[k_][:], func=AF.Square, accum_out=s_[:, 20:21]),
                         reads=[B_yf[k_]], writes=[B_jkC, Bs])
                    S.op("dve", lambda E, s_=s_: E.tensor_scalar(out=s_[:, 21:22], in0=s_[:, 20:21], scalar1=1.0 / 256, scalar2=EPS,
                                                                 op0=ALU.mult, op1=ALU.add), reads=[Bs], writes=[Bs])
                    S.op("act", lambda E, s_=s_: E.activation(out=s_[:, 22:23], in_=s_[:, 21:22], func=AF.Sqrt), reads=[Bs], writes=[Bs])
                    S.op("dve", lambda E, s_=s_: E.reciprocal(out=s_[:, 23:24], in_=s_[:, 22:23]), reads=[Bs], writes=[Bs])
                    S.op("dve", lambda E, k_=k_, s_=s_, g=g: E.scalar_tensor_tensor(
                        out=yb[k_][:], in0=yf[k_][:], scalar=s_[:, 23:24], in1=son_bc[:, g * 256:(g + 1) * 256], op0=ALU.mult, op1=ALU.mult),
                        reads=[B_yf[k_], Bs, B_cc], writes=[B_yb[k_]])
                    S.dma("sp", s_mix[ch * 128:(ch + 1) * 128, 2048 + g * 256:2048 + (g + 1) * 256], yb[k_][:], reads=[B_yb[k_]], writes=[B_mix])
            S.barrier()

        B_out = Buf(); B_hnT = Buf()
        if stage >= 4 and "D" not in SKIP:
          hdst = dbg["d_h"] if stage == 4 else out
          with ExitStack() as c:
            nf_bc = sb(c, "nf_bc", [128, D]); B_nf = Buf()
            S.dma("sp", nf_bc[:], norm_ffn.partition_broadcast(128), writes=[B_nf])
            mt = [sb(c, "mt%d" % i, [128, D], BF16) for i in range(2)]; B_mt = [Buf(), Buf()]
            mixT = sb(c, "mixT", [128, 32, 512], BF16); B_mixT = [Buf() for _ in range(4)]
            wb_ = [sb(c, "wbD%d" % i, [128, 32, 256], BF16) for i in range(2)]; B_wb = [Buf(), Buf()]
            ht = [sb(c, "ht%d" % i, [128, D]) for i in range(4)]; B_ht = [Buf() for _ in range(4)]
            xr = [sb(c, "xr%d" % i, [128, 256]) for i in range(3)]; B_xr = [Buf() for _ in range(3)]
            hb = sb(c, "hb", [128, D], BF16); B_hb = Buf()
            jkD = sb(c, "jkD", [128, D], BF16); B_jkD = Buf()
            hst = [sb(c, "hst%d" % i, [128, 8, 128], BF16) for i in range(2)]; B_hst = [Buf(), Buf()]
            sD = sb(c, "sD", [128, 8]); B_sD = Buf()
            tpD = [ps(c, "tpD%d" % i, [128, 8, 128], BF16) for i in range(2)]; B_tpD = [PB(), PB()]
            accD = [ps(c, "accD%d" % i, [128, 512]) for i in range(4)]; B_accD = [PB() for _ in range(4)]
            wo_v = w_out.rearrange("(k p) c -> p k c", p=128)
            hn_v = s_hnT.rearrange("(k p) t -> p k t", p=128)
            wi = 0; ai = 0; xi = 0; gi = 0
            for tb in range(2):
                for tt in range(4):
                    tile = tb * 4 + tt
                    mi = tile % 2
                    S.dma("sp", mt[mi][:], s_mix[tile * 128:(tile + 1) * 128, :], reads=[B_mix], writes=[B_mt[mi]])
                    for g in range(4):
                        ti = gi % 2; gi += 1
                        for j in range(8):
                            kk = g * 8 + j
                            S.op("pe", lambda E, ti=ti, j=j, kk=kk, mi=mi: E.transpose(out=tpD[ti][:, j, :], in_=mt[mi][:, kk * 128:(kk + 1) * 128],
                                                                                      identity=ident[:]), reads=[B_mt[mi], B_ident], writes=[B_tpD[ti]])
                        if g % 2 == 0:
                            S.op("act", lambda E, ti=ti, g=g, tt=tt: E.copy(out=mixT[:, g * 8:(g + 1) * 8, tt * 128:(tt + 1) * 128], in_=tpD[ti][:]),
                                 reads=[B_tpD[ti]], writes=[B_mixT[tt]])
                        else:
                            S.op("dve", lambda E, ti=ti, g=g, tt=tt: E.tensor_copy(out=mixT[:, g * 8:(g + 1) * 8, tt * 128:(tt + 1) * 128], in_=tpD[ti][:]),
                                 reads=[B_tpD[ti]], writes=[B_mixT[tt]])
                for cb in range(16):
                    w_ = wb_[wi % 2]; Bw = B_wb[wi % 2]; wi += 1
                    S.dma("pool", w_[:], wo_v[:, :, cb * 256:(cb + 1) * 256], writes=[Bw])
                    for tt in range(4):
                        tile = tb * 4 + tt
                        a = accD[ai % 4]; Ba = B_accD[ai % 4]; ai += 1
                        x_ = xr[xi % 3]; Bx = B_xr[xi % 3]; xi += 1
                        S.dma("sp", x_[:], x[tile * 128:(tile + 1) * 128, cb * 256:(cb + 1) * 256], writes=[Bx])
                        for kk in range(32):
                            S.op("pe", lambda E, a=a, w_=w_, kk=kk, tt=tt: E.matmul(a[:, 0:256], lhsT=mixT[:, kk, tt * 128:(tt + 1) * 128],
                                                                                   rhs=w_[:, kk, :], start=(kk == 0), stop=(kk == 31)),
                                 reads=[Bw, B_mixT[tt]], writes=[Ba])
                        S.op("dve", lambda E, a=a, x_=x_, tt=tt, cb=cb: E.tensor_tensor(out=ht[tt][:, cb * 256:(cb + 1) * 256], in0=a[:, 0:256],
                                                                                       in1=x_[:], op=ALU.add), reads=[Ba, Bx], writes=[B_ht[tt]])
                for tt in range(4):
                    tile = tb * 4 + tt
                    S.dma("sp", hdst[tile * 128:(tile + 1) * 128, :], ht[tt][:], reads=[B_ht[tt]], writes=[B_out])
                    S.op("act", lambda E, tt=tt: E.activation(out=jkD[:], in_=ht[tt][:], func=AF.Square, accum_out=sD[:, 0:1]),
                         reads=[B_ht[tt]], writes=[B_jkD, B_sD])
                    S.op("dve", lambda E: E.tensor_scalar(out=sD[:, 1:2], in0=sD[:, 0:1], scalar1=1.0 / D, scalar2=EPS, op0=ALU.mult, op1=ALU.add),
                         reads=[B_sD], writes=[B_sD])
                    S.op("act", lambda E: E.activation(out=sD[:, 2:3], in_=sD[:, 1:2], func=AF.Sqrt), reads=[B_sD], writes=[B_sD])
                    S.op("dve", lambda E: E.reciprocal(out=sD[:, 3:4], in_=sD[:, 2:3]), reads=[B_sD], writes=[B_sD])
                    S.op("dve", lambda E, tt=tt: E.scalar_tensor_tensor(out=hb[:], in0=ht[tt][:], scalar=sD[:, 3:4], in1=nf_bc[:],
                                                                        op0=ALU.mult, op1=ALU.mult), reads=[B_ht[tt], B_sD, B_nf], writes=[B_hb])
                    for g in range(4):
                        ti = gi % 2; gi += 1
                        for j in range(8):
                            kk = g * 8 + j
                            S.op("pe", lambda E, ti=ti, j=j, kk=kk: E.transpose(out=tpD[ti][:, j, :], in_=hb[:, kk * 128:(kk + 1) * 128],
                                                                               identity=ident[:]), reads=[B_hb, B_ident], writes=[B_tpD[ti]])
                        hs = hst[ti]; Bh = B_hst[ti]
                        if g % 2 == 0:
                            S.op("act", lambda E, ti=ti, hs=hs: E.copy(out=hs[:], in_=tpD[ti][:]), reads=[B_tpD[ti]], writes=[Bh])
                        else:
                            S.op("dve", lambda E, ti=ti, hs=hs: E.tensor_copy(out=hs[:], in_=tpD[ti][:]), reads=[B_tpD[ti]], writes=[Bh])
                        S.dma("sp", hn_v[:, g * 8:(g + 1) * 8, tile * 128:(tile + 1) * 128], hs[:], reads=[Bh], writes=[B_hnT])
            S.barrier()

        if stage >= 5:
          with ExitStack() as c:
            s2all = sb(c, "s2all", [128, 8, 8, 128]); A1all = sb(c, "A1all", [128, 8, 8, 128])
            wAll = sb(c, "wAll", [128, 8, 8])
            B_gin = Buf()
            hnT = sb(c, "hnT", [128, 32, TO], BF16); B_hn = Buf()
            hn_v = s_hnT.rearrange("(k p) t -> p k t", p=128)
            for g in range(4):
                S.dma("sp", hnT[:, g * 8:(g + 1) * 8, :], hn_v[:, g * 8:(g + 1) * 8, :], reads=[B_hnT], writes=[B_hn])
            identF2 = sb(c, "identF2", [128, 128]); B_idf = Buf()
            S.dma("sp", identF2[:], ident_in, writes=[B_idf])
            with ExitStack() as c2:
                qT = sb(c2, "qT", [128, 16, TO], BF16); B_qT = Buf()
                skT = sb(c2, "skT", [128, 16, 128], BF16); B_skT = Buf()
                S.dma("pool", skT[:], skT_in.rearrange("h d k -> d h k"), writes=[B_skT])
                wqb = [sb(c2, "wqb%d" % i, [128, 32, 128], BF16) for i in range(2)]; B_wqb = [Buf(), Buf()]
                pq_ = [ps(c2, "pqE%d" % i, [128, 512]) for i in range(4)]; B_pq_ = [PB() for _ in range(4)]
                pi_ = 0
                wq_v = w_query.rearrange("(k p) c -> p k c", p=128)
                for cb in range(16):
                    w_ = wqb[cb % 2]; Bw = B_wqb[cb % 2]
                    S.dma("pool", w_[:], wq_v[:, :, cb * 128:(cb + 1) * 128], writes=[Bw])
                    for hh in range(1):
                        hc = cb
                        for th in range(2):
                            p_ = pq_[pi_ % 4]; Bp = B_pq_[pi_ % 4]; pi_ += 1
                            for kk in range(32):
                                S.op("pe", lambda E, p_=p_, w_=w_, kk=kk, hh=hh, th=th: E.matmul(
                                    p_[:], lhsT=w_[:, kk, hh * 128:(hh + 1) * 128], rhs=hnT[:, kk, th * 512:(th + 1) * 512],
                                    start=(kk == 0), stop=(kk == 31)), reads=[Bw, B_hn], writes=[Bp])
                            if pi_ % 2 == 0:
                                S.op("act", lambda E, p_=p_, hc=hc, th=th: E.copy(out=qT[:, hc, th * 512:(th + 1) * 512], in_=p_[:]),
                                     reads=[Bp], writes=[B_qT])
                            else:
                                S.op("dve", lambda E, p_=p_, hc=hc, th=th: E.tensor_copy(out=qT[:, hc, th * 512:(th + 1) * 512], in_=p_[:]),
                                     reads=[Bp], writes=[B_qT])
                sc = sb(c2, "sc", [128, 16, 128]); B_sc = Buf()
                wk = sb(c2, "wk", [128, 256]); B_wk = Buf()
                v16 = sb(c2, "v16", [128, 16, 16]); B_v16 = Buf()
                cand = sb(c2, "cand", [128, 8, 256]); B_cand = Buf()
                t24 = sb(c2, "t24", [128, 8, 24]); B_t24 = Buf()
                sE = sb(c2, "sE", [128, 8, 8]); B_sE = Buf()
                jkE = sb(c2, "jkE", [128, 16]); B_jkE = Buf()
                for tt in range(8):
                    for q4 in range(4):
                        p_ = pq_[pi_ % 4]; Bp = B_pq_[pi_ % 4]; pi_ += 1
                        for u in range(4):
                            hc = q4 * 4 + u
                            S.op("pe", lambda E, p_=p_, u=u, hc=hc, tt=tt: E.matmul(
                                p_[:, u * 128:(u + 1) * 128], lhsT=qT[:, hc, tt * 128:(tt + 1) * 128], rhs=skT[:, hc, :],
                                start=True, stop=True), reads=[B_qT, B_skT], writes=[Bp])
                        S.op("act", lambda E, p_=p_, q4=q4: E.copy(out=sc[:, q4 * 4:(q4 + 1) * 4, :], in_=p_[:]), reads=[Bp], writes=[B_sc])
                    for hc in range(16):
                        S.op("dve", lambda E, hc=hc: E.max(out=v16[:, hc, 0:8], in_=sc[:, hc, :]), reads=[B_sc], writes=[B_v16])
                        S.op("dve", lambda E, hc=hc: E.match_replace(out=wk[:, 0:128], in_to_replace=v16[:, hc, 0:8], in_values=sc[:, hc, :],
                                                                     imm_value=-1e30), reads=[B_sc, B_v16], writes=[B_wk])
                        S.op("dve", lambda E, hc=hc: E.max(out=v16[:, hc, 8:16], in_=wk[:, 0:128]), reads=[B_wk], writes=[B_v16])
                    v4 = v16[:].rearrange("p (h c) k -> p h c k", c=2)
                    S.op("dve", lambda E, v4=v4: E.tensor_tensor(
                        out=cand[:].rearrange("p h (a b) -> p h a b", a=16),
                        in0=v4[:, :, 0, :].unsqueeze(3).broadcast_to([128, 8, 16, 16]),
                        in1=v4[:, :, 1, :].unsqueeze(2).broadcast_to([128, 8, 16, 16]), op=ALU.add), reads=[B_v16], writes=[B_cand])
                    for h in range(8):
                        S.op("dve", lambda E, h=h: E.max(out=t24[:, h, 0:8], in_=cand[:, h, :]), reads=[B_cand], writes=[B_t24])
                        S.op("dve", lambda E, h=h: E.match_replace(out=wk[:], in_to_replace=t24[:, h, 0:8], in_values=cand[:, h, :],
                                                                   imm_value=-1e30), reads=[B_cand, B_t24], writes=[B_wk])
                        S.op("dve", lambda E, h=h: E.max(out=t24[:, h, 8:16], in_=wk[:]), reads=[B_wk], writes=[B_t24])
                        S.op("dve", lambda E, h=h: E.match_replace(out=wk[:], in_to_replace=t24[:, h, 8:16], in_values=wk[:],
                                                                   imm_value=-1e30), reads=[B_t24], writes=[B_wk])
                        S.op("dve", lambda E, h=h: E.max(out=t24[:, h, 16:24], in_=wk[:]), reads=[B_wk], writes=[B_t24])
                    S.op("dve", lambda E: E.tensor_tensor(out=sE[:, :, 0], in0=t24[:, :, 15], in1=t24[:, :, 16], op=ALU.add), reads=[B_t24], writes=[B_sE])
                    S.op("dve", lambda E: E.tensor_scalar(out=sE[:, :, 0], in0=sE[:, :, 0], scalar1=0.5, scalar2=None, op0=ALU.mult), reads=[B_sE], writes=[B_sE])
                    S.op("dve", lambda E: E.tensor_scalar(out=sE[:, :, 6], in0=t24[:, :, 0], scalar1=-1.0, scalar2=None, op0=ALU.mult), reads=[B_t24], writes=[B_sE])
                    for h in range(8):
                        S.op("act", lambda E, h=h: E.activation(out=jkE[:], in_=t24[:, h, 0:16], func=AF.Exp, bias=sE[:, h, 6:7],
                                                                accum_out=sE[:, h, 2:3]), reads=[B_t24, B_sE], writes=[B_jkE, B_sE])
                    S.op("act", lambda E: E.activation(out=sE[:, :, 3], in_=sE[:, :, 2], func=AF.Ln), reads=[B_sE], writes=[B_sE])
                    S.op("dve", lambda E: E.tensor_tensor(out=sE[:, :, 4], in0=sE[:, :, 0], in1=sE[:, :, 6], op=ALU.add), reads=[B_sE], writes=[B_sE])
                    S.op("dve", lambda E: E.tensor_tensor(out=sE[:, :, 4], in0=sE[:, :, 4], in1=sE[:, :, 3], op=ALU.subtract), reads=[B_sE], writes=[B_sE])
                    S.op("act", lambda E: E.activation(out=sE[:, :, 5], in_=sE[:, :, 4], func=AF.Exp), reads=[B_sE], writes=[B_sE])
                    sc4 = sc[:].rearrange("p (h c) k -> p h c k", c=2)
                    S.op("dve", lambda E, sc4=sc4, tt=tt: E.tensor_tensor(out=A1all[:, tt, :, :], in0=sc4[:, :, 0, :],
                                                                          in1=sE[:, :, 0:1].broadcast_to([128, 8, 128]), op=ALU.subtract),
                         reads=[B_sc, B_sE], writes=[B_gin])
                    S.op("act", lambda E, sc4=sc4, tt=tt: E.copy(out=s2all[:, tt, :, :], in_=sc4[:, :, 1, :]), reads=[B_sc], writes=[B_gin])
                    S.op("dve", lambda E, tt=tt: E.tensor_copy(out=wAll[:, tt, :], in_=sE[:, :, 5]), reads=[B_sE], writes=[B_gin])
                S.barrier()
            with ExitStack() as c2:
                ut = [sb(c2, "ut%d" % i, [128, 32, 128], BF16) for i in range(3)]; B_ut = [Buf() for _ in range(3)]
                gact = [sb(c2, "gact%d" % i, [128, TO], BF16) for i in range(2)]; B_gact = [Buf(), Buf()]
                Yb = [sb(c2, "Yb%d" % i, [128, 8, 128]) for i in range(2)]; B_Yb = [Buf(), Buf()]
                Eb = [sb(c2, "Eb%d" % i, [128, 8, 128], BF16) for i in range(2)]; B_Eb = [Buf(), Buf()]
                Gb = [sb(c2, "Gb%d" % i, [128, 8, 128], BF16) for i in range(3)]; B_Gb = [Buf() for _ in range(3)]
                wt = [sb(c2, "wt%d" % i, [128, TO], BF16) for i in range(2)]; B_wt = [Buf(), Buf()]
                P_a = [ps(c2, "P_a%d" % i, [128, 512]) for i in range(4)]; B_Pa = [PB() for _ in range(4)]
                P_g = [ps(c2, "P_g%d" % i, [128, 512]) for i in range(4)]; B_Pg = [PB() for _ in range(4)]
                NCH = int(_os0.environ.get("KNCH", "128"))
                yi = 0; gi = 0
                Dg = sb(c2, "Dg", [128, 8, 8, 128], BF16)
                for tt in range(8):
                    for h in range(8):
                        S.op("dve", lambda E, tt=tt, h=h: E.tensor_scalar(out=Dg[:, tt, h, :], in0=identF2[:], scalar1=wAll[:, tt, h:h + 1], scalar2=None,
                                                                          op0=ALU.mult), reads=[B_idf, B_gin], writes=[B_gin])
                for i in range(NCH):
                    u_ = ut[i % 3]; Bu = B_ut[i % 3]
                    S.dma("pool", u_[:], UTt[i], writes=[Bu])
                    pb = (i % 2) * 2
                    ga = gact[i % 2]; Bga = B_gact[i % 2]
                    for th in range(2):
                        for kk in range(32):
                            S.op("pe", lambda E, pb=pb, th=th, kk=kk, u_=u_: E.matmul(P_a[pb + th][:], lhsT=u_[:, kk, :],
                                                                                     rhs=hnT[:, kk, th * 512:(th + 1) * 512],
                                                                                     start=(kk == 0), stop=(kk == 31)),
                                 reads=[Bu, B_hn], writes=[B_Pa[pb + th]])
                        S.op("act", lambda E, pb=pb, th=th, ga=ga: E.activation(out=ga[:, th * 512:(th + 1) * 512], in_=P_a[pb + th][:],
                                                                                func=AF.Gelu_apprx_tanh), reads=[B_Pa[pb + th]], writes=[Bga])
                    for tt in range(8):
                        y_ = Yb[yi % 2]; By = B_Yb[yi % 2]; e_ = Eb[yi % 2]; Be = B_Eb[yi % 2]; yi += 1
                        g_ = Gb[gi % 3]; Bg = B_Gb[gi % 3]; gi += 1
                        S.op("dve", lambda E, y_=y_, tt=tt, i=i: E.tensor_tensor(out=y_[:], in0=s2all[:, tt, :, :],
                                                                                 in1=A1all[:, tt, :, i:i + 1].broadcast_to([128, 8, 128]), op=ALU.add),
                             reads=[B_gin], writes=[By])
                        S.op("act", lambda E, y_=y_, e_=e_: E.activation(out=e_[:], in_=y_[:], func=AF.Exp), reads=[By], writes=[Be])
                        S.op("dve", lambda E, y_=y_, e_=e_, g_=g_: E.scalar_tensor_tensor(out=g_[:], in0=y_[:], scalar=0.0, in1=e_[:],
                                                                                          op0=ALU.is_ge, op1=ALU.mult), reads=[By, Be], writes=[Bg])
                        pg = P_g[pb + tt // 4]; Bpg = B_Pg[pb + tt // 4]
                        for h in range(8):
                            S.op("pe", lambda E, pg=pg, g_=g_, h=h, tt=tt: E.matmul(pg[:, (tt % 4) * 128:(tt % 4 + 1) * 128], lhsT=g_[:, h, :],
                                                                                    rhs=Dg[:, tt, h, :], start=(h == 0), stop=(h == 7)),
                                 reads=[Bg, B_gin], writes=[Bpg])
                    w_ = wt[i % 2]; Bwt = B_wt[i % 2]
                    for th in range(2):
                        S.op("dve", lambda E, w_=w_, th=th, pb=pb, ga=ga: E.tensor_tensor(out=w_[:, th * 512:(th + 1) * 512], in0=P_g[pb + th][:],
                                                                                         in1=ga[:, th * 512:(th + 1) * 512], op=ALU.mult),
                             reads=[B_Pg[pb + th], Bga], writes=[Bwt])
                    S.dma("sp", s_WT[i], w_[:], reads=[Bwt], writes=[B_swt])
                S.barrier()
          with ExitStack() as c:
            WTr = sb(c, "WTr", [128, 128, 512], BF16); B_WTr = Buf()
            vt = [sb(c, "vt%d" % i, [128, 4, 1024], BF16) for i in range(3)]; B_vt = [Buf() for _ in range(3)]
            hres = [sb(c, "hres%d" % i, [128, 512]) for i in range(3)]; B_hres = [Buf() for _ in range(3)]
            P_o = [ps(c, "P_o%d" % i, [128, 512]) for i in range(8)]; B_Po = [PB() for _ in range(8)]
            V_v = Vexp.rearrange("(i j) d -> j i d", j=128)
            wt_v = s_WT.rearrange("i j t -> j i t")
            vi = 0; hi_ = 0
            for tb in range(2):
                for g in range(8):
                    S.dma("sp", WTr[:, g * 16:(g + 1) * 16, :], wt_v[:, g * 16:(g + 1) * 16, tb * 512:(tb + 1) * 512],
                          reads=[B_swt], writes=[B_WTr])
                for dq in range(4):
                    for ig in range(NCH // 4):
                        v_ = vt[vi % 3]; Bv = B_vt[vi % 3]; vi += 1
                        S.dma("pool", v_[:], V_v[:, ig * 4:(ig + 1) * 4, dq * 1024:(dq + 1) * 1024], writes=[Bv])
                        for ii in range(4):
                            i = ig * 4 + ii
                            for tt in range(4):
                                for hf in range(2):
                                    S.op("pe", lambda E, tt=tt, hf=hf, i=i, ii=ii, v_=v_: E.matmul(
                                        P_o[tt * 2 + hf][:], lhsT=WTr[:, i, tt * 128:(tt + 1) * 128], rhs=v_[:, ii, hf * 512:(hf + 1) * 512],
                                        start=(i == 0), stop=(i == NCH - 1)), reads=[B_WTr, Bv], writes=[B_Po[tt * 2 + hf]])
                    for tt in range(4):
                        for hf in range(2):
                            t0 = tb * 512 + tt * 128
                            c0 = dq * 1024 + hf * 512
                            h_ = hres[hi_ % 3]; Bh = B_hres[hi_ % 3]; hi_ += 1
                            S.dma("sp", h_[:], out[t0:t0 + 128, c0:c0 + 512], reads=[B_out], writes=[Bh])
                            S.op("dve", lambda E, h_=h_, tt=tt, hf=hf: E.tensor_tensor(out=h_[:], in0=P_o[tt * 2 + hf][:], in1=h_[:], op=ALU.add),
                                 reads=[B_Po[tt * 2 + hf], Bh], writes=[Bh])
                            S.dma("sp", out[t0:t0 + 128, c0:c0 + 512], h_[:], reads=[Bh], writes=[B_fin])
            S.barrier()

        S.barrier()
        with nc.Block() as block:
            S.emit(block)
    return nc


def _prep_inputs(inputs):
    g = {k: np.asarray(v) for k, v in inputs.items()}
    w_in = g["w_in"][0]
    sp = np.cumsum([0, 1024, 512, 64, 2048, 4096, 32, 32])
    c_q, c_kv, k_rope, z, xbc, dtf, dtb = [w_in[:, sp[i]:sp[i + 1]] for i in range(7)]
    xs, bs, cs = xbc[:, :2048], xbc[:, 2048:3072], xbc[:, 3072:4096]
    w_perm = [np.ascontiguousarray(np.concatenate([c_kv, k_rope, a, b, xs, bs, cs, c_q, z], axis=1))
              for (a, b) in ((dtf, dtb), (dtb, dtf))]
    ident = np.eye(128, dtype=np.float32)
    invf = (10000.0 ** (-(np.arange(32, dtype=np.float32) / np.float32(32)))).astype(np.float32)
    cwm = g["conv_w"][0][:, 0, :]
    cw = [np.ascontiguousarray(cwm.T), np.ascontiguousarray(cwm[::-1].T)]
    kk, ll = np.meshgrid(np.arange(128), np.arange(128), indexing="ij")
    tri = np.stack([(kk <= ll), (kk >= ll), np.where(kk <= ll, 0.0, -30000.0), np.where(kk >= ll, 0.0, -30000.0)]).astype(np.float32)
    skT = np.ascontiguousarray(g["sub_keys"][0].reshape(16, 128, 128).transpose(0, 2, 1))
    U = g["expert_u"][0]
    UTt = np.ascontiguousarray(U.reshape(128, 128, 32, 128).transpose(0, 3, 2, 1))
    maps = []
    for c in range(8):
        b, half = c // 2, c % 2
        xl = g["x"][b]
        if half:
            xl = xl[::-1]
        pl = g["positions"][b].astype(np.int32)
        if half:
            pl = pl[::-1]
        m = {"x": np.ascontiguousarray(xl), "norm_mix": g["norm_mix"][0], "w_in": w_perm[half], "ident": ident,
             "pos": np.ascontiguousarray(pl), "invf": invf, "q_a_norm": g["q_a_norm"][0], "w_uq": g["w_uq"][0],
             "kv_a_norm": g["kv_a_norm"][0], "w_ukv": g["w_ukv"][0], "q_norm": g["q_norm"][0], "k_norm": g["k_norm"][0],
             "aon": np.ascontiguousarray(g["attn_out_norm"][0].reshape(-1)),
             "conv_w": cw[half], "conv_b": g["conv_b"][0],
             "a_log": np.concatenate([g["a_log_fwd"][0], g["a_log_bwd"][0]][::(-1 if half else 1)]),
             "dt_bias": np.concatenate([g["dt_bias_fwd"][0], g["dt_bias_bwd"][0]][::(-1 if half else 1)]),
             "d_skip": g["d_skip"][0], "son": g["ssm_out_norm"][0], "tri": tri,
             "w_out": g["w_out"][0], "norm_ffn": g["norm_ffn"][0], "w_query": g["w_query"][0],
             "skT": skT, "UTt": UTt, "Vexp": g["expert_v"][0]}
        maps.append(m)
    return maps


def kernel(**inputs):
    maps = _prep_inputs(inputs)
    nc = build()
    res = run_bass_kernel_spmd(nc, maps, core_ids=list(range(8)))
    outp = np.zeros((4, 2048, D), np.float32)
    for c in range(8):
        b, half = c // 2, c % 2
        o = res.results[c]["out"]
        if half:
            outp[b, 1024:] = o[::-1]
        else:
            outp[b, :1024] = o
    return outp
```

```python
from contextlib import ExitStack
import numpy as np
import concourse.bass as bass
import concourse.mybir as mybir
from concourse.bass_utils import run_bass_kernel_spmd

F32 = mybir.dt.float32
BF16 = mybir.dt.bfloat16
I32 = mybir.dt.int32
AF = mybir.ActivationFunctionType
ALU = mybir.AluOpType
AX = mybir.AxisListType

EPS = 1e-6
D = 4096
T = 2048
TO = 1024
NH = 16
O_KV, O_ROPE, O_DTF, O_DTB, O_X, O_B, O_C, O_Q, O_Z, O_END = 0, 512, 576, 608, 640, 2688, 3712, 4736, 5760, 7808


import types


def _snap(fn):
    if fn.__closure__ is None:
        return fn
    cells = []
    for cl in fn.__closure__:
        try:
            cells.append(types.CellType(cl.cell_contents))
        except ValueError:
            cells.append(cl)
    g = types.FunctionType(fn.__code__, fn.__globals__, fn.__name__, fn.__defaults__, tuple(cells))
    g.__kwdefaults__ = fn.__kwdefaults__
    return g


class Buf:
    __slots__ = ("name", "w", "r", "excl")

    def __init__(self, name="", excl=False):
        self.name = name
        self.w = None
        self.r = {}
        self.excl = excl


def PB():
    return Buf(excl=True)


class Sched:
    NDMA = 24

    def __init__(self, nc, ctx):
        self.nc = nc
        self.engs = ["pe", "dve", "act", "pool", "sp"]
        self.sem = {}
        self.cnt = {}
        for e in self.engs:
            self.sem[e] = ctx.enter_context(nc.semaphore("s_" + e))
            self.cnt[e] = 0
        self.dsem = [ctx.enter_context(nc.semaphore("d%d" % i)) for i in range(self.NDMA)]
        self.dcnt = [0] * self.NDMA
        self.dnext = {"sp": 0, "pool": 0, "act": 0}
        self.drange = {"sp": (0, 14), "pool": (14, 22), "act": (22, 24)}
        self.seen = {e: {} for e in self.engs}
        self.prog = {e: [] for e in self.engs}

    def _semobj(self, key):
        return self.sem[key] if isinstance(key, str) else self.dsem[key]

    def _wait(self, e, key, val):
        if key == "pe" and e == "pe":
            return
        if self.seen[e].get(key, 0) >= val:
            return
        so = self._semobj(key)
        self.prog[e].append(lambda E, so=so, val=val: E.wait_ge(so, val))
        self.seen[e][key] = val

    def _deps(self, e, reads, writes):
        for b in reads:
            if b.w is not None:
                self._wait(e, *b.w)
        for b in writes:
            if b.w is not None:
                self._wait(e, *b.w)
            for k, v in b.r.items():
                self._wait(e, k, v)

    def _commit(self, ev, reads, writes):
        for b in reads:
            if b.r.get(ev[0], 0) < ev[1]:
                b.r[ev[0]] = ev[1]
        for b in writes:
            b.w = ev
            b.r = {}

    def op(self, e, fn, reads=(), writes=()):
        fn = _snap(fn)
        if any(b.excl for b in reads):
            writes = list(writes) + [b for b in reads if b.excl]
            reads = [b for b in reads if not b.excl]
        self._deps(e, reads, writes)
        self.cnt[e] += 1
        so = self.sem[e]
        self.prog[e].append(lambda E, fn=fn, so=so: fn(E).then_inc(so, 1))
        ev = (e, self.cnt[e])
        self._commit(ev, reads, writes)
        return ev

    def dma(self, q, out, in_, reads=(), writes=(), **kw):
        lo, hi = self.drange[q]
        k = lo + self.dnext[q]
        self.dnext[q] = (self.dnext[q] + 1) % (hi - lo)
        if self.dcnt[k] > 0:
            self._wait(q, k, self.dcnt[k])
        self._deps(q, reads, writes)
        self.dcnt[k] += 16
        so = self.dsem[k]
        self.prog[q].append(
            lambda E, out=out, in_=in_, kw=kw, so=so: E.dma_start(out=out, in_=in_, **kw).then_inc(so, 16))
        ev = (k, self.dcnt[k])
        self._commit(ev, reads, writes)
        return ev

    def barrier(self):
        for e in self.engs:
            for o in self.engs:
                if o != e and self.cnt[o] > 0:
                    self._wait(e, o, self.cnt[o])
            for k in range(self.NDMA):
                if self.dcnt[k] > 0:
                    self._wait(e, k, self.dcnt[k])

    def emit(self, block):
        reg = {"pe": block.tensor, "dve": block.vector, "act": block.scalar, "pool": block.gpsimd, "sp": block.sync}
        for e in self.engs:
            lst = self.prog[e]
            if not lst:
                continue

            def body(E, lst=lst):
                for f in lst:
                    f(E)
            reg[e](body)


def bcast_rows(ap_1d, nparts):
    return ap_1d.partition_broadcast(nparts)


_rope_id = [0]


def rope_ops(S, src, B_src, dst, B_dst, cs, B_cs, tile, sb, c, tag, nheads):
    _rope_id[0] += 1
    nm = "rp%d_" % _rope_id[0]
    if nheads == 1:
        shp = [128, 32]
        t1, t2 = src[:, 0:32], src[:, 32:64]
        d1, d2 = dst[:, 0:32], dst[:, 32:64]
        co, si = cs[:, tile, 0:32], cs[:, tile, 32:64]
    else:
        shp = [128, nheads, 32]
        t1, t2 = src[:, :, 0:32], src[:, :, 32:64]
        d1, d2 = dst[:, :, 0:32], dst[:, :, 32:64]
        co = cs[:, tile, 0:32].unsqueeze(1).broadcast_to(shp)
        si = cs[:, tile, 32:64].unsqueeze(1).broadcast_to(shp)
    a = sb(c, nm + "a", shp)
    b = sb(c, nm + "b", shp)
    Ba, Bb = Buf(), Buf()
    S.op("dve", lambda E: E.tensor_tensor(out=a[:], in0=t1, in1=co, op=ALU.mult), reads=[B_src, B_cs], writes=[Ba])
    S.op("dve", lambda E: E.tensor_tensor(out=b[:], in0=t2, in1=si, op=ALU.mult), reads=[B_src, B_cs], writes=[Bb])
    S.op("dve", lambda E: E.tensor_tensor(out=d1, in0=a[:], in1=b[:], op=ALU.subtract), reads=[Ba, Bb], writes=[B_dst])
    S.op("dve", lambda E: E.tensor_tensor(out=a[:], in0=t1, in1=si, op=ALU.mult), reads=[B_src, B_cs, B_dst], writes=[Ba])
    S.op("dve", lambda E: E.tensor_tensor(out=b[:], in0=t2, in1=co, op=ALU.mult), reads=[B_src, B_cs, B_dst], writes=[Bb])
    S.op("dve", lambda E: E.tensor_tensor(out=d2, in0=a[:], in1=b[:], op=ALU.add), reads=[Ba, Bb], writes=[B_dst])

class K:
    pass


def build(stage=99):
    nc = bass.Bass("TRN2", target_bir_lowering=False)
    k = K()
    k.nc = nc

    def din(name, shape, dt=F32):
        return nc.dram_tensor(name, list(shape), dt, kind="ExternalInput").ap()

    def dscr(name, shape, dt=F32):
        return nc.dram_tensor(name, list(shape), dt, kind="Internal").ap()

    x = din("x", [T, D])
    norm_mix = din("norm_mix", [D])
    w_in = din("w_in", [D, O_END])
    ident_in = din("ident", [128, 128])
    pos_in = din("pos", [T], I32)
    invf_in = din("invf", [32])
    q_a_norm = din("q_a_norm", [1024])
    w_uq = din("w_uq", [1024, 3072])
    kv_a_norm = din("kv_a_norm", [512])
    w_ukv = din("w_ukv", [512, 4096])
    q_norm = din("q_norm", [192])
    k_norm = din("k_norm", [192])
    aon = din("aon", [2048])
    s_mix = dscr("s_mix", [TO, D], BF16)
    conv_w = din("conv_w", [4096, 5])
    conv_b = din("conv_b", [4096])
    a_log = din("a_log", [64])
    dt_bias = din("dt_bias", [64])
    d_skip = din("d_skip", [32])
    son = din("son", [2048])
    tri_in = din("tri", [4, 128, 128])
    w_out = din("w_out", [D, D])
    norm_ffn = din("norm_ffn", [D])
    w_query = din("w_query", [D, 2048])
    skT_in = din("skT", [16, 128, 128])
    UTt = din("UTt", [128, 128, 32, 128])
    Vexp = din("Vexp", [16384, D])
    s_hnT = dscr("s_hnT", [D, TO], BF16)
    s_WT = dscr("s_WT", [128, 128, TO], BF16)
    out = nc.dram_tensor("out", [TO, D], F32, kind="ExternalOutput").ap()

    s_kv = dscr("s_kv", [T, 640])
    s_xT = dscr("s_xT", [4096, T], BF16)
    s_q = dscr("s_q", [TO, 1024])
    s_z = dscr("s_z", [TO, 2048])

    dbg = {}
    if stage == 1:
        dbg["d_kv"] = nc.dram_tensor("d_kv", [T, 640], F32, kind="ExternalOutput").ap()
        dbg["d_xT"] = nc.dram_tensor("d_xT", [4096, T], BF16, kind="ExternalOutput").ap()
        dbg["d_q"] = nc.dram_tensor("d_q", [TO, 1024], F32, kind="ExternalOutput").ap()
        dbg["d_z"] = nc.dram_tensor("d_z", [TO, 2048], F32, kind="ExternalOutput").ap()
        s_kv, s_xT, s_q, s_z = dbg["d_kv"], dbg["d_xT"], dbg["d_q"], dbg["d_z"]

    if stage == 4:
        dbg["d_h"] = nc.dram_tensor("d_h", [TO, D], F32, kind="ExternalOutput").ap()
    if stage in (2, 3):
        dbg["d_mix"] = nc.dram_tensor("d_mix", [TO, D], BF16, kind="ExternalOutput").ap()
        s_mix = dbg["d_mix"]

    with ExitStack() as ctx:
        S = Sched(nc, ctx)
        uid = [0]

        def sb(c, name, shape, dt=F32):
            uid[0] += 1
            return c.enter_context(nc.sbuf_tensor("%s_%d" % (name, uid[0]), list(shape), dt))

        def ps(c, name, shape, dt=F32):
            uid[0] += 1
            return c.enter_context(nc.psum_tensor("%s_%d" % (name, uid[0]), list(shape), dt))

        ident = sb(ctx, "ident_sb", [128, 128], BF16)
        B_ident = Buf("ident")
        S.dma("pool", ident[:], ident_in, writes=[B_ident])
        outbufs = []
        B_mix = Buf()
        B_swt = Buf()
        B_fin = Buf()

        B_scr = Buf()
        import os as _os0
        SKIP = _os0.environ.get("KSKIP", "")
        with ExitStack() as c:
          if "A" not in SKIP:
              gain = sb(c, "gain", [128, D])
              B_gain = Buf()
              S.dma("sp", gain[:], bcast_rows(norm_mix, 128), writes=[B_gain])
              xt = [sb(c, "xt%d" % i, [128, D]) for i in range(2)]
              B_xt = [Buf() for _ in range(2)]
              junk = sb(c, "junk", [128, D], BF16)
              B_junk = Buf()
              xb = sb(c, "xb", [128, D], BF16)
              B_xb = Buf()
              st = sb(c, "stat", [128, 8])
              B_st = Buf()
              xnT = sb(c, "xnT", [128, 32, 512], BF16)
              B_xnT = [Buf() for _ in range(4)]
              wblk = [sb(c, "wblk%d" % i, [128, 32, 512], BF16) for i in range(2)]
              B_w = [Buf() for _ in range(2)]
              ost = [sb(c, "ost%d" % i, [128, 512]) for i in range(3)]
              B_ost = [Buf() for _ in range(3)]
              ostb = [sb(c, "ostb%d" % i, [128, 512], BF16) for i in range(3)]
              B_ostb = [Buf() for _ in range(3)]
              tp = [ps(c, "tp%d" % i, [128, 8, 128], BF16) for i in range(2)]
              B_tp = [PB() for _ in range(2)]
              acc = [ps(c, "acc%d" % i, [128, 512]) for i in range(4)]
              B_acc = [PB() for _ in range(4)]
              w_v = w_in.rearrange("(k p) c -> p k c", p=128)
              x_v = x.rearrange("(n p) d -> n p d", p=128)

              blocks = [(O_KV, 512, "kv"), (O_ROPE, 128, "kv")]
              blocks += [(O_X + 512 * i, 512, "feat") for i in range(8)]
              nb_other = len(blocks)
              blocks += [(O_Q + 512 * i, 512, "q") for i in range(2)]
              blocks += [(O_Z + 512 * i, 512, "z") for i in range(4)]
              wi = 0
              ai = 0
              oi = 0
              for tb in range(4):
                  for tt in range(4):
                      tile = tb * 4 + tt
                      xi = tile % 2
                      S.dma("sp", xt[xi][:], x_v[tile], writes=[B_xt[xi]])
                      S.op("act", lambda E, xi=xi: E.activation(out=junk[:], in_=xt[xi][:], func=AF.Square,
                                                                accum_out=st[:, 0:1]),
                           reads=[B_xt[xi]], writes=[B_junk, B_st])
                      S.op("dve", lambda E: E.tensor_scalar(out=st[:, 1:2], in0=st[:, 0:1], scalar1=1.0 / D,
                                                            scalar2=EPS, op0=ALU.mult, op1=ALU.add),
                           reads=[B_st], writes=[B_st])
                      S.op("act", lambda E: E.activation(out=st[:, 2:3], in_=st[:, 1:2], func=AF.Sqrt),
                           reads=[B_st], writes=[B_st])
                      S.op("dve", lambda E: E.reciprocal(out=st[:, 3:4], in_=st[:, 2:3]), reads=[B_st], writes=[B_st])
                      S.op("dve", lambda E, xi=xi: E.scalar_tensor_tensor(out=xb[:], in0=xt[xi][:], scalar=st[:, 3:4],
                                                                          in1=gain[:], op0=ALU.mult, op1=ALU.mult),
                           reads=[B_xt[xi], B_st, B_gain], writes=[B_xb])
                      for g in range(4):
                          ti = g % 2
                          for j in range(8):
                              kk = g * 8 + j
                              S.op("pe", lambda E, ti=ti, j=j, kk=kk: E.transpose(out=tp[ti][:, j, :],
                                                                                 in_=xb[:, kk * 128:(kk + 1) * 128],
                                                                                 identity=ident[:]),
                                   reads=[B_xb, B_ident], writes=[B_tp[ti]])
                          eng = "act" if g % 2 == 0 else "dve"
                          if eng == "act":
                              S.op("act", lambda E, ti=ti, g=g, tt=tt: E.copy(
                                  out=xnT[:, g * 8:(g + 1) * 8, tt * 128:(tt + 1) * 128], in_=tp[ti][:]),
                                  reads=[B_tp[ti]], writes=[B_xnT[tt]])
                          else:
                              S.op("dve", lambda E, ti=ti, g=g, tt=tt: E.tensor_copy(
                                  out=xnT[:, g * 8:(g + 1) * 8, tt * 128:(tt + 1) * 128], in_=tp[ti][:]),
                                  reads=[B_tp[ti]], writes=[B_xnT[tt]])
                  blks = blocks if tb < 2 else blocks[:nb_other]
                  for (c0, cw, kind) in blks:
                      wb = wblk[wi % 2]
                      Bw = B_w[wi % 2]
                      wi += 1
                      S.dma("pool", wb[:, :, 0:cw], w_v[:, :, c0:c0 + cw], writes=[Bw])
                      if kind == "feat":
                          for cc in range(cw // 128):
                              a = acc[ai % 4]
                              Ba = B_acc[ai % 4]
                              ai += 1
                              for kk in range(32):
                                  S.op("pe", lambda E, a=a, wb=wb, kk=kk, cc=cc: E.matmul(
                                      a[:], lhsT=wb[:, kk, cc * 128:(cc + 1) * 128], rhs=xnT[:, kk, :],
                                      start=(kk == 0), stop=(kk == 31)),
                                      reads=[Bw] + B_xnT, writes=[Ba])
                              o = ostb[oi % 3]
                              Bo = B_ostb[oi % 3]
                              if oi % 2 == 0:
                                  S.op("act", lambda E, o=o, a=a: E.copy(out=o[:], in_=a[:]), reads=[Ba], writes=[Bo])
                              else:
                                  S.op("dve", lambda E, o=o, a=a: E.tensor_copy(out=o[:], in_=a[:]), reads=[Ba], writes=[Bo])
                              oi += 1
                              r0 = c0 - O_X + cc * 128
                              S.dma("sp", s_xT[r0:r0 + 128, tb * 512:(tb + 1) * 512], o[:], reads=[Bo], writes=[B_scr])
                      else:
                          for tt in range(4):
                              a = acc[ai % 4]
                              Ba = B_acc[ai % 4]
                              ai += 1
                              for kk in range(32):
                                  S.op("pe", lambda E, a=a, wb=wb, kk=kk, tt=tt, cw=cw: E.matmul(
                                      a[:, 0:cw], lhsT=xnT[:, kk, tt * 128:(tt + 1) * 128], rhs=wb[:, kk, 0:cw],
                                      start=(kk == 0), stop=(kk == 31)),
                                      reads=[Bw, B_xnT[tt]], writes=[Ba])
                              o = ost[oi % 3]
                              Bo = B_ost[oi % 3]
                              if oi % 2 == 0:
                                  S.op("act", lambda E, o=o, a=a, cw=cw: E.copy(out=o[:, 0:cw], in_=a[:, 0:cw]),
                                       reads=[Ba], writes=[Bo])
                              else:
                                  S.op("dve", lambda E, o=o, a=a, cw=cw: E.tensor_copy(out=o[:, 0:cw], in_=a[:, 0:cw]),
                                       reads=[Ba], writes=[Bo])
                              oi += 1
                              t0 = tb * 512 + tt * 128
                              if kind == "kv":
                                  dst = s_kv[t0:t0 + 128, c0:c0 + cw]
                              elif kind == "q":
                                  dst = s_q[t0:t0 + 128, c0 - O_Q:c0 - O_Q + cw]
                              else:
                                  dst = s_z[t0:t0 + 128, c0 - O_Z:c0 - O_Z + cw]
                              S.dma("sp", dst, o[:, 0:cw], reads=[Bo], writes=[B_scr])
          outbufs.append(B_scr)
          S.barrier()

        if stage >= 2 and "B" not in SKIP:
          with ExitStack() as c:
            TWO_PI = 2.0 * np.pi
            C1 = 6.28125
            C2 = float(np.float32(TWO_PI - C1).view(np.uint32) & np.uint32(0xFFFFF000)) if False else 0.0019350051879882812
            C3 = float(TWO_PI - C1 - C2)
            MAGIC = 12582912.0
            SCALE = 192.0 ** -0.5
            qan = sb(c, "qan", [128, 1024]); kvan = sb(c, "kvan", [128, 512])
            qn_bc = sb(c, "qn_bc", [128, 192]); kn_bc = sb(c, "kn_bc", [128, 192])
            aon_bc = sb(c, "aon_bc", [128, 2048]); invf = sb(c, "invf_sb", [128, 32])
            B_const = Buf()
            for dst, src in ((qan, q_a_norm), (kvan, kv_a_norm), (qn_bc, q_norm), (kn_bc, k_norm), (aon_bc, aon), (invf, invf_in)):
                S.dma("sp", dst[:], src.partition_broadcast(128), writes=[B_const])
            posi = sb(c, "posi", [128, 16], I32)
            S.dma("sp", posi[:], pos_in.rearrange("(n p) -> p n", p=128), writes=[B_const], allow_slow_non_contiguous=True) if False else None
            posf = sb(c, "posf", [128, 16])
            ang = sb(c, "ang", [128, 16, 32]); kk_t = sb(c, "kk_t", [128, 16, 32]); rr = sb(c, "rr", [128, 16, 32])
            cs = sb(c, "cs", [128, 16, 64])
            B_cs = Buf()
            pos_t = sb(c, "pos_t", [16, 128], I32)
            for n in range(16):
                S.dma("sp", posi[:, n:n + 1], pos_in[n * 128:(n + 1) * 128].rearrange("(p o) -> p o", o=1), writes=[B_const])
            S.op("dve", lambda E: E.tensor_copy(out=posf[:], in_=posi[:]), reads=[B_const], writes=[B_cs])
            S.op("dve", lambda E: E.tensor_tensor(out=ang[:], in0=posf[:].unsqueeze(2).broadcast_to([128, 16, 32]),
                                                  in1=invf[:].unsqueeze(1).broadcast_to([128, 16, 32]), op=ALU.mult),
                 reads=[B_const, B_cs], writes=[B_cs])
            for which, shift in ((1, 0.0), (0, 0.25)):
                S.op("dve", lambda E, shift=shift: E.tensor_scalar(out=kk_t[:], in0=ang[:], scalar1=1.0 / TWO_PI, scalar2=shift,
                                                                   op0=ALU.mult, op1=ALU.add), reads=[B_cs], writes=[B_cs])
                S.op("dve", lambda E: E.tensor_scalar(out=kk_t[:], in0=kk_t[:], scalar1=MAGIC, scalar2=None, op0=ALU.add),
                     reads=[B_cs], writes=[B_cs])
                S.op("dve", lambda E: E.tensor_scalar(out=kk_t[:], in0=kk_t[:], scalar1=-MAGIC, scalar2=None, op0=ALU.add),
                     reads=[B_cs], writes=[B_cs])
                S.op("dve", lambda E: E.scalar_tensor_tensor(out=rr[:], in0=kk_t[:], scalar=-C1, in1=ang[:], op0=ALU.mult, op1=ALU.add),
                     reads=[B_cs], writes=[B_cs])
                S.op("dve", lambda E: E.scalar_tensor_tensor(out=rr[:], in0=kk_t[:], scalar=-C2, in1=rr[:], op0=ALU.mult, op1=ALU.add),
                     reads=[B_cs], writes=[B_cs])
                S.op("dve", lambda E: E.scalar_tensor_tensor(out=rr[:], in0=kk_t[:], scalar=-C3, in1=rr[:], op0=ALU.mult, op1=ALU.add),
                     reads=[B_cs], writes=[B_cs])
                if shift != 0.0:
                    S.op("dve", lambda E: E.tensor_scalar(out=rr[:], in0=rr[:], scalar1=float(np.pi / 2), scalar2=None, op0=ALU.add),
                         reads=[B_cs], writes=[B_cs])
                S.op("dve", lambda E: E.tensor_scalar(out=rr[:], in0=rr[:], scalar1=3.1415925, scalar2=-3.1415925,
                                                      op0=ALU.min, op1=ALU.max), reads=[B_cs], writes=[B_cs])
                S.op("act", lambda E, which=which: E.activation(out=cs[:, :, which * 32:(which + 1) * 32], in_=rr[:], func=AF.Sin),
                     reads=[B_cs], writes=[B_cs])

            ckvT = sb(c, "ckvT", [128, 4, T], BF16); B_ckvT = Buf()
            cqT = sb(c, "cqT", [128, 8, TO], BF16); B_cqT = Buf()
            krr = sb(c, "krr", [128, 16, 64]); B_krr = Buf()
            ssr = sb(c, "ssr", [128, 16]); B_ssr = Buf()
            stB = sb(c, "stB", [128, 16]); B_stB = Buf()
            with ExitStack() as c2:
                lt = [sb(c2, "lt%d" % i, [128, 1024]) for i in range(2)]; B_lt = [Buf(), Buf()]
                jk = sb(c2, "jkB", [128, 1024], BF16); B_jk = Buf()
                nb = sb(c2, "nbB", [128, 1024], BF16); B_nb = Buf()
                tq = sb(c2, "tqB", [128, 64]); B_tq = Buf()
                tpB = [ps(c2, "tpB%d" % i, [128, 8, 128], BF16) for i in range(2)]; B_tpB = [PB(), PB()]
                for tile in range(16):
                    li = tile % 2
                    S.dma("sp", lt[li][:, 0:576], s_kv[tile * 128:(tile + 1) * 128, 0:576], reads=[B_scr], writes=[B_lt[li]])
                    S.op("act", lambda E, li=li: E.activation(out=jk[:, 0:512], in_=lt[li][:, 0:512], func=AF.Square,
                                                              accum_out=stB[:, 0:1]), reads=[B_lt[li]], writes=[B_jk, B_stB])
                    S.op("act", lambda E, li=li, tile=tile: E.activation(out=jk[:, 512:576], in_=lt[li][:, 512:576], func=AF.Square,
                                                                         accum_out=ssr[:, tile:tile + 1]),
                         reads=[B_lt[li]], writes=[B_jk, B_ssr])
                    S.op("dve", lambda E: E.tensor_scalar(out=stB[:, 1:2], in0=stB[:, 0:1], scalar1=1.0 / 512, scalar2=EPS,
                                                          op0=ALU.mult, op1=ALU.add), reads=[B_stB], writes=[B_stB])
                    S.op("act", lambda E: E.activation(out=stB[:, 2:3], in_=stB[:, 1:2], func=AF.Sqrt), reads=[B_stB], writes=[B_stB])
                    S.op("dve", lambda E: E.reciprocal(out=stB[:, 3:4], in_=stB[:, 2:3]), reads=[B_stB], writes=[B_stB])
                    S.op("dve", lambda E, li=li: E.scalar_tensor_tensor(out=nb[:, 0:512], in0=lt[li][:, 0:512], scalar=stB[:, 3:4],
                                                                        in1=kvan[:], op0=ALU.mult, op1=ALU.mult),
                         reads=[B_lt[li], B_stB, B_const], writes=[B_nb])
                    ti = tile % 2
                    for j in range(4):
                        S.op("pe", lambda E, ti=ti, j=j: E.transpose(out=tpB[ti][:, j, :], in_=nb[:, j * 128:(j + 1) * 128],
                                                                     identity=ident[:]), reads=[B_nb, B_ident], writes=[B_tpB[ti]])
                    S.op("act", lambda E, ti=ti, tile=tile: E.copy(out=ckvT[:, :, tile * 128:(tile + 1) * 128], in_=tpB[ti][:, 0:4, :]),
                         reads=[B_tpB[ti]], writes=[B_ckvT])
                    S.op("dve", lambda E, li=li: E.tensor_tensor(out=tq[:], in0=lt[li][:, 512:576], in1=kn_bc[:, 128:192], op=ALU.mult),
                         reads=[B_lt[li], B_const], writes=[B_tq])
                    rope_ops(S, tq, B_tq, krr[:, tile, :], B_krr, cs, B_cs, tile, sb, c2, "k%d" % tile, nheads=1)
                for tile in range(8):
                    li = tile % 2
                    S.dma("sp", lt[li][:], s_q[tile * 128:(tile + 1) * 128, :], reads=[B_scr], writes=[B_lt[li]])
                    S.op("act", lambda E, li=li: E.activation(out=jk[:], in_=lt[li][:], func=AF.Square, accum_out=stB[:, 0:1]),
                         reads=[B_lt[li]], writes=[B_jk, B_stB])
                    S.op("dve", lambda E: E.tensor_scalar(out=stB[:, 1:2], in0=stB[:, 0:1], scalar1=1.0 / 1024, scalar2=EPS,
                                                          op0=ALU.mult, op1=ALU.add), reads=[B_stB], writes=[B_stB])
                    S.op("act", lambda E: E.activation(out=stB[:, 2:3], in_=stB[:, 1:2], func=AF.Sqrt), reads=[B_stB], writes=[B_stB])
                    S.op("dve", lambda E: E.reciprocal(out=stB[:, 3:4], in_=stB[:, 2:3]), reads=[B_stB], writes=[B_stB])
                    S.op("dve", lambda E, li=li: E.scalar_tensor_tensor(out=nb[:], in0=lt[li][:], scalar=stB[:, 3:4], in1=qan[:],
                                                                        op0=ALU.mult, op1=ALU.mult),
                         reads=[B_lt[li], B_stB, B_const], writes=[B_nb])
                    ti = tile % 2
                    for j in range(8):
                        S.op("pe", lambda E, ti=ti, j=j: E.transpose(out=tpB[ti][:, j, :], in_=nb[:, j * 128:(j + 1) * 128],
                                                                     identity=ident[:]), reads=[B_nb, B_ident], writes=[B_tpB[ti]])
                    S.op("act", lambda E, ti=ti, tile=tile: E.copy(out=cqT[:, :, tile * 128:(tile + 1) * 128], in_=tpB[ti][:]),
                         reads=[B_tpB[ti]], writes=[B_cqT])
                S.barrier()

            HG = 4
            KT = sb(c, "KT", [128, HG, T], BF16); B_KT = Buf()
            KTr = sb(c, "KTr", [64, HG, T], BF16); B_KTr = Buf()
            vext = sb(c, "vext", [128, 16, HG, 130], BF16); B_vext = Buf()
            QT = sb(c, "QT", [128, HG, TO], BF16); B_QT = Buf()
            QTr = sb(c, "QTr", [64, HG, TO], BF16); B_QTr = Buf()
            wkv = sb(c, "wkv", [128, 4, HG * 256], BF16); B_wkv = Buf()
            wq = sb(c, "wq", [128, 8, HG * 192], BF16); B_wq = Buf()
            S.op("pool", lambda E: E.memset(vext[:], 1.0), writes=[B_vext])
            for hg in range(NH // HG):
                S.dma("pool", wkv[:], w_ukv.rearrange("(k p) c -> p k c", p=128)[:, :, hg * HG * 256:(hg + 1) * HG * 256],
                      writes=[B_wkv])
                S.dma("pool", wq[:], w_uq.rearrange("(k p) c -> p k c", p=128)[:, :, hg * HG * 192:(hg + 1) * HG * 192],
                      writes=[B_wq])
                with ExitStack() as c2:
                    pk = [ps(c2, "pk%d" % i, [128, 512]) for i in range(4)]; B_pk = [PB() for _ in range(4)]
                    tpk = [ps(c2, "tpk%d" % i, [128, 8, 128], BF16) for i in range(2)]; B_tpk = [PB(), PB()]
                    tpr = [ps(c2, "tpr%d" % i, [128, 8, 128], BF16) for i in range(2)]; B_tpr = [PB(), PB()]
                    jk = sb(c2, "jkK", [128, 192], BF16); B_jk = Buf()
                    sk = [sb(c2, "sk%d" % i, [128, 16]) for i in range(2)]; B_sk = [Buf(), Buf()]
                    kn = [sb(c2, "kn%d" % i, [128, HG, 192], BF16) for i in range(2)]; B_kn = [Buf(), Buf()]
                    for tile in range(16):
                        pi = tile % 2
                        for b in range(2):
                            for kc in range(4):
                                S.op("pe", lambda E, pi=pi, b=b, kc=kc, tile=tile: E.matmul(
                                    pk[pi * 2 + b][:], lhsT=ckvT[:, kc, tile * 128:(tile + 1) * 128],
                                    rhs=wkv[:, kc, b * 512:(b + 1) * 512], start=(kc == 0), stop=(kc == 3)),
                                    reads=[B_ckvT, B_wkv], writes=[B_pk[pi * 2 + b]])
                        st_ = sk[pi]; Bs = B_sk[pi]
                        for hl in range(HG):
                            p_ = pk[pi * 2 + hl // 2]; Bp = B_pk[pi * 2 + hl // 2]; off = (hl % 2) * 256
                            S.op("act", lambda E, p_=p_, off=off, st_=st_, hl=hl: E.activation(
                                out=jk[:, 0:128], in_=p_[:, off:off + 128], func=AF.Square, accum_out=st_[:, hl:hl + 1]),
                                reads=[Bp], writes=[B_jk, Bs])
                        S.op("dve", lambda E, st_=st_, tile=tile: E.tensor_scalar(
                            out=st_[:, 4:8], in0=st_[:, 0:4], scalar1=ssr[:, tile:tile + 1], scalar2=1.0 / 192,
                            op0=ALU.add, op1=ALU.mult), reads=[Bs, B_ssr], writes=[Bs])
                        S.op("dve", lambda E, st_=st_: E.tensor_scalar(out=st_[:, 4:8], in0=st_[:, 4:8], scalar1=EPS, scalar2=None,
                                                                       op0=ALU.add), reads=[Bs], writes=[Bs])
                        S.op("act", lambda E, st_=st_: E.activation(out=st_[:, 8:12], in_=st_[:, 4:8], func=AF.Sqrt), reads=[Bs], writes=[Bs])
                        S.op("dve", lambda E, st_=st_: E.reciprocal(out=st_[:, 12:16], in_=st_[:, 8:12]), reads=[Bs], writes=[Bs])
                        kn_ = kn[pi]; Bk = B_kn[pi]
                        for hl in range(HG):
                            p_ = pk[pi * 2 + hl // 2]; Bp = B_pk[pi * 2 + hl // 2]; off = (hl % 2) * 256
                            S.op("dve", lambda E, p_=p_, off=off, st_=st_, hl=hl, kn_=kn_: E.scalar_tensor_tensor(
                                out=kn_[:, hl, 0:128], in0=p_[:, off:off + 128], scalar=st_[:, 12 + hl:13 + hl], in1=kn_bc[:, 0:128],
                                op0=ALU.mult, op1=ALU.mult), reads=[Bp, Bs, B_const], writes=[Bk])
                            S.op("dve", lambda E, st_=st_, hl=hl, kn_=kn_, tile=tile: E.tensor_scalar(
                                out=kn_[:, hl, 128:192], in0=krr[:, tile, :], scalar1=st_[:, 12 + hl:13 + hl], scalar2=None, op0=ALU.mult),
                                reads=[B_krr, Bs], writes=[Bk])
                            S.op("act", lambda E, p_=p_, off=off, hl=hl, tile=tile: E.copy(
                                out=vext[:, tile, hl, 0:128], in_=p_[:, off + 128:off + 256]), reads=[Bp], writes=[B_vext])
                        for hl in range(HG):
                            S.op("pe", lambda E, pi=pi, hl=hl, kn_=kn_: E.transpose(out=tpk[pi][:, hl, :], in_=kn_[:, hl, 0:128],
                                                                                   identity=ident[:]), reads=[Bk, B_ident], writes=[B_tpk[pi]])
                            S.op("pe", lambda E, pi=pi, hl=hl, kn_=kn_: E.transpose(out=tpr[pi][0:64, hl, :], in_=kn_[:, hl, 128:192],
                                                                                   identity=ident[:]), reads=[Bk, B_ident], writes=[B_tpr[pi]])
                        S.op("dve", lambda E, pi=pi, tile=tile: E.tensor_copy(out=KT[:, :, tile * 128:(tile + 1) * 128], in_=tpk[pi][:, 0:4, :]),
                             reads=[B_tpk[pi]], writes=[B_KT])
                        S.op("act", lambda E, pi=pi, tile=tile: E.copy(out=KTr[:, :, tile * 128:(tile + 1) * 128], in_=tpr[pi][0:64, 0:4, :]),
                             reads=[B_tpr[pi]], writes=[B_KTr])
                    S.barrier()
                with ExitStack() as c2:
                    pq = [ps(c2, "pq%d" % i, [128, 512]) for i in range(4)]; B_pq = [PB() for _ in range(4)]
                    tpk = [ps(c2, "tpq%d" % i, [128, 8, 128], BF16) for i in range(2)]; B_tpk = [PB(), PB()]
                    tpr = [ps(c2, "tpqr%d" % i, [128, 8, 128], BF16) for i in range(2)]; B_tpr = [PB(), PB()]
                    jk = sb(c2, "jkQ", [128, 192], BF16); B_jk = Buf()
                    sk = [sb(c2, "sq%d" % i, [128, 16]) for i in range(2)]; B_sk = [Buf(), Buf()]
                    qg = [sb(c2, "qg%d" % i, [128, HG, 192]) for i in range(2)]; B_qg = [Buf(), Buf()]
                    qn_ = [sb(c2, "qn%d" % i, [128, HG, 192], BF16) for i in range(2)]; B_qn = [Buf(), Buf()]
                    for tile in range(8):
                        pi = tile % 2
                        for b in range(2):
                            for kc in range(8):
                                S.op("pe", lambda E, pi=pi, b=b, kc=kc, tile=tile: E.matmul(
                                    pq[pi * 2 + b][:, 0:384], lhsT=cqT[:, kc, tile * 128:(tile + 1) * 128],
                                    rhs=wq[:, kc, b * 384:(b + 1) * 384], start=(kc == 0), stop=(kc == 7)),
                                    reads=[B_cqT, B_wq], writes=[B_pq[pi * 2 + b]])
                        st_ = sk[pi]; Bs = B_sk[pi]
                        for hl in range(HG):
                            p_ = pq[pi * 2 + hl // 2]; Bp = B_pq[pi * 2 + hl // 2]; off = (hl % 2) * 192
                            S.op("act", lambda E, p_=p_, off=off, st_=st_, hl=hl: E.activation(
                                out=jk[:], in_=p_[:, off:off + 192], func=AF.Square, accum_out=st_[:, hl:hl + 1]),
                                reads=[Bp], writes=[B_jk, Bs])
                        S.op("dve", lambda E, st_=st_: E.tensor_scalar(out=st_[:, 4:8], in0=st_[:, 0:4], scalar1=1.0 / 192, scalar2=EPS,
                                                                       op0=ALU.mult, op1=ALU.add), reads=[Bs], writes=[Bs])
                        S.op("act", lambda E, st_=st_: E.activation(out=st_[:, 8:12], in_=st_[:, 4:8], func=AF.Sqrt), reads=[Bs], writes=[Bs])
                        S.op("dve", lambda E, st_=st_: E.reciprocal(out=st_[:, 12:16], in_=st_[:, 8:12]), reads=[Bs], writes=[Bs])
                        g_ = qg[pi]; Bg = B_qg[pi]; n_ = qn_[pi]; Bn = B_qn[pi]
                        for hl in range(HG):
                            p_ = pq[pi * 2 + hl // 2]; Bp = B_pq[pi * 2 + hl // 2]; off = (hl % 2) * 192
                            S.op("dve", lambda E, p_=p_, off=off, st_=st_, hl=hl, g_=g_: E.scalar_tensor_tensor(
                                out=g_[:, hl, :], in0=p_[:, off:off + 192], scalar=st_[:, 12 + hl:13 + hl], in1=qn_bc[:],
                                op0=ALU.mult, op1=ALU.mult), reads=[Bp, Bs, B_const], writes=[Bg])
                        S.op("dve", lambda E, g_=g_, n_=n_: E.tensor_copy(out=n_[:, :, 0:128], in_=g_[:, :, 0:128]), reads=[Bg], writes=[Bn])
                        rope_ops(S, g_[:, :, 128:192], Bg, n_[:, :, 128:192], Bn, cs, B_cs, tile, sb, c2, "q%d_%d" % (hg, tile), nheads=HG)
                        for hl in range(HG):
                            S.op("pe", lambda E, pi=pi, hl=hl, n_=n_: E.transpose(out=tpk[pi][:, hl, :], in_=n_[:, hl, 0:128],
                                                                                  identity=ident[:]), reads=[Bn, B_ident], writes=[B_tpk[pi]])
                            S.op("pe", lambda E, pi=pi, hl=hl, n_=n_: E.transpose(out=tpr[pi][0:64, hl, :], in_=n_[:, hl, 128:192],
                                                                                  identity=ident[:]), reads=[Bn, B_ident], writes=[B_tpr[pi]])
                        S.op("dve", lambda E, pi=pi, tile=tile: E.tensor_copy(out=QT[:, :, tile * 128:(tile + 1) * 128], in_=tpk[pi][:, 0:4, :]),
                             reads=[B_tpk[pi]], writes=[B_QT])
                        S.op("act", lambda E, pi=pi, tile=tile: E.copy(out=QTr[:, :, tile * 128:(tile + 1) * 128], in_=tpr[pi][0:64, 0:4, :]),
                             reads=[B_tpr[pi]], writes=[B_QTr])
                    S.barrier()
                with ExitStack() as c2:
                    pS = [ps(c2, "pS%d" % i, [128, 512]) for i in range(3)]; B_pS = [PB() for _ in range(3)]
                    pO = [ps(c2, "pO%d" % i, [128, 512]) for i in range(4)]; B_pO = [PB() for _ in range(4)]
                    PT = [sb(c2, "PT%d" % i, [128, 512], BF16) for i in range(3)]; B_PT = [Buf() for _ in range(3)]
                    of = [sb(c2, "of%d" % i, [128, 128]) for i in range(2)]; B_of = [Buf(), Buf()]
                    jk = sb(c2, "jkA", [128, 128], BF16); B_jk = Buf()
                    sa = [sb(c2, "sa%d" % i, [128, 8]) for i in range(2)]; B_sa = [Buf(), Buf()]
                    ob = [sb(c2, "ob%d" % i, [128, 128], BF16) for i in range(2)]; B_ob = [Buf(), Buf()]
                    si = 0
                    oi2 = 0
                    for hl in range(HG):
                        h = hg * HG + hl
                        for tg in range(2):
                            def s_mm(tk, slot):
                                p_ = pS[slot % 3]; Bp = B_pS[slot % 3]
                                S.op("pe", lambda E, p_=p_, hl=hl, tk=tk, tg=tg: E.matmul(
                                    p_[:], lhsT=KT[:, hl, tk * 128:(tk + 1) * 128], rhs=QT[:, hl, tg * 512:(tg + 1) * 512],
                                    start=True, stop=False), reads=[B_KT, B_QT], writes=[Bp])
                                S.op("pe", lambda E, p_=p_, hl=hl, tk=tk, tg=tg: E.matmul(
                                    p_[:], lhsT=KTr[:, hl, tk * 128:(tk + 1) * 128], rhs=QTr[:, hl, tg * 512:(tg + 1) * 512],
                                    start=False, stop=True), reads=[B_KTr, B_QTr], writes=[Bp])
                            s_mm(0, si)
                            for tk in range(16):
                                p_ = pS[si % 3]; Bp = B_pS[si % 3]; pt = PT[si % 3]; Bpt = B_PT[si % 3]
                                if tk + 1 < 16:
                                    s_mm(tk + 1, si + 1)
                                si += 1
                                S.op("act", lambda E, p_=p_, pt=pt: E.activation(out=pt[:], in_=p_[:], func=AF.Exp, scale=SCALE),
                                     reads=[Bp], writes=[Bpt])
                                for tqt in range(4):
                                    S.op("pe", lambda E, pt=pt, tqt=tqt, tk=tk, hl=hl: E.matmul(
                                        pO[tqt][:, 0:129], lhsT=pt[:, tqt * 128:(tqt + 1) * 128], rhs=vext[:, tk, hl, 0:129],
                                        start=(tk == 0), stop=(tk == 15)), reads=[Bpt, B_vext], writes=[B_pO[tqt]])
                            for tqt in range(4):
                                o_ = of[oi2 % 2]; Bo = B_of[oi2 % 2]; s_ = sa[oi2 % 2]; Bs = B_sa[oi2 % 2]
                                b_ = ob[oi2 % 2]; Bb = B_ob[oi2 % 2]; oi2 += 1
                                S.op("dve", lambda E, s_=s_, tqt=tqt: E.reciprocal(out=s_[:, 0:1], in_=pO[tqt][:, 128:129]),
                                     reads=[B_pO[tqt]], writes=[Bs])
                                S.op("dve", lambda E, s_=s_, tqt=tqt, o_=o_: E.tensor_scalar(
                                    out=o_[:], in0=pO[tqt][:, 0:128], scalar1=s_[:, 0:1], scalar2=None, op0=ALU.mult),
                                    reads=[B_pO[tqt], Bs], writes=[Bo])
                                S.op("act", lambda E, o_=o_, s_=s_: E.activation(out=jk[:], in_=o_[:], func=AF.Square, accum_out=s_[:, 1:2]),
                                     reads=[Bo], writes=[B_jk, Bs])
                                S.op("dve", lambda E, s_=s_: E.tensor_scalar(out=s_[:, 2:3], in0=s_[:, 1:2], scalar1=1.0 / 128, scalar2=EPS,
                                                                             op0=ALU.mult, op1=ALU.add), reads=[Bs], writes=[Bs])
                                S.op("act", lambda E, s_=s_: E.activation(out=s_[:, 3:4], in_=s_[:, 2:3], func=AF.Sqrt), reads=[Bs], writes=[Bs])
                                S.op("dve", lambda E, s_=s_: E.reciprocal(out=s_[:, 4:5], in_=s_[:, 3:4]), reads=[Bs], writes=[Bs])
                                S.op("dve", lambda E, o_=o_, s_=s_, b_=b_, h=h: E.scalar_tensor_tensor(
                                    out=b_[:], in0=o_[:], scalar=s_[:, 4:5], in1=aon_bc[:, h * 128:(h + 1) * 128], op0=ALU.mult, op1=ALU.mult),
                                    reads=[Bo, Bs, B_const], writes=[Bb])
                                t0 = tg * 512 + tqt * 128
                                S.dma("sp", s_mix[t0:t0 + 128, h * 128:(h + 1) * 128], b_[:], reads=[Bb], writes=[B_mix])
                    S.barrier()
            S.barrier()

        if stage >= 3 and "C" not in SKIP:
          with ExitStack() as c:
            B_cc = Buf()
            tri = sb(c, "tri", [128, 4, 128])
            S.dma("sp", tri[:], tri_in.rearrange("f k l -> k f l"), writes=[B_cc])
            identF = sb(c, "identF", [128, 128])
            S.dma("sp", identF[:], ident_in, writes=[B_cc])
            alog_bc = sb(c, "alog_bc", [128, 64]); dtb_bc = sb(c, "dtb_bc", [128, 64]); dsk_bc = sb(c, "dsk_bc", [128, 32])
            son_bc = sb(c, "son_bc", [128, 2048])
            for dst, src in ((alog_bc, a_log), (dtb_bc, dt_bias), (dsk_bc, d_skip), (son_bc, son)):
                S.dma("sp", dst[:], src.partition_broadcast(128), writes=[B_cc])
            A_bc = sb(c, "A_bc", [128, 64])
            S.op("act", lambda E: E.activation(out=A_bc[:], in_=alog_bc[:], func=AF.Exp), reads=[B_cc], writes=[B_cc])
            S.op("dve", lambda E: E.tensor_scalar(out=A_bc[:], in0=A_bc[:], scalar1=-1.0, scalar2=None, op0=ALU.mult),
                 reads=[B_cc], writes=[B_cc])
            dtr = sb(c, "dtr", [128, 16, 64]); dtv = sb(c, "dtv", [128, 16, 64]); adt = sb(c, "adt", [128, 16, 64])
            tmpd = sb(c, "tmpd", [128, 16, 64])
            B_dt = Buf()
            for tile in range(16):
                S.dma("sp", dtr[:, tile, :], s_kv[tile * 128:(tile + 1) * 128, 576:640], reads=[B_scr], writes=[B_dt])
            bc64 = lambda t_: t_[:].unsqueeze(1).broadcast_to([128, 16, 64])
            S.op("dve", lambda E: E.tensor_tensor(out=dtr[:], in0=dtr[:], in1=bc64(dtb_bc), op=ALU.add), reads=[B_dt, B_cc], writes=[B_dt])
            S.op("act", lambda E: E.activation(out=tmpd[:], in_=dtr[:], func=AF.Abs), reads=[B_dt], writes=[B_dt])
            S.op("act", lambda E: E.activation(out=tmpd[:], in_=tmpd[:], func=AF.Exp, scale=-1.0), reads=[B_dt], writes=[B_dt])
            S.op("act", lambda E: E.activation(out=tmpd[:], in_=tmpd[:], func=AF.Ln, bias=1.0), reads=[B_dt], writes=[B_dt])
            S.op("dve", lambda E: E.scalar_tensor_tensor(out=dtv[:], in0=dtr[:], scalar=0.0, in1=tmpd[:], op0=ALU.max, op1=ALU.add),
                 reads=[B_dt], writes=[B_dt])
            S.op("dve", lambda E: E.tensor_tensor(out=adt[:], in0=dtv[:], in1=bc64(A_bc), op=ALU.mult), reads=[B_dt, B_cc], writes=[B_dt])

            import os as _os
            SUB = int(_os.environ.get("KSUB", "99"))
            cw_v = conv_w.rearrange("(n p) j -> n p j", p=128)
            cb_v = conv_b.rearrange("(n p o) -> n p o", p=128, o=1)
            cin = [sb(c, "cin%d" % i, [128, T + 4], BF16) for i in range(2)]; B_cin = [Buf(), Buf()]
            for i in range(2):
                S.op("pool", lambda E, i=i: E.memset(cin[i][:], 0.0), writes=[B_cin[i]])
            cacc = sb(c, "cacc", [128, T]); B_cacc = Buf()
            cwt = [sb(c, "cwt%d" % i, [128, 8]) for i in range(2)]; B_cwt = [Buf(), Buf()]
            cT = [sb(c, "cT%d" % i, [128, T], BF16) for i in range(4)]; B_cT = [Buf() for _ in range(4)]
            xtok = sb(c, "xtok", [128, 16, 256], BF16); B_xtok = Buf()
            Btok = sb(c, "Btok", [128, 16, 128], BF16); B_Btok = Buf()
            CBT = sb(c, "CBT", [128, 8, 128], BF16); B_CBT = Buf()
            yacc = sb(c, "yacc", [128, 8, 256]); B_yacc = Buf()
            state = sb(c, "state", [128, 256]); B_state = Buf()
            state_bf = sb(c, "state_bf", [128, 256], BF16); B_stbf = Buf()
            P_tp = ps(c, "P_tp", [128, 8, 128], BF16); B_Ptp = PB()
            P_ct = ps(c, "P_ct", [128, 512]); B_Pct = PB()
            P_cb = [ps(c, "P_cb%d" % i, [128, 4, 128]) for i in range(2)]; B_Pcb = [PB(), PB()]
            P_cbt = ps(c, "P_cbt", [128, 512]); B_Pcbt = PB()
            P_y = ps(c, "P_y", [128, 512]); B_Py = PB()
            P_yo = ps(c, "P_yo", [128, 512]); B_Pyo = PB()
            P_st = ps(c, "P_st", [128, 512]); B_Pst = PB()
            ci_n = 0
            sm = [sb(c, "sm%d" % i, [128, 32]) for i in range(2)]; B_sm = [Buf(), Buf()]
            arep = [sb(c, "arep%d" % i, [128, 4, 128]) for i in range(2)]; B_arep = [Buf(), Buf()]
            LT = [sb(c, "LT%d" % i, [128, 4, 128]) for i in range(2)]; B_LT = [Buf(), Buf()]
            MT = [sb(c, "MT%d" % i, [128, 4, 128], BF16) for i in range(2)]; B_MT = [Buf(), Buf()]
            xdt = [sb(c, "xdt%d" % i, [128, 4, 64], BF16) for i in range(2)]; B_xdt = [Buf(), Buf()]
            xdd = [sb(c, "xdd%d" % i, [128, 4, 64], BF16) for i in range(2)]; B_xdd = [Buf(), Buf()]
            zt = [sb(c, "zt%d" % i, [128, 256]) for i in range(2)]; B_zt = [Buf(), Buf()]
            yf = [sb(c, "yf%d" % i, [128, 256]) for i in range(2)]; B_yf = [Buf(), Buf()]
            yb = [sb(c, "yb%d" % i, [128, 256], BF16) for i in range(2)]; B_yb = [Buf(), Buf()]
            jkC = sb(c, "jkC", [128, 256], BF16); B_jkC = Buf()
            it = 0
            for g in range(8 if SUB > 0 else 0):
                chans = [g * 256, g * 256 + 128, 2048 + g * 128, 3072 + g * 128]
                for qi, ch0 in enumerate(chans):
                    ci = ci_n % 2; ci_n += 1
                    S.dma("sp", cin[ci][:, 2:2 + T], s_xT[ch0:ch0 + 128, :], reads=[B_scr], writes=[B_cin[ci]])
                    S.dma("sp", cwt[ci][:, 0:5], cw_v[ch0 // 128], writes=[B_cwt[ci]])
                    S.dma("sp", cwt[ci][:, 5:6], cb_v[ch0 // 128], writes=[B_cwt[ci]])
                    S.op("dve", lambda E, ci=ci: E.tensor_scalar(out=cacc[:], in0=cin[ci][:, 0:T], scalar1=cwt[ci][:, 0:1], scalar2=None,
                                                                 op0=ALU.mult), reads=[B_cin[ci], B_cwt[ci]], writes=[B_cacc])
                    for j in range(1, 5):
                        S.op("dve", lambda E, ci=ci, j=j: E.scalar_tensor_tensor(out=cacc[:], in0=cin[ci][:, j:j + T], scalar=cwt[ci][:, j:j + 1],
                                                                                in1=cacc[:], op0=ALU.mult, op1=ALU.add),
                             reads=[B_cin[ci], B_cwt[ci]], writes=[B_cacc])
                    S.op("act", lambda E, ci=ci, qi=qi: E.activation(out=cT[qi][:], in_=cacc[:], func=AF.Silu, bias=cwt[ci][:, 5:6]),
                         reads=[B_cacc, B_cwt[ci]], writes=[B_cT[qi]])
                if SUB < 2:
                    continue
                for tile in range(16):
                    for qi in range(3):
                        S.op("pe", lambda E, qi=qi, tile=tile: E.transpose(out=P_tp[:, qi, :], in_=cT[qi][:, tile * 128:(tile + 1) * 128],
                                                                           identity=ident[:]), reads=[B_cT[qi], B_ident], writes=[B_Ptp])
                    S.op("dve", lambda E, tile=tile: E.tensor_copy(out=xtok[:, tile, :], in_=P_tp[:, 0:2, :]), reads=[B_Ptp], writes=[B_xtok])
                    S.op("act", lambda E, tile=tile: E.copy(out=Btok[:, tile, :], in_=P_tp[:, 2, :]), reads=[B_Ptp], writes=[B_Btok])
                if SUB < 3:
                    continue
                for ch in range(8):
                    S.op("pe", lambda E, ch=ch: E.matmul(P_cbt[:, 0:128], lhsT=cT[2][:, ch * 128:(ch + 1) * 128],
                                                         rhs=cT[3][:, ch * 128:(ch + 1) * 128], start=True, stop=True),
                         reads=[B_cT[2], B_cT[3]], writes=[B_Pcbt])
                    S.op("act", lambda E, ch=ch: E.copy(out=CBT[:, ch, :], in_=P_cbt[:, 0:128]), reads=[B_Pcbt], writes=[B_CBT])
                S.op("dve", lambda E, g=g: E.tensor_tensor(
                    out=yacc[:].rearrange("p c (r q) -> p c r q", r=4),
                    in0=xtok[:, 0:8, :].rearrange("p c (r q) -> p c r q", r=4),
                    in1=dsk_bc[:, g * 4:(g + 1) * 4].unsqueeze(1).unsqueeze(3).broadcast_to([128, 8, 4, 64]), op=ALU.mult),
                    reads=[B_xtok, B_cc], writes=[B_yacc])
                for di in range(2 if SUB > 3 else 0):
                    colX = 127 if di == 0 else 0
                    triX = tri[:, di, :]
                    negX = tri[:, 2 + di, :]
                    order = list(range(8)) if di == 0 else list(range(15, -1, -1))
                    S.op("dve", lambda E: E.memset(state[:], 0.0), writes=[B_state])
                    S.op("dve", lambda E: E.memset(state_bf[:], 0.0), writes=[B_stbf])
                    for ch in order:
                        own = ch < 8
                        k_ = it % 2; it += 1
                        s_ = sm[k_]; Bs = B_sm[k_]
                        h0 = di * 32 + g * 4
                        adt4 = adt[:, ch, h0:h0 + 4]
                        S.op("pe", lambda E, triX=triX, adt4=adt4: E.matmul(P_ct[:, 0:4], lhsT=triX, rhs=adt4, start=True, stop=True),
                             reads=[B_cc, B_dt], writes=[B_Pct])
                        S.op("pool", lambda E, k_=k_, adt4=adt4: E.tensor_copy(out=arep[k_][:], in_=adt4.unsqueeze(2).broadcast_to([128, 4, 128])),
                             reads=[B_dt], writes=[B_arep[k_]])
                        pc = P_cb[k_]; Bpc = B_Pcb[k_]
                        for r in range(4):
                            S.op("pe", lambda E, pc=pc, r=r, k_=k_, triX=triX: E.matmul(pc[:, r, :], lhsT=arep[k_][:, r, :], rhs=triX,
                                                                                       start=True, stop=False),
                                 reads=[B_arep[k_], B_cc], writes=[Bpc])
                            S.op("pe", lambda E, pc=pc, r=r, negX=negX: E.matmul(pc[:, r, :], lhsT=identF[:], rhs=negX, start=False, stop=True),
                                 reads=[B_cc], writes=[Bpc])
                        S.op("dve", lambda E, s_=s_: E.tensor_scalar(out=s_[:, 0:4], in0=P_ct[:, 0:4], scalar1=-1.0, scalar2=None, op0=ALU.mult),
                             reads=[B_Pct], writes=[Bs])
                        S.op("act", lambda E, s_=s_: E.activation(out=s_[:, 4:8], in_=P_ct[:, 0:4], func=AF.Exp), reads=[B_Pct], writes=[Bs])
                        S.op("dve", lambda E, s_=s_, pc=pc, colX=colX: E.tensor_tensor(out=s_[:, 16:20], in0=pc[:, :, colX], in1=s_[:, 0:4], op=ALU.add),
                             reads=[Bpc, Bs], writes=[Bs])
                        S.op("act", lambda E, s_=s_: E.activation(out=s_[:, 8:12], in_=s_[:, 16:20], func=AF.Exp), reads=[Bs], writes=[Bs])
                        S.op("act", lambda E, s_=s_, pc=pc, colX=colX: E.activation(out=s_[:, 12:16], in_=pc[:, :, colX], func=AF.Exp),
                             reads=[Bpc], writes=[Bs])
                        dtc = dtv[:, ch, h0:h0 + 4]
                        S.op("dve", lambda E, k_=k_, ch=ch, dtc=dtc: E.tensor_tensor(
                            out=xdt[k_][:], in0=xtok[:, ch, :].rearrange("p (r q) -> p r q", r=4),
                            in1=dtc.unsqueeze(2).broadcast_to([128, 4, 64]), op=ALU.mult), reads=[B_xtok, B_dt], writes=[B_xdt[k_]])
                        if own:
                            for r in range(4):
                                S.op("act", lambda E, k_=k_, r=r, pc=pc, s_=s_: E.activation(out=LT[k_][:, r, :], in_=pc[:, r, :], func=AF.Exp,
                                                                                             bias=s_[:, r:r + 1]), reads=[Bpc, Bs], writes=[B_LT[k_]])
                            S.op("dve", lambda E, k_=k_, ch=ch: E.tensor_tensor(out=MT[k_][:], in0=LT[k_][:],
                                                                                in1=CBT[:, ch, :].unsqueeze(1).broadcast_to([128, 4, 128]), op=ALU.mult),
                                 reads=[B_LT[k_], B_CBT], writes=[B_MT[k_]])
                            for r in range(4):
                                S.op("pe", lambda E, k_=k_, r=r: E.matmul(P_y[:, r * 64:(r + 1) * 64], lhsT=MT[k_][:, r, :], rhs=xdt[k_][:, r, :],
                                                                          start=True, stop=True), reads=[B_MT[k_], B_xdt[k_]], writes=[B_Py])
                            S.op("pe", lambda E, ch=ch: E.matmul(P_yo[:, 0:256], lhsT=cT[3][:, ch * 128:(ch + 1) * 128], rhs=state_bf[:],
                                                                 start=True, stop=True), reads=[B_cT[3], B_stbf], writes=[B_Pyo])
                            S.op("dve", lambda E, ch=ch: E.tensor_tensor(out=yacc[:, ch, :], in0=yacc[:, ch, :], in1=P_y[:, 0:256], op=ALU.add),
                                 reads=[B_Py], writes=[B_yacc])
                            for r in range(4):
                                S.op("dve", lambda E, ch=ch, r=r, s_=s_: E.scalar_tensor_tensor(
                                    out=yacc[:, ch, r * 64:(r + 1) * 64], in0=P_yo[:, r * 64:(r + 1) * 64], scalar=s_[:, 4 + r:5 + r],
                                    in1=yacc[:, ch, r * 64:(r + 1) * 64], op0=ALU.mult, op1=ALU.add), reads=[B_Pyo, Bs], writes=[B_yacc])
                        S.op("dve", lambda E, k_=k_, s_=s_: E.tensor_tensor(out=xdd[k_][:], in0=xdt[k_][:],
                                                                            in1=s_[:, 8:12].unsqueeze(2).broadcast_to([128, 4, 64]), op=ALU.mult),
                             reads=[B_xdt[k_], Bs], writes=[B_xdd[k_]])
                        S.op("pe", lambda E, k_=k_, ch=ch: E.matmul(P_st[:, 0:256], lhsT=Btok[:, ch, :],
                                                                    rhs=xdd[k_][:].rearrange("p r q -> p (r q)"), start=True, stop=True),
                             reads=[B_Btok, B_xdd[k_]], writes=[B_Pst])
                        for r in range(4):
                            S.op("dve", lambda E, r=r, s_=s_: E.scalar_tensor_tensor(
                                out=state[:, r * 64:(r + 1) * 64], in0=state[:, r * 64:(r + 1) * 64], scalar=s_[:, 12 + r:13 + r],
                                in1=P_st[:, r * 64:(r + 1) * 64], op0=ALU.mult, op1=ALU.add), reads=[B_Pst, Bs, B_Pyo], writes=[B_state])
                        S.op("dve", lambda E: E.tensor_copy(out=state_bf[:], in_=state[:]), reads=[B_state, B_Pyo], writes=[B_stbf])
                for ch in range(8 if SUB > 4 else 0):
                    k_ = ch % 2
                    S.dma("sp", zt[k_][:], s_z[ch * 128:(ch + 1) * 128, g * 256:(g + 1) * 256], reads=[B_scr], writes=[B_zt[k_]])
                    S.op("act", lambda E, k_=k_: E.activation(out=zt[k_][:], in_=zt[k_][:], func=AF.Silu), reads=[B_zt[k_]], writes=[B_zt[k_]])
                    S.op("dve", lambda E, k_=k_, ch=ch: E.tensor_tensor(out=yf[k_][:], in0=yacc[:, ch, :], in1=zt[k_][:], op=ALU.mult),
                         reads=[B_yacc, B_zt[k_]], writes=[B_yf[k_]])
                    s_ = sm[k_]; Bs = B_sm[k_]
                    S.op("act", lambda E, k_=k_, s_=s_: E.activation(out=jkC[:], in_=yf[k_][:], func=AF.Square, accum_out=s_[:, 20:21]),
                         reads=[B_yf[k_]], writes=[B_jkC, Bs])
                    S.op("dve", lambda E, s_=s_: E.tensor_scalar(out=s_[:, 21:22], in0=s_[:, 20:21], scalar1=1.0 / 256, scalar2=EPS,
                                                                 op0=ALU.mult, op1=ALU.add), reads=[Bs], writes=[Bs])
                    S.op("act", lambda E, s_=s_: E.activation(out=s_[:, 22:23], in_=s_[:, 21:22], func=AF.Sqrt), reads=[Bs], writes=[Bs])
                    S.op("dve", lambda E, s_=s_: E.reciprocal(out=s_[:, 23:24], in_=s_[:, 22:23]), reads=[Bs], writes=[Bs])
                    S.op("dve", lambda E, k_=k_, s_=s_, g=g: E.scalar_tensor_tensor(
                        out=yb[k_][:], in0=yf[k_][:], scalar=s_[:, 23:24], in1=son_bc[:, g * 256:(g + 1) * 256], op0=ALU.mult, op1=ALU.mult),
                        reads=[B_yf[k_], Bs, B_cc], writes=[B_yb[k_]])
                    S.dma("sp", s_mix[ch * 128:(ch + 1) * 128, 2048 + g * 256:2048 + (g + 1) * 256], yb[k_][:], reads=[B_yb[k_]], writes=[B_mix])
            S.barrier()

        B_out = Buf(); B_hnT = Buf()
        if stage >= 4 and "D" not in SKIP:
          hdst = dbg["d_h"] if stage == 4 else out
          with ExitStack() as c:
            nf_bc = sb(c, "nf_bc", [128, D]); B_nf = Buf()
            S.dma("sp", nf_bc[:], norm_ffn.partition_broadcast(128), writes=[B_nf])
            mt = [sb(c, "mt%d" % i, [128, D], BF16) for i in range(2)]; B_mt = [Buf(), Buf()]
            mixT = sb(c, "mixT", [128, 32, 512], BF16); B_mixT = [Buf() for _ in range(4)]
            wb_ = [sb(c, "wbD%d" % i, [128, 32, 256], BF16) for i in range(2)]; B_wb = [Buf(), Buf()]
            ht = [sb(c, "ht%d" % i, [128, D]) for i in range(4)]; B_ht = [Buf() for _ in range(4)]
            xr = [sb(c, "xr%d" % i, [128, 256]) for i in range(3)]; B_xr = [Buf() for _ in range(3)]
            hb = sb(c, "hb", [128, D], BF16); B_hb = Buf()
            jkD = sb(c, "jkD", [128, D], BF16); B_jkD = Buf()
            hst = [sb(c, "hst%d" % i, [128, 8, 128], BF16) for i in range(2)]; B_hst = [Buf(), Buf()]
            sD = sb(c, "sD", [128, 8]); B_sD = Buf()
            tpD = [ps(c, "tpD%d" % i, [128, 8, 128], BF16) for i in range(2)]; B_tpD = [PB(), PB()]
            accD = [ps(c, "accD%d" % i, [128, 512]) for i in range(4)]; B_accD = [PB() for _ in range(4)]
            wo_v = w_out.rearrange("(k p) c -> p k c", p=128)
            hn_v = s_hnT.rearrange("(k p) t -> p k t", p=128)
            wi = 0; ai = 0; xi = 0; gi = 0
            for tb in range(2):
                for tt in range(4):
                    tile = tb * 4 + tt
                    mi = tile % 2
                    S.dma("sp", mt[mi][:], s_mix[tile * 128:(tile + 1) * 128, :], reads=[B_mix], writes=[B_mt[mi]])
                    for g in range(4):
                        ti = gi % 2; gi += 1
                        for j in range(8):
                            kk = g * 8 + j
                            S.op("pe", lambda E, ti=ti, j=j, kk=kk, mi=mi: E.transpose(out=tpD[ti][:, j, :], in_=mt[mi][:, kk * 128:(kk + 1) * 128],
                                                                                      identity=ident[:]), reads=[B_mt[mi], B_ident], writes=[B_tpD[ti]])
                        if g % 2 == 0:
                            S.op("act", lambda E, ti=ti, g=g, tt=tt: E.copy(out=mixT[:, g * 8:(g + 1) * 8, tt * 128:(tt + 1) * 128], in_=tpD[ti][:]),
                                 reads=[B_tpD[ti]], writes=[B_mixT[tt]])
                        else:
                            S.op("dve", lambda E, ti=ti, g=g, tt=tt: E.tensor_copy(out=mixT[:, g * 8:(g + 1) * 8, tt * 128:(tt + 1) * 128], in_=tpD[ti][:]),
                                 reads=[B_tpD[ti]], writes=[B_mixT[tt]])
                for cb in range(16):
                    w_ = wb_[wi % 2]; Bw = B_wb[wi % 2]; wi += 1
                    S.dma("pool", w_[:], wo_v[:, :, cb * 256:(cb + 1) * 256], writes=[Bw])
                    for tt in range(4):
                        tile = tb * 4 + tt
                        a = accD[ai % 4]; Ba = B_accD[ai % 4]; ai += 1
                        x_ = xr[xi % 3]; Bx = B_xr[xi % 3]; xi += 1
                        S.dma("sp", x_[:], x[tile * 128:(tile + 1) * 128, cb * 256:(cb + 1) * 256], writes=[Bx])
                        for kk in range(32):
                            S.op("pe", lambda E, a=a, w_=w_, kk=kk, tt=tt: E.matmul(a[:, 0:256], lhsT=mixT[:, kk, tt * 128:(tt + 1) * 128],
                                                                                   rhs=w_[:, kk, :], start=(kk == 0), stop=(kk == 31)),
                                 reads=[Bw, B_mixT[tt]], writes=[Ba])
                        S.op("dve", lambda E, a=a, x_=x_, tt=tt, cb=cb: E.tensor_tensor(out=ht[tt][:, cb * 256:(cb + 1) * 256], in0=a[:, 0:256],
                                                                                       in1=x_[:], op=ALU.add), reads=[Ba, Bx], writes=[B_ht[tt]])
                for tt in range(4):
                    tile = tb * 4 + tt
                    S.dma("sp", hdst[tile * 128:(tile + 1) * 128, :], ht[tt][:], reads=[B_ht[tt]], writes=[B_out])
                    S.op("act", lambda E, tt=tt: E.activation(out=jkD[:], in_=ht[tt][:], func=AF.Square, accum_out=sD[:, 0:1]),
                         reads=[B_ht[tt]], writes=[B_jkD, B_sD])
                    S.op("dve", lambda E: E.tensor_scalar(out=sD[:, 1:2], in0=sD[:, 0:1], scalar1=1.0 / D, scalar2=EPS, op0=ALU.mult, op1=ALU.add),
                         reads=[B_sD], writes=[B_sD])
                    S.op("act", lambda E: E.activation(out=sD[:, 2:3], in_=sD[:, 1:2], func=AF.Sqrt), reads=[B_sD], writes=[B_sD])
                    S.op("dve", lambda E: E.reciprocal(out=sD[:, 3:4], in_=sD[:, 2:3]), reads=[B_sD], writes=[B_sD])
                    S.op("dve", lambda E, tt=tt: E.scalar_tensor_tensor(out=hb[:], in0=ht[tt][:], scalar=sD[:, 3:4], in1=nf_bc[:],
                                                                        op0=ALU.mult, op1=ALU.mult), reads=[B_ht[tt], B_sD, B_nf], writes=[B_hb])
                    for g in range(4):
                        ti = gi % 2; gi += 1
                        for j in range(8):
                            kk = g * 8 + j
                            S.op("pe", lambda E, ti=ti, j=j, kk=kk: E.transpose(out=tpD[ti][:, j, :], in_=hb[:, kk * 128:(kk + 1) * 128],
                                                                               identity=ident[:]), reads=[B_hb, B_ident], writes=[B_tpD[ti]])
                        hs = hst[ti]; Bh = B_hst[ti]
                        if g % 2 == 0:
                            S.op("act", lambda E, ti=ti, hs=hs: E.copy(out=hs[:], in_=tpD[ti][:]), reads=[B_tpD[ti]], writes=[Bh])
                        else:
                            S.op("dve", lambda E, ti=ti, hs=hs: E.tensor_copy(out=hs[:], in_=tpD[ti][:]), reads=[B_tpD[ti]], writes=[Bh])
                        S.dma("sp", hn_v[:, g * 8:(g + 1) * 8, tile * 128:(tile + 1) * 128], hs[:], reads=[Bh], writes=[B_hnT])
            S.barrier()

        if stage >= 5:
          with ExitStack() as c:
            s2all = sb(c, "s2all", [128, 8, 8, 128]); A1all = sb(c, "A1all", [128, 8, 8, 128])
            wAll = sb(c, "wAll", [128, 8, 8])
            B_gin = Buf()
            hnT = sb(c, "hnT", [128, 32, TO], BF16); B_hn = Buf()
            hn_v = s_hnT.rearrange("(k p) t -> p k t", p=128)
            for g in range(4):
                S.dma("sp", hnT[:, g * 8:(g + 1) * 8, :], hn_v[:, g * 8:(g + 1) * 8, :], reads=[B_hnT], writes=[B_hn])
            identF2 = sb(c, "identF2", [128, 128]); B_idf = Buf()
            S.dma("sp", identF2[:], ident_in, writes=[B_idf])
            with ExitStack() as c2:
                qT = sb(c2, "qT", [128, 16, TO], BF16); B_qT = Buf()
                skT = sb(c2, "skT", [128, 16, 128], BF16); B_skT = Buf()
                S.dma("pool", skT[:], skT_in.rearrange("h d k -> d h k"), writes=[B_skT])
                wqb = [sb(c2, "wqb%d" % i, [128, 32, 128], BF16) for i in range(2)]; B_wqb = [Buf(), Buf()]
                pq_ = [ps(c2, "pqE%d" % i, [128, 512]) for i in range(4)]; B_pq_ = [PB() for _ in range(4)]
                pi_ = 0
                wq_v = w_query.rearrange("(k p) c -> p k c", p=128)
                for cb in range(16):
                    w_ = wqb[cb % 2]; Bw = B_wqb[cb % 2]
                    S.dma("pool", w_[:], wq_v[:, :, cb * 128:(cb + 1) * 128], writes=[Bw])
                    for hh in range(1):
                        hc = cb
                        for th in range(2):
                            p_ = pq_[pi_ % 4]; Bp = B_pq_[pi_ % 4]; pi_ += 1
                            for kk in range(32):
                                S.op("pe", lambda E, p_=p_, w_=w_, kk=kk, hh=hh, th=th: E.matmul(
                                    p_[:], lhsT=w_[:, kk, hh * 128:(hh + 1) * 128], rhs=hnT[:, kk, th * 512:(th + 1) * 512],
                                    start=(kk == 0), stop=(kk == 31)), reads=[Bw, B_hn], writes=[Bp])
                            if pi_ % 2 == 0:
                                S.op("act", lambda E, p_=p_, hc=hc, th=th: E.copy(out=qT[:, hc, th * 512:(th + 1) * 512], in_=p_[:]),
                                     reads=[Bp], writes=[B_qT])
                            else:
                                S.op("dve", lambda E, p_=p_, hc=hc, th=th: E.tensor_copy(out=qT[:, hc, th * 512:(th + 1) * 512], in_=p_[:]),
                                     reads=[Bp], writes=[B_qT])
                sc = sb(c2, "sc", [128, 16, 128]); B_sc = Buf()
                wk = sb(c2, "wk", [128, 256]); B_wk = Buf()
                v16 = sb(c2, "v16", [128, 16, 16]); B_v16 = Buf()
                cand = sb(c2, "cand", [128, 8, 256]); B_cand = Buf()
                t24 = sb(c2, "t24", [128, 8, 24]); B_t24 = Buf()
                sE = sb(c2, "sE", [128, 8, 8]); B_sE = Buf()
                jkE = sb(c2, "jkE", [128, 16]); B_jkE = Buf()
                for tt in range(8):
                    for q4 in range(4):
                        p_ = pq_[pi_ % 4]; Bp = B_pq_[pi_ % 4]; pi_ += 1
                        for u in range(4):
                            hc = q4 * 4 + u
                            S.op("pe", lambda E, p_=p_, u=u, hc=hc, tt=tt: E.matmul(
                                p_[:, u * 128:(u + 1) * 128], lhsT=qT[:, hc, tt * 128:(tt + 1) * 128], rhs=skT[:, hc, :],
                                start=True, stop=True), reads=[B_qT, B_skT], writes=[Bp])
                        S.op("act", lambda E, p_=p_, q4=q4: E.copy(out=sc[:, q4 * 4:(q4 + 1) * 4, :], in_=p_[:]), reads=[Bp], writes=[B_sc])
                    for hc in range(16):
                        S.op("dve", lambda E, hc=hc: E.max(out=v16[:, hc, 0:8], in_=sc[:, hc, :]), reads=[B_sc], writes=[B_v16])
                        S.op("dve", lambda E, hc=hc: E.match_replace(out=wk[:, 0:128], in_to_replace=v16[:, hc, 0:8], in_values=sc[:, hc, :],
                                                                     imm_value=-1e30), reads=[B_sc, B_v16], writes=[B_wk])
                        S.op("dve", lambda E, hc=hc: E.max(out=v16[:, hc, 8:16], in_=wk[:, 0:128]), reads=[B_wk], writes=[B_v16])
                    v4 = v16[:].rearrange("p (h c) k -> p h c k", c=2)
                    S.op("dve", lambda E, v4=v4: E.tensor_tensor(
                        out=cand[:].rearrange("p h (a b) -> p h a b", a=16),
                        in0=v4[:, :, 0, :].unsqueeze(3).broadcast_to([128, 8, 16, 16]),
                        in1=v4[:, :, 1, :].unsqueeze(2).broadcast_to([128, 8, 16, 16]), op=ALU.add), reads=[B_v16], writes=[B_cand])
                    for h in range(8):
                        S.op("dve", lambda E, h=h: E.max(out=t24[:, h, 0:8], in_=cand[:, h, :]), reads=[B_cand], writes=[B_t24])
                        S.op("dve", lambda E, h=h: E.match_replace(out=wk[:], in_to_replace=t24[:, h, 0:8], in_values=cand[:, h, :],
                                                                   imm_value=-1e30), reads=[B_cand, B_t24], writes=[B_wk])
                        S.op("dve", lambda E, h=h: E.max(out=t24[:, h, 8:16], in_=wk[:]), reads=[B_wk], writes=[B_t24])
                        S.op("dve", lambda E, h=h: E.match_replace(out=wk[:], in_to_replace=t24[:, h, 8:16], in_values=wk[:],
                                                                   imm_value=-1e30), reads=[B_t24], writes=[B_wk])
                        S.op("dve", lambda E, h=h: E.max(out=t24[:, h, 16:24], in_=wk[:]), reads=[B_wk], writes=[B_t24])
                    S.op("dve", lambda E: E.tensor_tensor(out=sE[:, :, 0], in0=t24[:, :, 15], in1=t24[:, :, 16], op=ALU.add), reads=[B_t24], writes=[B_sE])
                    S.op("dve", lambda E: E.tensor_scalar(out=sE[:, :, 0], in0=sE[:, :, 0], scalar1=0.5, scalar2=None, op0=ALU.mult), reads=[B_sE], writes=[B_sE])
                    S.op("dve", lambda E: E.tensor_scalar(out=sE[:, :, 6], in0=t24[:, :, 0], scalar1=-1.0, scalar2=None, op0=ALU.mult), reads=[B_t24], writes=[B_sE])
                    for h in range(8):
                        S.op("act", lambda E, h=h: E.activation(out=jkE[:], in_=t24[:, h, 0:16], func=AF.Exp, bias=sE[:, h, 6:7],
                                                                accum_out=sE[:, h, 2:3]), reads=[B_t24, B_sE], writes=[B_jkE, B_sE])
                    S.op("act", lambda E: E.activation(out=sE[:, :, 3], in_=sE[:, :, 2], func=AF.Ln), reads=[B_sE], writes=[B_sE])
                    S.op("dve", lambda E: E.tensor_tensor(out=sE[:, :, 4], in0=sE[:, :, 0], in1=sE[:, :, 6], op=ALU.add), reads=[B_sE], writes=[B_sE])
                    S.op("dve", lambda E: E.tensor_tensor(out=sE[:, :, 4], in0=sE[:, :, 4], in1=sE[:, :, 3], op=ALU.subtract), reads=[B_sE], writes=[B_sE])
                    S.op("act", lambda E: E.activation(out=sE[:, :, 5], in_=sE[:, :, 4], func=AF.Exp), reads=[B_sE], writes=[B_sE])
                    sc4 = sc[:].rearrange("p (h c) k -> p h c k", c=2)
                    S.op("dve", lambda E, sc4=sc4, tt=tt: E.tensor_tensor(out=A1all[:, tt, :, :], in0=sc4[:, :, 0, :],
                                                                          in1=sE[:, :, 0:1].broadcast_to([128, 8, 128]), op=ALU.subtract),
                         reads=[B_sc, B_sE], writes=[B_gin])
                    S.op("act", lambda E, sc4=sc4, tt=tt: E.copy(out=s2all[:, tt, :, :], in_=sc4[:, :, 1, :]), reads=[B_sc], writes=[B_gin])
                    S.op("dve", lambda E, tt=tt: E.tensor_copy(out=wAll[:, tt, :], in_=sE[:, :, 5]), reads=[B_sE], writes=[B_gin])
                S.barrier()
            with ExitStack() as c2:
                ut = [sb(c2, "ut%d" % i, [128, 32, 128], BF16) for i in range(3)]; B_ut = [Buf() for _ in range(3)]
                gact = [sb(c2, "gact%d" % i, [128, TO], BF16) for i in range(2)]; B_gact = [Buf(), Buf()]
                NYB = 4
                Yb = [sb(c2, "Yb%d" % i, [128, 8, 128]) for i in range(NYB)]; B_Yb = [Buf() for _ in range(NYB)]
                Eb = [sb(c2, "Eb%d" % i, [128, 8, 128], BF16) for i in range(NYB)]; B_Eb = [Buf() for _ in range(NYB)]
                Gb = [sb(c2, "Gb%d" % i, [128, 8, 128], BF16) for i in range(3)]; B_Gb = [Buf() for _ in range(3)]
                wt = [sb(c2, "wt%d" % i, [128, TO], BF16) for i in range(2)]; B_wt = [Buf(), Buf()]
                P_a = [ps(c2, "P_a%d" % i, [128, 512]) for i in range(4)]; B_Pa = [PB() for _ in range(4)]
                P_g = [ps(c2, "P_g%d" % i, [128, 512]) for i in range(4)]; B_Pg = [PB() for _ in range(4)]
                NCH = int(_os0.environ.get("KNCH", "128"))
                yi = 0; gi = 0
                POOL_SET = [int(v) for v in _os0.environ.get("KPOOLSET", "1,4,6").split(",") if v != ""]
                STT_SET = [int(v) for v in _os0.environ.get("KSTTSET", "0,1,2,3,4,5,6,7").split(",") if v != ""]
                Dg = sb(c2, "Dg", [128, 8, 8, 128], BF16)
                for tt in range(8):
                    for h in range(8):
                        S.op("dve", lambda E, tt=tt, h=h: E.tensor_scalar(out=Dg[:, tt, h, :], in0=identF2[:], scalar1=wAll[:, tt, h:h + 1], scalar2=None,
                                                                          op0=ALU.mult), reads=[B_idf, B_gin], writes=[B_gin])
                gbuf = {}

                def load_u(i):
                    S.dma("pool", ut[i % 3][:], UTt[i], writes=[B_ut[i % 3]])

                def stage1_mm(i, grp):
                    u_ = ut[i % 3]; Bu = B_ut[i % 3]
                    th = grp // 4
                    pa = P_a[(i % 2) * 2 + th]; Bpa = B_Pa[(i % 2) * 2 + th]
                    for kk in range((grp % 4) * 8, (grp % 4) * 8 + 8):
                        S.op("pe", lambda E, pa=pa, th=th, kk=kk, u_=u_: E.matmul(pa[:], lhsT=u_[:, kk, :], rhs=hnT[:, kk, th * 512:(th + 1) * 512],
                                                                                 start=(kk == 0), stop=(kk == 31)), reads=[Bu, B_hn], writes=[Bpa])

                def unit_pre(i, tt):
                    k = i * 8 + tt
                    y_ = Yb[k % NYB]; By = B_Yb[k % NYB]; e_ = Eb[k % NYB]; Be = B_Eb[k % NYB]
                    g_ = Gb[k % 3]; Bg = B_Gb[k % 3]
                    gbuf[(i, tt)] = (g_, Bg)
                    S.op("pool" if tt in POOL_SET else "dve", lambda E, y_=y_, tt=tt, i=i: E.tensor_tensor(
                        out=y_[:], in0=s2all[:, tt, :, :], in1=A1all[:, tt, :, i:i + 1].broadcast_to([128, 8, 128]), op=ALU.add),
                        reads=[B_gin], writes=[By])
                    if tt in STT_SET:
                        S.op("act", lambda E, y_=y_, e_=e_: E.activation(out=e_[:], in_=y_[:], func=AF.Exp), reads=[By], writes=[Be])
                        S.op("dve", lambda E, y_=y_, e_=e_, g_=g_: E.scalar_tensor_tensor(out=g_[:], in0=y_[:], scalar=0.0, in1=e_[:],
                                                                                          op0=ALU.is_ge, op1=ALU.mult), reads=[By, Be], writes=[Bg])
                    else:
                        S.op("act", lambda E, y_=y_: E.activation(out=y_[:], in_=y_[:], func=AF.Prelu, alpha=1e30), reads=[By], writes=[By])
                        S.op("act", lambda E, y_=y_, g_=g_: E.activation(out=g_[:], in_=y_[:], func=AF.Exp), reads=[By], writes=[Bg])

                def unit_mm(i, tt):
                    g_, Bg = gbuf.pop((i, tt))
                    pb = (i % 2) * 2
                    pg = P_g[pb + tt // 4]; Bpg = B_Pg[pb + tt // 4]
                    for h in range(8):
                        S.op("pe", lambda E, pg=pg, g_=g_, h=h, tt=tt: E.matmul(pg[:, (tt % 4) * 128:(tt % 4 + 1) * 128], lhsT=g_[:, h, :],
                                                                                rhs=Dg[:, tt, h, :], start=(h == 0), stop=(h == 7)),
                             reads=[Bg, B_gin], writes=[Bpg])

                def stage3(i):
                    pb = (i % 2) * 2
                    ga = gact[i % 2]; Bga = B_gact[i % 2]
                    w_ = wt[i % 2]; Bwt = B_wt[i % 2]
                    for th in range(2):
                        S.op("act", lambda E, pb=pb, th=th, ga=ga: E.activation(out=ga[:, th * 512:(th + 1) * 512], in_=P_a[pb + th][:],
                                                                                func=AF.Gelu_apprx_tanh), reads=[B_Pa[pb + th]], writes=[Bga])
                    for th in range(2):
                        S.op("dve", lambda E, w_=w_, th=th, pb=pb, ga=ga: E.tensor_tensor(out=w_[:, th * 512:(th + 1) * 512], in0=P_g[pb + th][:],
                                                                                         in1=ga[:, th * 512:(th + 1) * 512], op=ALU.mult),
                             reads=[B_Pg[pb + th], Bga], writes=[Bwt])
                    S.dma("sp", s_WT[i], w_[:], reads=[Bwt], writes=[B_swt])

                load_u(0)
                if NCH > 1:
                    load_u(1)
                for grp in range(8):
                    stage1_mm(0, grp)
                for i in range(NCH):
                    if i + 2 < NCH:
                        load_u(i + 2)
                    for tt in range(8):
                        unit_pre(i, tt)
                        if i + 1 < NCH:
                            stage1_mm(i + 1, tt)
                        if tt >= 1:
                            unit_mm(i, tt - 1)
                    unit_mm(i, 7)
                    stage3(i)
                S.barrier()
          with ExitStack() as c:
            WTr = sb(c, "WTr", [128, 128, 512], BF16); B_WTr = Buf()
            vt = [sb(c, "vt%d" % i, [128, 4, 1024], BF16) for i in range(3)]; B_vt = [Buf() for _ in range(3)]
            hres = [sb(c, "hres%d" % i, [128, 512]) for i in range(3)]; B_hres = [Buf() for _ in range(3)]
            P_o = [ps(c, "P_o%d" % i, [128, 512]) for i in range(8)]; B_Po = [PB() for _ in range(8)]
            V_v = Vexp.rearrange("(i j) d -> j i d", j=128)
            wt_v = s_WT.rearrange("i j t -> j i t")
            vi = 0; hi_ = 0
            for tb in range(0 if "3" in SKIP else 2):
                for g in range(8):
                    S.dma("sp", WTr[:, g * 16:(g + 1) * 16, :], wt_v[:, g * 16:(g + 1) * 16, tb * 512:(tb + 1) * 512],
                          reads=[B_swt], writes=[B_WTr])
                for dq in range(4):
                    for ig in range(NCH // 4):
                        v_ = vt[vi % 3]; Bv = B_vt[vi % 3]; vi += 1
                        S.dma("pool", v_[:], V_v[:, ig * 4:(ig + 1) * 4, dq * 1024:(dq + 1) * 1024], writes=[Bv])
                        for ii in range(4):
                            i = ig * 4 + ii
                            for tt in range(4):
                                for hf in range(2):
                                    S.op("pe", lambda E, tt=tt, hf=hf, i=i, ii=ii, v_=v_: E.matmul(
                                        P_o[tt * 2 + hf][:], lhsT=WTr[:, i, tt * 128:(tt + 1) * 128], rhs=v_[:, ii, hf * 512:(hf + 1) * 512],
                                        start=(i == 0), stop=(i == NCH - 1)), reads=[B_WTr, Bv], writes=[B_Po[tt * 2 + hf]])
                    for tt in range(4):
                        for hf in range(2):
                            t0 = tb * 512 + tt * 128
                            c0 = dq * 1024 + hf * 512
                            h_ = hres[hi_ % 3]; Bh = B_hres[hi_ % 3]; hi_ += 1
                            S.dma("sp", h_[:], out[t0:t0 + 128, c0:c0 + 512], reads=[B_out], writes=[Bh])
                            S.op("dve", lambda E, h_=h_, tt=tt, hf=hf: E.tensor_tensor(out=h_[:], in0=P_o[tt * 2 + hf][:], in1=h_[:], op=ALU.add),
                                 reads=[B_Po[tt * 2 + hf], Bh], writes=[Bh])
                            S.dma("sp", out[t0:t0 + 128, c0:c0 + 512], h_[:], reads=[Bh], writes=[B_fin])
            S.barrier()

        S.barrier()
        with nc.Block() as block:
            S.emit(block)
    return nc


def _prep_inputs(inputs):
    g = {k: np.asarray(v) for k, v in inputs.items()}
    w_in = g["w_in"][0]
    sp = np.cumsum([0, 1024, 512, 64, 2048, 4096, 32, 32])
    c_q, c_kv, k_rope, z, xbc, dtf, dtb = [w_in[:, sp[i]:sp[i + 1]] for i in range(7)]
    xs, bs, cs = xbc[:, :2048], xbc[:, 2048:3072], xbc[:, 3072:4096]
    w_perm = [np.ascontiguousarray(np.concatenate([c_kv, k_rope, a, b, xs, bs, cs, c_q, z], axis=1))
              for (a, b) in ((dtf, dtb), (dtb, dtf))]
    ident = np.eye(128, dtype=np.float32)
    invf = (10000.0 ** (-(np.arange(32, dtype=np.float32) / np.float32(32)))).astype(np.float32)
    cwm = g["conv_w"][0][:, 0, :]
    cw = [np.ascontiguousarray(cwm.T), np.ascontiguousarray(cwm[::-1].T)]
    kk, ll = np.meshgrid(np.arange(128), np.arange(128), indexing="ij")
    tri = np.stack([(kk <= ll), (kk >= ll), np.where(kk <= ll, 0.0, -30000.0), np.where(kk >= ll, 0.0, -30000.0)]).astype(np.float32)
    skT = np.ascontiguousarray(g["sub_keys"][0].reshape(16, 128, 128).transpose(0, 2, 1))
    U = g["expert_u"][0]
    UTt = np.ascontiguousarray(U.reshape(128, 128, 32, 128).transpose(0, 3, 2, 1))
    maps = []
    for c in range(8):
        b, half = c // 2, c % 2
        xl = g["x"][b]
        if half:
            xl = xl[::-1]
        pl = g["positions"][b].astype(np.int32)
        if half:
            pl = pl[::-1]
        m = {"x": np.ascontiguousarray(xl), "norm_mix": g["norm_mix"][0], "w_in": w_perm[half], "ident": ident,
             "pos": np.ascontiguousarray(pl), "invf": invf, "q_a_norm": g["q_a_norm"][0], "w_uq": g["w_uq"][0],
             "kv_a_norm": g["kv_a_norm"][0], "w_ukv": g["w_ukv"][0], "q_norm": g["q_norm"][0], "k_norm": g["k_norm"][0],
             "aon": np.ascontiguousarray(g["attn_out_norm"][0].reshape(-1)),
             "conv_w": cw[half], "conv_b": g["conv_b"][0],
             "a_log": np.concatenate([g["a_log_fwd"][0], g["a_log_bwd"][0]][::(-1 if half else 1)]),
             "dt_bias": np.concatenate([g["dt_bias_fwd"][0], g["dt_bias_bwd"][0]][::(-1 if half else 1)]),
             "d_skip": g["d_skip"][0], "son": g["ssm_out_norm"][0], "tri": tri,
             "w_out": g["w_out"][0], "norm_ffn": g["norm_ffn"][0], "w_query": g["w_query"][0],
             "skT": skT, "UTt": UTt, "Vexp": g["expert_v"][0]}
        maps.append(m)
    return maps


def kernel(**inputs):
    maps = _prep_inputs(inputs)
    nc = build()
    res = run_bass_kernel_spmd(nc, maps, core_ids=list(range(8)))
    outp = np.zeros((4, 2048, D), np.float32)
    for c in range(8):
        b, half = c // 2, c % 2
        o = res.results[c]["out"]
        if half:
            outp[b, 1024:] = o[::-1]
        else:
            outp[b, :1024] = o
    return outp
```

```python
from contextlib import ExitStack
import numpy as np
import concourse.bass as bass
import concourse.mybir as mybir
from concourse.bass_utils import run_bass_kernel_spmd

F32 = mybir.dt.float32
BF16 = mybir.dt.bfloat16
I32 = mybir.dt.int32
AF = mybir.ActivationFunctionType
ALU = mybir.AluOpType
AX = mybir.AxisListType

EPS = 1e-6
D = 4096
T = 2048
TO = 1024
NH = 16
O_KV, O_ROPE, O_DTF, O_DTB, O_X, O_B, O_C, O_Q, O_Z, O_END = 0, 512, 576, 608, 640, 2688, 3712, 4736, 5760, 7808


import types


def _snap(fn):
    if fn.__closure__ is None:
        return fn
    cells = []
    for cl in fn.__closure__:
        try:
            cells.append(types.CellType(cl.cell_contents))
        except ValueError:
            cells.append(cl)
    g = types.FunctionType(fn.__code__, fn.__globals__, fn.__name__, fn.__defaults__, tuple(cells))
    g.__kwdefaults__ = fn.__kwdefaults__
    return g


class Buf:
    __slots__ = ("name", "w", "r", "excl")

    def __init__(self, name="", excl=False):
        self.name = name
        self.w = None
        self.r = {}
        self.excl = excl


def PB():
    return Buf(excl=True)


class Sched:
    NDMA = 24

    def __init__(self, nc, ctx):
        self.nc = nc
        self.engs = ["pe", "dve", "act", "pool", "sp"]
        self.sem = {}
        self.cnt = {}
        for e in self.engs:
            self.sem[e] = ctx.enter_context(nc.semaphore("s_" + e))
            self.cnt[e] = 0
        self.dsem = [ctx.enter_context(nc.semaphore("d%d" % i)) for i in range(self.NDMA)]
        self.dcnt = [0] * self.NDMA
        self.dnext = {"sp": 0, "pool": 0, "act": 0}
        self.drange = {"sp": (0, 14), "pool": (14, 22), "act": (22, 24)}
        self.seen = {e: {} for e in self.engs}
        self.prog = {e: [] for e in self.engs}

    def _semobj(self, key):
        return self.sem[key] if isinstance(key, str) else self.dsem[key]

    def _wait(self, e, key, val):
        if key == "pe" and e == "pe":
            return
        if self.seen[e].get(key, 0) >= val:
            return
        so = self._semobj(key)
        self.prog[e].append(lambda E, so=so, val=val: E.wait_ge(so, val))
        self.seen[e][key] = val

    def _deps(self, e, reads, writes):
        for b in reads:
            if b.w is not None:
                self._wait(e, *b.w)
        for b in writes:
            if b.w is not None:
                self._wait(e, *b.w)
            for k, v in b.r.items():
                self._wait(e, k, v)

    def _commit(self, ev, reads, writes):
        for b in reads:
            if b.r.get(ev[0], 0) < ev[1]:
                b.r[ev[0]] = ev[1]
        for b in writes:
            b.w = ev
            b.r = {}

    def op(self, e, fn, reads=(), writes=()):
        fn = _snap(fn)
        if any(b.excl for b in reads):
            writes = list(writes) + [b for b in reads if b.excl]
            reads = [b for b in reads if not b.excl]
        self._deps(e, reads, writes)
        self.cnt[e] += 1
        so = self.sem[e]
        self.prog[e].append(lambda E, fn=fn, so=so: fn(E).then_inc(so, 1))
        ev = (e, self.cnt[e])
        self._commit(ev, reads, writes)
        return ev

    def dma(self, q, out, in_, reads=(), writes=(), **kw):
        lo, hi = self.drange[q]
        k = lo + self.dnext[q]
        self.dnext[q] = (self.dnext[q] + 1) % (hi - lo)
        if self.dcnt[k] > 0:
            self._wait(q, k, self.dcnt[k])
        self._deps(q, reads, writes)
        self.dcnt[k] += 16
        so = self.dsem[k]
        self.prog[q].append(
            lambda E, out=out, in_=in_, kw=kw, so=so: E.dma_start(out=out, in_=in_, **kw).then_inc(so, 16))
        ev = (k, self.dcnt[k])
        self._commit(ev, reads, writes)
        return ev

    def barrier(self):
        for e in self.engs:
            for o in self.engs:
                if o != e and self.cnt[o] > 0:
                    self._wait(e, o, self.cnt[o])
            for k in range(self.NDMA):
                if self.dcnt[k] > 0:
                    self._wait(e, k, self.dcnt[k])

    def emit(self, block):
        reg = {"pe": block.tensor, "dve": block.vector, "act": block.scalar, "pool": block.gpsimd, "sp": block.sync}
        for e in self.engs:
            lst = self.prog[e]
            if not lst:
                continue

            def body(E, lst=lst):
                for f in lst:
                    f(E)
            reg[e](body)


def bcast_rows(ap_1d, nparts):
    return ap_1d.partition_broadcast(nparts)


_rope_id = [0]


def rope_ops(S, src, B_src, dst, B_dst, cs, B_cs, tile, sb, c, tag, nheads):
    _rope_id[0] += 1
    nm = "rp%d_" % _rope_id[0]
    if nheads == 1:
        shp = [128, 32]
        t1, t2 = src[:, 0:32], src[:, 32:64]
        d1, d2 = dst[:, 0:32], dst[:, 32:64]
        co, si = cs[:, tile, 0:32], cs[:, tile, 32:64]
    else:
        shp = [128, nheads, 32]
        t1, t2 = src[:, :, 0:32], src[:, :, 32:64]
        d1, d2 = dst[:, :, 0:32], dst[:, :, 32:64]
        co = cs[:, tile, 0:32].unsqueeze(1).broadcast_to(shp)
        si = cs[:, tile, 32:64].unsqueeze(1).broadcast_to(shp)
    a = sb(c, nm + "a", shp)
    b = sb(c, nm + "b", shp)
    Ba, Bb = Buf(), Buf()
    S.op("dve", lambda E: E.tensor_tensor(out=a[:], in0=t1, in1=co, op=ALU.mult), reads=[B_src, B_cs], writes=[Ba])
    S.op("dve", lambda E: E.tensor_tensor(out=b[:], in0=t2, in1=si, op=ALU.mult), reads=[B_src, B_cs], writes=[Bb])
    S.op("dve", lambda E: E.tensor_tensor(out=d1, in0=a[:], in1=b[:], op=ALU.subtract), reads=[Ba, Bb], writes=[B_dst])
    S.op("dve", lambda E: E.tensor_tensor(out=a[:], in0=t1, in1=si, op=ALU.mult), reads=[B_src, B_cs, B_dst], writes=[Ba])
    S.op("dve", lambda E: E.tensor_tensor(out=b[:], in0=t2, in1=co, op=ALU.mult), reads=[B_src, B_cs, B_dst], writes=[Bb])
    S.op("dve", lambda E: E.tensor_tensor(out=d2, in0=a[:], in1=b[:], op=ALU.add), reads=[Ba, Bb], writes=[B_dst])

class K:
    pass


def build(stage=99):
    nc = bass.Bass("TRN2", target_bir_lowering=False)
    k = K()
    k.nc = nc

    def din(name, shape, dt=F32):
        return nc.dram_tensor(name, list(shape), dt, kind="ExternalInput").ap()

    def dscr(name, shape, dt=F32):
        return nc.dram_tensor(name, list(shape), dt, kind="Internal").ap()

    x = din("x", [T, D])
    norm_mix = din("norm_mix", [D])
    w_in = din("w_in", [D, O_END])
    ident_in = din("ident", [128, 128])
    pos_in = din("pos", [T], I32)
    invf_in = din("invf", [32])
    q_a_norm = din("q_a_norm", [1024])
    w_uq = din("w_uq", [1024, 3072])
    kv_a_norm = din("kv_a_norm", [512])
    w_ukv = din("w_ukv", [512, 4096])
    q_norm = din("q_norm", [192])
    k_norm = din("k_norm", [192])
    aon = din("aon", [2048])
    s_mix = dscr("s_mix", [TO, D], BF16)
    conv_w = din("conv_w", [4096, 5])
    conv_b = din("conv_b", [4096])
    a_log = din("a_log", [64])
    dt_bias = din("dt_bias", [64])
    d_skip = din("d_skip", [32])
    son = din("son", [2048])
    tri_in = din("tri", [4, 128, 128])
    w_out = din("w_out", [D, D])
    norm_ffn = din("norm_ffn", [D])
    w_query = din("w_query", [D, 2048])
    skT_in = din("skT", [16, 128, 128])
    UTt = din("UTt", [128, 128, 32, 128])
    Vexp = din("Vexp", [16384, D])
    s_hnT = dscr("s_hnT", [D, TO], BF16)
    s_WT = dscr("s_WT", [128, 128, TO], BF16)
    out = nc.dram_tensor("out", [TO, D], F32, kind="ExternalOutput").ap()

    s_kv = dscr("s_kv", [T, 640])
    s_xT = dscr("s_xT", [4096, T], BF16)
    s_q = dscr("s_q", [TO, 1024])
    s_z = dscr("s_z", [TO, 2048])

    dbg = {}
    if stage == 1:
        dbg["d_kv"] = nc.dram_tensor("d_kv", [T, 640], F32, kind="ExternalOutput").ap()
        dbg["d_xT"] = nc.dram_tensor("d_xT", [4096, T], BF16, kind="ExternalOutput").ap()
        dbg["d_q"] = nc.dram_tensor("d_q", [TO, 1024], F32, kind="ExternalOutput").ap()
        dbg["d_z"] = nc.dram_tensor("d_z", [TO, 2048], F32, kind="ExternalOutput").ap()
        s_kv, s_xT, s_q, s_z = dbg["d_kv"], dbg["d_xT"], dbg["d_q"], dbg["d_z"]

    if stage == 4:
        dbg["d_h"] = nc.dram_tensor("d_h", [TO, D], F32, kind="ExternalOutput").ap()
    if stage in (2, 3):
        dbg["d_mix"] = nc.dram_tensor("d_mix", [TO, D], BF16, kind="ExternalOutput").ap()
        s_mix = dbg["d_mix"]

    with ExitStack() as ctx:
        S = Sched(nc, ctx)
        uid = [0]

        def sb(c, name, shape, dt=F32):
            uid[0] += 1
            return c.enter_context(nc.sbuf_tensor("%s_%d" % (name, uid[0]), list(shape), dt))

        def ps(c, name, shape, dt=F32):
            uid[0] += 1
            return c.enter_context(nc.psum_tensor("%s_%d" % (name, uid[0]), list(shape), dt))

        ident = sb(ctx, "ident_sb", [128, 128], BF16)
        B_ident = Buf("ident")
        S.dma("pool", ident[:], ident_in, writes=[B_ident])
        outbufs = []
        B_mix = Buf()
        B_swt = Buf()
        B_fin = Buf()

        B_scr = Buf()
        import os as _os0
        SKIP = _os0.environ.get("KSKIP", "")
        with ExitStack() as c:
          if "A" not in SKIP:
              gain = sb(c, "gain", [128, D])
              B_gain = Buf()
              S.dma("sp", gain[:], bcast_rows(norm_mix, 128), writes=[B_gain])
              xt = [sb(c, "xt%d" % i, [128, D]) for i in range(2)]
              B_xt = [Buf() for _ in range(2)]
              junk = sb(c, "junk", [128, D], BF16)
              B_junk = Buf()
              xb = sb(c, "xb", [128, D], BF16)
              B_xb = Buf()
              st = sb(c, "stat", [128, 8])
              B_st = Buf()
              xnT = sb(c, "xnT", [128, 32, 512], BF16)
              B_xnT = [Buf() for _ in range(4)]
              wblk = [sb(c, "wblk%d" % i, [128, 32, 512], BF16) for i in range(2)]
              B_w = [Buf() for _ in range(2)]
              ost = [sb(c, "ost%d" % i, [128, 512]) for i in range(3)]
              B_ost = [Buf() for _ in range(3)]
              ostb = [sb(c, "ostb%d" % i, [128, 512], BF16) for i in range(3)]
              B_ostb = [Buf() for _ in range(3)]
              tp = [ps(c, "tp%d" % i, [128, 8, 128], BF16) for i in range(2)]
              B_tp = [PB() for _ in range(2)]
              acc = [ps(c, "acc%d" % i, [128, 512]) for i in range(4)]
              B_acc = [PB() for _ in range(4)]
              w_v = w_in.rearrange("(k p) c -> p k c", p=128)
              x_v = x.rearrange("(n p) d -> n p d", p=128)

              blocks = [(O_KV, 512, "kv"), (O_ROPE, 128, "kv")]
              blocks += [(O_X + 512 * i, 512, "feat") for i in range(8)]
              nb_other = len(blocks)
              blocks += [(O_Q + 512 * i, 512, "q") for i in range(2)]
              blocks += [(O_Z + 512 * i, 512, "z") for i in range(4)]
              wi = 0
              ai = 0
              oi = 0
              for tb in range(4):
                  for tt in range(4):
                      tile = tb * 4 + tt
                      xi = tile % 2
                      S.dma("sp", xt[xi][:], x_v[tile], writes=[B_xt[xi]])
                      S.op("act", lambda E, xi=xi: E.activation(out=junk[:], in_=xt[xi][:], func=AF.Square,
                                                                accum_out=st[:, 0:1]),
                           reads=[B_xt[xi]], writes=[B_junk, B_st])
                      S.op("dve", lambda E: E.tensor_scalar(out=st[:, 1:2], in0=st[:, 0:1], scalar1=1.0 / D,
                                                            scalar2=EPS, op0=ALU.mult, op1=ALU.add),
                           reads=[B_st], writes=[B_st])
                      S.op("act", lambda E: E.activation(out=st[:, 2:3], in_=st[:, 1:2], func=AF.Sqrt),
                           reads=[B_st], writes=[B_st])
                      S.op("dve", lambda E: E.reciprocal(out=st[:, 3:4], in_=st[:, 2:3]), reads=[B_st], writes=[B_st])
                      S.op("dve", lambda E, xi=xi: E.scalar_tensor_tensor(out=xb[:], in0=xt[xi][:], scalar=st[:, 3:4],
                                                                          in1=gain[:], op0=ALU.mult, op1=ALU.mult),
                           reads=[B_xt[xi], B_st, B_gain], writes=[B_xb])
                      for g in range(4):
                          ti = g % 2
                          for j in range(8):
                              kk = g * 8 + j
                              S.op("pe", lambda E, ti=ti, j=j, kk=kk: E.transpose(out=tp[ti][:, j, :],
                                                                                 in_=xb[:, kk * 128:(kk + 1) * 128],
                                                                                 identity=ident[:]),
                                   reads=[B_xb, B_ident], writes=[B_tp[ti]])
                          eng = "act" if g % 2 == 0 else "dve"
                          if eng == "act":
                              S.op("act", lambda E, ti=ti, g=g, tt=tt: E.copy(
                                  out=xnT[:, g * 8:(g + 1) * 8, tt * 128:(tt + 1) * 128], in_=tp[ti][:]),
                                  reads=[B_tp[ti]], writes=[B_xnT[tt]])
                          else:
                              S.op("dve", lambda E, ti=ti, g=g, tt=tt: E.tensor_copy(
                                  out=xnT[:, g * 8:(g + 1) * 8, tt * 128:(tt + 1) * 128], in_=tp[ti][:]),
                                  reads=[B_tp[ti]], writes=[B_xnT[tt]])
                  blks = blocks if tb < 2 else blocks[:nb_other]
                  for (c0, cw, kind) in blks:
                      wb = wblk[wi % 2]
                      Bw = B_w[wi % 2]
                      wi += 1
                      S.dma("pool", wb[:, :, 0:cw], w_v[:, :, c0:c0 + cw], writes=[Bw])
                      if kind == "feat":
                          for cc in range(cw // 128):
                              a = acc[ai % 4]
                              Ba = B_acc[ai % 4]
                              ai += 1
                              for kk in range(32):
                                  S.op("pe", lambda E, a=a, wb=wb, kk=kk, cc=cc: E.matmul(
                                      a[:], lhsT=wb[:, kk, cc * 128:(cc + 1) * 128], rhs=xnT[:, kk, :],
                                      start=(kk == 0), stop=(kk == 31)),
                                      reads=[Bw] + B_xnT, writes=[Ba])
                              o = ostb[oi % 3]
                              Bo = B_ostb[oi % 3]
                              if oi % 2 == 0:
                                  S.op("act", lambda E, o=o, a=a: E.copy(out=o[:], in_=a[:]), reads=[Ba], writes=[Bo])
                              else:
                                  S.op("dve", lambda E, o=o, a=a: E.tensor_copy(out=o[:], in_=a[:]), reads=[Ba], writes=[Bo])
                              oi += 1
                              r0 = c0 - O_X + cc * 128
                              S.dma("sp", s_xT[r0:r0 + 128, tb * 512:(tb + 1) * 512], o[:], reads=[Bo], writes=[B_scr])
                      else:
                          for tt in range(4):
                              a = acc[ai % 4]
                              Ba = B_acc[ai % 4]
                              ai += 1
                              for kk in range(32):
                                  S.op("pe", lambda E, a=a, wb=wb, kk=kk, tt=tt, cw=cw: E.matmul(
                                      a[:, 0:cw], lhsT=xnT[:, kk, tt * 128:(tt + 1) * 128], rhs=wb[:, kk, 0:cw],
                                      start=(kk == 0), stop=(kk == 31)),
                                      reads=[Bw, B_xnT[tt]], writes=[Ba])
                              o = ost[oi % 3]
                              Bo = B_ost[oi % 3]
                              if oi % 2 == 0:
                                  S.op("act", lambda E, o=o, a=a, cw=cw: E.copy(out=o[:, 0:cw], in_=a[:, 0:cw]),
                                       reads=[Ba], writes=[Bo])
                              else:
                                  S.op("dve", lambda E, o=o, a=a, cw=cw: E.tensor_copy(out=o[:, 0:cw], in_=a[:, 0:cw]),
                                       reads=[Ba], writes=[Bo])
                              oi += 1
                              t0 = tb * 512 + tt * 128
                              if kind == "kv":
                                  dst = s_kv[t0:t0 + 128, c0:c0 + cw]
                              elif kind == "q":
                                  dst = s_q[t0:t0 + 128, c0 - O_Q:c0 - O_Q + cw]
                              else:
                                  dst = s_z[t0:t0 + 128, c0 - O_Z:c0 - O_Z + cw]
                              S.dma("sp", dst, o[:, 0:cw], reads=[Bo], writes=[B_scr])
          outbufs.append(B_scr)
          S.barrier()

        if stage >= 2 and "B" not in SKIP:
          with ExitStack() as c:
            TWO_PI = 2.0 * np.pi
            C1 = 6.28125
            C2 = float(np.float32(TWO_PI - C1).view(np.uint32) & np.uint32(0xFFFFF000)) if False else 0.0019350051879882812
            C3 = float(TWO_PI - C1 - C2)
            MAGIC = 12582912.0
            SCALE = 192.0 ** -0.5
            qan = sb(c, "qan", [128, 1024]); kvan = sb(c, "kvan", [128, 512])
            qn_bc = sb(c, "qn_bc", [128, 192]); kn_bc = sb(c, "kn_bc", [128, 192])
            aon_bc = sb(c, "aon_bc", [128, 2048]); invf = sb(c, "invf_sb", [128, 32])
            B_const = Buf()
            for dst, src in ((qan, q_a_norm), (kvan, kv_a_norm), (qn_bc, q_norm), (kn_bc, k_norm), (aon_bc, aon), (invf, invf_in)):
                S.dma("sp", dst[:], src.partition_broadcast(128), writes=[B_const])
            posi = sb(c, "posi", [128, 16], I32)
            S.dma("sp", posi[:], pos_in.rearrange("(n p) -> p n", p=128), writes=[B_const], allow_slow_non_contiguous=True) if False else None
            posf = sb(c, "posf", [128, 16])
            ang = sb(c, "ang", [128, 16, 32]); kk_t = sb(c, "kk_t", [128, 16, 32]); rr = sb(c, "rr", [128, 16, 32])
            cs = sb(c, "cs", [128, 16, 64])
            B_cs = Buf()
            pos_t = sb(c, "pos_t", [16, 128], I32)
            for n in range(16):
                S.dma("sp", posi[:, n:n + 1], pos_in[n * 128:(n + 1) * 128].rearrange("(p o) -> p o", o=1), writes=[B_const])
            S.op("dve", lambda E: E.tensor_copy(out=posf[:], in_=posi[:]), reads=[B_const], writes=[B_cs])
            S.op("dve", lambda E: E.tensor_tensor(out=ang[:], in0=posf[:].unsqueeze(2).broadcast_to([128, 16, 32]),
                                                  in1=invf[:].unsqueeze(1).broadcast_to([128, 16, 32]), op=ALU.mult),
                 reads=[B_const, B_cs], writes=[B_cs])
            for which, shift in ((1, 0.0), (0, 0.25)):
                S.op("dve", lambda E, shift=shift: E.tensor_scalar(out=kk_t[:], in0=ang[:], scalar1=1.0 / TWO_PI, scalar2=shift,
                                                                   op0=ALU.mult, op1=ALU.add), reads=[B_cs], writes=[B_cs])
                S.op("dve", lambda E: E.tensor_scalar(out=kk_t[:], in0=kk_t[:], scalar1=MAGIC, scalar2=None, op0=ALU.add),
                     reads=[B_cs], writes=[B_cs])
                S.op("dve", lambda E: E.tensor_scalar(out=kk_t[:], in0=kk_t[:], scalar1=-MAGIC, scalar2=None, op0=ALU.add),
                     reads=[B_cs], writes=[B_cs])
                S.op("dve", lambda E: E.scalar_tensor_tensor(out=rr[:], in0=kk_t[:], scalar=-C1, in1=ang[:], op0=ALU.mult, op1=ALU.add),
                     reads=[B_cs], writes=[B_cs])
                S.op("dve", lambda E: E.scalar_tensor_tensor(out=rr[:], in0=kk_t[:], scalar=-C2, in1=rr[:], op0=ALU.mult, op1=ALU.add),
                     reads=[B_cs], writes=[B_cs])
                S.op("dve", lambda E: E.scalar_tensor_tensor(out=rr[:], in0=kk_t[:], scalar=-C3, in1=rr[:], op0=ALU.mult, op1=ALU.add),
                     reads=[B_cs], writes=[B_cs])
                if shift != 0.0:
                    S.op("dve", lambda E: E.tensor_scalar(out=rr[:], in0=rr[:], scalar1=float(np.pi / 2), scalar2=None, op0=ALU.add),
                         reads=[B_cs], writes=[B_cs])
                S.op("dve", lambda E: E.tensor_scalar(out=rr[:], in0=rr[:], scalar1=3.1415925, scalar2=-3.1415925,
                                                      op0=ALU.min, op1=ALU.max), reads=[B_cs], writes=[B_cs])
                S.op("act", lambda E, which=which: E.activation(out=cs[:, :, which * 32:(which + 1) * 32], in_=rr[:], func=AF.Sin),
                     reads=[B_cs], writes=[B_cs])

            ckvT = sb(c, "ckvT", [128, 4, T], BF16); B_ckvT = Buf()
            cqT = sb(c, "cqT", [128, 8, TO], BF16); B_cqT = Buf()
            krr = sb(c, "krr", [128, 16, 64]); B_krr = Buf()
            ssr = sb(c, "ssr", [128, 16]); B_ssr = Buf()
            stB = sb(c, "stB", [128, 16]); B_stB = Buf()
            with ExitStack() as c2:
                lt = [sb(c2, "lt%d" % i, [128, 1024]) for i in range(2)]; B_lt = [Buf(), Buf()]
                jk = sb(c2, "jkB", [128, 1024], BF16); B_jk = Buf()
                nb = sb(c2, "nbB", [128, 1024], BF16); B_nb = Buf()
                tq = sb(c2, "tqB", [128, 64]); B_tq = Buf()
                tpB = [ps(c2, "tpB%d" % i, [128, 8, 128], BF16) for i in range(2)]; B_tpB = [PB(), PB()]
                for tile in range(16):
                    li = tile % 2
                    S.dma("sp", lt[li][:, 0:576], s_kv[tile * 128:(tile + 1) * 128, 0:576], reads=[B_scr], writes=[B_lt[li]])
                    S.op("act", lambda E, li=li: E.activation(out=jk[:, 0:512], in_=lt[li][:, 0:512], func=AF.Square,
                                                              accum_out=stB[:, 0:1]), reads=[B_lt[li]], writes=[B_jk, B_stB])
                    S.op("act", lambda E, li=li, tile=tile: E.activation(out=jk[:, 512:576], in_=lt[li][:, 512:576], func=AF.Square,
                                                                         accum_out=ssr[:, tile:tile + 1]),
                         reads=[B_lt[li]], writes=[B_jk, B_ssr])
                    S.op("dve", lambda E: E.tensor_scalar(out=stB[:, 1:2], in0=stB[:, 0:1], scalar1=1.0 / 512, scalar2=EPS,
                                                          op0=ALU.mult, op1=ALU.add), reads=[B_stB], writes=[B_stB])
                    S.op("act", lambda E: E.activation(out=stB[:, 2:3], in_=stB[:, 1:2], func=AF.Sqrt), reads=[B_stB], writes=[B_stB])
                    S.op("dve", lambda E: E.reciprocal(out=stB[:, 3:4], in_=stB[:, 2:3]), reads=[B_stB], writes=[B_stB])
                    S.op("dve", lambda E, li=li: E.scalar_tensor_tensor(out=nb[:, 0:512], in0=lt[li][:, 0:512], scalar=stB[:, 3:4],
                                                                        in1=kvan[:], op0=ALU.mult, op1=ALU.mult),
                         reads=[B_lt[li], B_stB, B_const], writes=[B_nb])
                    ti = tile % 2
                    for j in range(4):
                        S.op("pe", lambda E, ti=ti, j=j: E.transpose(out=tpB[ti][:, j, :], in_=nb[:, j * 128:(j + 1) * 128],
                                                                     identity=ident[:]), reads=[B_nb, B_ident], writes=[B_tpB[ti]])
                    S.op("act", lambda E, ti=ti, tile=tile: E.copy(out=ckvT[:, :, tile * 128:(tile + 1) * 128], in_=tpB[ti][:, 0:4, :]),
                         reads=[B_tpB[ti]], writes=[B_ckvT])
                    S.op("dve", lambda E, li=li: E.tensor_tensor(out=tq[:], in0=lt[li][:, 512:576], in1=kn_bc[:, 128:192], op=ALU.mult),
                         reads=[B_lt[li], B_const], writes=[B_tq])
                    rope_ops(S, tq, B_tq, krr[:, tile, :], B_krr, cs, B_cs, tile, sb, c2, "k%d" % tile, nheads=1)
                for tile in range(8):
                    li = tile % 2
                    S.dma("sp", lt[li][:], s_q[tile * 128:(tile + 1) * 128, :], reads=[B_scr], writes=[B_lt[li]])
                    S.op("act", lambda E, li=li: E.activation(out=jk[:], in_=lt[li][:], func=AF.Square, accum_out=stB[:, 0:1]),
                         reads=[B_lt[li]], writes=[B_jk, B_stB])
                    S.op("dve", lambda E: E.tensor_scalar(out=stB[:, 1:2], in0=stB[:, 0:1], scalar1=1.0 / 1024, scalar2=EPS,
                                                          op0=ALU.mult, op1=ALU.add), reads=[B_stB], writes=[B_stB])
                    S.op("act", lambda E: E.activation(out=stB[:, 2:3], in_=stB[:, 1:2], func=AF.Sqrt), reads=[B_stB], writes=[B_stB])
                    S.op("dve", lambda E: E.reciprocal(out=stB[:, 3:4], in_=stB[:, 2:3]), reads=[B_stB], writes=[B_stB])
                    S.op("dve", lambda E, li=li: E.scalar_tensor_tensor(out=nb[:], in0=lt[li][:], scalar=stB[:, 3:4], in1=qan[:],
                                                                        op0=ALU.mult, op1=ALU.mult),
                         reads=[B_lt[li], B_stB, B_const], writes=[B_nb])
                    ti = tile % 2
                    for j in range(8):
                        S.op("pe", lambda E, ti=ti, j=j: E.transpose(out=tpB[ti][:, j, :], in_=nb[:, j * 128:(j + 1) * 128],
                                                                     identity=ident[:]), reads=[B_nb, B_ident], writes=[B_tpB[ti]])
                    S.op("act", lambda E, ti=ti, tile=tile: E.copy(out=cqT[:, :, tile * 128:(tile + 1) * 128], in_=tpB[ti][:]),
                         reads=[B_tpB[ti]], writes=[B_cqT])
                S.barrier()

            HG = 4
            KT = sb(c, "KT", [128, HG, T], BF16); B_KT = Buf()
            KTr = sb(c, "KTr", [64, HG, T], BF16); B_KTr = Buf()
            vext = sb(c, "vext", [128, 16, HG, 130], BF16); B_vext = Buf()
            QT = sb(c, "QT", [128, HG, TO], BF16); B_QT = Buf()
            QTr = sb(c, "QTr", [64, HG, TO], BF16); B_QTr = Buf()
            wkv = sb(c, "wkv", [128, 4, HG * 256], BF16); B_wkv = Buf()
            wq = sb(c, "wq", [128, 8, HG * 192], BF16); B_wq = Buf()
            S.op("pool", lambda E: E.memset(vext[:], 1.0), writes=[B_vext])
            for hg in range(NH // HG):
                S.dma("pool", wkv[:], w_ukv.rearrange("(k p) c -> p k c", p=128)[:, :, hg * HG * 256:(hg + 1) * HG * 256],
                      writes=[B_wkv])
                S.dma("pool", wq[:], w_uq.rearrange("(k p) c -> p k c", p=128)[:, :, hg * HG * 192:(hg + 1) * HG * 192],
                      writes=[B_wq])
                with ExitStack() as c2:
                    pk = [ps(c2, "pk%d" % i, [128, 512]) for i in range(4)]; B_pk = [PB() for _ in range(4)]
                    tpk = [ps(c2, "tpk%d" % i, [128, 8, 128], BF16) for i in range(2)]; B_tpk = [PB(), PB()]
                    tpr = [ps(c2, "tpr%d" % i, [128, 8, 128], BF16) for i in range(2)]; B_tpr = [PB(), PB()]
                    jk = sb(c2, "jkK", [128, 192], BF16); B_jk = Buf()
                    sk = [sb(c2, "sk%d" % i, [128, 16]) for i in range(2)]; B_sk = [Buf(), Buf()]
                    kn = [sb(c2, "kn%d" % i, [128, HG, 192], BF16) for i in range(2)]; B_kn = [Buf(), Buf()]
                    for tile in range(16):
                        pi = tile % 2
                        for b in range(2):
                            for kc in range(4):
                                S.op("pe", lambda E, pi=pi, b=b, kc=kc, tile=tile: E.matmul(
                                    pk[pi * 2 + b][:], lhsT=ckvT[:, kc, tile * 128:(tile + 1) * 128],
                                    rhs=wkv[:, kc, b * 512:(b + 1) * 512], start=(kc == 0), stop=(kc == 3)),
                                    reads=[B_ckvT, B_wkv], writes=[B_pk[pi * 2 + b]])
                        st_ = sk[pi]; Bs = B_sk[pi]
                        for hl in range(HG):
                            p_ = pk[pi * 2 + hl // 2]; Bp = B_pk[pi * 2 + hl // 2]; off = (hl % 2) * 256
                            S.op("act", lambda E, p_=p_, off=off, st_=st_, hl=hl: E.activation(
                                out=jk[:, 0:128], in_=p_[:, off:off + 128], func=AF.Square, accum_out=st_[:, hl:hl + 1]),
                                reads=[Bp], writes=[B_jk, Bs])
                        S.op("dve", lambda E, st_=st_, tile=tile: E.tensor_scalar(
                            out=st_[:, 4:8], in0=st_[:, 0:4], scalar1=ssr[:, tile:tile + 1], scalar2=1.0 / 192,
                            op0=ALU.add, op1=ALU.mult), reads=[Bs, B_ssr], writes=[Bs])
                        S.op("dve", lambda E, st_=st_: E.tensor_scalar(out=st_[:, 4:8], in0=st_[:, 4:8], scalar1=EPS, scalar2=None,
                                                                       op0=ALU.add), reads=[Bs], writes=[Bs])
                        S.op("act", lambda E, st_=st_: E.activation(out=st_[:, 8:12], in_=st_[:, 4:8], func=AF.Sqrt), reads=[Bs], writes=[Bs])
                        S.op("dve", lambda E, st_=st_: E.reciprocal(out=st_[:, 12:16], in_=st_[:, 8:12]), reads=[Bs], writes=[Bs])
                        kn_ = kn[pi]; Bk = B_kn[pi]
                        for hl in range(HG):
                            p_ = pk[pi * 2 + hl // 2]; Bp = B_pk[pi * 2 + hl // 2]; off = (hl % 2) * 256
                            S.op("dve", lambda E, p_=p_, off=off, st_=st_, hl=hl, kn_=kn_: E.scalar_tensor_tensor(
                                out=kn_[:, hl, 0:128], in0=p_[:, off:off + 128], scalar=st_[:, 12 + hl:13 + hl], in1=kn_bc[:, 0:128],
                                op0=ALU.mult, op1=ALU.mult), reads=[Bp, Bs, B_const], writes=[Bk])
                            S.op("dve", lambda E, st_=st_, hl=hl, kn_=kn_, tile=tile: E.tensor_scalar(
                                out=kn_[:, hl, 128:192], in0=krr[:, tile, :], scalar1=st_[:, 12 + hl:13 + hl], scalar2=None, op0=ALU.mult),
                                reads=[B_krr, Bs], writes=[Bk])
                            S.op("act", lambda E, p_=p_, off=off, hl=hl, tile=tile: E.copy(
                                out=vext[:, tile, hl, 0:128], in_=p_[:, off + 128:off + 256]), reads=[Bp], writes=[B_vext])
                        for hl in range(HG):
                            S.op("pe", lambda E, pi=pi, hl=hl, kn_=kn_: E.transpose(out=tpk[pi][:, hl, :], in_=kn_[:, hl, 0:128],
                                                                                   identity=ident[:]), reads=[Bk, B_ident], writes=[B_tpk[pi]])
                            S.op("pe", lambda E, pi=pi, hl=hl, kn_=kn_: E.transpose(out=tpr[pi][0:64, hl, :], in_=kn_[:, hl, 128:192],
                                                                                   identity=ident[:]), reads=[Bk, B_ident], writes=[B_tpr[pi]])
                        S.op("dve", lambda E, pi=pi, tile=tile: E.tensor_copy(out=KT[:, :, tile * 128:(tile + 1) * 128], in_=tpk[pi][:, 0:4, :]),
                             reads=[B_tpk[pi]], writes=[B_KT])
                        S.op("act", lambda E, pi=pi, tile=tile: E.copy(out=KTr[:, :, tile * 128:(tile + 1) * 128], in_=tpr[pi][0:64, 0:4, :]),
                             reads=[B_tpr[pi]], writes=[B_KTr])
                    S.barrier()
                with ExitStack() as c2:
                    pq = [ps(c2, "pq%d" % i, [128, 512]) for i in range(4)]; B_pq = [PB() for _ in range(4)]
                    tpk = [ps(c2, "tpq%d" % i, [128, 8, 128], BF16) for i in range(2)]; B_tpk = [PB(), PB()]
                    tpr = [ps(c2, "tpqr%d" % i, [128, 8, 128], BF16) for i in range(2)]; B_tpr = [PB(), PB()]
                    jk = sb(c2, "jkQ", [128, 192], BF16); B_jk = Buf()
                    sk = [sb(c2, "sq%d" % i, [128, 16]) for i in range(2)]; B_sk = [Buf(), Buf()]
                    qg = [sb(c2, "qg%d" % i, [128, HG, 192]) for i in range(2)]; B_qg = [Buf(), Buf()]
                    qn_ = [sb(c2, "qn%d" % i, [128, HG, 192], BF16) for i in range(2)]; B_qn = [Buf(), Buf()]
                    for tile in range(8):
                        pi = tile % 2
                        for b in range(2):
                            for kc in range(8):
                                S.op("pe", lambda E, pi=pi, b=b, kc=kc, tile=tile: E.matmul(
                                    pq[pi * 2 + b][:, 0:384], lhsT=cqT[:, kc, tile * 128:(tile + 1) * 128],
                                    rhs=wq[:, kc, b * 384:(b + 1) * 384], start=(kc == 0), stop=(kc == 7)),
                                    reads=[B_cqT, B_wq], writes=[B_pq[pi * 2 + b]])
                        st_ = sk[pi]; Bs = B_sk[pi]
                        for hl in range(HG):
                            p_ = pq[pi * 2 + hl // 2]; Bp = B_pq[pi * 2 + hl // 2]; off = (hl % 2) * 192
                            S.op("act", lambda E, p_=p_, off=off, st_=st_, hl=hl: E.activation(
                                out=jk[:], in_=p_[:, off:off + 192], func=AF.Square, accum_out=st_[:, hl:hl + 1]),
                                reads=[Bp], writes=[B_jk, Bs])
                        S.op("dve", lambda E, st_=st_: E.tensor_scalar(out=st_[:, 4:8], in0=st_[:, 0:4], scalar1=1.0 / 192, scalar2=EPS,
                                                                       op0=ALU.mult, op1=ALU.add), reads=[Bs], writes=[Bs])
                        S.op("act", lambda E, st_=st_: E.activation(out=st_[:, 8:12], in_=st_[:, 4:8], func=AF.Sqrt), reads=[Bs], writes=[Bs])
                        S.op("dve", lambda E, st_=st_: E.reciprocal(out=st_[:, 12:16], in_=st_[:, 8:12]), reads=[Bs], writes=[Bs])
                        g_ = qg[pi]; Bg = B_qg[pi]; n_ = qn_[pi]; Bn = B_qn[pi]
                        for hl in range(HG):
                            p_ = pq[pi * 2 + hl // 2]; Bp = B_pq[pi * 2 + hl // 2]; off = (hl % 2) * 192
                            S.op("dve", lambda E, p_=p_, off=off, st_=st_, hl=hl, g_=g_: E.scalar_tensor_tensor(
                                out=g_[:, hl, :], in0=p_[:, off:off + 192], scalar=st_[:, 12 + hl:13 + hl], in1=qn_bc[:],
                                op0=ALU.mult, op1=ALU.mult), reads=[Bp, Bs, B_const], writes=[Bg])
                        S.op("dve", lambda E, g_=g_, n_=n_: E.tensor_copy(out=n_[:, :, 0:128], in_=g_[:, :, 0:128]), reads=[Bg], writes=[Bn])
                        rope_ops(S, g_[:, :, 128:192], Bg, n_[:, :, 128:192], Bn, cs, B_cs, tile, sb, c2, "q%d_%d" % (hg, tile), nheads=HG)
                        for hl in range(HG):
                            S.op("pe", lambda E, pi=pi, hl=hl, n_=n_: E.transpose(out=tpk[pi][:, hl, :], in_=n_[:, hl, 0:128],
                                                                                  identity=ident[:]), reads=[Bn, B_ident], writes=[B_tpk[pi]])
                            S.op("pe", lambda E, pi=pi, hl=hl, n_=n_: E.transpose(out=tpr[pi][0:64, hl, :], in_=n_[:, hl, 128:192],
                                                                                  identity=ident[:]), reads=[Bn, B_ident], writes=[B_tpr[pi]])
                        S.op("dve", lambda E, pi=pi, tile=tile: E.tensor_copy(out=QT[:, :, tile * 128:(tile + 1) * 128], in_=tpk[pi][:, 0:4, :]),
                             reads=[B_tpk[pi]], writes=[B_QT])
                        S.op("act", lambda E, pi=pi, tile=tile: E.copy(out=QTr[:, :, tile * 128:(tile + 1) * 128], in_=tpr[pi][0:64, 0:4, :]),
                             reads=[B_tpr[pi]], writes=[B_QTr])
                    S.barrier()
                with ExitStack() as c2:
                    pS = [ps(c2, "pS%d" % i, [128, 512]) for i in range(3)]; B_pS = [PB() for _ in range(3)]
                    pO = [ps(c2, "pO%d" % i, [128, 512]) for i in range(4)]; B_pO = [PB() for _ in range(4)]
                    PT = [sb(c2, "PT%d" % i, [128, 512], BF16) for i in range(3)]; B_PT = [Buf() for _ in range(3)]
                    of = [sb(c2, "of%d" % i, [128, 128]) for i in range(2)]; B_of = [Buf(), Buf()]
                    jk = sb(c2, "jkA", [128, 128], BF16); B_jk = Buf()
                    sa = [sb(c2, "sa%d" % i, [128, 8]) for i in range(2)]; B_sa = [Buf(), Buf()]
                    ob = [sb(c2, "ob%d" % i, [128, 128], BF16) for i in range(2)]; B_ob = [Buf(), Buf()]
                    si = 0
                    oi2 = 0
                    for hl in range(HG):
                        h = hg * HG + hl
                        for tg in range(2):
                            def s_mm(tk, slot):
                                p_ = pS[slot % 3]; Bp = B_pS[slot % 3]
                                S.op("pe", lambda E, p_=p_, hl=hl, tk=tk, tg=tg: E.matmul(
                                    p_[:], lhsT=KT[:, hl, tk * 128:(tk + 1) * 128], rhs=QT[:, hl, tg * 512:(tg + 1) * 512],
                                    start=True, stop=False), reads=[B_KT, B_QT], writes=[Bp])
                                S.op("pe", lambda E, p_=p_, hl=hl, tk=tk, tg=tg: E.matmul(
                                    p_[:], lhsT=KTr[:, hl, tk * 128:(tk + 1) * 128], rhs=QTr[:, hl, tg * 512:(tg + 1) * 512],
                                    start=False, stop=True), reads=[B_KTr, B_QTr], writes=[Bp])
                            s_mm(0, si)
                            for tk in range(16):
                                p_ = pS[si % 3]; Bp = B_pS[si % 3]; pt = PT[si % 3]; Bpt = B_PT[si % 3]
                                if tk + 1 < 16:
                                    s_mm(tk + 1, si + 1)
                                si += 1
                                S.op("act", lambda E, p_=p_, pt=pt: E.activation(out=pt[:], in_=p_[:], func=AF.Exp, scale=SCALE),
                                     reads=[Bp], writes=[Bpt])
                                for tqt in range(4):
                                    S.op("pe", lambda E, pt=pt, tqt=tqt, tk=tk, hl=hl: E.matmul(
                                        pO[tqt][:, 0:129], lhsT=pt[:, tqt * 128:(tqt + 1) * 128], rhs=vext[:, tk, hl, 0:129],
                                        start=(tk == 0), stop=(tk == 15)), reads=[Bpt, B_vext], writes=[B_pO[tqt]])
                            for tqt in range(4):
                                o_ = of[oi2 % 2]; Bo = B_of[oi2 % 2]; s_ = sa[oi2 % 2]; Bs = B_sa[oi2 % 2]
                                b_ = ob[oi2 % 2]; Bb = B_ob[oi2 % 2]; oi2 += 1
                                S.op("dve", lambda E, s_=s_, tqt=tqt: E.reciprocal(out=s_[:, 0:1], in_=pO[tqt][:, 128:129]),
                                     reads=[B_pO[tqt]], writes=[Bs])
                                S.op("dve", lambda E, s_=s_, tqt=tqt, o_=o_: E.tensor_scalar(
                                    out=o_[:], in0=pO[tqt][:, 0:128], scalar1=s_[:, 0:1], scalar2=None, op0=ALU.mult),
                                    reads=[B_pO[tqt], Bs], writes=[Bo])
                                S.op("act", lambda E, o_=o_, s_=s_: E.activation(out=jk[:], in_=o_[:], func=AF.Square, accum_out=s_[:, 1:2]),
                                     reads=[Bo], writes=[B_jk, Bs])
                                S.op("dve", lambda E, s_=s_: E.tensor_scalar(out=s_[:, 2:3], in0=s_[:, 1:2], scalar1=1.0 / 128, scalar2=EPS,
                                                                             op0=ALU.mult, op1=ALU.add), reads=[Bs], writes=[Bs])
                                S.op("act", lambda E, s_=s_: E.activation(out=s_[:, 3:4], in_=s_[:, 2:3], func=AF.Sqrt), reads=[Bs], writes=[Bs])
                                S.op("dve", lambda E, s_=s_: E.reciprocal(out=s_[:, 4:5], in_=s_[:, 3:4]), reads=[Bs], writes=[Bs])
                                S.op("dve", lambda E, o_=o_, s_=s_, b_=b_, h=h: E.scalar_tensor_tensor(
                                    out=b_[:], in0=o_[:], scalar=s_[:, 4:5], in1=aon_bc[:, h * 128:(h + 1) * 128], op0=ALU.mult, op1=ALU.mult),
                                    reads=[Bo, Bs, B_const], writes=[Bb])
                                t0 = tg * 512 + tqt * 128
                                S.dma("sp", s_mix[t0:t0 + 128, h * 128:(h + 1) * 128], b_[:], reads=[Bb], writes=[B_mix])
                    S.barrier()
            S.barrier()

        if stage >= 3 and "C" not in SKIP:
          with ExitStack() as c:
            B_cc = Buf()
            tri = sb(c, "tri", [128, 4, 128])
            S.dma("sp", tri[:], tri_in.rearrange("f k l -> k f l"), writes=[B_cc])
            identF = sb(c, "identF", [128, 128])
            S.dma("sp", identF[:], ident_in, writes=[B_cc])
            onesF = sb(c, "onesF", [128, 128])
            S.op("dve", lambda E: E.memset(onesF[:], 1.0), writes=[B_cc])
            neg4 = [sb(c, "neg4_%d" % i, [128, 4, 128], BF16) for i in range(2)]
            for i in range(2):
                S.op("dve", lambda E, i=i: E.tensor_copy(out=neg4[i][:], in_=tri[:, 2 + i, :].unsqueeze(1).broadcast_to([128, 4, 128])),
                     reads=[B_cc], writes=[B_cc])
            alog_bc = sb(c, "alog_bc", [128, 64]); dtb_bc = sb(c, "dtb_bc", [128, 64]); dsk_bc = sb(c, "dsk_bc", [128, 32])
            son_bc = sb(c, "son_bc", [128, 2048])
            for dst, src in ((alog_bc, a_log), (dtb_bc, dt_bias), (dsk_bc, d_skip), (son_bc, son)):
                S.dma("sp", dst[:], src.partition_broadcast(128), writes=[B_cc])
            A_bc = sb(c, "A_bc", [128, 64])
            S.op("act", lambda E: E.activation(out=A_bc[:], in_=alog_bc[:], func=AF.Exp), reads=[B_cc], writes=[B_cc])
            S.op("dve", lambda E: E.tensor_scalar(out=A_bc[:], in0=A_bc[:], scalar1=-1.0, scalar2=None, op0=ALU.mult),
                 reads=[B_cc], writes=[B_cc])
            dtr = sb(c, "dtr", [128, 16, 64]); dtv = sb(c, "dtv", [128, 16, 64]); adt = sb(c, "adt", [128, 16, 64])
            tmpd = sb(c, "tmpd", [128, 16, 64])
            B_dt = Buf()
            for tile in range(16):
                S.dma("sp", dtr[:, tile, :], s_kv[tile * 128:(tile + 1) * 128, 576:640], reads=[B_scr], writes=[B_dt])
            bc64 = lambda t_: t_[:].unsqueeze(1).broadcast_to([128, 16, 64])
            S.op("dve", lambda E: E.tensor_tensor(out=dtr[:], in0=dtr[:], in1=bc64(dtb_bc), op=ALU.add), reads=[B_dt, B_cc], writes=[B_dt])
            S.op("act", lambda E: E.activation(out=tmpd[:], in_=dtr[:], func=AF.Abs), reads=[B_dt], writes=[B_dt])
            S.op("act", lambda E: E.activation(out=tmpd[:], in_=tmpd[:], func=AF.Exp, scale=-1.0), reads=[B_dt], writes=[B_dt])
            S.op("act", lambda E: E.activation(out=tmpd[:], in_=tmpd[:], func=AF.Ln, bias=1.0), reads=[B_dt], writes=[B_dt])
            S.op("dve", lambda E: E.scalar_tensor_tensor(out=dtv[:], in0=dtr[:], scalar=0.0, in1=tmpd[:], op0=ALU.max, op1=ALU.add),
                 reads=[B_dt], writes=[B_dt])
            S.op("dve", lambda E: E.tensor_tensor(out=adt[:], in0=dtv[:], in1=bc64(A_bc), op=ALU.mult), reads=[B_dt, B_cc], writes=[B_dt])

            import os as _os
            SUB = int(_os.environ.get("KSUB", "99"))
            cw_v = conv_w.rearrange("(n p) j -> n p j", p=128)
            cb_v = conv_b.rearrange("(n p o) -> n p o", p=128, o=1)
            cin = [sb(c, "cin%d" % i, [128, T + 4], BF16) for i in range(2)]; B_cin = [Buf(), Buf()]
            for i in range(2):
                S.op("pool", lambda E, i=i: E.memset(cin[i][:], 0.0), writes=[B_cin[i]])
            cacc = sb(c, "cacc", [128, T]); B_cacc = Buf()
            cwt = [sb(c, "cwt%d" % i, [128, 8]) for i in range(2)]; B_cwt = [Buf(), Buf()]
            cT = [sb(c, "cT%d" % i, [128, T], BF16) for i in range(4)]; B_cT = [Buf() for _ in range(4)]
            xtok = sb(c, "xtok", [128, 16, 256], BF16); B_xtok = Buf()
            Btok = sb(c, "Btok", [128, 16, 128], BF16); B_Btok = Buf()
            CBT = sb(c, "CBT", [128, 8, 128], BF16); B_CBT = Buf()
            yacc = sb(c, "yacc", [128, 8, 256]); B_yacc = Buf()
            stateL = [sb(c, "state%d" % i, [128, 256]) for i in range(2)]; B_stateL = [Buf(), Buf()]
            state_bfL = [sb(c, "state_bf%d" % i, [128, 256], BF16) for i in range(2)]; B_stbfL = [Buf(), Buf()]
            P_tp = ps(c, "P_tp", [128, 8, 128], BF16); B_Ptp = PB()
            P_ct = ps(c, "P_ct", [128, 512]); B_Pct = PB()
            P_cb = [ps(c, "P_cb%d" % i, [128, 4, 128]) for i in range(2)]; B_Pcb = [PB(), PB()]
            P_cbt = ps(c, "P_cbt", [128, 512]); B_Pcbt = PB()
            P_y = ps(c, "P_y", [128, 512]); B_Py = PB()
            P_yo = ps(c, "P_yo", [128, 512]); B_Pyo = PB()
            P_st = ps(c, "P_st", [128, 512]); B_Pst = PB()
            bankA = [P_ct, P_st]; B_bankA = [B_Pct, B_Pst]
            bankB = P_cb; B_bankB = B_Pcb
            bankC = [P_y, P_yo]; B_bankC = [B_Py, B_Pyo]
            ci_n = 0
            sm = [sb(c, "sm%d" % i, [128, 32]) for i in range(4)]; B_sm = [Buf() for _ in range(4)]
            arep = [sb(c, "arep%d" % i, [128, 4, 128]) for i in range(4)]; B_arep = [Buf() for _ in range(4)]
            LT = [sb(c, "LT%d" % i, [128, 4, 128]) for i in range(4)]; B_LT = [Buf() for _ in range(4)]
            MT = [sb(c, "MT%d" % i, [128, 4, 128], BF16) for i in range(4)]; B_MT = [Buf() for _ in range(4)]
            xdt = [sb(c, "xdt%d" % i, [128, 4, 64], BF16) for i in range(4)]; B_xdt = [Buf() for _ in range(4)]
            xdd = [sb(c, "xdd%d" % i, [128, 4, 64], BF16) for i in range(4)]; B_xdd = [Buf() for _ in range(4)]
            zt = [sb(c, "zt%d" % i, [128, 256]) for i in range(2)]; B_zt = [Buf(), Buf()]
            yf = [sb(c, "yf%d" % i, [128, 256]) for i in range(2)]; B_yf = [Buf(), Buf()]
            yb = [sb(c, "yb%d" % i, [128, 256], BF16) for i in range(2)]; B_yb = [Buf(), Buf()]
            jkC = sb(c, "jkC", [128, 256], BF16); B_jkC = Buf()
            it = 0
            for g in range(8 if SUB > 0 else 0):
                chans = [g * 256, g * 256 + 128, 2048 + g * 128, 3072 + g * 128]
                for qi, ch0 in enumerate(chans):
                    ci = ci_n % 2; ci_n += 1
                    S.dma("sp", cin[ci][:, 2:2 + T], s_xT[ch0:ch0 + 128, :], reads=[B_scr], writes=[B_cin[ci]])
                    S.dma("sp", cwt[ci][:, 0:5], cw_v[ch0 // 128], writes=[B_cwt[ci]])
                    S.dma("sp", cwt[ci][:, 5:6], cb_v[ch0 // 128], writes=[B_cwt[ci]])
                    S.op("dve", lambda E, ci=ci: E.tensor_scalar(out=cacc[:], in0=cin[ci][:, 0:T], scalar1=cwt[ci][:, 0:1], scalar2=None,
                                                                 op0=ALU.mult), reads=[B_cin[ci], B_cwt[ci]], writes=[B_cacc])
                    for j in range(1, 5):
                        S.op("dve", lambda E, ci=ci, j=j: E.scalar_tensor_tensor(out=cacc[:], in0=cin[ci][:, j:j + T], scalar=cwt[ci][:, j:j + 1],
                                                                                in1=cacc[:], op0=ALU.mult, op1=ALU.add),
                             reads=[B_cin[ci], B_cwt[ci]], writes=[B_cacc])
                    S.op("act", lambda E, ci=ci, qi=qi: E.activation(out=cT[qi][:], in_=cacc[:], func=AF.Silu, bias=cwt[ci][:, 5:6]),
                         reads=[B_cacc, B_cwt[ci]], writes=[B_cT[qi]])
                if SUB < 2:
                    continue
                for tile in range(16):
                    for qi in range(3):
                        S.op("pe", lambda E, qi=qi, tile=tile: E.transpose(out=P_tp[:, qi, :], in_=cT[qi][:, tile * 128:(tile + 1) * 128],
                                                                           identity=ident[:]), reads=[B_cT[qi], B_ident], writes=[B_Ptp])
                    S.op("dve", lambda E, tile=tile: E.tensor_copy(out=xtok[:, tile, :], in_=P_tp[:, 0:2, :]), reads=[B_Ptp], writes=[B_xtok])
                    S.op("act", lambda E, tile=tile: E.copy(out=Btok[:, tile, :], in_=P_tp[:, 2, :]), reads=[B_Ptp], writes=[B_Btok])
                if SUB < 3:
                    continue
                for ch in range(8):
                    S.op("pe", lambda E, ch=ch: E.matmul(P_cbt[:, 0:128], lhsT=cT[2][:, ch * 128:(ch + 1) * 128],
                                                         rhs=cT[3][:, ch * 128:(ch + 1) * 128], start=True, stop=True),
                         reads=[B_cT[2], B_cT[3]], writes=[B_Pcbt])
                    S.op("act", lambda E, ch=ch: E.copy(out=CBT[:, ch, :], in_=P_cbt[:, 0:128]), reads=[B_Pcbt], writes=[B_CBT])
                S.op("dve", lambda E, g=g: E.tensor_tensor(
                    out=yacc[:].rearrange("p c (r q) -> p c r q", r=4),
                    in0=xtok[:, 0:8, :].rearrange("p c (r q) -> p c r q", r=4),
                    in1=dsk_bc[:, g * 4:(g + 1) * 4].unsqueeze(1).unsqueeze(3).broadcast_to([128, 8, 4, 64]), op=ALU.mult),
                    reads=[B_xtok, B_cc], writes=[B_yacc])
                cnt_d = [0, 0]

                def scan_chunk(di, ch):
                    colX = 127 if di == 0 else 0
                    triX = tri[:, di, :]
                    negX = tri[:, 2 + di, :]
                    own = ch < 8
                    k_ = di * 2 + cnt_d[di] % 2; cnt_d[di] += 1
                    s_ = sm[k_]; Bs = B_sm[k_]
                    h0 = di * 32 + g * 4
                    adt4 = adt[:, ch, h0:h0 + 4]
                    bA = bankA[di]; BbA = B_bankA[di]; pc = bankB[di]; Bpc = B_bankB[di]; bC = bankC[di]; BbC = B_bankC[di]
                    st_ = stateL[di]; Bst = B_stateL[di]; sbf = state_bfL[di]; Bsbf = B_stbfL[di]
                    S.op("pe", lambda E, triX=triX, adt4=adt4, bA=bA: E.matmul(bA[:, 0:4], lhsT=triX, rhs=adt4, start=True, stop=True),
                         reads=[B_cc, B_dt], writes=[BbA])
                    yield
                    S.op("dve", lambda E, k_=k_, adt4=adt4, triX=triX: E.tensor_tensor(
                        out=arep[k_][:], in0=triX.unsqueeze(1).broadcast_to([128, 4, 128]),
                        in1=adt4.unsqueeze(2).broadcast_to([128, 4, 128]), op=ALU.mult), reads=[B_dt, B_cc], writes=[B_arep[k_]])
                    yield
                    S.op("pe", lambda E, pc=pc, k_=k_: E.matmul(pc[:].rearrange("p r l -> p (r l)"), lhsT=onesF[:],
                                                                rhs=arep[k_][:].rearrange("p r l -> p (r l)"), start=True, stop=False),
                         reads=[B_arep[k_], B_cc], writes=[Bpc])
                    yield
                    S.op("pe", lambda E, pc=pc, di=di: E.matmul(pc[:].rearrange("p r l -> p (r l)"), lhsT=ident[:],
                                                                rhs=neg4[di][:].rearrange("p r l -> p (r l)"), start=False, stop=True),
                         reads=[B_cc, B_ident], writes=[Bpc])
                    yield
                    S.op("dve", lambda E, s_=s_, bA=bA: E.tensor_scalar(out=s_[:, 0:4], in0=bA[:, 0:4], scalar1=-1.0, scalar2=None, op0=ALU.mult),
                         reads=[BbA], writes=[Bs])
                    yield
                    S.op("act", lambda E, s_=s_, bA=bA: E.activation(out=s_[:, 4:8], in_=bA[:, 0:4], func=AF.Exp), reads=[BbA], writes=[Bs])
                    yield
                    S.op("dve", lambda E, s_=s_, pc=pc, colX=colX: E.tensor_tensor(out=s_[:, 16:20], in0=pc[:, :, colX], in1=s_[:, 0:4], op=ALU.add),
                         reads=[Bpc, Bs], writes=[Bs])
                    yield
                    S.op("act", lambda E, s_=s_: E.activation(out=s_[:, 8:12], in_=s_[:, 16:20], func=AF.Exp), reads=[Bs], writes=[Bs])
                    yield
                    S.op("act", lambda E, s_=s_, pc=pc, colX=colX: E.activation(out=s_[:, 12:16], in_=pc[:, :, colX], func=AF.Exp),
                         reads=[Bpc], writes=[Bs])
                    yield
                    dtc = dtv[:, ch, h0:h0 + 4]
                    S.op("dve", lambda E, k_=k_, ch=ch, dtc=dtc: E.tensor_tensor(
                        out=xdt[k_][:], in0=xtok[:, ch, :].rearrange("p (r q) -> p r q", r=4),
                        in1=dtc.unsqueeze(2).broadcast_to([128, 4, 64]), op=ALU.mult), reads=[B_xtok, B_dt], writes=[B_xdt[k_]])
                    yield
                    if own:
                        for r in range(4):
                            S.op("act", lambda E, k_=k_, r=r, pc=pc, s_=s_: E.activation(out=LT[k_][:, r, :], in_=pc[:, r, :], func=AF.Exp,
                                                                                         bias=s_[:, r:r + 1]), reads=[Bpc, Bs], writes=[B_LT[k_]])
                            yield
                        S.op("dve", lambda E, k_=k_, ch=ch: E.tensor_tensor(out=MT[k_][:], in0=LT[k_][:],
                                                                            in1=CBT[:, ch, :].unsqueeze(1).broadcast_to([128, 4, 128]), op=ALU.mult),
                             reads=[B_LT[k_], B_CBT], writes=[B_MT[k_]])
                        yield
                        for r in range(4):
                            S.op("pe", lambda E, k_=k_, r=r, bC=bC: E.matmul(bC[:, r * 64:(r + 1) * 64], lhsT=MT[k_][:, r, :], rhs=xdt[k_][:, r, :],
                                                                             start=True, stop=True), reads=[B_MT[k_], B_xdt[k_]], writes=[BbC])
                            yield
                        S.op("pe", lambda E, ch=ch, bC=bC, sbf=sbf: E.matmul(bC[:, 256:512], lhsT=cT[3][:, ch * 128:(ch + 1) * 128], rhs=sbf[:],
                                                                             start=True, stop=True), reads=[B_cT[3], Bsbf], writes=[BbC])
                        yield
                        S.op("dve", lambda E, ch=ch, bC=bC: E.tensor_tensor(out=yacc[:, ch, :], in0=yacc[:, ch, :], in1=bC[:, 0:256], op=ALU.add),
                             reads=[BbC], writes=[B_yacc])
                        yield
                        for r in range(4):
                            S.op("dve", lambda E, ch=ch, r=r, s_=s_, bC=bC: E.scalar_tensor_tensor(
                                out=yacc[:, ch, r * 64:(r + 1) * 64], in0=bC[:, 256 + r * 64:256 + (r + 1) * 64], scalar=s_[:, 4 + r:5 + r],
                                in1=yacc[:, ch, r * 64:(r + 1) * 64], op0=ALU.mult, op1=ALU.add), reads=[BbC, Bs], writes=[B_yacc])
                            yield
                    S.op("dve", lambda E, k_=k_, s_=s_: E.tensor_tensor(out=xdd[k_][:], in0=xdt[k_][:],
                                                                        in1=s_[:, 8:12].unsqueeze(2).broadcast_to([128, 4, 64]), op=ALU.mult),
                         reads=[B_xdt[k_], Bs], writes=[B_xdd[k_]])
                    yield
                    S.op("pe", lambda E, k_=k_, ch=ch, bA=bA: E.matmul(bA[:, 256:512], lhsT=Btok[:, ch, :],
                                                                       rhs=xdd[k_][:].rearrange("p r q -> p (r q)"), start=True, stop=True),
                         reads=[B_Btok, B_xdd[k_]], writes=[BbA])
                    yield
                    for r in range(4):
                        S.op("dve", lambda E, r=r, s_=s_, bA=bA, st_=st_: E.scalar_tensor_tensor(
                            out=st_[:, r * 64:(r + 1) * 64], in0=st_[:, r * 64:(r + 1) * 64], scalar=s_[:, 12 + r:13 + r],
                            in1=bA[:, 256 + r * 64:256 + (r + 1) * 64], op0=ALU.mult, op1=ALU.add), reads=[BbA, Bs], writes=[Bst])
                        yield
                    S.op("dve", lambda E, st_=st_, sbf=sbf: E.tensor_copy(out=sbf[:], in_=st_[:]), reads=[Bst], writes=[Bsbf])
                    yield

                if SUB > 3:
                    for di in range(2):
                        S.op("dve", lambda E, di=di: E.memset(stateL[di][:], 0.0), writes=[B_stateL[di]])
                        S.op("dve", lambda E, di=di: E.memset(state_bfL[di][:], 0.0), writes=[B_stbfL[di]])
                    for step in range(16):
                        if step < 8:
                            gens = [scan_chunk(1, 15 - step)]
                        else:
                            gens = [scan_chunk(1, 15 - step), scan_chunk(0, step - 8)]
                        while gens:
                            for gen in list(gens):
                                try:
                                    next(gen)
                                except StopIteration:
                                    gens.remove(gen)
                for ch in range(8 if SUB > 4 else 0):
                    k_ = ch % 2
                    S.dma("sp", zt[k_][:], s_z[ch * 128:(ch + 1) * 128, g * 256:(g + 1) * 256], reads=[B_scr], writes=[B_zt[k_]])
                    S.op("act", lambda E, k_=k_: E.activation(out=zt[k_][:], in_=zt[k_][:], func=AF.Silu), reads=[B_zt[k_]], writes=[B_zt[k_]])
                    S.op("dve", lambda E, k_=k_, ch=ch: E.tensor_tensor(out=yf[k_][:], in0=yacc[:, ch, :], in1=zt[k_][:], op=ALU.mult),
                         reads=[B_yacc, B_zt[k_]], writes=[B_yf[k_]])
                    s_ = sm[k_]; Bs = B_sm[k_]
                    S.op("act", lambda E, k_=k_, s_=s_: E.activation(out=jkC[:], in_=yf[k_][:], func=AF.Square, accum_out=s_[:, 20:21]),
                         reads=[B_yf[k_]], writes=[B_jkC, Bs])
                    S.op("dve", lambda E, s_=s_: E.tensor_scalar(out=s_[:, 21:22], in0=s_[:, 20:21], scalar1=1.0 / 256, scalar2=EPS,
                                                                 op0=ALU.mult, op1=ALU.add), reads=[Bs], writes=[Bs])
                    S.op("act", lambda E, s_=s_: E.activation(out=s_[:, 22:23], in_=s_[:, 21:22], func=AF.Sqrt), reads=[Bs], writes=[Bs])
                    S.op("dve", lambda E, s_=s_: E.reciprocal(out=s_[:, 23:24], in_=s_[:, 22:23]), reads=[Bs], writes=[Bs])
                    S.op("dve", lambda E, k_=k_, s_=s_, g=g: E.scalar_tensor_tensor(
                        out=yb[k_][:], in0=yf[k_][:], scalar=s_[:, 23:24], in1=son_bc[:, g * 256:(g + 1) * 256], op0=ALU.mult, op1=ALU.mult),
                        reads=[B_yf[k_], Bs, B_cc], writes=[B_yb[k_]])
                    S.dma("sp", s_mix[ch * 128:(ch + 1) * 128, 2048 + g * 256:2048 + (g + 1) * 256], yb[k_][:], reads=[B_yb[k_]], writes=[B_mix])
            S.barrier()

        B_out = Buf(); B_hnT = Buf()
        if stage >= 4 and "D" not in SKIP:
          hdst = dbg["d_h"] if stage == 4 else out
          with ExitStack() as c:
            nf_bc = sb(c, "nf_bc", [128, D]); B_nf = Buf()
            S.dma("sp", nf_bc[:], norm_ffn.partition_broadcast(128), writes=[B_nf])
            mt = [sb(c, "mt%d" % i, [128, D], BF16) for i in range(2)]; B_mt = [Buf(), Buf()]
            mixT = sb(c, "mixT", [128, 32, 512], BF16); B_mixT = [Buf() for _ in range(4)]
            wb_ = [sb(c, "wbD%d" % i, [128, 32, 256], BF16) for i in range(2)]; B_wb = [Buf(), Buf()]
            ht = [sb(c, "ht%d" % i, [128, D]) for i in range(4)]; B_ht = [Buf() for _ in range(4)]
            xr = [sb(c, "xr%d" % i, [128, 256]) for i in range(3)]; B_xr = [Buf() for _ in range(3)]
            hb = sb(c, "hb", [128, D], BF16); B_hb = Buf()
            jkD = sb(c, "jkD", [128, D], BF16); B_jkD = Buf()
            hst = [sb(c, "hst%d" % i, [128, 8, 128], BF16) for i in range(2)]; B_hst = [Buf(), Buf()]
            sD = sb(c, "sD", [128, 8]); B_sD = Buf()
            tpD = [ps(c, "tpD%d" % i, [128, 8, 128], BF16) for i in range(2)]; B_tpD = [PB(), PB()]
            accD = [ps(c, "accD%d" % i, [128, 512]) for i in range(4)]; B_accD = [PB() for _ in range(4)]
            wo_v = w_out.rearrange("(k p) c -> p k c", p=128)
            hn_v = s_hnT.rearrange("(k p) t -> p k t", p=128)
            wi = 0; ai = 0; xi = 0; gi = 0
            for tb in range(2):
                for tt in range(4):
                    tile = tb * 4 + tt
                    mi = tile % 2
                    S.dma("sp", mt[mi][:], s_mix[tile * 128:(tile + 1) * 128, :], reads=[B_mix], writes=[B_mt[mi]])
                    for g in range(4):
                        ti = gi % 2; gi += 1
                        for j in range(8):
                            kk = g * 8 + j
                            S.op("pe", lambda E, ti=ti, j=j, kk=kk, mi=mi: E.transpose(out=tpD[ti][:, j, :], in_=mt[mi][:, kk * 128:(kk + 1) * 128],
                                                                                      identity=ident[:]), reads=[B_mt[mi], B_ident], writes=[B_tpD[ti]])
                        if g % 2 == 0:
                            S.op("act", lambda E, ti=ti, g=g, tt=tt: E.copy(out=mixT[:, g * 8:(g + 1) * 8, tt * 128:(tt + 1) * 128], in_=tpD[ti][:]),
                                 reads=[B_tpD[ti]], writes=[B_mixT[tt]])
                        else:
                            S.op("dve", lambda E, ti=ti, g=g, tt=tt: E.tensor_copy(out=mixT[:, g * 8:(g + 1) * 8, tt * 128:(tt + 1) * 128], in_=tpD[ti][:]),
                                 reads=[B_tpD[ti]], writes=[B_mixT[tt]])
                for cb in range(16):
                    w_ = wb_[wi % 2]; Bw = B_wb[wi % 2]; wi += 1
                    S.dma("pool", w_[:], wo_v[:, :, cb * 256:(cb + 1) * 256], writes=[Bw])
                    for tt in range(4):
                        tile = tb * 4 + tt
                        a = accD[ai % 4]; Ba = B_accD[ai % 4]; ai += 1
                        x_ = xr[xi % 3]; Bx = B_xr[xi % 3]; xi += 1
                        S.dma("sp", x_[:], x[tile * 128:(tile + 1) * 128, cb * 256:(cb + 1) * 256], writes=[Bx])
                        for kk in range(32):
                            S.op("pe", lambda E, a=a, w_=w_, kk=kk, tt=tt: E.matmul(a[:, 0:256], lhsT=mixT[:, kk, tt * 128:(tt + 1) * 128],
                                                                                   rhs=w_[:, kk, :], start=(kk == 0), stop=(kk == 31)),
                                 reads=[Bw, B_mixT[tt]], writes=[Ba])
                        S.op("dve", lambda E, a=a, x_=x_, tt=tt, cb=cb: E.tensor_tensor(out=ht[tt][:, cb * 256:(cb + 1) * 256], in0=a[:, 0:256],
                                                                                       in1=x_[:], op=ALU.add), reads=[Ba, Bx], writes=[B_ht[tt]])
                for tt in range(4):
                    tile = tb * 4 + tt
                    S.dma("sp", hdst[tile * 128:(tile + 1) * 128, :], ht[tt][:], reads=[B_ht[tt]], writes=[B_out])
                    S.op("act", lambda E, tt=tt: E.activation(out=jkD[:], in_=ht[tt][:], func=AF.Square, accum_out=sD[:, 0:1]),
                         reads=[B_ht[tt]], writes=[B_jkD, B_sD])
                    S.op("dve", lambda E: E.tensor_scalar(out=sD[:, 1:2], in0=sD[:, 0:1], scalar1=1.0 / D, scalar2=EPS, op0=ALU.mult, op1=ALU.add),
                         reads=[B_sD], writes=[B_sD])
                    S.op("act", lambda E: E.activation(out=sD[:, 2:3], in_=sD[:, 1:2], func=AF.Sqrt), reads=[B_sD], writes=[B_sD])
                    S.op("dve", lambda E: E.reciprocal(out=sD[:, 3:4], in_=sD[:, 2:3]), reads=[B_sD], writes=[B_sD])
                    S.op("dve", lambda E, tt=tt: E.scalar_tensor_tensor(out=hb[:], in0=ht[tt][:], scalar=sD[:, 3:4], in1=nf_bc[:],
                                                                        op0=ALU.mult, op1=ALU.mult), reads=[B_ht[tt], B_sD, B_nf], writes=[B_hb])
                    for g in range(4):
                        ti = gi % 2; gi += 1
                        for j in range(8):
                            kk = g * 8 + j
                            S.op("pe", lambda E, ti=ti, j=j, kk=kk: E.transpose(out=tpD[ti][:, j, :], in_=hb[:, kk * 128:(kk + 1) * 128],
                                                                               identity=ident[:]), reads=[B_hb, B_ident], writes=[B_tpD[ti]])
                        hs = hst[ti]; Bh = B_hst[ti]
                        if g % 2 == 0:
                            S.op("act", lambda E, ti=ti, hs=hs: E.copy(out=hs[:], in_=tpD[ti][:]), reads=[B_tpD[ti]], writes=[Bh])
                        else:
                            S.op("dve", lambda E, ti=ti, hs=hs: E.tensor_copy(out=hs[:], in_=tpD[ti][:]), reads=[B_tpD[ti]], writes=[Bh])
                        S.dma("sp", hn_v[:, g * 8:(g + 1) * 8, tile * 128:(tile + 1) * 128], hs[:], reads=[Bh], writes=[B_hnT])
            S.barrier()

        if stage >= 5:
          with ExitStack() as c:
            s2all = sb(c, "s2all", [128, 8, 8, 128]); A1all = sb(c, "A1all", [128, 8, 8, 128])
            wAll = sb(c, "wAll", [128, 8, 8])
            B_gin = Buf()
            hnT = sb(c, "hnT", [128, 32, TO], BF16); B_hn = Buf()
            hn_v = s_hnT.rearrange("(k p) t -> p k t", p=128)
            for g in range(4):
                S.dma("sp", hnT[:, g * 8:(g + 1) * 8, :], hn_v[:, g * 8:(g + 1) * 8, :], reads=[B_hnT], writes=[B_hn])
            identF2 = sb(c, "identF2", [128, 128]); B_idf = Buf()
            S.dma("sp", identF2[:], ident_in, writes=[B_idf])
            with ExitStack() as c2:
                qT = sb(c2, "qT", [128, 16, TO], BF16); B_qT = Buf()
                skT = sb(c2, "skT", [128, 16, 128], BF16); B_skT = Buf()
                S.dma("pool", skT[:], skT_in.rearrange("h d k -> d h k"), writes=[B_skT])
                wqb = [sb(c2, "wqb%d" % i, [128, 32, 128], BF16) for i in range(2)]; B_wqb = [Buf(), Buf()]
                pq_ = [ps(c2, "pqE%d" % i, [128, 512]) for i in range(4)]; B_pq_ = [PB() for _ in range(4)]
                pi_ = 0
                wq_v = w_query.rearrange("(k p) c -> p k c", p=128)
                for cb in range(16):
                    w_ = wqb[cb % 2]; Bw = B_wqb[cb % 2]
                    S.dma("pool", w_[:], wq_v[:, :, cb * 128:(cb + 1) * 128], writes=[Bw])
                    for hh in range(1):
                        hc = cb
                        for th in range(2):
                            p_ = pq_[pi_ % 4]; Bp = B_pq_[pi_ % 4]; pi_ += 1
                            for kk in range(32):
                                S.op("pe", lambda E, p_=p_, w_=w_, kk=kk, hh=hh, th=th: E.matmul(
                                    p_[:], lhsT=w_[:, kk, hh * 128:(hh + 1) * 128], rhs=hnT[:, kk, th * 512:(th + 1) * 512],
                                    start=(kk == 0), stop=(kk == 31)), reads=[Bw, B_hn], writes=[Bp])
                            if pi_ % 2 == 0:
                                S.op("act", lambda E, p_=p_, hc=hc, th=th: E.copy(out=qT[:, hc, th * 512:(th + 1) * 512], in_=p_[:]),
                                     reads=[Bp], writes=[B_qT])
                            else:
                                S.op("dve", lambda E, p_=p_, hc=hc, th=th: E.tensor_copy(out=qT[:, hc, th * 512:(th + 1) * 512], in_=p_[:]),
                                     reads=[Bp], writes=[B_qT])
                sc = sb(c2, "sc", [128, 16, 128]); B_sc = Buf()
                wk = sb(c2, "wk", [128, 256]); B_wk = Buf()
                v16 = sb(c2, "v16", [128, 16, 16]); B_v16 = Buf()
                cand = sb(c2, "cand", [128, 8, 256]); B_cand = Buf()
                t24 = sb(c2, "t24", [128, 8, 24]); B_t24 = Buf()
                sE = sb(c2, "sE", [128, 8, 8]); B_sE = Buf()
                jkE = sb(c2, "jkE", [128, 16]); B_jkE = Buf()
                for tt in range(8):
                    for q4 in range(4):
                        p_ = pq_[pi_ % 4]; Bp = B_pq_[pi_ % 4]; pi_ += 1
                        for u in range(4):
                            hc = q4 * 4 + u
                            S.op("pe", lambda E, p_=p_, u=u, hc=hc, tt=tt: E.matmul(
                                p_[:, u * 128:(u + 1) * 128], lhsT=qT[:, hc, tt * 128:(tt + 1) * 128], rhs=skT[:, hc, :],
                                start=True, stop=True), reads=[B_qT, B_skT], writes=[Bp])
                        S.op("act", lambda E, p_=p_, q4=q4: E.copy(out=sc[:, q4 * 4:(q4 + 1) * 4, :], in_=p_[:]), reads=[Bp], writes=[B_sc])
                    for hc in range(16):
                        S.op("dve", lambda E, hc=hc: E.max(out=v16[:, hc, 0:8], in_=sc[:, hc, :]), reads=[B_sc], writes=[B_v16])
                        S.op("dve", lambda E, hc=hc: E.match_replace(out=wk[:, 0:128], in_to_replace=v16[:, hc, 0:8], in_values=sc[:, hc, :],
                                                                     imm_value=-1e30), reads=[B_sc, B_v16], writes=[B_wk])
                        S.op("dve", lambda E, hc=hc: E.max(out=v16[:, hc, 8:16], in_=wk[:, 0:128]), reads=[B_wk], writes=[B_v16])
                    v4 = v16[:].rearrange("p (h c) k -> p h c k", c=2)
                    S.op("dve", lambda E, v4=v4: E.tensor_tensor(
                        out=cand[:].rearrange("p h (a b) -> p h a b", a=16),
                        in0=v4[:, :, 0, :].unsqueeze(3).broadcast_to([128, 8, 16, 16]),
                        in1=v4[:, :, 1, :].unsqueeze(2).broadcast_to([128, 8, 16, 16]), op=ALU.add), reads=[B_v16], writes=[B_cand])
                    for h in range(8):
                        S.op("dve", lambda E, h=h: E.max(out=t24[:, h, 0:8], in_=cand[:, h, :]), reads=[B_cand], writes=[B_t24])
                        S.op("dve", lambda E, h=h: E.match_replace(out=wk[:], in_to_replace=t24[:, h, 0:8], in_values=cand[:, h, :],
                                                                   imm_value=-1e30), reads=[B_cand, B_t24], writes=[B_wk])
                        S.op("dve", lambda E, h=h: E.max(out=t24[:, h, 8:16], in_=wk[:]), reads=[B_wk], writes=[B_t24])
                        S.op("dve", lambda E, h=h: E.match_replace(out=wk[:], in_to_replace=t24[:, h, 8:16], in_values=wk[:],
                                                                   imm_value=-1e30), reads=[B_t24], writes=[B_wk])
                        S.op("dve", lambda E, h=h: E.max(out=t24[:, h, 16:24], in_=wk[:]), reads=[B_wk], writes=[B_t24])
                    S.op("dve", lambda E: E.tensor_tensor(out=sE[:, :, 0], in0=t24[:, :, 15], in1=t24[:, :, 16], op=ALU.add), reads=[B_t24], writes=[B_sE])
                    S.op("dve", lambda E: E.tensor_scalar(out=sE[:, :, 0], in0=sE[:, :, 0], scalar1=0.5, scalar2=None, op0=ALU.mult), reads=[B_sE], writes=[B_sE])
                    S.op("dve", lambda E: E.tensor_scalar(out=sE[:, :, 6], in0=t24[:, :, 0], scalar1=-1.0, scalar2=None, op0=ALU.mult), reads=[B_t24], writes=[B_sE])
                    for h in range(8):
                        S.op("act", lambda E, h=h: E.activation(out=jkE[:], in_=t24[:, h, 0:16], func=AF.Exp, bias=sE[:, h, 6:7],
                                                                accum_out=sE[:, h, 2:3]), reads=[B_t24, B_sE], writes=[B_jkE, B_sE])
                    S.op("act", lambda E: E.activation(out=sE[:, :, 3], in_=sE[:, :, 2], func=AF.Ln), reads=[B_sE], writes=[B_sE])
                    S.op("dve", lambda E: E.tensor_tensor(out=sE[:, :, 4], in0=sE[:, :, 0], in1=sE[:, :, 6], op=ALU.add), reads=[B_sE], writes=[B_sE])
                    S.op("dve", lambda E: E.tensor_tensor(out=sE[:, :, 4], in0=sE[:, :, 4], in1=sE[:, :, 3], op=ALU.subtract), reads=[B_sE], writes=[B_sE])
                    S.op("act", lambda E: E.activation(out=sE[:, :, 5], in_=sE[:, :, 4], func=AF.Exp), reads=[B_sE], writes=[B_sE])
                    sc4 = sc[:].rearrange("p (h c) k -> p h c k", c=2)
                    S.op("dve", lambda E, sc4=sc4, tt=tt: E.tensor_tensor(out=A1all[:, tt, :, :], in0=sc4[:, :, 0, :],
                                                                          in1=sE[:, :, 0:1].broadcast_to([128, 8, 128]), op=ALU.subtract),
                         reads=[B_sc, B_sE], writes=[B_gin])
                    S.op("act", lambda E, sc4=sc4, tt=tt: E.copy(out=s2all[:, tt, :, :], in_=sc4[:, :, 1, :]), reads=[B_sc], writes=[B_gin])
                    S.op("dve", lambda E, tt=tt: E.tensor_copy(out=wAll[:, tt, :], in_=sE[:, :, 5]), reads=[B_sE], writes=[B_gin])
                S.barrier()
            with ExitStack() as c2:
                ut = [sb(c2, "ut%d" % i, [128, 32, 128], BF16) for i in range(3)]; B_ut = [Buf() for _ in range(3)]
                gact = [sb(c2, "gact%d" % i, [128, TO], BF16) for i in range(2)]; B_gact = [Buf(), Buf()]
                NYB = 4
                Yb = [sb(c2, "Yb%d" % i, [128, 8, 128]) for i in range(NYB)]; B_Yb = [Buf() for _ in range(NYB)]
                Eb = [sb(c2, "Eb%d" % i, [128, 8, 128], BF16) for i in range(NYB)]; B_Eb = [Buf() for _ in range(NYB)]
                Gb = [sb(c2, "Gb%d" % i, [128, 8, 128], BF16) for i in range(3)]; B_Gb = [Buf() for _ in range(3)]
                wt = [sb(c2, "wt%d" % i, [128, TO], BF16) for i in range(2)]; B_wt = [Buf(), Buf()]
                P_a = [ps(c2, "P_a%d" % i, [128, 512]) for i in range(4)]; B_Pa = [PB() for _ in range(4)]
                P_g = [ps(c2, "P_g%d" % i, [128, 512]) for i in range(4)]; B_Pg = [PB() for _ in range(4)]
                NCH = int(_os0.environ.get("KNCH", "128"))
                yi = 0; gi = 0
                POOL_SET = [int(v) for v in _os0.environ.get("KPOOLSET", "1,4,6").split(",") if v != ""]
                STT_SET = [int(v) for v in _os0.environ.get("KSTTSET", "0,1,2,3,4,5,6,7").split(",") if v != ""]
                Dg = sb(c2, "Dg", [128, 8, 8, 128], BF16)
                for tt in range(8):
                    for h in range(8):
                        S.op("dve", lambda E, tt=tt, h=h: E.tensor_scalar(out=Dg[:, tt, h, :], in0=identF2[:], scalar1=wAll[:, tt, h:h + 1], scalar2=None,
                                                                          op0=ALU.mult), reads=[B_idf, B_gin], writes=[B_gin])
                gbuf = {}

                def load_u(i):
                    S.dma("pool", ut[i % 3][:], UTt[i], writes=[B_ut[i % 3]])

                def stage1_mm(i, grp):
                    u_ = ut[i % 3]; Bu = B_ut[i % 3]
                    th = grp // 4
                    pa = P_a[(i % 2) * 2 + th]; Bpa = B_Pa[(i % 2) * 2 + th]
                    for kk in range((grp % 4) * 8, (grp % 4) * 8 + 8):
                        S.op("pe", lambda E, pa=pa, th=th, kk=kk, u_=u_: E.matmul(pa[:], lhsT=u_[:, kk, :], rhs=hnT[:, kk, th * 512:(th + 1) * 512],
                                                                                 start=(kk == 0), stop=(kk == 31)), reads=[Bu, B_hn], writes=[Bpa])

                def unit_pre(i, tt):
                    k = i * 8 + tt
                    y_ = Yb[k % NYB]; By = B_Yb[k % NYB]; e_ = Eb[k % NYB]; Be = B_Eb[k % NYB]
                    g_ = Gb[k % 3]; Bg = B_Gb[k % 3]
                    gbuf[(i, tt)] = (g_, Bg)
                    S.op("pool" if tt in POOL_SET else "dve", lambda E, y_=y_, tt=tt, i=i: E.tensor_tensor(
                        out=y_[:], in0=s2all[:, tt, :, :], in1=A1all[:, tt, :, i:i + 1].broadcast_to([128, 8, 128]), op=ALU.add),
                        reads=[B_gin], writes=[By])
                    if tt in STT_SET:
                        S.op("act", lambda E, y_=y_, e_=e_: E.activation(out=e_[:], in_=y_[:], func=AF.Exp), reads=[By], writes=[Be])
                        S.op("dve", lambda E, y_=y_, e_=e_, g_=g_: E.scalar_tensor_tensor(out=g_[:], in0=y_[:], scalar=0.0, in1=e_[:],
                                                                                          op0=ALU.is_ge, op1=ALU.mult), reads=[By, Be], writes=[Bg])
                    else:
                        S.op("act", lambda E, y_=y_: E.activation(out=y_[:], in_=y_[:], func=AF.Prelu, alpha=1e30), reads=[By], writes=[By])
                        S.op("act", lambda E, y_=y_, g_=g_: E.activation(out=g_[:], in_=y_[:], func=AF.Exp), reads=[By], writes=[Bg])

                def unit_mm(i, tt):
                    g_, Bg = gbuf.pop((i, tt))
                    pb = (i % 2) * 2
                    pg = P_g[pb + tt // 4]; Bpg = B_Pg[pb + tt // 4]
                    for h in range(8):
                        S.op("pe", lambda E, pg=pg, g_=g_, h=h, tt=tt: E.matmul(pg[:, (tt % 4) * 128:(tt % 4 + 1) * 128], lhsT=g_[:, h, :],
                                                                                rhs=Dg[:, tt, h, :], start=(h == 0), stop=(h == 7)),
                             reads=[Bg, B_gin], writes=[Bpg])

                def stage3(i):
                    pb = (i % 2) * 2
                    ga = gact[i % 2]; Bga = B_gact[i % 2]
                    w_ = wt[i % 2]; Bwt = B_wt[i % 2]
                    for th in range(2):
                        S.op("act", lambda E, pb=pb, th=th, ga=ga: E.activation(out=ga[:, th * 512:(th + 1) * 512], in_=P_a[pb + th][:],
                                                                                func=AF.Gelu_apprx_tanh), reads=[B_Pa[pb + th]], writes=[Bga])
                    for th in range(2):
                        S.op("dve", lambda E, w_=w_, th=th, pb=pb, ga=ga: E.tensor_tensor(out=w_[:, th * 512:(th + 1) * 512], in0=P_g[pb + th][:],
                                                                                         in1=ga[:, th * 512:(th + 1) * 512], op=ALU.mult),
                             reads=[B_Pg[pb + th], Bga], writes=[Bwt])
                    S.dma("sp", s_WT[i], w_[:], reads=[Bwt], writes=[B_swt])

                load_u(0)
                if NCH > 1:
                    load_u(1)
                for grp in range(8):
                    stage1_mm(0, grp)
                for i in range(NCH):
                    if i + 2 < NCH:
                        load_u(i + 2)
                    for tt in range(8):
                        unit_pre(i, tt)
                        if i + 1 < NCH:
                            stage1_mm(i + 1, tt)
                        if tt >= 1:
                            unit_mm(i, tt - 1)
                    unit_mm(i, 7)
                    stage3(i)
                S.barrier()
          with ExitStack() as c:
            WTr = sb(c, "WTr", [128, 128, 512], BF16); B_WTr = Buf()
            vt = [sb(c, "vt%d" % i, [128, 4, 1024], BF16) for i in range(3)]; B_vt = [Buf() for _ in range(3)]
            hres = [sb(c, "hres%d" % i, [128, 512]) for i in range(3)]; B_hres = [Buf() for _ in range(3)]
            P_o = [ps(c, "P_o%d" % i, [128, 512]) for i in range(8)]; B_Po = [PB() for _ in range(8)]
            V_v = Vexp.rearrange("(i j) d -> j i d", j=128)
            wt_v = s_WT.rearrange("i j t -> j i t")
            vi = 0; hi_ = 0
            for tb in range(0 if "3" in SKIP else 2):
                for g in range(8):
                    S.dma("sp", WTr[:, g * 16:(g + 1) * 16, :], wt_v[:, g * 16:(g + 1) * 16, tb * 512:(tb + 1) * 512],
                          reads=[B_swt], writes=[B_WTr])
                for dq in range(4):
                    for ig in range(NCH // 4):
                        v_ = vt[vi % 3]; Bv = B_vt[vi % 3]; vi += 1
                        S.dma("pool", v_[:], V_v[:, ig * 4:(ig + 1) * 4, dq * 1024:(dq + 1) * 1024], writes=[Bv])
                        for ii in range(4):
                            i = ig * 4 + ii
                            for tt in range(4):
                                for hf in range(2):
                                    S.op("pe", lambda E, tt=tt, hf=hf, i=i, ii=ii, v_=v_: E.matmul(
                                        P_o[tt * 2 + hf][:], lhsT=WTr[:, i, tt * 128:(tt + 1) * 128], rhs=v_[:, ii, hf * 512:(hf + 1) * 512],
                                        start=(i == 0), stop=(i == NCH - 1)), reads=[B_WTr, Bv], writes=[B_Po[tt * 2 + hf]])
                    for tt in range(4):
                        for hf in range(2):
                            t0 = tb * 512 + tt * 128
                            c0 = dq * 1024 + hf * 512
                            h_ = hres[hi_ % 3]; Bh = B_hres[hi_ % 3]; hi_ += 1
                            S.dma("sp", h_[:], out[t0:t0 + 128, c0:c0 + 512], reads=[B_out], writes=[Bh])
                            S.op("dve", lambda E, h_=h_, tt=tt, hf=hf: E.tensor_tensor(out=h_[:], in0=P_o[tt * 2 + hf][:], in1=h_[:], op=ALU.add),
                                 reads=[B_Po[tt * 2 + hf], Bh], writes=[Bh])
                            S.dma("sp", out[t0:t0 + 128, c0:c0 + 512], h_[:], reads=[Bh], writes=[B_fin])
            S.barrier()

        S.barrier()
        with nc.Block() as block:
            S.emit(block)
    return nc


def _prep_inputs(inputs):
    g = {k: np.asarray(v) for k, v in inputs.items()}
    w_in = g["w_in"][0]
    sp = np.cumsum([0, 1024, 512, 64, 2048, 4096, 32, 32])
    c_q, c_kv, k_rope, z, xbc, dtf, dtb = [w_in[:, sp[i]:sp[i + 1]] for i in range(7)]
    xs, bs, cs = xbc[:, :2048], xbc[:, 2048:3072], xbc[:, 3072:4096]
    w_perm = [np.ascontiguousarray(np.concatenate([c_kv, k_rope, a, b, xs, bs, cs, c_q, z], axis=1))
              for (a, b) in ((dtf, dtb), (dtb, dtf))]
    ident = np.eye(128, dtype=np.float32)
    invf = (10000.0 ** (-(np.arange(32, dtype=np.float32) / np.float32(32)))).astype(np.float32)
    cwm = g["conv_w"][0][:, 0, :]
    cw = [np.ascontiguousarray(cwm.T), np.ascontiguousarray(cwm[::-1].T)]
    kk, ll = np.meshgrid(np.arange(128), np.arange(128), indexing="ij")
    tri = np.stack([(kk <= ll), (kk >= ll), np.where(kk <= ll, 0.0, -30000.0), np.where(kk >= ll, 0.0, -30000.0)]).astype(np.float32)
    skT = np.ascontiguousarray(g["sub_keys"][0].reshape(16, 128, 128).transpose(0, 2, 1))
    U = g["expert_u"][0]
    UTt = np.ascontiguousarray(U.reshape(128, 128, 32, 128).transpose(0, 3, 2, 1))
    maps = []
    for c in range(8):
        b, half = c // 2, c % 2
        xl = g["x"][b]
        if half:
            xl = xl[::-1]
        pl = g["positions"][b].astype(np.int32)
        if half:
            pl = pl[::-1]
        m = {"x": np.ascontiguousarray(xl), "norm_mix": g["norm_mix"][0], "w_in": w_perm[half], "ident": ident,
             "pos": np.ascontiguousarray(pl), "invf": invf, "q_a_norm": g["q_a_norm"][0], "w_uq": g["w_uq"][0],
             "kv_a_norm": g["kv_a_norm"][0], "w_ukv": g["w_ukv"][0], "q_norm": g["q_norm"][0], "k_norm": g["k_norm"][0],
             "aon": np.ascontiguousarray(g["attn_out_norm"][0].reshape(-1)),
             "conv_w": cw[half], "conv_b": g["conv_b"][0],
             "a_log": np.concatenate([g["a_log_fwd"][0], g["a_log_bwd"][0]][::(-1 if half else 1)]),
             "dt_bias": np.concatenate([g["dt_bias_fwd"][0], g["dt_bias_bwd"][0]][::(-1 if half else 1)]),
             "d_skip": g["d_skip"][0], "son": g["ssm_out_norm"][0], "tri": tri,
             "w_out": g["w_out"][0], "norm_ffn": g["norm_ffn"][0], "w_query": g["w_query"][0],
             "skT": skT, "UTt": UTt, "Vexp": g["expert_v"][0]}
        maps.append(m)
    return maps


def kernel(**inputs):
    maps = _prep_inputs(inputs)
    nc = build()
    res = run_bass_kernel_spmd(nc, maps, core_ids=list(range(8)))
    outp = np.zeros((4, 2048, D), np.float32)
    for c in range(8):
        b, half = c // 2, c % 2
        o = res.results[c]["out"]
        if half:
            outp[b, 1024:] = o[::-1]
        else:
            outp[b, :1024] = o
    return outp
```

```python
from contextlib import ExitStack
import numpy as np
import concourse.bass as bass
import concourse.mybir as mybir
from concourse.bass_utils import run_bass_kernel_spmd

F32 = mybir.dt.float32
BF16 = mybir.dt.bfloat16
I32 = mybir.dt.int32
AF = mybir.ActivationFunctionType
ALU = mybir.AluOpType
AX = mybir.AxisListType

EPS = 1e-6
D = 4096
T = 2048
TO = 1024
NH = 16
O_KV, O_ROPE, O_DTF, O_DTB, O_X, O_B, O_C, O_Q, O_Z, O_END = 0, 512, 576, 608, 640, 2688, 3712, 4736, 5760, 7808


import types


def _snap(fn):
    if fn.__closure__ is None:
        return fn
    cells = []
    for cl in fn.__closure__:
        try:
            cells.append(types.CellType(cl.cell_contents))
        except ValueError:
            cells.append(cl)
    g = types.FunctionType(fn.__code__, fn.__globals__, fn.__name__, fn.__defaults__, tuple(cells))
    g.__kwdefaults__ = fn.__kwdefaults__
    return g


class Buf:
    __slots__ = ("name", "w", "r", "excl")

    def __init__(self, name="", excl=False):
        self.name = name
        self.w = None
        self.r = {}
        self.excl = excl


def PB():
    return Buf(excl=True)


class Sched:
    NDMA = 24

    def __init__(self, nc, ctx):
        self.nc = nc
        self.engs = ["pe", "dve", "act", "pool", "sp"]
        self.sem = {}
        self.cnt = {}
        for e in self.engs:
            self.sem[e] = ctx.enter_context(nc.semaphore("s_" + e))
            self.cnt[e] = 0
        self.dsem = [ctx.enter_context(nc.semaphore("d%d" % i)) for i in range(self.NDMA)]
        self.dcnt = [0] * self.NDMA
        self.dnext = {"sp": 0, "pool": 0, "act": 0}
        self.drange = {"sp": (0, 14), "pool": (14, 22), "act": (22, 24)}
        self.seen = {e: {} for e in self.engs}
        self.prog = {e: [] for e in self.engs}

    def _semobj(self, key):
        return self.sem[key] if isinstance(key, str) else self.dsem[key]

    def _wait(self, e, key, val):
        if key == "pe" and e == "pe":
            return
        if self.seen[e].get(key, 0) >= val:
            return
        so = self._semobj(key)
        self.prog[e].append(lambda E, so=so, val=val: E.wait_ge(so, val))
        self.seen[e][key] = val

    def _deps(self, e, reads, writes):
        for b in reads:
            if b.w is not None:
                self._wait(e, *b.w)
        for b in writes:
            if b.w is not None:
                self._wait(e, *b.w)
            for k, v in b.r.items():
                self._wait(e, k, v)

    def _commit(self, ev, reads, writes):
        for b in reads:
            if b.r.get(ev[0], 0) < ev[1]:
                b.r[ev[0]] = ev[1]
        for b in writes:
            b.w = ev
            b.r = {}

    def op(self, e, fn, reads=(), writes=()):
        fn = _snap(fn)
        if any(b.excl for b in reads):
            writes = list(writes) + [b for b in reads if b.excl]
            reads = [b for b in reads if not b.excl]
        self._deps(e, reads, writes)
        self.cnt[e] += 1
        so = self.sem[e]
        self.prog[e].append(lambda E, fn=fn, so=so: fn(E).then_inc(so, 1))
        ev = (e, self.cnt[e])
        self._commit(ev, reads, writes)
        return ev

    def dma(self, q, out, in_, reads=(), writes=(), **kw):
        lo, hi = self.drange[q]
        k = lo + self.dnext[q]
        self.dnext[q] = (self.dnext[q] + 1) % (hi - lo)
        if self.dcnt[k] > 0:
            self._wait(q, k, self.dcnt[k])
        self._deps(q, reads, writes)
        self.dcnt[k] += 16
        so = self.dsem[k]
        self.prog[q].append(
            lambda E, out=out, in_=in_, kw=kw, so=so: E.dma_start(out=out, in_=in_, **kw).then_inc(so, 16))
        ev = (k, self.dcnt[k])
        self._commit(ev, reads, writes)
        return ev

    def barrier(self):
        for e in self.engs:
            for o in self.engs:
                if o != e and self.cnt[o] > 0:
                    self._wait(e, o, self.cnt[o])
            for k in range(self.NDMA):
                if self.dcnt[k] > 0:
                    self._wait(e, k, self.dcnt[k])

    def emit(self, block):
        reg = {"pe": block.tensor, "dve": block.vector, "act": block.scalar, "pool": block.gpsimd, "sp": block.sync}
        for e in self.engs:
            lst = self.prog[e]
            if not lst:
                continue

            def body(E, lst=lst):
                for f in lst:
                    f(E)
            reg[e](body)


def bcast_rows(ap_1d, nparts):
    return ap_1d.partition_broadcast(nparts)


_rope_id = [0]


def rope_ops(S, src, B_src, dst, B_dst, cs, B_cs, tile, sb, c, tag, nheads):
    _rope_id[0] += 1
    nm = "rp%d_" % _rope_id[0]
    if nheads == 1:
        shp = [128, 32]
        t1, t2 = src[:, 0:32], src[:, 32:64]
        d1, d2 = dst[:, 0:32], dst[:, 32:64]
        co, si = cs[:, tile, 0:32], cs[:, tile, 32:64]
    else:
        shp = [128, nheads, 32]
        t1, t2 = src[:, :, 0:32], src[:, :, 32:64]
        d1, d2 = dst[:, :, 0:32], dst[:, :, 32:64]
        co = cs[:, tile, 0:32].unsqueeze(1).broadcast_to(shp)
        si = cs[:, tile, 32:64].unsqueeze(1).broadcast_to(shp)
    a = sb(c, nm + "a", shp)
    b = sb(c, nm + "b", shp)
    Ba, Bb = Buf(), Buf()
    S.op("dve", lambda E: E.tensor_tensor(out=a[:], in0=t1, in1=co, op=ALU.mult), reads=[B_src, B_cs], writes=[Ba])
    S.op("dve", lambda E: E.tensor_tensor(out=b[:], in0=t2, in1=si, op=ALU.mult), reads=[B_src, B_cs], writes=[Bb])
    S.op("dve", lambda E: E.tensor_tensor(out=d1, in0=a[:], in1=b[:], op=ALU.subtract), reads=[Ba, Bb], writes=[B_dst])
    S.op("dve", lambda E: E.tensor_tensor(out=a[:], in0=t1, in1=si, op=ALU.mult), reads=[B_src, B_cs, B_dst], writes=[Ba])
    S.op("dve", lambda E: E.tensor_tensor(out=b[:], in0=t2, in1=co, op=ALU.mult), reads=[B_src, B_cs, B_dst], writes=[Bb])
    S.op("dve", lambda E: E.tensor_tensor(out=d2, in0=a[:], in1=b[:], op=ALU.add), reads=[Ba, Bb], writes=[B_dst])

class K:
    pass


def build(stage=99):
    nc = bass.Bass("TRN2", target_bir_lowering=False)
    k = K()
    k.nc = nc

    def din(name, shape, dt=F32):
        return nc.dram_tensor(name, list(shape), dt, kind="ExternalInput").ap()

    def dscr(name, shape, dt=F32):
        return nc.dram_tensor(name, list(shape), dt, kind="Internal").ap()

    x = din("x", [T, D])
    norm_mix = din("norm_mix", [D])
    w_in = din("w_in", [D, O_END])
    ident_in = din("ident", [128, 128])
    pos_in = din("pos", [T], I32)
    invf_in = din("invf", [32])
    q_a_norm = din("q_a_norm", [1024])
    w_uq = din("w_uq", [1024, 3072])
    kv_a_norm = din("kv_a_norm", [512])
    w_ukv = din("w_ukv", [512, 4096])
    q_norm = din("q_norm", [192])
    k_norm = din("k_norm", [192])
    aon = din("aon", [2048])
    s_mix = dscr("s_mix", [TO, D], BF16)
    conv_w = din("conv_w", [4096, 5])
    conv_b = din("conv_b", [4096])
    a_log = din("a_log", [64])
    dt_bias = din("dt_bias", [64])
    d_skip = din("d_skip", [32])
    son = din("son", [2048])
    tri_in = din("tri", [4, 128, 128])
    w_out = din("w_out", [D, D])
    norm_ffn = din("norm_ffn", [D])
    w_query = din("w_query", [D, 2048])
    skT_in = din("skT", [16, 128, 128])
    UTt = din("UTt", [128, 128, 32, 128])
    Vexp = din("Vexp", [16384, D])
    s_hnT = dscr("s_hnT", [D, TO], BF16)
    s_WT = dscr("s_WT", [128, 128, TO], BF16)
    out = nc.dram_tensor("out", [TO, D], F32, kind="ExternalOutput").ap()

    s_kv = dscr("s_kv", [T, 640])
    s_xT = dscr("s_xT", [4096, T], BF16)
    s_q = dscr("s_q", [TO, 1024])
    s_z = dscr("s_z", [TO, 2048])

    dbg = {}
    if stage == 1:
        dbg["d_kv"] = nc.dram_tensor("d_kv", [T, 640], F32, kind="ExternalOutput").ap()
        dbg["d_xT"] = nc.dram_tensor("d_xT", [4096, T], BF16, kind="ExternalOutput").ap()
        dbg["d_q"] = nc.dram_tensor("d_q", [TO, 1024], F32, kind="ExternalOutput").ap()
        dbg["d_z"] = nc.dram_tensor("d_z", [TO, 2048], F32, kind="ExternalOutput").ap()
        s_kv, s_xT, s_q, s_z = dbg["d_kv"], dbg["d_xT"], dbg["d_q"], dbg["d_z"]

    if stage == 4:
        dbg["d_h"] = nc.dram_tensor("d_h", [TO, D], F32, kind="ExternalOutput").ap()
    if stage in (2, 3):
        dbg["d_mix"] = nc.dram_tensor("d_mix", [TO, D], BF16, kind="ExternalOutput").ap()
        s_mix = dbg["d_mix"]

    with ExitStack() as ctx:
        S = Sched(nc, ctx)
        uid = [0]

        def sb(c, name, shape, dt=F32):
            uid[0] += 1
            return c.enter_context(nc.sbuf_tensor("%s_%d" % (name, uid[0]), list(shape), dt))

        def ps(c, name, shape, dt=F32):
            uid[0] += 1
            return c.enter_context(nc.psum_tensor("%s_%d" % (name, uid[0]), list(shape), dt))

        ident = sb(ctx, "ident_sb", [128, 128], BF16)
        B_ident = Buf("ident")
        S.dma("pool", ident[:], ident_in, writes=[B_ident])
        outbufs = []
        B_mix = Buf()
        B_swt = Buf()
        B_fin = Buf()

        B_scr = Buf()
        import os as _os0
        SKIP = _os0.environ.get("KSKIP", "")
        with ExitStack() as c:
          if "A" not in SKIP:
              gain = sb(c, "gain", [128, D])
              B_gain = Buf()
              S.dma("sp", gain[:], bcast_rows(norm_mix, 128), writes=[B_gain])
              xt = [sb(c, "xt%d" % i, [128, D]) for i in range(2)]
              B_xt = [Buf() for _ in range(2)]
              junk = sb(c, "junk", [128, D], BF16)
              B_junk = Buf()
              xb = sb(c, "xb", [128, D], BF16)
              B_xb = Buf()
              st = sb(c, "stat", [128, 8])
              B_st = Buf()
              xnT = sb(c, "xnT", [128, 32, 512], BF16)
              B_xnT = [Buf() for _ in range(4)]
              wblk = [sb(c, "wblk%d" % i, [128, 32, 512], BF16) for i in range(2)]
              B_w = [Buf() for _ in range(2)]
              ost = [sb(c, "ost%d" % i, [128, 512]) for i in range(3)]
              B_ost = [Buf() for _ in range(3)]
              ostb = [sb(c, "ostb%d" % i, [128, 512], BF16) for i in range(3)]
              B_ostb = [Buf() for _ in range(3)]
              tp = [ps(c, "tp%d" % i, [128, 8, 128], BF16) for i in range(2)]
              B_tp = [PB() for _ in range(2)]
              acc = [ps(c, "acc%d" % i, [128, 512]) for i in range(4)]
              B_acc = [PB() for _ in range(4)]
              w_v = w_in.rearrange("(k p) c -> p k c", p=128)
              x_v = x.rearrange("(n p) d -> n p d", p=128)

              blocks = [(O_KV, 512, "kv"), (O_ROPE, 128, "kv")]
              blocks += [(O_X + 512 * i, 512, "feat") for i in range(8)]
              nb_other = len(blocks)
              blocks += [(O_Q + 512 * i, 512, "q") for i in range(2)]
              blocks += [(O_Z + 512 * i, 512, "z") for i in range(4)]
              wi = 0
              ai = 0
              oi = 0
              for tb in range(4):
                  for tt in range(4):
                      tile = tb * 4 + tt
                      xi = tile % 2
                      S.dma("sp", xt[xi][:], x_v[tile], writes=[B_xt[xi]])
                      S.op("act", lambda E, xi=xi: E.activation(out=junk[:], in_=xt[xi][:], func=AF.Square,
                                                                accum_out=st[:, 0:1]),
                           reads=[B_xt[xi]], writes=[B_junk, B_st])
                      S.op("dve", lambda E: E.tensor_scalar(out=st[:, 1:2], in0=st[:, 0:1], scalar1=1.0 / D,
                                                            scalar2=EPS, op0=ALU.mult, op1=ALU.add),
                           reads=[B_st], writes=[B_st])
                      S.op("act", lambda E: E.activation(out=st[:, 2:3], in_=st[:, 1:2], func=AF.Sqrt),
                           reads=[B_st], writes=[B_st])
                      S.op("dve", lambda E: E.reciprocal(out=st[:, 3:4], in_=st[:, 2:3]), reads=[B_st], writes=[B_st])
                      S.op("dve", lambda E, xi=xi: E.scalar_tensor_tensor(out=xb[:], in0=xt[xi][:], scalar=st[:, 3:4],
                                                                          in1=gain[:], op0=ALU.mult, op1=ALU.mult),
                           reads=[B_xt[xi], B_st, B_gain], writes=[B_xb])
                      for g in range(4):
                          ti = g % 2
                          for j in range(8):
                              kk = g * 8 + j
                              S.op("pe", lambda E, ti=ti, j=j, kk=kk: E.transpose(out=tp[ti][:, j, :],
                                                                                 in_=xb[:, kk * 128:(kk + 1) * 128],
                                                                                 identity=ident[:]),
                                   reads=[B_xb, B_ident], writes=[B_tp[ti]])
                          eng = "act" if g % 2 == 0 else "dve"
                          if eng == "act":
                              S.op("act", lambda E, ti=ti, g=g, tt=tt: E.copy(
                                  out=xnT[:, g * 8:(g + 1) * 8, tt * 128:(tt + 1) * 128], in_=tp[ti][:]),
                                  reads=[B_tp[ti]], writes=[B_xnT[tt]])
                          else:
                              S.op("dve", lambda E, ti=ti, g=g, tt=tt: E.tensor_copy(
                                  out=xnT[:, g * 8:(g + 1) * 8, tt * 128:(tt + 1) * 128], in_=tp[ti][:]),
                                  reads=[B_tp[ti]], writes=[B_xnT[tt]])
                  blks = blocks if tb < 2 else blocks[:nb_other]
                  for (c0, cw, kind) in blks:
                      wb = wblk[wi % 2]
                      Bw = B_w[wi % 2]
                      wi += 1
                      S.dma("pool", wb[:, :, 0:cw], w_v[:, :, c0:c0 + cw], writes=[Bw])
                      if kind == "feat":
                          for cc in range(cw // 128):
                              a = acc[ai % 4]
                              Ba = B_acc[ai % 4]
                              ai += 1
                              for kk in range(32):
                                  S.op("pe", lambda E, a=a, wb=wb, kk=kk, cc=cc: E.matmul(
                                      a[:], lhsT=wb[:, kk, cc * 128:(cc + 1) * 128], rhs=xnT[:, kk, :],
                                      start=(kk == 0), stop=(kk == 31)),
                                      reads=[Bw] + B_xnT, writes=[Ba])
                              o = ostb[oi % 3]
                              Bo = B_ostb[oi % 3]
                              if oi % 2 == 0:
                                  S.op("act", lambda E, o=o, a=a: E.copy(out=o[:], in_=a[:]), reads=[Ba], writes=[Bo])
                              else:
                                  S.op("dve", lambda E, o=o, a=a: E.tensor_copy(out=o[:], in_=a[:]), reads=[Ba], writes=[Bo])
                              oi += 1
                              r0 = c0 - O_X + cc * 128
                              S.dma("sp", s_xT[r0:r0 + 128, tb * 512:(tb + 1) * 512], o[:], reads=[Bo], writes=[B_scr])
                      else:
                          for tt in range(4):
                              a = acc[ai % 4]
                              Ba = B_acc[ai % 4]
                              ai += 1
                              for kk in range(32):
                                  S.op("pe", lambda E, a=a, wb=wb, kk=kk, tt=tt, cw=cw: E.matmul(
                                      a[:, 0:cw], lhsT=xnT[:, kk, tt * 128:(tt + 1) * 128], rhs=wb[:, kk, 0:cw],
                                      start=(kk == 0), stop=(kk == 31)),
                                      reads=[Bw, B_xnT[tt]], writes=[Ba])
                              o = ost[oi % 3]
                              Bo = B_ost[oi % 3]
                              if oi % 2 == 0:
                                  S.op("act", lambda E, o=o, a=a, cw=cw: E.copy(out=o[:, 0:cw], in_=a[:, 0:cw]),
                                       reads=[Ba], writes=[Bo])
                              else:
                                  S.op("dve", lambda E, o=o, a=a, cw=cw: E.tensor_copy(out=o[:, 0:cw], in_=a[:, 0:cw]),
                                       reads=[Ba], writes=[Bo])
                              oi += 1
                              t0 = tb * 512 + tt * 128
                              if kind == "kv":
                                  dst = s_kv[t0:t0 + 128, c0:c0 + cw]
                              elif kind == "q":
                                  dst = s_q[t0:t0 + 128, c0 - O_Q:c0 - O_Q + cw]
                              else:
                                  dst = s_z[t0:t0 + 128, c0 - O_Z:c0 - O_Z + cw]
                              S.dma("sp", dst, o[:, 0:cw], reads=[Bo], writes=[B_scr])
          outbufs.append(B_scr)
          S.barrier()

        if stage >= 2 and "B" not in SKIP:
          with ExitStack() as c:
            TWO_PI = 2.0 * np.pi
            C1 = 6.28125
            C2 = float(np.float32(TWO_PI - C1).view(np.uint32) & np.uint32(0xFFFFF000)) if False else 0.0019350051879882812
            C3 = float(TWO_PI - C1 - C2)
            MAGIC = 12582912.0
            SCALE = 192.0 ** -0.5
            qan = sb(c, "qan", [128, 1024]); kvan = sb(c, "kvan", [128, 512])
            qn_bc = sb(c, "qn_bc", [128, 192]); kn_bc = sb(c, "kn_bc", [128, 192])
            aon_bc = sb(c, "aon_bc", [128, 2048]); invf = sb(c, "invf_sb", [128, 32])
            B_const = Buf()
            for dst, src in ((qan, q_a_norm), (kvan, kv_a_norm), (qn_bc, q_norm), (kn_bc, k_norm), (aon_bc, aon), (invf, invf_in)):
                S.dma("sp", dst[:], src.partition_broadcast(128), writes=[B_const])
            posi = sb(c, "posi", [128, 16], I32)
            S.dma("sp", posi[:], pos_in.rearrange("(n p) -> p n", p=128), writes=[B_const], allow_slow_non_contiguous=True) if False else None
            posf = sb(c, "posf", [128, 16])
            ang = sb(c, "ang", [128, 16, 32]); kk_t = sb(c, "kk_t", [128, 16, 32]); rr = sb(c, "rr", [128, 16, 32])
            cs = sb(c, "cs", [128, 16, 64])
            B_cs = Buf()
            pos_t = sb(c, "pos_t", [16, 128], I32)
            for n in range(16):
                S.dma("sp", posi[:, n:n + 1], pos_in[n * 128:(n + 1) * 128].rearrange("(p o) -> p o", o=1), writes=[B_const])
            S.op("dve", lambda E: E.tensor_copy(out=posf[:], in_=posi[:]), reads=[B_const], writes=[B_cs])
            S.op("dve", lambda E: E.tensor_tensor(out=ang[:], in0=posf[:].unsqueeze(2).broadcast_to([128, 16, 32]),
                                                  in1=invf[:].unsqueeze(1).broadcast_to([128, 16, 32]), op=ALU.mult),
                 reads=[B_const, B_cs], writes=[B_cs])
            for which, shift in ((1, 0.0), (0, 0.25)):
                S.op("dve", lambda E, shift=shift: E.tensor_scalar(out=kk_t[:], in0=ang[:], scalar1=1.0 / TWO_PI, scalar2=shift,
                                                                   op0=ALU.mult, op1=ALU.add), reads=[B_cs], writes=[B_cs])
                S.op("dve", lambda E: E.tensor_scalar(out=kk_t[:], in0=kk_t[:], scalar1=MAGIC, scalar2=None, op0=ALU.add),
                     reads=[B_cs], writes=[B_cs])
                S.op("dve", lambda E: E.tensor_scalar(out=kk_t[:], in0=kk_t[:], scalar1=-MAGIC, scalar2=None, op0=ALU.add),
                     reads=[B_cs], writes=[B_cs])
                S.op("dve", lambda E: E.scalar_tensor_tensor(out=rr[:], in0=kk_t[:], scalar=-C1, in1=ang[:], op0=ALU.mult, op1=ALU.add),
                     reads=[B_cs], writes=[B_cs])
                S.op("dve", lambda E: E.scalar_tensor_tensor(out=rr[:], in0=kk_t[:], scalar=-C2, in1=rr[:], op0=ALU.mult, op1=ALU.add),
                     reads=[B_cs], writes=[B_cs])
                S.op("dve", lambda E: E.scalar_tensor_tensor(out=rr[:], in0=kk_t[:], scalar=-C3, in1=rr[:], op0=ALU.mult, op1=ALU.add),
                     reads=[B_cs], writes=[B_cs])
                if shift != 0.0:
                    S.op("dve", lambda E: E.tensor_scalar(out=rr[:], in0=rr[:], scalar1=float(np.pi / 2), scalar2=None, op0=ALU.add),
                         reads=[B_cs], writes=[B_cs])
                S.op("dve", lambda E: E.tensor_scalar(out=rr[:], in0=rr[:], scalar1=3.1415925, scalar2=-3.1415925,
                                                      op0=ALU.min, op1=ALU.max), reads=[B_cs], writes=[B_cs])
                S.op("act", lambda E, which=which: E.activation(out=cs[:, :, which * 32:(which + 1) * 32], in_=rr[:], func=AF.Sin),
                     reads=[B_cs], writes=[B_cs])

            ckvT = sb(c, "ckvT", [128, 4, T], BF16); B_ckvT = Buf()
            cqT = sb(c, "cqT", [128, 8, TO], BF16); B_cqT = Buf()
            krr = sb(c, "krr", [128, 16, 64]); B_krr = Buf()
            ssr = sb(c, "ssr", [128, 16]); B_ssr = Buf()
            stB = sb(c, "stB", [128, 16]); B_stB = Buf()
            with ExitStack() as c2:
                lt = [sb(c2, "lt%d" % i, [128, 1024]) for i in range(2)]; B_lt = [Buf(), Buf()]
                jk = sb(c2, "jkB", [128, 1024], BF16); B_jk = Buf()
                nb = sb(c2, "nbB", [128, 1024], BF16); B_nb = Buf()
                tq = sb(c2, "tqB", [128, 64]); B_tq = Buf()
                tpB = [ps(c2, "tpB%d" % i, [128, 8, 128], BF16) for i in range(2)]; B_tpB = [PB(), PB()]
                for tile in range(16):
                    li = tile % 2
                    S.dma("sp", lt[li][:, 0:576], s_kv[tile * 128:(tile + 1) * 128, 0:576], reads=[B_scr], writes=[B_lt[li]])
                    S.op("act", lambda E, li=li: E.activation(out=jk[:, 0:512], in_=lt[li][:, 0:512], func=AF.Square,
                                                              accum_out=stB[:, 0:1]), reads=[B_lt[li]], writes=[B_jk, B_stB])
                    S.op("act", lambda E, li=li, tile=tile: E.activation(out=jk[:, 512:576], in_=lt[li][:, 512:576], func=AF.Square,
                                                                         accum_out=ssr[:, tile:tile + 1]),
                         reads=[B_lt[li]], writes=[B_jk, B_ssr])
                    S.op("dve", lambda E: E.tensor_scalar(out=stB[:, 1:2], in0=stB[:, 0:1], scalar1=1.0 / 512, scalar2=EPS,
                                                          op0=ALU.mult, op1=ALU.add), reads=[B_stB], writes=[B_stB])
                    S.op("act", lambda E: E.activation(out=stB[:, 2:3], in_=stB[:, 1:2], func=AF.Sqrt), reads=[B_stB], writes=[B_stB])
                    S.op("dve", lambda E: E.reciprocal(out=stB[:, 3:4], in_=stB[:, 2:3]), reads=[B_stB], writes=[B_stB])
                    S.op("dve", lambda E, li=li: E.scalar_tensor_tensor(out=nb[:, 0:512], in0=lt[li][:, 0:512], scalar=stB[:, 3:4],
                                                                        in1=kvan[:], op0=ALU.mult, op1=ALU.mult),
                         reads=[B_lt[li], B_stB, B_const], writes=[B_nb])
                    ti = tile % 2
                    for j in range(4):
                        S.op("pe", lambda E, ti=ti, j=j: E.transpose(out=tpB[ti][:, j, :], in_=nb[:, j * 128:(j + 1) * 128],
                                                                     identity=ident[:]), reads=[B_nb, B_ident], writes=[B_tpB[ti]])
                    S.op("act", lambda E, ti=ti, tile=tile: E.copy(out=ckvT[:, :, tile * 128:(tile + 1) * 128], in_=tpB[ti][:, 0:4, :]),
                         reads=[B_tpB[ti]], writes=[B_ckvT])
                    S.op("dve", lambda E, li=li: E.tensor_tensor(out=tq[:], in0=lt[li][:, 512:576], in1=kn_bc[:, 128:192], op=ALU.mult),
                         reads=[B_lt[li], B_const], writes=[B_tq])
                    rope_ops(S, tq, B_tq, krr[:, tile, :], B_krr, cs, B_cs, tile, sb, c2, "k%d" % tile, nheads=1)
                for tile in range(8):
                    li = tile % 2
                    S.dma("sp", lt[li][:], s_q[tile * 128:(tile + 1) * 128, :], reads=[B_scr], writes=[B_lt[li]])
                    S.op("act", lambda E, li=li: E.activation(out=jk[:], in_=lt[li][:], func=AF.Square, accum_out=stB[:, 0:1]),
                         reads=[B_lt[li]], writes=[B_jk, B_stB])
                    S.op("dve", lambda E: E.tensor_scalar(out=stB[:, 1:2], in0=stB[:, 0:1], scalar1=1.0 / 1024, scalar2=EPS,
                                                          op0=ALU.mult, op1=ALU.add), reads=[B_stB], writes=[B_stB])
                    S.op("act", lambda E: E.activation(out=stB[:, 2:3], in_=stB[:, 1:2], func=AF.Sqrt), reads=[B_stB], writes=[B_stB])
                    S.op("dve", lambda E: E.reciprocal(out=stB[:, 3:4], in_=stB[:, 2:3]), reads=[B_stB], writes=[B_stB])
                    S.op("dve", lambda E, li=li: E.scalar_tensor_tensor(out=nb[:], in0=lt[li][:], scalar=stB[:, 3:4], in1=qan[:],
                                                                        op0=ALU.mult, op1=ALU.mult),
                         reads=[B_lt[li], B_stB, B_const], writes=[B_nb])
                    ti = tile % 2
                    for j in range(8):
                        S.op("pe", lambda E, ti=ti, j=j: E.transpose(out=tpB[ti][:, j, :], in_=nb[:, j * 128:(j + 1) * 128],
                                                                     identity=ident[:]), reads=[B_nb, B_ident], writes=[B_tpB[ti]])
                    S.op("act", lambda E, ti=ti, tile=tile: E.copy(out=cqT[:, :, tile * 128:(tile + 1) * 128], in_=tpB[ti][:]),
                         reads=[B_tpB[ti]], writes=[B_cqT])
                S.barrier()

            HG = 4
            KT = sb(c, "KT", [128, HG, T], BF16); B_KT = Buf()
            KTr = sb(c, "KTr", [64, HG, T], BF16); B_KTr = Buf()
            vext = sb(c, "vext", [128, 16, HG, 130], BF16); B_vext = Buf()
            QT = sb(c, "QT", [128, HG, TO], BF16); B_QT = Buf()
            QTr = sb(c, "QTr", [64, HG, TO], BF16); B_QTr = Buf()
            wkv = sb(c, "wkv", [128, 4, HG * 256], BF16); B_wkv = Buf()
            wq = sb(c, "wq", [128, 8, HG * 192], BF16); B_wq = Buf()
            S.op("pool", lambda E: E.memset(vext[:], 1.0), writes=[B_vext])
            for hg in range(NH // HG):
                S.dma("pool", wkv[:], w_ukv.rearrange("(k p) c -> p k c", p=128)[:, :, hg * HG * 256:(hg + 1) * HG * 256],
                      writes=[B_wkv])
                S.dma("pool", wq[:], w_uq.rearrange("(k p) c -> p k c", p=128)[:, :, hg * HG * 192:(hg + 1) * HG * 192],
                      writes=[B_wq])
                with ExitStack() as c2:
                    pk = [ps(c2, "pk%d" % i, [128, 512]) for i in range(4)]; B_pk = [PB() for _ in range(4)]
                    tpk = [ps(c2, "tpk%d" % i, [128, 8, 128], BF16) for i in range(2)]; B_tpk = [PB(), PB()]
                    tpr = [ps(c2, "tpr%d" % i, [128, 8, 128], BF16) for i in range(2)]; B_tpr = [PB(), PB()]
                    jk = sb(c2, "jkK", [128, 192], BF16); B_jk = Buf()
                    sk = [sb(c2, "sk%d" % i, [128, 16]) for i in range(2)]; B_sk = [Buf(), Buf()]
                    kn = [sb(c2, "kn%d" % i, [128, HG, 192], BF16) for i in range(2)]; B_kn = [Buf(), Buf()]
                    def kprep_tile(tile):
                        pi = tile % 2
                        for b in range(2):
                            for kc in range(4):
                                S.op("pe", lambda E, pi=pi, b=b, kc=kc, tile=tile: E.matmul(
                                    pk[pi * 2 + b][:], lhsT=ckvT[:, kc, tile * 128:(tile + 1) * 128],
                                    rhs=wkv[:, kc, b * 512:(b + 1) * 512], start=(kc == 0), stop=(kc == 3)),
                                    reads=[B_ckvT, B_wkv], writes=[B_pk[pi * 2 + b]])
                                yield
                        st_ = sk[pi]; Bs = B_sk[pi]
                        for hl in range(HG):
                            p_ = pk[pi * 2 + hl // 2]; Bp = B_pk[pi * 2 + hl // 2]; off = (hl % 2) * 256
                            S.op("act", lambda E, p_=p_, off=off, st_=st_, hl=hl: E.activation(
                                out=jk[:, 0:128], in_=p_[:, off:off + 128], func=AF.Square, accum_out=st_[:, hl:hl + 1]),
                                reads=[Bp], writes=[B_jk, Bs])
                            yield
                        S.op("dve", lambda E, st_=st_, tile=tile: E.tensor_scalar(
                            out=st_[:, 4:8], in0=st_[:, 0:4], scalar1=ssr[:, tile:tile + 1], scalar2=1.0 / 192,
                            op0=ALU.add, op1=ALU.mult), reads=[Bs, B_ssr], writes=[Bs])
                        yield
                        S.op("dve", lambda E, st_=st_: E.tensor_scalar(out=st_[:, 4:8], in0=st_[:, 4:8], scalar1=EPS, scalar2=None,
                                                                       op0=ALU.add), reads=[Bs], writes=[Bs])
                        yield
                        S.op("act", lambda E, st_=st_: E.activation(out=st_[:, 8:12], in_=st_[:, 4:8], func=AF.Sqrt), reads=[Bs], writes=[Bs])
                        yield
                        S.op("dve", lambda E, st_=st_: E.reciprocal(out=st_[:, 12:16], in_=st_[:, 8:12]), reads=[Bs], writes=[Bs])
                        yield
                        kn_ = kn[pi]; Bk = B_kn[pi]
                        for hl in range(HG):
                            p_ = pk[pi * 2 + hl // 2]; Bp = B_pk[pi * 2 + hl // 2]; off = (hl % 2) * 256
                            S.op("dve", lambda E, p_=p_, off=off, st_=st_, hl=hl, kn_=kn_: E.scalar_tensor_tensor(
                                out=kn_[:, hl, 0:128], in0=p_[:, off:off + 128], scalar=st_[:, 12 + hl:13 + hl], in1=kn_bc[:, 0:128],
                                op0=ALU.mult, op1=ALU.mult), reads=[Bp, Bs, B_const], writes=[Bk])
                            yield
                            S.op("dve", lambda E, st_=st_, hl=hl, kn_=kn_, tile=tile: E.tensor_scalar(
                                out=kn_[:, hl, 128:192], in0=krr[:, tile, :], scalar1=st_[:, 12 + hl:13 + hl], scalar2=None, op0=ALU.mult),
                                reads=[B_krr, Bs], writes=[Bk])
                            yield
                            S.op("act", lambda E, p_=p_, off=off, hl=hl, tile=tile: E.copy(
                                out=vext[:, tile, hl, 0:128], in_=p_[:, off + 128:off + 256]), reads=[Bp], writes=[B_vext])
                            yield
                        for hl in range(HG):
                            S.op("pe", lambda E, pi=pi, hl=hl, kn_=kn_: E.transpose(out=tpk[pi][:, hl, :], in_=kn_[:, hl, 0:128],
                                                                                   identity=ident[:]), reads=[Bk, B_ident], writes=[B_tpk[pi]])
                            yield
                            S.op("pe", lambda E, pi=pi, hl=hl, kn_=kn_: E.transpose(out=tpr[pi][0:64, hl, :], in_=kn_[:, hl, 128:192],
                                                                                   identity=ident[:]), reads=[Bk, B_ident], writes=[B_tpr[pi]])
                            yield
                        S.op("dve", lambda E, pi=pi, tile=tile: E.tensor_copy(out=KT[:, :, tile * 128:(tile + 1) * 128], in_=tpk[pi][:, 0:4, :]),
                             reads=[B_tpk[pi]], writes=[B_KT])
                        yield
                        S.op("act", lambda E, pi=pi, tile=tile: E.copy(out=KTr[:, :, tile * 128:(tile + 1) * 128], in_=tpr[pi][0:64, 0:4, :]),
                             reads=[B_tpr[pi]], writes=[B_KTr])
                        yield
                    for t0_ in range(0, 16, 2):
                        gens = [kprep_tile(t0_), kprep_tile(t0_ + 1)]
                        while gens:
                            for gen in list(gens):
                                try:
                                    next(gen)
                                except StopIteration:
                                    gens.remove(gen)
                    S.barrier()
                with ExitStack() as c2:
                    pq = [ps(c2, "pq%d" % i, [128, 512]) for i in range(4)]; B_pq = [PB() for _ in range(4)]
                    tpk = [ps(c2, "tpq%d" % i, [128, 8, 128], BF16) for i in range(2)]; B_tpk = [PB(), PB()]
                    tpr = [ps(c2, "tpqr%d" % i, [128, 8, 128], BF16) for i in range(2)]; B_tpr = [PB(), PB()]
                    jk = sb(c2, "jkQ", [128, 192], BF16); B_jk = Buf()
                    sk = [sb(c2, "sq%d" % i, [128, 16]) for i in range(2)]; B_sk = [Buf(), Buf()]
                    qg = [sb(c2, "qg%d" % i, [128, HG, 192]) for i in range(2)]; B_qg = [Buf(), Buf()]
                    qn_ = [sb(c2, "qn%d" % i, [128, HG, 192], BF16) for i in range(2)]; B_qn = [Buf(), Buf()]
                    def qprep_tile(tile):
                        pi = tile % 2
                        for b in range(2):
                            for kc in range(8):
                                S.op("pe", lambda E, pi=pi, b=b, kc=kc, tile=tile: E.matmul(
                                    pq[pi * 2 + b][:, 0:384], lhsT=cqT[:, kc, tile * 128:(tile + 1) * 128],
                                    rhs=wq[:, kc, b * 384:(b + 1) * 384], start=(kc == 0), stop=(kc == 7)),
                                    reads=[B_cqT, B_wq], writes=[B_pq[pi * 2 + b]])
                                yield
                        st_ = sk[pi]; Bs = B_sk[pi]
                        for hl in range(HG):
                            p_ = pq[pi * 2 + hl // 2]; Bp = B_pq[pi * 2 + hl // 2]; off = (hl % 2) * 192
                            S.op("act", lambda E, p_=p_, off=off, st_=st_, hl=hl: E.activation(
                                out=jk[:], in_=p_[:, off:off + 192], func=AF.Square, accum_out=st_[:, hl:hl + 1]),
                                reads=[Bp], writes=[B_jk, Bs])
                            yield
                        S.op("dve", lambda E, st_=st_: E.tensor_scalar(out=st_[:, 4:8], in0=st_[:, 0:4], scalar1=1.0 / 192, scalar2=EPS,
                                                                       op0=ALU.mult, op1=ALU.add), reads=[Bs], writes=[Bs])
                        yield
                        S.op("act", lambda E, st_=st_: E.activation(out=st_[:, 8:12], in_=st_[:, 4:8], func=AF.Sqrt), reads=[Bs], writes=[Bs])
                        yield
                        S.op("dve", lambda E, st_=st_: E.reciprocal(out=st_[:, 12:16], in_=st_[:, 8:12]), reads=[Bs], writes=[Bs])
                        yield
                        g_ = qg[pi]; Bg = B_qg[pi]; n_ = qn_[pi]; Bn = B_qn[pi]
                        for hl in range(HG):
                            p_ = pq[pi * 2 + hl // 2]; Bp = B_pq[pi * 2 + hl // 2]; off = (hl % 2) * 192
                            S.op("dve", lambda E, p_=p_, off=off, st_=st_, hl=hl, g_=g_: E.scalar_tensor_tensor(
                                out=g_[:, hl, :], in0=p_[:, off:off + 192], scalar=st_[:, 12 + hl:13 + hl], in1=qn_bc[:],
                                op0=ALU.mult, op1=ALU.mult), reads=[Bp, Bs, B_const], writes=[Bg])
                            yield
                        S.op("dve", lambda E, g_=g_, n_=n_: E.tensor_copy(out=n_[:, :, 0:128], in_=g_[:, :, 0:128]), reads=[Bg], writes=[Bn])
                        yield
                        rope_ops(S, g_[:, :, 128:192], Bg, n_[:, :, 128:192], Bn, cs, B_cs, tile, sb, c2, "q%d_%d" % (hg, tile), nheads=HG)
                        yield
                        for hl in range(HG):
                            S.op("pe", lambda E, pi=pi, hl=hl, n_=n_: E.transpose(out=tpk[pi][:, hl, :], in_=n_[:, hl, 0:128],
                                                                                  identity=ident[:]), reads=[Bn, B_ident], writes=[B_tpk[pi]])
                            yield
                            S.op("pe", lambda E, pi=pi, hl=hl, n_=n_: E.transpose(out=tpr[pi][0:64, hl, :], in_=n_[:, hl, 128:192],
                                                                                  identity=ident[:]), reads=[Bn, B_ident], writes=[B_tpr[pi]])
                            yield
                        S.op("dve", lambda E, pi=pi, tile=tile: E.tensor_copy(out=QT[:, :, tile * 128:(tile + 1) * 128], in_=tpk[pi][:, 0:4, :]),
                             reads=[B_tpk[pi]], writes=[B_QT])
                        yield
                        S.op("act", lambda E, pi=pi, tile=tile: E.copy(out=QTr[:, :, tile * 128:(tile + 1) * 128], in_=tpr[pi][0:64, 0:4, :]),
                             reads=[B_tpr[pi]], writes=[B_QTr])
                        yield
                    for t0_ in range(0, 8, 2):
                        gens = [qprep_tile(t0_), qprep_tile(t0_ + 1)]
                        while gens:
                            for gen in list(gens):
                                try:
                                    next(gen)
                                except StopIteration:
                                    gens.remove(gen)
                    S.barrier()
                with ExitStack() as c2:
                    pS = [ps(c2, "pS%d" % i, [128, 512]) for i in range(3)]; B_pS = [PB() for _ in range(3)]
                    pO = [ps(c2, "pO%d" % i, [128, 512]) for i in range(4)]; B_pO = [PB() for _ in range(4)]
                    PT = [sb(c2, "PT%d" % i, [128, 512], BF16) for i in range(3)]; B_PT = [Buf() for _ in range(3)]
                    of = [sb(c2, "of%d" % i, [128, 128]) for i in range(2)]; B_of = [Buf(), Buf()]
                    jk = sb(c2, "jkA", [128, 128], BF16); B_jk = Buf()
                    sa = [sb(c2, "sa%d" % i, [128, 8]) for i in range(2)]; B_sa = [Buf(), Buf()]
                    ob = [sb(c2, "ob%d" % i, [128, 128], BF16) for i in range(2)]; B_ob = [Buf(), Buf()]
                    si = 0
                    oi2 = 0
                    for hl in range(HG):
                        h = hg * HG + hl
                        for tg in range(2):
                            def s_mm(tk, slot):
                                p_ = pS[slot % 3]; Bp = B_pS[slot % 3]
                                S.op("pe", lambda E, p_=p_, hl=hl, tk=tk, tg=tg: E.matmul(
                                    p_[:], lhsT=KT[:, hl, tk * 128:(tk + 1) * 128], rhs=QT[:, hl, tg * 512:(tg + 1) * 512],
                                    start=True, stop=False), reads=[B_KT, B_QT], writes=[Bp])
                                S.op("pe", lambda E, p_=p_, hl=hl, tk=tk, tg=tg: E.matmul(
                                    p_[:], lhsT=KTr[:, hl, tk * 128:(tk + 1) * 128], rhs=QTr[:, hl, tg * 512:(tg + 1) * 512],
                                    start=False, stop=True), reads=[B_KTr, B_QTr], writes=[Bp])
                            s_mm(0, si)
                            for tk in range(16):
                                p_ = pS[si % 3]; Bp = B_pS[si % 3]; pt = PT[si % 3]; Bpt = B_PT[si % 3]
                                if tk + 1 < 16:
                                    s_mm(tk + 1, si + 1)
                                si += 1
                                S.op("act", lambda E, p_=p_, pt=pt: E.activation(out=pt[:], in_=p_[:], func=AF.Exp, scale=SCALE),
                                     reads=[Bp], writes=[Bpt])
                                for tqt in range(4):
                                    S.op("pe", lambda E, pt=pt, tqt=tqt, tk=tk, hl=hl: E.matmul(
                                        pO[tqt][:, 0:129], lhsT=pt[:, tqt * 128:(tqt + 1) * 128], rhs=vext[:, tk, hl, 0:129],
                                        start=(tk == 0), stop=(tk == 15)), reads=[Bpt, B_vext], writes=[B_pO[tqt]])
                            for tqt in range(4):
                                o_ = of[oi2 % 2]; Bo = B_of[oi2 % 2]; s_ = sa[oi2 % 2]; Bs = B_sa[oi2 % 2]
                                b_ = ob[oi2 % 2]; Bb = B_ob[oi2 % 2]; oi2 += 1
                                S.op("dve", lambda E, s_=s_, tqt=tqt: E.reciprocal(out=s_[:, 0:1], in_=pO[tqt][:, 128:129]),
                                     reads=[B_pO[tqt]], writes=[Bs])
                                S.op("dve", lambda E, s_=s_, tqt=tqt, o_=o_: E.tensor_scalar(
                                    out=o_[:], in0=pO[tqt][:, 0:128], scalar1=s_[:, 0:1], scalar2=None, op0=ALU.mult),
                                    reads=[B_pO[tqt], Bs], writes=[Bo])
                                S.op("act", lambda E, o_=o_, s_=s_: E.activation(out=jk[:], in_=o_[:], func=AF.Square, accum_out=s_[:, 1:2]),
                                     reads=[Bo], writes=[B_jk, Bs])
                                S.op("dve", lambda E, s_=s_: E.tensor_scalar(out=s_[:, 2:3], in0=s_[:, 1:2], scalar1=1.0 / 128, scalar2=EPS,
                                                                             op0=ALU.mult, op1=ALU.add), reads=[Bs], writes=[Bs])
                                S.op("act", lambda E, s_=s_: E.activation(out=s_[:, 3:4], in_=s_[:, 2:3], func=AF.Sqrt), reads=[Bs], writes=[Bs])
                                S.op("dve", lambda E, s_=s_: E.reciprocal(out=s_[:, 4:5], in_=s_[:, 3:4]), reads=[Bs], writes=[Bs])
                                S.op("dve", lambda E, o_=o_, s_=s_, b_=b_, h=h: E.scalar_tensor_tensor(
                                    out=b_[:], in0=o_[:], scalar=s_[:, 4:5], in1=aon_bc[:, h * 128:(h + 1) * 128], op0=ALU.mult, op1=ALU.mult),
                                    reads=[Bo, Bs, B_const], writes=[Bb])
                                t0 = tg * 512 + tqt * 128
                                S.dma("sp", s_mix[t0:t0 + 128, h * 128:(h + 1) * 128], b_[:], reads=[Bb], writes=[B_mix])
                    S.barrier()
            S.barrier()

        if stage >= 3 and "C" not in SKIP:
          with ExitStack() as c:
            B_cc = Buf()
            tri = sb(c, "tri", [128, 4, 128])
            S.dma("sp", tri[:], tri_in.rearrange("f k l -> k f l"), writes=[B_cc])
            identF = sb(c, "identF", [128, 128])
            S.dma("sp", identF[:], ident_in, writes=[B_cc])
            onesF = sb(c, "onesF", [128, 128])
            S.op("dve", lambda E: E.memset(onesF[:], 1.0), writes=[B_cc])
            neg4 = [sb(c, "neg4_%d" % i, [128, 4, 128], BF16) for i in range(2)]
            for i in range(2):
                S.op("dve", lambda E, i=i: E.tensor_copy(out=neg4[i][:], in_=tri[:, 2 + i, :].unsqueeze(1).broadcast_to([128, 4, 128])),
                     reads=[B_cc], writes=[B_cc])
            alog_bc = sb(c, "alog_bc", [128, 64]); dtb_bc = sb(c, "dtb_bc", [128, 64]); dsk_bc = sb(c, "dsk_bc", [128, 32])
            son_bc = sb(c, "son_bc", [128, 2048])
            for dst, src in ((alog_bc, a_log), (dtb_bc, dt_bias), (dsk_bc, d_skip), (son_bc, son)):
                S.dma("sp", dst[:], src.partition_broadcast(128), writes=[B_cc])
            A_bc = sb(c, "A_bc", [128, 64])
            S.op("act", lambda E: E.activation(out=A_bc[:], in_=alog_bc[:], func=AF.Exp), reads=[B_cc], writes=[B_cc])
            S.op("dve", lambda E: E.tensor_scalar(out=A_bc[:], in0=A_bc[:], scalar1=-1.0, scalar2=None, op0=ALU.mult),
                 reads=[B_cc], writes=[B_cc])
            dtr = sb(c, "dtr", [128, 16, 64]); dtv = sb(c, "dtv", [128, 16, 64]); adt = sb(c, "adt", [128, 16, 64])
            tmpd = sb(c, "tmpd", [128, 16, 64])
            B_dt = Buf()
            for tile in range(16):
                S.dma("sp", dtr[:, tile, :], s_kv[tile * 128:(tile + 1) * 128, 576:640], reads=[B_scr], writes=[B_dt])
            bc64 = lambda t_: t_[:].unsqueeze(1).broadcast_to([128, 16, 64])
            S.op("dve", lambda E: E.tensor_tensor(out=dtr[:], in0=dtr[:], in1=bc64(dtb_bc), op=ALU.add), reads=[B_dt, B_cc], writes=[B_dt])
            S.op("act", lambda E: E.activation(out=tmpd[:], in_=dtr[:], func=AF.Abs), reads=[B_dt], writes=[B_dt])
            S.op("act", lambda E: E.activation(out=tmpd[:], in_=tmpd[:], func=AF.Exp, scale=-1.0), reads=[B_dt], writes=[B_dt])
            S.op("act", lambda E: E.activation(out=tmpd[:], in_=tmpd[:], func=AF.Ln, bias=1.0), reads=[B_dt], writes=[B_dt])
            S.op("dve", lambda E: E.scalar_tensor_tensor(out=dtv[:], in0=dtr[:], scalar=0.0, in1=tmpd[:], op0=ALU.max, op1=ALU.add),
                 reads=[B_dt], writes=[B_dt])
            S.op("dve", lambda E: E.tensor_tensor(out=adt[:], in0=dtv[:], in1=bc64(A_bc), op=ALU.mult), reads=[B_dt, B_cc], writes=[B_dt])

            import os as _os
            SUB = int(_os.environ.get("KSUB", "99"))
            cw_v = conv_w.rearrange("(n p) j -> n p j", p=128)
            cb_v = conv_b.rearrange("(n p o) -> n p o", p=128, o=1)
            cin = [sb(c, "cin%d" % i, [128, T + 4], BF16) for i in range(2)]; B_cin = [Buf(), Buf()]
            for i in range(2):
                S.op("pool", lambda E, i=i: E.memset(cin[i][:], 0.0), writes=[B_cin[i]])
            cacc = sb(c, "cacc", [128, T]); B_cacc = Buf()
            cwt = [sb(c, "cwt%d" % i, [128, 8]) for i in range(2)]; B_cwt = [Buf(), Buf()]
            cT = [sb(c, "cT%d" % i, [128, T], BF16) for i in range(4)]; B_cT = [Buf() for _ in range(4)]
            xtok = sb(c, "xtok", [128, 16, 256], BF16); B_xtok = Buf()
            Btok = sb(c, "Btok", [128, 16, 128], BF16); B_Btok = Buf()
            CBT = sb(c, "CBT", [128, 8, 128], BF16); B_CBT = Buf()
            yacc = sb(c, "yacc", [128, 8, 256]); B_yacc = Buf()
            stateL = [sb(c, "state%d" % i, [128, 256]) for i in range(2)]; B_stateL = [Buf(), Buf()]
            state_bfL = [sb(c, "state_bf%d" % i, [128, 256], BF16) for i in range(2)]; B_stbfL = [Buf(), Buf()]
            P_tp = ps(c, "P_tp", [128, 8, 128], BF16); B_Ptp = PB()
            P_ct = ps(c, "P_ct", [128, 512]); B_Pct = PB()
            P_cb = [ps(c, "P_cb%d" % i, [128, 4, 128]) for i in range(2)]; B_Pcb = [PB(), PB()]
            P_cbt = ps(c, "P_cbt", [128, 512]); B_Pcbt = PB()
            P_y = ps(c, "P_y", [128, 512]); B_Py = PB()
            P_yo = ps(c, "P_yo", [128, 512]); B_Pyo = PB()
            P_st = ps(c, "P_st", [128, 512]); B_Pst = PB()
            bankA = [P_ct, P_st]; B_bankA = [B_Pct, B_Pst]
            bankB = P_cb; B_bankB = B_Pcb
            bankC = [P_y, P_yo]; B_bankC = [B_Py, B_Pyo]
            ci_n = 0
            sm = [sb(c, "sm%d" % i, [128, 32]) for i in range(4)]; B_sm = [Buf() for _ in range(4)]
            arep = [sb(c, "arep%d" % i, [128, 4, 128]) for i in range(4)]; B_arep = [Buf() for _ in range(4)]
            LT = [sb(c, "LT%d" % i, [128, 4, 128]) for i in range(4)]; B_LT = [Buf() for _ in range(4)]
            MT = [sb(c, "MT%d" % i, [128, 4, 128], BF16) for i in range(4)]; B_MT = [Buf() for _ in range(4)]
            xdt = [sb(c, "xdt%d" % i, [128, 4, 64], BF16) for i in range(4)]; B_xdt = [Buf() for _ in range(4)]
            xdd = [sb(c, "xdd%d" % i, [128, 4, 64], BF16) for i in range(4)]; B_xdd = [Buf() for _ in range(4)]
            zt = [sb(c, "zt%d" % i, [128, 256]) for i in range(2)]; B_zt = [Buf(), Buf()]
            yf = [sb(c, "yf%d" % i, [128, 256]) for i in range(2)]; B_yf = [Buf(), Buf()]
            yb = [sb(c, "yb%d" % i, [128, 256], BF16) for i in range(2)]; B_yb = [Buf(), Buf()]
            jkC = sb(c, "jkC", [128, 256], BF16); B_jkC = Buf()
            it = 0
            for g in range(8 if SUB > 0 else 0):
                chans = [g * 256, g * 256 + 128, 2048 + g * 128, 3072 + g * 128]
                for qi, ch0 in enumerate(chans):
                    ci = ci_n % 2; ci_n += 1
                    S.dma("sp", cin[ci][:, 2:2 + T], s_xT[ch0:ch0 + 128, :], reads=[B_scr], writes=[B_cin[ci]])
                    S.dma("sp", cwt[ci][:, 0:5], cw_v[ch0 // 128], writes=[B_cwt[ci]])
                    S.dma("sp", cwt[ci][:, 5:6], cb_v[ch0 // 128], writes=[B_cwt[ci]])
                    S.op("dve", lambda E, ci=ci: E.tensor_scalar(out=cacc[:], in0=cin[ci][:, 0:T], scalar1=cwt[ci][:, 0:1], scalar2=None,
                                                                 op0=ALU.mult), reads=[B_cin[ci], B_cwt[ci]], writes=[B_cacc])
                    for j in range(1, 5):
                        S.op("dve", lambda E, ci=ci, j=j: E.scalar_tensor_tensor(out=cacc[:], in0=cin[ci][:, j:j + T], scalar=cwt[ci][:, j:j + 1],
                                                                                in1=cacc[:], op0=ALU.mult, op1=ALU.add),
                             reads=[B_cin[ci], B_cwt[ci]], writes=[B_cacc])
                    S.op("act", lambda E, ci=ci, qi=qi: E.activation(out=cT[qi][:], in_=cacc[:], func=AF.Silu, bias=cwt[ci][:, 5:6]),
                         reads=[B_cacc, B_cwt[ci]], writes=[B_cT[qi]])
                if SUB < 2:
                    continue
                for tile in range(16):
                    for qi in range(3):
                        S.op("pe", lambda E, qi=qi, tile=tile: E.transpose(out=P_tp[:, qi, :], in_=cT[qi][:, tile * 128:(tile + 1) * 128],
                                                                           identity=ident[:]), reads=[B_cT[qi], B_ident], writes=[B_Ptp])
                    S.op("dve", lambda E, tile=tile: E.tensor_copy(out=xtok[:, tile, :], in_=P_tp[:, 0:2, :]), reads=[B_Ptp], writes=[B_xtok])
                    S.op("act", lambda E, tile=tile: E.copy(out=Btok[:, tile, :], in_=P_tp[:, 2, :]), reads=[B_Ptp], writes=[B_Btok])
                if SUB < 3:
                    continue
                for ch in range(8):
                    S.op("pe", lambda E, ch=ch: E.matmul(P_cbt[:, 0:128], lhsT=cT[2][:, ch * 128:(ch + 1) * 128],
                                                         rhs=cT[3][:, ch * 128:(ch + 1) * 128], start=True, stop=True),
                         reads=[B_cT[2], B_cT[3]], writes=[B_Pcbt])
                    S.op("act", lambda E, ch=ch: E.copy(out=CBT[:, ch, :], in_=P_cbt[:, 0:128]), reads=[B_Pcbt], writes=[B_CBT])
                S.op("dve", lambda E, g=g: E.tensor_tensor(
                    out=yacc[:].rearrange("p c (r q) -> p c r q", r=4),
                    in0=xtok[:, 0:8, :].rearrange("p c (r q) -> p c r q", r=4),
                    in1=dsk_bc[:, g * 4:(g + 1) * 4].unsqueeze(1).unsqueeze(3).broadcast_to([128, 8, 4, 64]), op=ALU.mult),
                    reads=[B_xtok, B_cc], writes=[B_yacc])
                cnt_d = [0, 0]

                def scan_chunk(di, ch):
                    colX = 127 if di == 0 else 0
                    triX = tri[:, di, :]
                    negX = tri[:, 2 + di, :]
                    own = ch < 8
                    k_ = di * 2 + cnt_d[di] % 2; cnt_d[di] += 1
                    s_ = sm[k_]; Bs = B_sm[k_]
                    h0 = di * 32 + g * 4
                    adt4 = adt[:, ch, h0:h0 + 4]
                    bA = bankA[di]; BbA = B_bankA[di]; pc = bankB[di]; Bpc = B_bankB[di]; bC = bankC[di]; BbC = B_bankC[di]
                    st_ = stateL[di]; Bst = B_stateL[di]; sbf = state_bfL[di]; Bsbf = B_stbfL[di]
                    S.op("pe", lambda E, triX=triX, adt4=adt4, bA=bA: E.matmul(bA[:, 0:4], lhsT=triX, rhs=adt4, start=True, stop=True),
                         reads=[B_cc, B_dt], writes=[BbA])
                    yield
                    S.op("dve", lambda E, k_=k_, adt4=adt4, triX=triX: E.tensor_tensor(
                        out=arep[k_][:], in0=triX.unsqueeze(1).broadcast_to([128, 4, 128]),
                        in1=adt4.unsqueeze(2).broadcast_to([128, 4, 128]), op=ALU.mult), reads=[B_dt, B_cc], writes=[B_arep[k_]])
                    yield
                    S.op("pe", lambda E, pc=pc, k_=k_: E.matmul(pc[:].rearrange("p r l -> p (r l)"), lhsT=onesF[:],
                                                                rhs=arep[k_][:].rearrange("p r l -> p (r l)"), start=True, stop=False),
                         reads=[B_arep[k_], B_cc], writes=[Bpc])
                    yield
                    S.op("pe", lambda E, pc=pc, di=di: E.matmul(pc[:].rearrange("p r l -> p (r l)"), lhsT=ident[:],
                                                                rhs=neg4[di][:].rearrange("p r l -> p (r l)"), start=False, stop=True),
                         reads=[B_cc, B_ident], writes=[Bpc])
                    yield
                    S.op("dve", lambda E, s_=s_, bA=bA: E.tensor_scalar(out=s_[:, 0:4], in0=bA[:, 0:4], scalar1=-1.0, scalar2=None, op0=ALU.mult),
                         reads=[BbA], writes=[Bs])
                    yield
                    S.op("act", lambda E, s_=s_, bA=bA: E.activation(out=s_[:, 4:8], in_=bA[:, 0:4], func=AF.Exp), reads=[BbA], writes=[Bs])
                    yield
                    S.op("dve", lambda E, s_=s_, pc=pc, colX=colX: E.tensor_tensor(out=s_[:, 16:20], in0=pc[:, :, colX], in1=s_[:, 0:4], op=ALU.add),
                         reads=[Bpc, Bs], writes=[Bs])
                    yield
                    S.op("act", lambda E, s_=s_: E.activation(out=s_[:, 8:12], in_=s_[:, 16:20], func=AF.Exp), reads=[Bs], writes=[Bs])
                    yield
                    S.op("act", lambda E, s_=s_, pc=pc, colX=colX: E.activation(out=s_[:, 12:16], in_=pc[:, :, colX], func=AF.Exp),
                         reads=[Bpc], writes=[Bs])
                    yield
                    dtc = dtv[:, ch, h0:h0 + 4]
                    S.op("dve", lambda E, k_=k_, ch=ch, dtc=dtc: E.tensor_tensor(
                        out=xdt[k_][:], in0=xtok[:, ch, :].rearrange("p (r q) -> p r q", r=4),
                        in1=dtc.unsqueeze(2).broadcast_to([128, 4, 64]), op=ALU.mult), reads=[B_xtok, B_dt], writes=[B_xdt[k_]])
                    yield
                    if own:
                        for r in range(4):
                            S.op("act", lambda E, k_=k_, r=r, pc=pc, s_=s_: E.activation(out=LT[k_][:, r, :], in_=pc[:, r, :], func=AF.Exp,
                                                                                         bias=s_[:, r:r + 1]), reads=[Bpc, Bs], writes=[B_LT[k_]])
                            yield
                        S.op("dve", lambda E, k_=k_, ch=ch: E.tensor_tensor(out=MT[k_][:], in0=LT[k_][:],
                                                                            in1=CBT[:, ch, :].unsqueeze(1).broadcast_to([128, 4, 128]), op=ALU.mult),
                             reads=[B_LT[k_], B_CBT], writes=[B_MT[k_]])
                        yield
                        for r in range(4):
                            S.op("pe", lambda E, k_=k_, r=r, bC=bC: E.matmul(bC[:, r * 64:(r + 1) * 64], lhsT=MT[k_][:, r, :], rhs=xdt[k_][:, r, :],
                                                                             start=True, stop=True), reads=[B_MT[k_], B_xdt[k_]], writes=[BbC])
                            yield
                        S.op("pe", lambda E, ch=ch, bC=bC, sbf=sbf: E.matmul(bC[:, 256:512], lhsT=cT[3][:, ch * 128:(ch + 1) * 128], rhs=sbf[:],
                                                                             start=True, stop=True), reads=[B_cT[3], Bsbf], writes=[BbC])
                        yield
                        S.op("dve", lambda E, ch=ch, bC=bC: E.tensor_tensor(out=yacc[:, ch, :], in0=yacc[:, ch, :], in1=bC[:, 0:256], op=ALU.add),
                             reads=[BbC], writes=[B_yacc])
                        yield
                        for r in range(4):
                            S.op("dve", lambda E, ch=ch, r=r, s_=s_, bC=bC: E.scalar_tensor_tensor(
                                out=yacc[:, ch, r * 64:(r + 1) * 64], in0=bC[:, 256 + r * 64:256 + (r + 1) * 64], scalar=s_[:, 4 + r:5 + r],
                                in1=yacc[:, ch, r * 64:(r + 1) * 64], op0=ALU.mult, op1=ALU.add), reads=[BbC, Bs], writes=[B_yacc])
                            yield
                    S.op("dve", lambda E, k_=k_, s_=s_: E.tensor_tensor(out=xdd[k_][:], in0=xdt[k_][:],
                                                                        in1=s_[:, 8:12].unsqueeze(2).broadcast_to([128, 4, 64]), op=ALU.mult),
                         reads=[B_xdt[k_], Bs], writes=[B_xdd[k_]])
                    yield
                    S.op("pe", lambda E, k_=k_, ch=ch, bA=bA: E.matmul(bA[:, 256:512], lhsT=Btok[:, ch, :],
                                                                       rhs=xdd[k_][:].rearrange("p r q -> p (r q)"), start=True, stop=True),
                         reads=[B_Btok, B_xdd[k_]], writes=[BbA])
                    yield
                    for r in range(4):
                        S.op("dve", lambda E, r=r, s_=s_, bA=bA, st_=st_: E.scalar_tensor_tensor(
                            out=st_[:, r * 64:(r + 1) * 64], in0=st_[:, r * 64:(r + 1) * 64], scalar=s_[:, 12 + r:13 + r],
                            in1=bA[:, 256 + r * 64:256 + (r + 1) * 64], op0=ALU.mult, op1=ALU.add), reads=[BbA, Bs], writes=[Bst])
                        yield
                    S.op("dve", lambda E, st_=st_, sbf=sbf: E.tensor_copy(out=sbf[:], in_=st_[:]), reads=[Bst], writes=[Bsbf])
                    yield

                if SUB > 3:
                    for di in range(2):
                        S.op("dve", lambda E, di=di: E.memset(stateL[di][:], 0.0), writes=[B_stateL[di]])
                        S.op("dve", lambda E, di=di: E.memset(state_bfL[di][:], 0.0), writes=[B_stbfL[di]])
                    for step in range(16):
                        if step < 8:
                            gens = [scan_chunk(1, 15 - step)]
                        else:
                            gens = [scan_chunk(1, 15 - step), scan_chunk(0, step - 8)]
                        while gens:
                            for gen in list(gens):
                                try:
                                    next(gen)
                                except StopIteration:
                                    gens.remove(gen)
                for ch in range(8 if SUB > 4 else 0):
                    k_ = ch % 2
                    S.dma("sp", zt[k_][:], s_z[ch * 128:(ch + 1) * 128, g * 256:(g + 1) * 256], reads=[B_scr], writes=[B_zt[k_]])
                    S.op("act", lambda E, k_=k_: E.activation(out=zt[k_][:], in_=zt[k_][:], func=AF.Silu), reads=[B_zt[k_]], writes=[B_zt[k_]])
                    S.op("dve", lambda E, k_=k_, ch=ch: E.tensor_tensor(out=yf[k_][:], in0=yacc[:, ch, :], in1=zt[k_][:], op=ALU.mult),
                         reads=[B_yacc, B_zt[k_]], writes=[B_yf[k_]])
                    s_ = sm[k_]; Bs = B_sm[k_]
                    S.op("act", lambda E, k_=k_, s_=s_: E.activation(out=jkC[:], in_=yf[k_][:], func=AF.Square, accum_out=s_[:, 20:21]),
                         reads=[B_yf[k_]], writes=[B_jkC, Bs])
                    S.op("dve", lambda E, s_=s_: E.tensor_scalar(out=s_[:, 21:22], in0=s_[:, 20:21], scalar1=1.0 / 256, scalar2=EPS,
                                                                 op0=ALU.mult, op1=ALU.add), reads=[Bs], writes=[Bs])
                    S.op("act", lambda E, s_=s_: E.activation(out=s_[:, 22:23], in_=s_[:, 21:22], func=AF.Sqrt), reads=[Bs], writes=[Bs])
                    S.op("dve", lambda E, s_=s_: E.reciprocal(out=s_[:, 23:24], in_=s_[:, 22:23]), reads=[Bs], writes=[Bs])
                    S.op("dve", lambda E, k_=k_, s_=s_, g=g: E.scalar_tensor_tensor(
                        out=yb[k_][:], in0=yf[k_][:], scalar=s_[:, 23:24], in1=son_bc[:, g * 256:(g + 1) * 256], op0=ALU.mult, op1=ALU.mult),
                        reads=[B_yf[k_], Bs, B_cc], writes=[B_yb[k_]])
                    S.dma("sp", s_mix[ch * 128:(ch + 1) * 128, 2048 + g * 256:2048 + (g + 1) * 256], yb[k_][:], reads=[B_yb[k_]], writes=[B_mix])
            S.barrier()

        B_out = Buf(); B_hnT = Buf()
        if stage >= 4 and "D" not in SKIP:
          hdst = dbg["d_h"] if stage == 4 else out
          with ExitStack() as c:
            nf_bc = sb(c, "nf_bc", [128, D]); B_nf = Buf()
            S.dma("sp", nf_bc[:], norm_ffn.partition_broadcast(128), writes=[B_nf])
            mt = [sb(c, "mt%d" % i, [128, D], BF16) for i in range(2)]; B_mt = [Buf(), Buf()]
            mixT = sb(c, "mixT", [128, 32, 512], BF16); B_mixT = [Buf() for _ in range(4)]
            wb_ = [sb(c, "wbD%d" % i, [128, 32, 256], BF16) for i in range(2)]; B_wb = [Buf(), Buf()]
            ht = [sb(c, "ht%d" % i, [128, D]) for i in range(4)]; B_ht = [Buf() for _ in range(4)]
            xr = [sb(c, "xr%d" % i, [128, 256]) for i in range(3)]; B_xr = [Buf() for _ in range(3)]
            hb = sb(c, "hb", [128, D], BF16); B_hb = Buf()
            jkD = sb(c, "jkD", [128, D], BF16); B_jkD = Buf()
            hst = [sb(c, "hst%d" % i, [128, 8, 128], BF16) for i in range(2)]; B_hst = [Buf(), Buf()]
            sD = sb(c, "sD", [128, 8]); B_sD = Buf()
            tpD = [ps(c, "tpD%d" % i, [128, 8, 128], BF16) for i in range(2)]; B_tpD = [PB(), PB()]
            accD = [ps(c, "accD%d" % i, [128, 512]) for i in range(4)]; B_accD = [PB() for _ in range(4)]
            wo_v = w_out.rearrange("(k p) c -> p k c", p=128)
            hn_v = s_hnT.rearrange("(k p) t -> p k t", p=128)
            wi = 0; ai = 0; xi = 0; gi = 0
            for tb in range(2):
                for tt in range(4):
                    tile = tb * 4 + tt
                    mi = tile % 2
                    S.dma("sp", mt[mi][:], s_mix[tile * 128:(tile + 1) * 128, :], reads=[B_mix], writes=[B_mt[mi]])
                    for g in range(4):
                        ti = gi % 2; gi += 1
                        for j in range(8):
                            kk = g * 8 + j
                            S.op("pe", lambda E, ti=ti, j=j, kk=kk, mi=mi: E.transpose(out=tpD[ti][:, j, :], in_=mt[mi][:, kk * 128:(kk + 1) * 128],
                                                                                      identity=ident[:]), reads=[B_mt[mi], B_ident], writes=[B_tpD[ti]])
                        if g % 2 == 0:
                            S.op("act", lambda E, ti=ti, g=g, tt=tt: E.copy(out=mixT[:, g * 8:(g + 1) * 8, tt * 128:(tt + 1) * 128], in_=tpD[ti][:]),
                                 reads=[B_tpD[ti]], writes=[B_mixT[tt]])
                        else:
                            S.op("dve", lambda E, ti=ti, g=g, tt=tt: E.tensor_copy(out=mixT[:, g * 8:(g + 1) * 8, tt * 128:(tt + 1) * 128], in_=tpD[ti][:]),
                                 reads=[B_tpD[ti]], writes=[B_mixT[tt]])
                for cb in range(16):
                    w_ = wb_[wi % 2]; Bw = B_wb[wi % 2]; wi += 1
                    S.dma("pool", w_[:], wo_v[:, :, cb * 256:(cb + 1) * 256], writes=[Bw])
                    for tt in range(4):
                        tile = tb * 4 + tt
                        a = accD[ai % 4]; Ba = B_accD[ai % 4]; ai += 1
                        x_ = xr[xi % 3]; Bx = B_xr[xi % 3]; xi += 1
                        S.dma("sp", x_[:], x[tile * 128:(tile + 1) * 128, cb * 256:(cb + 1) * 256], writes=[Bx])
                        for kk in range(32):
                            S.op("pe", lambda E, a=a, w_=w_, kk=kk, tt=tt: E.matmul(a[:, 0:256], lhsT=mixT[:, kk, tt * 128:(tt + 1) * 128],
                                                                                   rhs=w_[:, kk, :], start=(kk == 0), stop=(kk == 31)),
                                 reads=[Bw, B_mixT[tt]], writes=[Ba])
                        S.op("dve", lambda E, a=a, x_=x_, tt=tt, cb=cb: E.tensor_tensor(out=ht[tt][:, cb * 256:(cb + 1) * 256], in0=a[:, 0:256],
                                                                                       in1=x_[:], op=ALU.add), reads=[Ba, Bx], writes=[B_ht[tt]])
                for tt in range(4):
                    tile = tb * 4 + tt
                    S.dma("sp", hdst[tile * 128:(tile + 1) * 128, :], ht[tt][:], reads=[B_ht[tt]], writes=[B_out])
                    S.op("act", lambda E, tt=tt: E.activation(out=jkD[:], in_=ht[tt][:], func=AF.Square, accum_out=sD[:, 0:1]),
                         reads=[B_ht[tt]], writes=[B_jkD, B_sD])
                    S.op("dve", lambda E: E.tensor_scalar(out=sD[:, 1:2], in0=sD[:, 0:1], scalar1=1.0 / D, scalar2=EPS, op0=ALU.mult, op1=ALU.add),
                         reads=[B_sD], writes=[B_sD])
                    S.op("act", lambda E: E.activation(out=sD[:, 2:3], in_=sD[:, 1:2], func=AF.Sqrt), reads=[B_sD], writes=[B_sD])
                    S.op("dve", lambda E: E.reciprocal(out=sD[:, 3:4], in_=sD[:, 2:3]), reads=[B_sD], writes=[B_sD])
                    S.op("dve", lambda E, tt=tt: E.scalar_tensor_tensor(out=hb[:], in0=ht[tt][:], scalar=sD[:, 3:4], in1=nf_bc[:],
                                                                        op0=ALU.mult, op1=ALU.mult), reads=[B_ht[tt], B_sD, B_nf], writes=[B_hb])
                    for g in range(4):
                        ti = gi % 2; gi += 1
                        for j in range(8):
                            kk = g * 8 + j
                            S.op("pe", lambda E, ti=ti, j=j, kk=kk: E.transpose(out=tpD[ti][:, j, :], in_=hb[:, kk * 128:(kk + 1) * 128],
                                                                               identity=ident[:]), reads=[B_hb, B_ident], writes=[B_tpD[ti]])
                        hs = hst[ti]; Bh = B_hst[ti]
                        if g % 2 == 0:
                            S.op("act", lambda E, ti=ti, hs=hs: E.copy(out=hs[:], in_=tpD[ti][:]), reads=[B_tpD[ti]], writes=[Bh])
                        else:
                            S.op("dve", lambda E, ti=ti, hs=hs: E.tensor_copy(out=hs[:], in_=tpD[ti][:]), reads=[B_tpD[ti]], writes=[Bh])
                        S.dma("sp", hn_v[:, g * 8:(g + 1) * 8, tile * 128:(tile + 1) * 128], hs[:], reads=[Bh], writes=[B_hnT])
            S.barrier()

        if stage >= 5:
          with ExitStack() as c:
            s2all = sb(c, "s2all", [128, 8, 8, 128]); A1all = sb(c, "A1all", [128, 8, 8, 128])
            wAll = sb(c, "wAll", [128, 8, 8])
            B_gin = Buf()
            hnT = sb(c, "hnT", [128, 32, TO], BF16); B_hn = Buf()
            hn_v = s_hnT.rearrange("(k p) t -> p k t", p=128)
            for g in range(4):
                S.dma("sp", hnT[:, g * 8:(g + 1) * 8, :], hn_v[:, g * 8:(g + 1) * 8, :], reads=[B_hnT], writes=[B_hn])
            identF2 = sb(c, "identF2", [128, 128]); B_idf = Buf()
            S.dma("sp", identF2[:], ident_in, writes=[B_idf])
            with ExitStack() as c2:
                qT = sb(c2, "qT", [128, 16, TO], BF16); B_qT = Buf()
                skT = sb(c2, "skT", [128, 16, 128], BF16); B_skT = Buf()
                S.dma("pool", skT[:], skT_in.rearrange("h d k -> d h k"), writes=[B_skT])
                wqb = [sb(c2, "wqb%d" % i, [128, 32, 128], BF16) for i in range(2)]; B_wqb = [Buf(), Buf()]
                pq_ = [ps(c2, "pqE%d" % i, [128, 512]) for i in range(4)]; B_pq_ = [PB() for _ in range(4)]
                pi_ = 0
                wq_v = w_query.rearrange("(k p) c -> p k c", p=128)
                for cb in range(16):
                    w_ = wqb[cb % 2]; Bw = B_wqb[cb % 2]
                    S.dma("pool", w_[:], wq_v[:, :, cb * 128:(cb + 1) * 128], writes=[Bw])
                    for hh in range(1):
                        hc = cb
                        for th in range(2):
                            p_ = pq_[pi_ % 4]; Bp = B_pq_[pi_ % 4]; pi_ += 1
                            for kk in range(32):
                                S.op("pe", lambda E, p_=p_, w_=w_, kk=kk, hh=hh, th=th: E.matmul(
                                    p_[:], lhsT=w_[:, kk, hh * 128:(hh + 1) * 128], rhs=hnT[:, kk, th * 512:(th + 1) * 512],
                                    start=(kk == 0), stop=(kk == 31)), reads=[Bw, B_hn], writes=[Bp])
                            if pi_ % 2 == 0:
                                S.op("act", lambda E, p_=p_, hc=hc, th=th: E.copy(out=qT[:, hc, th * 512:(th + 1) * 512], in_=p_[:]),
                                     reads=[Bp], writes=[B_qT])
                            else:
                                S.op("dve", lambda E, p_=p_, hc=hc, th=th: E.tensor_copy(out=qT[:, hc, th * 512:(th + 1) * 512], in_=p_[:]),
                                     reads=[Bp], writes=[B_qT])
                sc = sb(c2, "sc", [128, 16, 128]); B_sc = Buf()
                wk = sb(c2, "wk", [128, 256]); B_wk = Buf()
                v16 = sb(c2, "v16", [128, 16, 16]); B_v16 = Buf()
                cand = sb(c2, "cand", [128, 8, 256]); B_cand = Buf()
                t24 = sb(c2, "t24", [128, 8, 24]); B_t24 = Buf()
                sE = sb(c2, "sE", [128, 8, 8]); B_sE = Buf()
                jkE = sb(c2, "jkE", [128, 16]); B_jkE = Buf()
                for tt in range(8):
                    for q4 in range(4):
                        p_ = pq_[pi_ % 4]; Bp = B_pq_[pi_ % 4]; pi_ += 1
                        for u in range(4):
                            hc = q4 * 4 + u
                            S.op("pe", lambda E, p_=p_, u=u, hc=hc, tt=tt: E.matmul(
                                p_[:, u * 128:(u + 1) * 128], lhsT=qT[:, hc, tt * 128:(tt + 1) * 128], rhs=skT[:, hc, :],
                                start=True, stop=True), reads=[B_qT, B_skT], writes=[Bp])
                        S.op("act", lambda E, p_=p_, q4=q4: E.copy(out=sc[:, q4 * 4:(q4 + 1) * 4, :], in_=p_[:]), reads=[Bp], writes=[B_sc])
                    for hc in range(16):
                        S.op("dve", lambda E, hc=hc: E.max(out=v16[:, hc, 0:8], in_=sc[:, hc, :]), reads=[B_sc], writes=[B_v16])
                        S.op("dve", lambda E, hc=hc: E.match_replace(out=wk[:, 0:128], in_to_replace=v16[:, hc, 0:8], in_values=sc[:, hc, :],
                                                                     imm_value=-1e30), reads=[B_sc, B_v16], writes=[B_wk])
                        S.op("dve", lambda E, hc=hc: E.max(out=v16[:, hc, 8:16], in_=wk[:, 0:128]), reads=[B_wk], writes=[B_v16])
                    v4 = v16[:].rearrange("p (h c) k -> p h c k", c=2)
                    S.op("dve", lambda E, v4=v4: E.tensor_tensor(
                        out=cand[:].rearrange("p h (a b) -> p h a b", a=16),
                        in0=v4[:, :, 0, :].unsqueeze(3).broadcast_to([128, 8, 16, 16]),
                        in1=v4[:, :, 1, :].unsqueeze(2).broadcast_to([128, 8, 16, 16]), op=ALU.add), reads=[B_v16], writes=[B_cand])
                    for h in range(8):
                        S.op("dve", lambda E, h=h: E.max(out=t24[:, h, 0:8], in_=cand[:, h, :]), reads=[B_cand], writes=[B_t24])
                        S.op("dve", lambda E, h=h: E.match_replace(out=wk[:], in_to_replace=t24[:, h, 0:8], in_values=cand[:, h, :],
                                                                   imm_value=-1e30), reads=[B_cand, B_t24], writes=[B_wk])
                        S.op("dve", lambda E, h=h: E.max(out=t24[:, h, 8:16], in_=wk[:]), reads=[B_wk], writes=[B_t24])
                        S.op("dve", lambda E, h=h: E.match_replace(out=wk[:], in_to_replace=t24[:, h, 8:16], in_values=wk[:],
                                                                   imm_value=-1e30), reads=[B_t24], writes=[B_wk])
                        S.op("dve", lambda E, h=h: E.max(out=t24[:, h, 16:24], in_=wk[:]), reads=[B_wk], writes=[B_t24])
                    S.op("dve", lambda E: E.tensor_tensor(out=sE[:, :, 0], in0=t24[:, :, 15], in1=t24[:, :, 16], op=ALU.add), reads=[B_t24], writes=[B_sE])
                    S.op("dve", lambda E: E.tensor_scalar(out=sE[:, :, 0], in0=sE[:, :, 0], scalar1=0.5, scalar2=None, op0=ALU.mult), reads=[B_sE], writes=[B_sE])
                    S.op("dve", lambda E: E.tensor_scalar(out=sE[:, :, 6], in0=t24[:, :, 0], scalar1=-1.0, scalar2=None, op0=ALU.mult), reads=[B_t24], writes=[B_sE])
                    for h in range(8):
                        S.op("act", lambda E, h=h: E.activation(out=jkE[:], in_=t24[:, h, 0:16], func=AF.Exp, bias=sE[:, h, 6:7],
                                                                accum_out=sE[:, h, 2:3]), reads=[B_t24, B_sE], writes=[B_jkE, B_sE])
                    S.op("act", lambda E: E.activation(out=sE[:, :, 3], in_=sE[:, :, 2], func=AF.Ln), reads=[B_sE], writes=[B_sE])
                    S.op("dve", lambda E: E.tensor_tensor(out=sE[:, :, 4], in0=sE[:, :, 0], in1=sE[:, :, 6], op=ALU.add), reads=[B_sE], writes=[B_sE])
                    S.op("dve", lambda E: E.tensor_tensor(out=sE[:, :, 4], in0=sE[:, :, 4], in1=sE[:, :, 3], op=ALU.subtract), reads=[B_sE], writes=[B_sE])
                    S.op("act", lambda E: E.activation(out=sE[:, :, 5], in_=sE[:, :, 4], func=AF.Exp), reads=[B_sE], writes=[B_sE])
                    sc4 = sc[:].rearrange("p (h c) k -> p h c k", c=2)
                    S.op("dve", lambda E, sc4=sc4, tt=tt: E.tensor_tensor(out=A1all[:, tt, :, :], in0=sc4[:, :, 0, :],
                                                                          in1=sE[:, :, 0:1].broadcast_to([128, 8, 128]), op=ALU.subtract),
                         reads=[B_sc, B_sE], writes=[B_gin])
                    S.op("act", lambda E, sc4=sc4, tt=tt: E.copy(out=s2all[:, tt, :, :], in_=sc4[:, :, 1, :]), reads=[B_sc], writes=[B_gin])
                    S.op("dve", lambda E, tt=tt: E.tensor_copy(out=wAll[:, tt, :], in_=sE[:, :, 5]), reads=[B_sE], writes=[B_gin])
                S.barrier()
            with ExitStack() as c2:
                ut = [sb(c2, "ut%d" % i, [128, 32, 128], BF16) for i in range(3)]; B_ut = [Buf() for _ in range(3)]
                gact = [sb(c2, "gact%d" % i, [128, TO], BF16) for i in range(2)]; B_gact = [Buf(), Buf()]
                NYB = 4
                Yb = [sb(c2, "Yb%d" % i, [128, 8, 128]) for i in range(NYB)]; B_Yb = [Buf() for _ in range(NYB)]
                Eb = [sb(c2, "Eb%d" % i, [128, 8, 128], BF16) for i in range(NYB)]; B_Eb = [Buf() for _ in range(NYB)]
                Gb = [sb(c2, "Gb%d" % i, [128, 8, 128], BF16) for i in range(3)]; B_Gb = [Buf() for _ in range(3)]
                wt = [sb(c2, "wt%d" % i, [128, TO], BF16) for i in range(2)]; B_wt = [Buf(), Buf()]
                P_a = [ps(c2, "P_a%d" % i, [128, 512]) for i in range(4)]; B_Pa = [PB() for _ in range(4)]
                P_g = [ps(c2, "P_g%d" % i, [128, 512]) for i in range(4)]; B_Pg = [PB() for _ in range(4)]
                NCH = int(_os0.environ.get("KNCH", "128"))
                yi = 0; gi = 0
                POOL_SET = [int(v) for v in _os0.environ.get("KPOOLSET", "1,4,6").split(",") if v != ""]
                STT_SET = [int(v) for v in _os0.environ.get("KSTTSET", "0,1,2,3,4,5,6,7").split(",") if v != ""]
                Dg = sb(c2, "Dg", [128, 8, 8, 128], BF16)
                for tt in range(8):
                    for h in range(8):
                        S.op("dve", lambda E, tt=tt, h=h: E.tensor_scalar(out=Dg[:, tt, h, :], in0=identF2[:], scalar1=wAll[:, tt, h:h + 1], scalar2=None,
                                                                          op0=ALU.mult), reads=[B_idf, B_gin], writes=[B_gin])
                gbuf = {}

                def load_u(i):
                    S.dma("pool", ut[i % 3][:], UTt[i], writes=[B_ut[i % 3]])

                def stage1_mm(i, grp):
                    u_ = ut[i % 3]; Bu = B_ut[i % 3]
                    th = grp // 4
                    pa = P_a[(i % 2) * 2 + th]; Bpa = B_Pa[(i % 2) * 2 + th]
                    for kk in range((grp % 4) * 8, (grp % 4) * 8 + 8):
                        S.op("pe", lambda E, pa=pa, th=th, kk=kk, u_=u_: E.matmul(pa[:], lhsT=u_[:, kk, :], rhs=hnT[:, kk, th * 512:(th + 1) * 512],
                                                                                 start=(kk == 0), stop=(kk == 31)), reads=[Bu, B_hn], writes=[Bpa])

                def unit_pre(i, tt):
                    k = i * 8 + tt
                    y_ = Yb[k % NYB]; By = B_Yb[k % NYB]; e_ = Eb[k % NYB]; Be = B_Eb[k % NYB]
                    g_ = Gb[k % 3]; Bg = B_Gb[k % 3]
                    gbuf[(i, tt)] = (g_, Bg)
                    S.op("pool" if tt in POOL_SET else "dve", lambda E, y_=y_, tt=tt, i=i: E.tensor_tensor(
                        out=y_[:], in0=s2all[:, tt, :, :], in1=A1all[:, tt, :, i:i + 1].broadcast_to([128, 8, 128]), op=ALU.add),
                        reads=[B_gin], writes=[By])
                    if tt in STT_SET:
                        S.op("act", lambda E, y_=y_, e_=e_: E.activation(out=e_[:], in_=y_[:], func=AF.Exp), reads=[By], writes=[Be])
                        S.op("dve", lambda E, y_=y_, e_=e_, g_=g_: E.scalar_tensor_tensor(out=g_[:], in0=y_[:], scalar=0.0, in1=e_[:],
                                                                                          op0=ALU.is_ge, op1=ALU.mult), reads=[By, Be], writes=[Bg])
                    else:
                        S.op("act", lambda E, y_=y_: E.activation(out=y_[:], in_=y_[:], func=AF.Prelu, alpha=1e30), reads=[By], writes=[By])
                        S.op("act", lambda E, y_=y_, g_=g_: E.activation(out=g_[:], in_=y_[:], func=AF.Exp), reads=[By], writes=[Bg])

                def unit_mm(i, tt):
                    g_, Bg = gbuf.pop((i, tt))
                    pb = (i % 2) * 2
                    pg = P_g[pb + tt // 4]; Bpg = B_Pg[pb + tt // 4]
                    for h in range(8):
                        S.op("pe", lambda E, pg=pg, g_=g_, h=h, tt=tt: E.matmul(pg[:, (tt % 4) * 128:(tt % 4 + 1) * 128], lhsT=g_[:, h, :],
                                                                                rhs=Dg[:, tt, h, :], start=(h == 0), stop=(h == 7)),
                             reads=[Bg, B_gin], writes=[Bpg])

                def stage3(i):
                    pb = (i % 2) * 2
                    ga = gact[i % 2]; Bga = B_gact[i % 2]
                    w_ = wt[i % 2]; Bwt = B_wt[i % 2]
                    for th in range(2):
                        S.op("act", lambda E, pb=pb, th=th, ga=ga: E.activation(out=ga[:, th * 512:(th + 1) * 512], in_=P_a[pb + th][:],
                                                                                func=AF.Gelu_apprx_tanh), reads=[B_Pa[pb + th]], writes=[Bga])
                    for th in range(2):
                        S.op("dve", lambda E, w_=w_, th=th, pb=pb, ga=ga: E.tensor_tensor(out=w_[:, th * 512:(th + 1) * 512], in0=P_g[pb + th][:],
                                                                                         in1=ga[:, th * 512:(th + 1) * 512], op=ALU.mult),
                             reads=[B_Pg[pb + th], Bga], writes=[Bwt])
                    S.dma("sp", s_WT[i], w_[:], reads=[Bwt], writes=[B_swt])

                load_u(0)
                if NCH > 1:
                    load_u(1)
                for grp in range(8):
                    stage1_mm(0, grp)
                for i in range(NCH):
                    if i + 2 < NCH:
                        load_u(i + 2)
                    for tt in range(8):
                        unit_pre(i, tt)
                        if i + 1 < NCH:
                            stage1_mm(i + 1, tt)
                        if tt >= 1:
                            unit_mm(i, tt - 1)
                    unit_mm(i, 7)
                    stage3(i)
                S.barrier()
          with ExitStack() as c:
            WTr = sb(c, "WTr", [128, 128, 512], BF16); B_WTr = Buf()
            vt = [sb(c, "vt%d" % i, [128, 4, 1024], BF16) for i in range(3)]; B_vt = [Buf() for _ in range(3)]
            hres = [sb(c, "hres%d" % i, [128, 512]) for i in range(3)]; B_hres = [Buf() for _ in range(3)]
            P_o = [ps(c, "P_o%d" % i, [128, 512]) for i in range(8)]; B_Po = [PB() for _ in range(8)]
            V_v = Vexp.rearrange("(i j) d -> j i d", j=128)
            wt_v = s_WT.rearrange("i j t -> j i t")
            vi = 0; hi_ = 0
            for tb in range(0 if "3" in SKIP else 2):
                for g in range(8):
                    S.dma("sp", WTr[:, g * 16:(g + 1) * 16, :], wt_v[:, g * 16:(g + 1) * 16, tb * 512:(tb + 1) * 512],
                          reads=[B_swt], writes=[B_WTr])
                for dq in range(4):
                    for ig in range(NCH // 4):
                        v_ = vt[vi % 3]; Bv = B_vt[vi % 3]; vi += 1
                        S.dma("pool", v_[:], V_v[:, ig * 4:(ig + 1) * 4, dq * 1024:(dq + 1) * 1024], writes=[Bv])
                        for ii in range(4):
                            i = ig * 4 + ii
                            for tt in range(4):
                                for hf in range(2):
                                    S.op("pe", lambda E, tt=tt, hf=hf, i=i, ii=ii, v_=v_: E.matmul(
                                        P_o[tt * 2 + hf][:], lhsT=WTr[:, i, tt * 128:(tt + 1) * 128], rhs=v_[:, ii, hf * 512:(hf + 1) * 512],
                                        start=(i == 0), stop=(i == NCH - 1)), reads=[B_WTr, Bv], writes=[B_Po[tt * 2 + hf]])
                    for tt in range(4):
                        for hf in range(2):
                            t0 = tb * 512 + tt * 128
                            c0 = dq * 1024 + hf * 512
                            h_ = hres[hi_ % 3]; Bh = B_hres[hi_ % 3]; hi_ += 1
                            S.dma("sp", h_[:], out[t0:t0 + 128, c0:c0 + 512], reads=[B_out], writes=[Bh])
                            S.op("dve", lambda E, h_=h_, tt=tt, hf=hf: E.tensor_tensor(out=h_[:], in0=P_o[tt * 2 + hf][:], in1=h_[:], op=ALU.add),
                                 reads=[B_Po[tt * 2 + hf], Bh], writes=[Bh])
                            S.dma("sp", out[t0:t0 + 128, c0:c0 + 512], h_[:], reads=[Bh], writes=[B_fin])
            S.barrier()

        S.barrier()
        with nc.Block() as block:
            S.emit(block)
    return nc


def _prep_inputs(inputs):
    g = {k: np.asarray(v) for k, v in inputs.items()}
    w_in = g["w_in"][0]
    sp = np.cumsum([0, 1024, 512, 64, 2048, 4096, 32, 32])
    c_q, c_kv, k_rope, z, xbc, dtf, dtb = [w_in[:, sp[i]:sp[i + 1]] for i in range(7)]
    xs, bs, cs = xbc[:, :2048], xbc[:, 2048:3072], xbc[:, 3072:4096]
    w_perm = [np.ascontiguousarray(np.concatenate([c_kv, k_rope, a, b, xs, bs, cs, c_q, z], axis=1))
              for (a, b) in ((dtf, dtb), (dtb, dtf))]
    ident = np.eye(128, dtype=np.float32)
    invf = (10000.0 ** (-(np.arange(32, dtype=np.float32) / np.float32(32)))).astype(np.float32)
    cwm = g["conv_w"][0][:, 0, :]
    cw = [np.ascontiguousarray(cwm.T), np.ascontiguousarray(cwm[::-1].T)]
    kk, ll = np.meshgrid(np.arange(128), np.arange(128), indexing="ij")
    tri = np.stack([(kk <= ll), (kk >= ll), np.where(kk <= ll, 0.0, -30000.0), np.where(kk >= ll, 0.0, -30000.0)]).astype(np.float32)
    skT = np.ascontiguousarray(g["sub_keys"][0].reshape(16, 128, 128).transpose(0, 2, 1))
    U = g["expert_u"][0]
    UTt = np.ascontiguousarray(U.reshape(128, 128, 32, 128).transpose(0, 3, 2, 1))
    maps = []
    for c in range(8):
        b, half = c // 2, c % 2
        xl = g["x"][b]
        if half:
            xl = xl[::-1]
        pl = g["positions"][b].astype(np.int32)
        if half:
            pl = pl[::-1]
        m = {"x": np.ascontiguousarray(xl), "norm_mix": g["norm_mix"][0], "w_in": w_perm[half], "ident": ident,
             "pos": np.ascontiguousarray(pl), "invf": invf, "q_a_norm": g["q_a_norm"][0], "w_uq": g["w_uq"][0],
             "kv_a_norm": g["kv_a_norm"][0], "w_ukv": g["w_ukv"][0], "q_norm": g["q_norm"][0], "k_norm": g["k_norm"][0],
             "aon": np.ascontiguousarray(g["attn_out_norm"][0].reshape(-1)),
             "conv_w": cw[half], "conv_b": g["conv_b"][0],
             "a_log": np.concatenate([g["a_log_fwd"][0], g["a_log_bwd"][0]][::(-1 if half else 1)]),
             "dt_bias": np.concatenate([g["dt_bias_fwd"][0], g["dt_bias_bwd"][0]][::(-1 if half else 1)]),
             "d_skip": g["d_skip"][0], "son": g["ssm_out_norm"][0], "tri": tri,
             "w_out": g["w_out"][0], "norm_ffn": g["norm_ffn"][0], "w_query": g["w_query"][0],
             "skT": skT, "UTt": UTt, "Vexp": g["expert_v"][0]}
        maps.append(m)
    return maps


def kernel(**inputs):
    maps = _prep_inputs(inputs)
    nc = build()
    res = run_bass_kernel_spmd(nc, maps, core_ids=list(range(8)))
    outp = np.zeros((4, 2048, D), np.float32)
    for c in range(8):
        b, half = c // 2, c % 2
        o = res.results[c]["out"]
        if half:
            outp[b, 1024:] = o[::-1]
        else:
            outp[b, :1024] = o
    return outp
```

```python
from contextlib import ExitStack
import numpy as np
import concourse.bass as bass
import concourse.mybir as mybir
from concourse.bass_utils import run_bass_kernel_spmd

F32 = mybir.dt.float32
BF16 = mybir.dt.bfloat16
I32 = mybir.dt.int32
AF = mybir.ActivationFunctionType
ALU = mybir.AluOpType
AX = mybir.AxisListType

EPS = 1e-6
D = 4096
T = 2048
TO = 1024
NH = 16
O_KV, O_ROPE, O_DTF, O_DTB, O_X, O_B, O_C, O_Q, O_Z, O_END = 0, 512, 576, 608, 640, 2688, 3712, 4736, 5760, 7808


import types


def _snap(fn):
    if fn.__closure__ is None:
        return fn
    cells = []
    for cl in fn.__closure__:
        try:
            cells.append(types.CellType(cl.cell_contents))
        except ValueError:
            cells.append(cl)
    g = types.FunctionType(fn.__code__, fn.__globals__, fn.__name__, fn.__defaults__, tuple(cells))
    g.__kwdefaults__ = fn.__kwdefaults__
    return g


class Buf:
    __slots__ = ("name", "w", "r", "excl")

    def __init__(self, name="", excl=False):
        self.name = name
        self.w = None
        self.r = {}
        self.excl = excl


def PB():
    return Buf(excl=True)


class Sched:
    NDMA = 24

    def __init__(self, nc, ctx):
        self.nc = nc
        self.engs = ["pe", "dve", "act", "pool", "sp"]
        self.sem = {}
        self.cnt = {}
        for e in self.engs:
            self.sem[e] = ctx.enter_context(nc.semaphore("s_" + e))
            self.cnt[e] = 0
        self.dsem = [ctx.enter_context(nc.semaphore("d%d" % i)) for i in range(self.NDMA)]
        self.dcnt = [0] * self.NDMA
        self.dnext = {"sp": 0, "pool": 0, "act": 0}
        self.drange = {"sp": (0, 14), "pool": (14, 22), "act": (22, 24)}
        self.seen = {e: {} for e in self.engs}
        self.prog = {e: [] for e in self.engs}

    def _semobj(self, key):
        return self.sem[key] if isinstance(key, str) else self.dsem[key]

    def _wait(self, e, key, val):
        if key == "pe" and e == "pe":
            return
        if self.seen[e].get(key, 0) >= val:
            return
        so = self._semobj(key)
        self.prog[e].append(lambda E, so=so, val=val: E.wait_ge(so, val))
        self.seen[e][key] = val

    def _deps(self, e, reads, writes):
        for b in reads:
            if b.w is not None:
                self._wait(e, *b.w)
        for b in writes:
            if b.w is not None:
                self._wait(e, *b.w)
            for k, v in b.r.items():
                self._wait(e, k, v)

    def _commit(self, ev, reads, writes):
        for b in reads:
            if b.r.get(ev[0], 0) < ev[1]:
                b.r[ev[0]] = ev[1]
        for b in writes:
            b.w = ev
            b.r = {}

    def op(self, e, fn, reads=(), writes=()):
        fn = _snap(fn)
        if any(b.excl for b in reads):
            writes = list(writes) + [b for b in reads if b.excl]
            reads = [b for b in reads if not b.excl]
        self._deps(e, reads, writes)
        self.cnt[e] += 1
        so = self.sem[e]
        self.prog[e].append(lambda E, fn=fn, so=so: fn(E).then_inc(so, 1))
        ev = (e, self.cnt[e])
        self._commit(ev, reads, writes)
        return ev

    def dma(self, q, out, in_, reads=(), writes=(), **kw):
        lo, hi = self.drange[q]
        k = lo + self.dnext[q]
        self.dnext[q] = (self.dnext[q] + 1) % (hi - lo)
        if self.dcnt[k] > 0:
            self._wait(q, k, self.dcnt[k])
        self._deps(q, reads, writes)
        self.dcnt[k] += 16
        so = self.dsem[k]
        self.prog[q].append(
            lambda E, out=out, in_=in_, kw=kw, so=so: E.dma_start(out=out, in_=in_, **kw).then_inc(so, 16))
        ev = (k, self.dcnt[k])
        self._commit(ev, reads, writes)
        return ev

    def barrier(self):
        for e in self.engs:
            for o in self.engs:
                if o != e and self.cnt[o] > 0:
                    self._wait(e, o, self.cnt[o])
            for k in range(self.NDMA):
                if self.dcnt[k] > 0:
                    self._wait(e, k, self.dcnt[k])

    def emit(self, block):
        reg = {"pe": block.tensor, "dve": block.vector, "act": block.scalar, "pool": block.gpsimd, "sp": block.sync}
        for e in self.engs:
            lst = self.prog[e]
            if not lst:
                continue

            def body(E, lst=lst):
                for f in lst:
                    f(E)
            reg[e](body)


def bcast_rows(ap_1d, nparts):
    return ap_1d.partition_broadcast(nparts)


_rope_id = [0]


def rope_ops(S, src, B_src, dst, B_dst, cs, B_cs, tile, sb, c, tag, nheads):
    _rope_id[0] += 1
    nm = "rp%d_" % _rope_id[0]
    if nheads == 1:
        shp = [128, 32]
        t1, t2 = src[:, 0:32], src[:, 32:64]
        d1, d2 = dst[:, 0:32], dst[:, 32:64]
        co, si = cs[:, tile, 0:32], cs[:, tile, 32:64]
    else:
        shp = [128, nheads, 32]
        t1, t2 = src[:, :, 0:32], src[:, :, 32:64]
        d1, d2 = dst[:, :, 0:32], dst[:, :, 32:64]
        co = cs[:, tile, 0:32].unsqueeze(1).broadcast_to(shp)
        si = cs[:, tile, 32:64].unsqueeze(1).broadcast_to(shp)
    a = sb(c, nm + "a", shp)
    b = sb(c, nm + "b", shp)
    Ba, Bb = Buf(), Buf()
    S.op("dve", lambda E: E.tensor_tensor(out=a[:], in0=t1, in1=co, op=ALU.mult), reads=[B_src, B_cs], writes=[Ba])
    S.op("dve", lambda E: E.tensor_tensor(out=b[:], in0=t2, in1=si, op=ALU.mult), reads=[B_src, B_cs], writes=[Bb])
    S.op("dve", lambda E: E.tensor_tensor(out=d1, in0=a[:], in1=b[:], op=ALU.subtract), reads=[Ba, Bb], writes=[B_dst])
    S.op("dve", lambda E: E.tensor_tensor(out=a[:], in0=t1, in1=si, op=ALU.mult), reads=[B_src, B_cs, B_dst], writes=[Ba])
    S.op("dve", lambda E: E.tensor_tensor(out=b[:], in0=t2, in1=co, op=ALU.mult), reads=[B_src, B_cs, B_dst], writes=[Bb])
    S.op("dve", lambda E: E.tensor_tensor(out=d2, in0=a[:], in1=b[:], op=ALU.add), reads=[Ba, Bb], writes=[B_dst])

class K:
    pass


def build(stage=99):
    nc = bass.Bass("TRN2", target_bir_lowering=False)
    k = K()
    k.nc = nc

    def din(name, shape, dt=F32):
        return nc.dram_tensor(name, list(shape), dt, kind="ExternalInput").ap()

    def dscr(name, shape, dt=F32):
        return nc.dram_tensor(name, list(shape), dt, kind="Internal").ap()

    x = din("x", [T, D])
    norm_mix = din("norm_mix", [D])
    w_in = din("w_in", [D, O_END])
    ident_in = din("ident", [128, 128])
    pos_in = din("pos", [T], I32)
    invf_in = din("invf", [32])
    q_a_norm = din("q_a_norm", [1024])
    w_uq = din("w_uq", [1024, 3072])
    kv_a_norm = din("kv_a_norm", [512])
    w_ukv = din("w_ukv", [512, 4096])
    q_norm = din("q_norm", [192])
    k_norm = din("k_norm", [192])
    aon = din("aon", [2048])
    s_mix = dscr("s_mix", [TO, D], BF16)
    conv_w = din("conv_w", [4096, 5])
    conv_b = din("conv_b", [4096])
    a_log = din("a_log", [64])
    dt_bias = din("dt_bias", [64])
    d_skip = din("d_skip", [32])
    son = din("son", [2048])
    tri_in = din("tri", [4, 128, 128])
    w_out = din("w_out", [D, D])
    norm_ffn = din("norm_ffn", [D])
    w_query = din("w_query", [D, 2048])
    skT_in = din("skT", [16, 128, 128])
    UTt = din("UTt", [128, 128, 32, 128])
    Vexp = din("Vexp", [16384, D])
    s_hnT = dscr("s_hnT", [D, TO], BF16)
    s_WT = dscr("s_WT", [128, 128, TO], BF16)
    out = nc.dram_tensor("out", [TO, D], F32, kind="ExternalOutput").ap()

    s_kv = dscr("s_kv", [T, 640])
    s_xT = dscr("s_xT", [4096, T], BF16)
    s_q = dscr("s_q", [TO, 1024])
    s_z = dscr("s_z", [TO, 2048])

    dbg = {}
    if stage == 1:
        dbg["d_kv"] = nc.dram_tensor("d_kv", [T, 640], F32, kind="ExternalOutput").ap()
        dbg["d_xT"] = nc.dram_tensor("d_xT", [4096, T], BF16, kind="ExternalOutput").ap()
        dbg["d_q"] = nc.dram_tensor("d_q", [TO, 1024], F32, kind="ExternalOutput").ap()
        dbg["d_z"] = nc.dram_tensor("d_z", [TO, 2048], F32, kind="ExternalOutput").ap()
        s_kv, s_xT, s_q, s_z = dbg["d_kv"], dbg["d_xT"], dbg["d_q"], dbg["d_z"]

    if stage == 4:
        dbg["d_h"] = nc.dram_tensor("d_h", [TO, D], F32, kind="ExternalOutput").ap()
    if stage in (2, 3):
        dbg["d_mix"] = nc.dram_tensor("d_mix", [TO, D], BF16, kind="ExternalOutput").ap()
        s_mix = dbg["d_mix"]

    with ExitStack() as ctx:
        S = Sched(nc, ctx)
        uid = [0]

        def sb(c, name, shape, dt=F32):
            uid[0] += 1
            return c.enter_context(nc.sbuf_tensor("%s_%d" % (name, uid[0]), list(shape), dt))

        def ps(c, name, shape, dt=F32):
            uid[0] += 1
            return c.enter_context(nc.psum_tensor("%s_%d" % (name, uid[0]), list(shape), dt))

        ident = sb(ctx, "ident_sb", [128, 128], BF16)
        B_ident = Buf("ident")
        S.dma("pool", ident[:], ident_in, writes=[B_ident])
        outbufs = []
        B_mix = Buf()
        B_swt = Buf()
        B_fin = Buf()

        B_scr = Buf()
        import os as _os0
        SKIP = _os0.environ.get("KSKIP", "")
        with ExitStack() as c:
          if "A" not in SKIP:
              gain = sb(c, "gain", [128, D])
              B_gain = Buf()
              S.dma("sp", gain[:], bcast_rows(norm_mix, 128), writes=[B_gain])
              xt = [sb(c, "xt%d" % i, [128, D]) for i in range(2)]
              B_xt = [Buf() for _ in range(2)]
              junk = sb(c, "junk", [128, D], BF16)
              B_junk = Buf()
              xb = sb(c, "xb", [128, D], BF16)
              B_xb = Buf()
              st = sb(c, "stat", [128, 8])
              B_st = Buf()
              xnT = sb(c, "xnT", [128, 32, 512], BF16)
              B_xnT = [Buf() for _ in range(4)]
              wblk = [sb(c, "wblk%d" % i, [128, 32, 512], BF16) for i in range(2)]
              B_w = [Buf() for _ in range(2)]
              ost = [sb(c, "ost%d" % i, [128, 512]) for i in range(3)]
              B_ost = [Buf() for _ in range(3)]
              ostb = [sb(c, "ostb%d" % i, [128, 512], BF16) for i in range(3)]
              B_ostb = [Buf() for _ in range(3)]
              tp = [ps(c, "tp%d" % i, [128, 8, 128], BF16) for i in range(2)]
              B_tp = [PB() for _ in range(2)]
              acc = [ps(c, "acc%d" % i, [128, 512]) for i in range(4)]
              B_acc = [PB() for _ in range(4)]
              w_v = w_in.rearrange("(k p) c -> p k c", p=128)
              x_v = x.rearrange("(n p) d -> n p d", p=128)

              blocks = [(O_KV, 512, "kv"), (O_ROPE, 128, "kv")]
              blocks += [(O_X + 512 * i, 512, "feat") for i in range(8)]
              nb_other = len(blocks)
              blocks += [(O_Q + 512 * i, 512, "q") for i in range(2)]
              blocks += [(O_Z + 512 * i, 512, "z") for i in range(4)]
              wi = 0
              ai = 0
              oi = 0
              for tb in range(4):
                  for tt in range(4):
                      tile = tb * 4 + tt
                      xi = tile % 2
                      S.dma("sp", xt[xi][:], x_v[tile], writes=[B_xt[xi]])
                      S.op("act", lambda E, xi=xi: E.activation(out=junk[:], in_=xt[xi][:], func=AF.Square,
                                                                accum_out=st[:, 0:1]),
                           reads=[B_xt[xi]], writes=[B_junk, B_st])
                      S.op("dve", lambda E: E.tensor_scalar(out=st[:, 1:2], in0=st[:, 0:1], scalar1=1.0 / D,
                                                            scalar2=EPS, op0=ALU.mult, op1=ALU.add),
                           reads=[B_st], writes=[B_st])
                      S.op("act", lambda E: E.activation(out=st[:, 2:3], in_=st[:, 1:2], func=AF.Sqrt),
                           reads=[B_st], writes=[B_st])
                      S.op("dve", lambda E: E.reciprocal(out=st[:, 3:4], in_=st[:, 2:3]), reads=[B_st], writes=[B_st])
                      S.op("dve", lambda E, xi=xi: E.scalar_tensor_tensor(out=xb[:], in0=xt[xi][:], scalar=st[:, 3:4],
                                                                          in1=gain[:], op0=ALU.mult, op1=ALU.mult),
                           reads=[B_xt[xi], B_st, B_gain], writes=[B_xb])
                      for g in range(4):
                          ti = g % 2
                          for j in range(8):
                              kk = g * 8 + j
                              S.op("pe", lambda E, ti=ti, j=j, kk=kk: E.transpose(out=tp[ti][:, j, :],
                                                                                 in_=xb[:, kk * 128:(kk + 1) * 128],
                                                                                 identity=ident[:]),
                                   reads=[B_xb, B_ident], writes=[B_tp[ti]])
                          eng = "act" if g % 2 == 0 else "dve"
                          if eng == "act":
                              S.op("act", lambda E, ti=ti, g=g, tt=tt: E.copy(
                                  out=xnT[:, g * 8:(g + 1) * 8, tt * 128:(tt + 1) * 128], in_=tp[ti][:]),
                                  reads=[B_tp[ti]], writes=[B_xnT[tt]])
                          else:
                              S.op("dve", lambda E, ti=ti, g=g, tt=tt: E.tensor_copy(
                                  out=xnT[:, g * 8:(g + 1) * 8, tt * 128:(tt + 1) * 128], in_=tp[ti][:]),
                                  reads=[B_tp[ti]], writes=[B_xnT[tt]])
                  blks = blocks if tb < 2 else blocks[:nb_other]
                  for (c0, cw, kind) in blks:
                      wb = wblk[wi % 2]
                      Bw = B_w[wi % 2]
                      wi += 1
                      S.dma("pool", wb[:, :, 0:cw], w_v[:, :, c0:c0 + cw], writes=[Bw])
                      if kind == "feat":
                          for cc in range(cw // 128):
                              a = acc[ai % 4]
                              Ba = B_acc[ai % 4]
                              ai += 1
                              for kk in range(32):
                                  S.op("pe", lambda E, a=a, wb=wb, kk=kk, cc=cc: E.matmul(
                                      a[:], lhsT=wb[:, kk, cc * 128:(cc + 1) * 128], rhs=xnT[:, kk, :],
                                      start=(kk == 0), stop=(kk == 31)),
                                      reads=[Bw] + B_xnT, writes=[Ba])
                              o = ostb[oi % 3]
                              Bo = B_ostb[oi % 3]
                              if oi % 2 == 0:
                                  S.op("act", lambda E, o=o, a=a: E.copy(out=o[:], in_=a[:]), reads=[Ba], writes=[Bo])
                              else:
                                  S.op("dve", lambda E, o=o, a=a: E.tensor_copy(out=o[:], in_=a[:]), reads=[Ba], writes=[Bo])
                              oi += 1
                              r0 = c0 - O_X + cc * 128
                              S.dma("sp", s_xT[r0:r0 + 128, tb * 512:(tb + 1) * 512], o[:], reads=[Bo], writes=[B_scr])
                      else:
                          for tt in range(4):
                              a = acc[ai % 4]
                              Ba = B_acc[ai % 4]
                              ai += 1
                              for kk in range(32):
                                  S.op("pe", lambda E, a=a, wb=wb, kk=kk, tt=tt, cw=cw: E.matmul(
                                      a[:, 0:cw], lhsT=xnT[:, kk, tt * 128:(tt + 1) * 128], rhs=wb[:, kk, 0:cw],
                                      start=(kk == 0), stop=(kk == 31)),
                                      reads=[Bw, B_xnT[tt]], writes=[Ba])
                              o = ost[oi % 3]
                              Bo = B_ost[oi % 3]
                              if oi % 2 == 0:
                                  S.op("act", lambda E, o=o, a=a, cw=cw: E.copy(out=o[:, 0:cw], in_=a[:, 0:cw]),
                                       reads=[Ba], writes=[Bo])
                              else:
                                  S.op("dve", lambda E, o=o, a=a, cw=cw: E.tensor_copy(out=o[:, 0:cw], in_=a[:, 0:cw]),
                                       reads=[Ba], writes=[Bo])
                              oi += 1
                              t0 = tb * 512 + tt * 128
                              if kind == "kv":
                                  dst = s_kv[t0:t0 + 128, c0:c0 + cw]
                              elif kind == "q":
                                  dst = s_q[t0:t0 + 128, c0 - O_Q:c0 - O_Q + cw]
                              else:
                                  dst = s_z[t0:t0 + 128, c0 - O_Z:c0 - O_Z + cw]
                              S.dma("sp", dst, o[:, 0:cw], reads=[Bo], writes=[B_scr])
          outbufs.append(B_scr)
          S.barrier()

        if stage >= 2 and "B" not in SKIP:
          with ExitStack() as c:
            TWO_PI = 2.0 * np.pi
            C1 = 6.28125
            C2 = float(np.float32(TWO_PI - C1).view(np.uint32) & np.uint32(0xFFFFF000)) if False else 0.0019350051879882812
            C3 = float(TWO_PI - C1 - C2)
            MAGIC = 12582912.0
            SCALE = 192.0 ** -0.5
            qan = sb(c, "qan", [128, 1024]); kvan = sb(c, "kvan", [128, 512])
            qn_bc = sb(c, "qn_bc", [128, 192]); kn_bc = sb(c, "kn_bc", [128, 192])
            aon_bc = sb(c, "aon_bc", [128, 2048]); invf = sb(c, "invf_sb", [128, 32])
            B_const = Buf()
            for dst, src in ((qan, q_a_norm), (kvan, kv_a_norm), (qn_bc, q_norm), (kn_bc, k_norm), (aon_bc, aon), (invf, invf_in)):
                S.dma("sp", dst[:], src.partition_broadcast(128), writes=[B_const])
            posi = sb(c, "posi", [128, 16], I32)
            S.dma("sp", posi[:], pos_in.rearrange("(n p) -> p n", p=128), writes=[B_const], allow_slow_non_contiguous=True) if False else None
            posf = sb(c, "posf", [128, 16])
            ang = sb(c, "ang", [128, 16, 32]); kk_t = sb(c, "kk_t", [128, 16, 32]); rr = sb(c, "rr", [128, 16, 32])
            cs = sb(c, "cs", [128, 16, 64])
            B_cs = Buf()
            pos_t = sb(c, "pos_t", [16, 128], I32)
            for n in range(16):
                S.dma("sp", posi[:, n:n + 1], pos_in[n * 128:(n + 1) * 128].rearrange("(p o) -> p o", o=1), writes=[B_const])
            S.op("dve", lambda E: E.tensor_copy(out=posf[:], in_=posi[:]), reads=[B_const], writes=[B_cs])
            S.op("dve", lambda E: E.tensor_tensor(out=ang[:], in0=posf[:].unsqueeze(2).broadcast_to([128, 16, 32]),
                                                  in1=invf[:].unsqueeze(1).broadcast_to([128, 16, 32]), op=ALU.mult),
                 reads=[B_const, B_cs], writes=[B_cs])
            for which, shift in ((1, 0.0), (0, 0.25)):
                S.op("dve", lambda E, shift=shift: E.tensor_scalar(out=kk_t[:], in0=ang[:], scalar1=1.0 / TWO_PI, scalar2=shift,
                                                                   op0=ALU.mult, op1=ALU.add), reads=[B_cs], writes=[B_cs])
                S.op("dve", lambda E: E.tensor_scalar(out=kk_t[:], in0=kk_t[:], scalar1=MAGIC, scalar2=None, op0=ALU.add),
                     reads=[B_cs], writes=[B_cs])
                S.op("dve", lambda E: E.tensor_scalar(out=kk_t[:], in0=kk_t[:], scalar1=-MAGIC, scalar2=None, op0=ALU.add),
                     reads=[B_cs], writes=[B_cs])
                S.op("dve", lambda E: E.scalar_tensor_tensor(out=rr[:], in0=kk_t[:], scalar=-C1, in1=ang[:], op0=ALU.mult, op1=ALU.add),
                     reads=[B_cs], writes=[B_cs])
                S.op("dve", lambda E: E.scalar_tensor_tensor(out=rr[:], in0=kk_t[:], scalar=-C2, in1=rr[:], op0=ALU.mult, op1=ALU.add),
                     reads=[B_cs], writes=[B_cs])
                S.op("dve", lambda E: E.scalar_tensor_tensor(out=rr[:], in0=kk_t[:], scalar=-C3, in1=rr[:], op0=ALU.mult, op1=ALU.add),
                     reads=[B_cs], writes=[B_cs])
                if shift != 0.0:
                    S.op("dve", lambda E: E.tensor_scalar(out=rr[:], in0=rr[:], scalar1=float(np.pi / 2), scalar2=None, op0=ALU.add),
                         reads=[B_cs], writes=[B_cs])
                S.op("dve", lambda E: E.tensor_scalar(out=rr[:], in0=rr[:], scalar1=3.1415925, scalar2=-3.1415925,
                                                      op0=ALU.min, op1=ALU.max), reads=[B_cs], writes=[B_cs])
                S.op("act", lambda E, which=which: E.activation(out=cs[:, :, which * 32:(which + 1) * 32], in_=rr[:], func=AF.Sin),
                     reads=[B_cs], writes=[B_cs])

            ckvT = sb(c, "ckvT", [128, 4, T], BF16); B_ckvT = Buf()
            cqT = sb(c, "cqT", [128, 8, TO], BF16); B_cqT = Buf()
            krr = sb(c, "krr", [128, 16, 64]); B_krr = Buf()
            ssr = sb(c, "ssr", [128, 16]); B_ssr = Buf()
            stB = sb(c, "stB", [128, 16]); B_stB = Buf()
            with ExitStack() as c2:
                lt = [sb(c2, "lt%d" % i, [128, 1024]) for i in range(2)]; B_lt = [Buf(), Buf()]
                jk = sb(c2, "jkB", [128, 1024], BF16); B_jk = Buf()
                nb = sb(c2, "nbB", [128, 1024], BF16); B_nb = Buf()
                tq = sb(c2, "tqB", [128, 64]); B_tq = Buf()
                tpB = [ps(c2, "tpB%d" % i, [128, 8, 128], BF16) for i in range(2)]; B_tpB = [PB(), PB()]
                for tile in range(16):
                    li = tile % 2
                    S.dma("sp", lt[li][:, 0:576], s_kv[tile * 128:(tile + 1) * 128, 0:576], reads=[B_scr], writes=[B_lt[li]])
                    S.op("act", lambda E, li=li: E.activation(out=jk[:, 0:512], in_=lt[li][:, 0:512], func=AF.Square,
                                                              accum_out=stB[:, 0:1]), reads=[B_lt[li]], writes=[B_jk, B_stB])
                    S.op("act", lambda E, li=li, tile=tile: E.activation(out=jk[:, 512:576], in_=lt[li][:, 512:576], func=AF.Square,
                                                                         accum_out=ssr[:, tile:tile + 1]),
                         reads=[B_lt[li]], writes=[B_jk, B_ssr])
                    S.op("dve", lambda E: E.tensor_scalar(out=stB[:, 1:2], in0=stB[:, 0:1], scalar1=1.0 / 512, scalar2=EPS,
                                                          op0=ALU.mult, op1=ALU.add), reads=[B_stB], writes=[B_stB])
                    S.op("act", lambda E: E.activation(out=stB[:, 2:3], in_=stB[:, 1:2], func=AF.Sqrt), reads=[B_stB], writes=[B_stB])
                    S.op("dve", lambda E: E.reciprocal(out=stB[:, 3:4], in_=stB[:, 2:3]), reads=[B_stB], writes=[B_stB])
                    S.op("dve", lambda E, li=li: E.scalar_tensor_tensor(out=nb[:, 0:512], in0=lt[li][:, 0:512], scalar=stB[:, 3:4],
                                                                        in1=kvan[:], op0=ALU.mult, op1=ALU.mult),
                         reads=[B_lt[li], B_stB, B_const], writes=[B_nb])
                    ti = tile % 2
                    for j in range(4):
                        S.op("pe", lambda E, ti=ti, j=j: E.transpose(out=tpB[ti][:, j, :], in_=nb[:, j * 128:(j + 1) * 128],
                                                                     identity=ident[:]), reads=[B_nb, B_ident], writes=[B_tpB[ti]])
                    S.op("act", lambda E, ti=ti, tile=tile: E.copy(out=ckvT[:, :, tile * 128:(tile + 1) * 128], in_=tpB[ti][:, 0:4, :]),
                         reads=[B_tpB[ti]], writes=[B_ckvT])
                    S.op("dve", lambda E, li=li: E.tensor_tensor(out=tq[:], in0=lt[li][:, 512:576], in1=kn_bc[:, 128:192], op=ALU.mult),
                         reads=[B_lt[li], B_const], writes=[B_tq])
                    rope_ops(S, tq, B_tq, krr[:, tile, :], B_krr, cs, B_cs, tile, sb, c2, "k%d" % tile, nheads=1)
                for tile in range(8):
                    li = tile % 2
                    S.dma("sp", lt[li][:], s_q[tile * 128:(tile + 1) * 128, :], reads=[B_scr], writes=[B_lt[li]])
                    S.op("act", lambda E, li=li: E.activation(out=jk[:], in_=lt[li][:], func=AF.Square, accum_out=stB[:, 0:1]),
                         reads=[B_lt[li]], writes=[B_jk, B_stB])
                    S.op("dve", lambda E: E.tensor_scalar(out=stB[:, 1:2], in0=stB[:, 0:1], scalar1=1.0 / 1024, scalar2=EPS,
                                                          op0=ALU.mult, op1=ALU.add), reads=[B_stB], writes=[B_stB])
                    S.op("act", lambda E: E.activation(out=stB[:, 2:3], in_=stB[:, 1:2], func=AF.Sqrt), reads=[B_stB], writes=[B_stB])
                    S.op("dve", lambda E: E.reciprocal(out=stB[:, 3:4], in_=stB[:, 2:3]), reads=[B_stB], writes=[B_stB])
                    S.op("dve", lambda E, li=li: E.scalar_tensor_tensor(out=nb[:], in0=lt[li][:], scalar=stB[:, 3:4], in1=qan[:],
                                                                        op0=ALU.mult, op1=ALU.mult),
                         reads=[B_lt[li], B_stB, B_const], writes=[B_nb])
                    ti = tile % 2
                    for j in range(8):
                        S.op("pe", lambda E, ti=ti, j=j: E.transpose(out=tpB[ti][:, j, :], in_=nb[:, j * 128:(j + 1) * 128],
                                                                     identity=ident[:]), reads=[B_nb, B_ident], writes=[B_tpB[ti]])
                    S.op("act", lambda E, ti=ti, tile=tile: E.copy(out=cqT[:, :, tile * 128:(tile + 1) * 128], in_=tpB[ti][:]),
                         reads=[B_tpB[ti]], writes=[B_cqT])
                S.barrier()

            HG = 4
            KT = sb(c, "KT", [128, HG, T], BF16); B_KT = Buf()
            KTr = sb(c, "KTr", [64, HG, T], BF16); B_KTr = Buf()
            vext = sb(c, "vext", [128, 16, HG, 130], BF16); B_vext = Buf()
            QT = sb(c, "QT", [128, HG, TO], BF16); B_QT = Buf()
            QTr = sb(c, "QTr", [64, HG, TO], BF16); B_QTr = Buf()
            wkv = sb(c, "wkv", [128, 4, HG * 256], BF16); B_wkv = Buf()
            wq = sb(c, "wq", [128, 8, HG * 192], BF16); B_wq = Buf()
            S.op("pool", lambda E: E.memset(vext[:], 1.0), writes=[B_vext])
            for hg in range(NH // HG):
                S.dma("pool", wkv[:], w_ukv.rearrange("(k p) c -> p k c", p=128)[:, :, hg * HG * 256:(hg + 1) * HG * 256],
                      writes=[B_wkv])
                S.dma("pool", wq[:], w_uq.rearrange("(k p) c -> p k c", p=128)[:, :, hg * HG * 192:(hg + 1) * HG * 192],
                      writes=[B_wq])
                with ExitStack() as c2:
                    pk = [ps(c2, "pk%d" % i, [128, 512]) for i in range(4)]; B_pk = [PB() for _ in range(4)]
                    tpk = [ps(c2, "tpk%d" % i, [128, 8, 128], BF16) for i in range(2)]; B_tpk = [PB(), PB()]
                    tpr = [ps(c2, "tpr%d" % i, [128, 8, 128], BF16) for i in range(2)]; B_tpr = [PB(), PB()]
                    jk = sb(c2, "jkK", [128, 192], BF16); B_jk = Buf()
                    sk = [sb(c2, "sk%d" % i, [128, 16]) for i in range(2)]; B_sk = [Buf(), Buf()]
                    kn = [sb(c2, "kn%d" % i, [128, HG, 192], BF16) for i in range(2)]; B_kn = [Buf(), Buf()]
                    def kprep_tile(tile):
                        pi = tile % 2
                        for b in range(2):
                            for kc in range(4):
                                S.op("pe", lambda E, pi=pi, b=b, kc=kc, tile=tile: E.matmul(
                                    pk[pi * 2 + b][:], lhsT=ckvT[:, kc, tile * 128:(tile + 1) * 128],
                                    rhs=wkv[:, kc, b * 512:(b + 1) * 512], start=(kc == 0), stop=(kc == 3)),
                                    reads=[B_ckvT, B_wkv], writes=[B_pk[pi * 2 + b]])
                                yield
                        st_ = sk[pi]; Bs = B_sk[pi]
                        for hl in range(HG):
                            p_ = pk[pi * 2 + hl // 2]; Bp = B_pk[pi * 2 + hl // 2]; off = (hl % 2) * 256
                            S.op("act", lambda E, p_=p_, off=off, st_=st_, hl=hl: E.activation(
                                out=jk[:, 0:128], in_=p_[:, off:off + 128], func=AF.Square, accum_out=st_[:, hl:hl + 1]),
                                reads=[Bp], writes=[B_jk, Bs])
                            yield
                        S.op("dve", lambda E, st_=st_, tile=tile: E.tensor_scalar(
                            out=st_[:, 4:8], in0=st_[:, 0:4], scalar1=ssr[:, tile:tile + 1], scalar2=1.0 / 192,
                            op0=ALU.add, op1=ALU.mult), reads=[Bs, B_ssr], writes=[Bs])
                        yield
                        S.op("dve", lambda E, st_=st_: E.tensor_scalar(out=st_[:, 4:8], in0=st_[:, 4:8], scalar1=EPS, scalar2=None,
                                                                       op0=ALU.add), reads=[Bs], writes=[Bs])
                        yield
                        S.op("act", lambda E, st_=st_: E.activation(out=st_[:, 8:12], in_=st_[:, 4:8], func=AF.Sqrt), reads=[Bs], writes=[Bs])
                        yield
                        S.op("dve", lambda E, st_=st_: E.reciprocal(out=st_[:, 12:16], in_=st_[:, 8:12]), reads=[Bs], writes=[Bs])
                        yield
                        kn_ = kn[pi]; Bk = B_kn[pi]
                        for hl in range(HG):
                            p_ = pk[pi * 2 + hl // 2]; Bp = B_pk[pi * 2 + hl // 2]; off = (hl % 2) * 256
                            S.op("dve", lambda E, p_=p_, off=off, st_=st_, hl=hl, kn_=kn_: E.scalar_tensor_tensor(
                                out=kn_[:, hl, 0:128], in0=p_[:, off:off + 128], scalar=st_[:, 12 + hl:13 + hl], in1=kn_bc[:, 0:128],
                                op0=ALU.mult, op1=ALU.mult), reads=[Bp, Bs, B_const], writes=[Bk])
                            yield
                            S.op("dve", lambda E, st_=st_, hl=hl, kn_=kn_, tile=tile: E.tensor_scalar(
                                out=kn_[:, hl, 128:192], in0=krr[:, tile, :], scalar1=st_[:, 12 + hl:13 + hl], scalar2=None, op0=ALU.mult),
                                reads=[B_krr, Bs], writes=[Bk])
                            yield
                            S.op("act", lambda E, p_=p_, off=off, hl=hl, tile=tile: E.copy(
                                out=vext[:, tile, hl, 0:128], in_=p_[:, off + 128:off + 256]), reads=[Bp], writes=[B_vext])
                            yield
                        for hl in range(HG):
                            S.op("pe", lambda E, pi=pi, hl=hl, kn_=kn_: E.transpose(out=tpk[pi][:, hl, :], in_=kn_[:, hl, 0:128],
                                                                                   identity=ident[:]), reads=[Bk, B_ident], writes=[B_tpk[pi]])
                            yield
                            S.op("pe", lambda E, pi=pi, hl=hl, kn_=kn_: E.transpose(out=tpr[pi][0:64, hl, :], in_=kn_[:, hl, 128:192],
                                                                                   identity=ident[:]), reads=[Bk, B_ident], writes=[B_tpr[pi]])
                            yield
                        S.op("dve", lambda E, pi=pi, tile=tile: E.tensor_copy(out=KT[:, :, tile * 128:(tile + 1) * 128], in_=tpk[pi][:, 0:4, :]),
                             reads=[B_tpk[pi]], writes=[B_KT])
                        yield
                        S.op("act", lambda E, pi=pi, tile=tile: E.copy(out=KTr[:, :, tile * 128:(tile + 1) * 128], in_=tpr[pi][0:64, 0:4, :]),
                             reads=[B_tpr[pi]], writes=[B_KTr])
                        yield
                    for t0_ in range(0, 16, 2):
                        gens = [kprep_tile(t0_), kprep_tile(t0_ + 1)]
                        while gens:
                            for gen in list(gens):
                                try:
                                    next(gen)
                                except StopIteration:
                                    gens.remove(gen)
                    S.barrier()
                with ExitStack() as c2:
                    pq = [ps(c2, "pq%d" % i, [128, 512]) for i in range(4)]; B_pq = [PB() for _ in range(4)]
                    tpk = [ps(c2, "tpq%d" % i, [128, 8, 128], BF16) for i in range(2)]; B_tpk = [PB(), PB()]
                    tpr = [ps(c2, "tpqr%d" % i, [128, 8, 128], BF16) for i in range(2)]; B_tpr = [PB(), PB()]
                    jk = sb(c2, "jkQ", [128, 192], BF16); B_jk = Buf()
                    sk = [sb(c2, "sq%d" % i, [128, 16]) for i in range(2)]; B_sk = [Buf(), Buf()]
                    qg = [sb(c2, "qg%d" % i, [128, HG, 192]) for i in range(2)]; B_qg = [Buf(), Buf()]
                    qn_ = [sb(c2, "qn%d" % i, [128, HG, 192], BF16) for i in range(2)]; B_qn = [Buf(), Buf()]
                    def qprep_tile(tile):
                        pi = tile % 2
                        for b in range(2):
                            for kc in range(8):
                                S.op("pe", lambda E, pi=pi, b=b, kc=kc, tile=tile: E.matmul(
                                    pq[pi * 2 + b][:, 0:384], lhsT=cqT[:, kc, tile * 128:(tile + 1) * 128],
                                    rhs=wq[:, kc, b * 384:(b + 1) * 384], start=(kc == 0), stop=(kc == 7)),
                                    reads=[B_cqT, B_wq], writes=[B_pq[pi * 2 + b]])
                                yield
                        st_ = sk[pi]; Bs = B_sk[pi]
                        for hl in range(HG):
                            p_ = pq[pi * 2 + hl // 2]; Bp = B_pq[pi * 2 + hl // 2]; off = (hl % 2) * 192
                            S.op("act", lambda E, p_=p_, off=off, st_=st_, hl=hl: E.activation(
                                out=jk[:], in_=p_[:, off:off + 192], func=AF.Square, accum_out=st_[:, hl:hl + 1]),
                                reads=[Bp], writes=[B_jk, Bs])
                            yield
                        S.op("dve", lambda E, st_=st_: E.tensor_scalar(out=st_[:, 4:8], in0=st_[:, 0:4], scalar1=1.0 / 192, scalar2=EPS,
                                                                       op0=ALU.mult, op1=ALU.add), reads=[Bs], writes=[Bs])
                        yield
                        S.op("act", lambda E, st_=st_: E.activation(out=st_[:, 8:12], in_=st_[:, 4:8], func=AF.Sqrt), reads=[Bs], writes=[Bs])
                        yield
                        S.op("dve", lambda E, st_=st_: E.reciprocal(out=st_[:, 12:16], in_=st_[:, 8:12]), reads=[Bs], writes=[Bs])
                        yield
                        g_ = qg[pi]; Bg = B_qg[pi]; n_ = qn_[pi]; Bn = B_qn[pi]
                        for hl in range(HG):
                            p_ = pq[pi * 2 + hl // 2]; Bp = B_pq[pi * 2 + hl // 2]; off = (hl % 2) * 192
                            S.op("dve", lambda E, p_=p_, off=off, st_=st_, hl=hl, g_=g_: E.scalar_tensor_tensor(
                                out=g_[:, hl, :], in0=p_[:, off:off + 192], scalar=st_[:, 12 + hl:13 + hl], in1=qn_bc[:],
                                op0=ALU.mult, op1=ALU.mult), reads=[Bp, Bs, B_const], writes=[Bg])
                            yield
                        S.op("dve", lambda E, g_=g_, n_=n_: E.tensor_copy(out=n_[:, :, 0:128], in_=g_[:, :, 0:128]), reads=[Bg], writes=[Bn])
                        yield
                        rope_ops(S, g_[:, :, 128:192], Bg, n_[:, :, 128:192], Bn, cs, B_cs, tile, sb, c2, "q%d_%d" % (hg, tile), nheads=HG)
                        yield
                        for hl in range(HG):
                            S.op("pe", lambda E, pi=pi, hl=hl, n_=n_: E.transpose(out=tpk[pi][:, hl, :], in_=n_[:, hl, 0:128],
                                                                                  identity=ident[:]), reads=[Bn, B_ident], writes=[B_tpk[pi]])
                            yield
                            S.op("pe", lambda E, pi=pi, hl=hl, n_=n_: E.transpose(out=tpr[pi][0:64, hl, :], in_=n_[:, hl, 128:192],
                                                                                  identity=ident[:]), reads=[Bn, B_ident], writes=[B_tpr[pi]])
                            yield
                        S.op("dve", lambda E, pi=pi, tile=tile: E.tensor_copy(out=QT[:, :, tile * 128:(tile + 1) * 128], in_=tpk[pi][:, 0:4, :]),
                             reads=[B_tpk[pi]], writes=[B_QT])
                        yield
                        S.op("act", lambda E, pi=pi, tile=tile: E.copy(out=QTr[:, :, tile * 128:(tile + 1) * 128], in_=tpr[pi][0:64, 0:4, :]),
                             reads=[B_tpr[pi]], writes=[B_QTr])
                        yield
                    for t0_ in range(0, 8, 2):
                        gens = [qprep_tile(t0_), qprep_tile(t0_ + 1)]
                        while gens:
                            for gen in list(gens):
                                try:
                                    next(gen)
                                except StopIteration:
                                    gens.remove(gen)
                    S.barrier()
                with ExitStack() as c2:
                    pS = [ps(c2, "pS%d" % i, [128, 512]) for i in range(3)]; B_pS = [PB() for _ in range(3)]
                    pO = [ps(c2, "pO%d" % i, [128, 512]) for i in range(4)]; B_pO = [PB() for _ in range(4)]
                    PT = [sb(c2, "PT%d" % i, [128, 512], BF16) for i in range(3)]; B_PT = [Buf() for _ in range(3)]
                    of = [sb(c2, "of%d" % i, [128, 128]) for i in range(2)]; B_of = [Buf(), Buf()]
                    jk = sb(c2, "jkA", [128, 128], BF16); B_jk = Buf()
                    sa = [sb(c2, "sa%d" % i, [128, 8]) for i in range(2)]; B_sa = [Buf(), Buf()]
                    ob = [sb(c2, "ob%d" % i, [128, 128], BF16) for i in range(2)]; B_ob = [Buf(), Buf()]
                    si = 0
                    oi2 = 0
                    for hl in range(HG):
                        h = hg * HG + hl
                        for tg in range(2):
                            def s_mm(tk, slot):
                                p_ = pS[slot % 3]; Bp = B_pS[slot % 3]
                                S.op("pe", lambda E, p_=p_, hl=hl, tk=tk, tg=tg: E.matmul(
                                    p_[:], lhsT=KT[:, hl, tk * 128:(tk + 1) * 128], rhs=QT[:, hl, tg * 512:(tg + 1) * 512],
                                    start=True, stop=False), reads=[B_KT, B_QT], writes=[Bp])
                                S.op("pe", lambda E, p_=p_, hl=hl, tk=tk, tg=tg: E.matmul(
                                    p_[:], lhsT=KTr[:, hl, tk * 128:(tk + 1) * 128], rhs=QTr[:, hl, tg * 512:(tg + 1) * 512],
                                    start=False, stop=True), reads=[B_KTr, B_QTr], writes=[Bp])
                            s_mm(0, si)
                            for tk in range(16):
                                p_ = pS[si % 3]; Bp = B_pS[si % 3]; pt = PT[si % 3]; Bpt = B_PT[si % 3]
                                if tk + 1 < 16:
                                    s_mm(tk + 1, si + 1)
                                si += 1
                                S.op("act", lambda E, p_=p_, pt=pt: E.activation(out=pt[:], in_=p_[:], func=AF.Exp, scale=SCALE),
                                     reads=[Bp], writes=[Bpt])
                                for tqt in range(4):
                                    S.op("pe", lambda E, pt=pt, tqt=tqt, tk=tk, hl=hl: E.matmul(
                                        pO[tqt][:, 0:129], lhsT=pt[:, tqt * 128:(tqt + 1) * 128], rhs=vext[:, tk, hl, 0:129],
                                        start=(tk == 0), stop=(tk == 15)), reads=[Bpt, B_vext], writes=[B_pO[tqt]])
                            for tqt in range(4):
                                o_ = of[oi2 % 2]; Bo = B_of[oi2 % 2]; s_ = sa[oi2 % 2]; Bs = B_sa[oi2 % 2]
                                b_ = ob[oi2 % 2]; Bb = B_ob[oi2 % 2]; oi2 += 1
                                S.op("dve", lambda E, s_=s_, tqt=tqt: E.reciprocal(out=s_[:, 0:1], in_=pO[tqt][:, 128:129]),
                                     reads=[B_pO[tqt]], writes=[Bs])
                                S.op("dve", lambda E, s_=s_, tqt=tqt, o_=o_: E.tensor_scalar(
                                    out=o_[:], in0=pO[tqt][:, 0:128], scalar1=s_[:, 0:1], scalar2=None, op0=ALU.mult),
                                    reads=[B_pO[tqt], Bs], writes=[Bo])
                                S.op("act", lambda E, o_=o_, s_=s_: E.activation(out=jk[:], in_=o_[:], func=AF.Square, accum_out=s_[:, 1:2]),
                                     reads=[Bo], writes=[B_jk, Bs])
                                S.op("dve", lambda E, s_=s_: E.tensor_scalar(out=s_[:, 2:3], in0=s_[:, 1:2], scalar1=1.0 / 128, scalar2=EPS,
                                                                             op0=ALU.mult, op1=ALU.add), reads=[Bs], writes=[Bs])
                                S.op("act", lambda E, s_=s_: E.activation(out=s_[:, 3:4], in_=s_[:, 2:3], func=AF.Sqrt), reads=[Bs], writes=[Bs])
                                S.op("dve", lambda E, s_=s_: E.reciprocal(out=s_[:, 4:5], in_=s_[:, 3:4]), reads=[Bs], writes=[Bs])
                                S.op("dve", lambda E, o_=o_, s_=s_, b_=b_, h=h: E.scalar_tensor_tensor(
                                    out=b_[:], in0=o_[:], scalar=s_[:, 4:5], in1=aon_bc[:, h * 128:(h + 1) * 128], op0=ALU.mult, op1=ALU.mult),
                                    reads=[Bo, Bs, B_const], writes=[Bb])
                                t0 = tg * 512 + tqt * 128
                                S.dma("sp", s_mix[t0:t0 + 128, h * 128:(h + 1) * 128], b_[:], reads=[Bb], writes=[B_mix])
                    S.barrier()
            S.barrier()

        if stage >= 3 and "C" not in SKIP:
          with ExitStack() as c:
            B_cc = Buf()
            tri = sb(c, "tri", [128, 4, 128])
            S.dma("sp", tri[:], tri_in.rearrange("f k l -> k f l"), writes=[B_cc])
            identF = sb(c, "identF", [128, 128])
            S.dma("sp", identF[:], ident_in, writes=[B_cc])
            onesF = sb(c, "onesF", [128, 128])
            S.op("dve", lambda E: E.memset(onesF[:], 1.0), writes=[B_cc])
            neg4 = [sb(c, "neg4_%d" % i, [128, 4, 128], BF16) for i in range(2)]
            for i in range(2):
                S.op("dve", lambda E, i=i: E.tensor_copy(out=neg4[i][:], in_=tri[:, 2 + i, :].unsqueeze(1).broadcast_to([128, 4, 128])),
                     reads=[B_cc], writes=[B_cc])
            alog_bc = sb(c, "alog_bc", [128, 64]); dtb_bc = sb(c, "dtb_bc", [128, 64]); dsk_bc = sb(c, "dsk_bc", [128, 32])
            son_bc = sb(c, "son_bc", [128, 2048])
            for dst, src in ((alog_bc, a_log), (dtb_bc, dt_bias), (dsk_bc, d_skip), (son_bc, son)):
                S.dma("sp", dst[:], src.partition_broadcast(128), writes=[B_cc])
            A_bc = sb(c, "A_bc", [128, 64])
            S.op("act", lambda E: E.activation(out=A_bc[:], in_=alog_bc[:], func=AF.Exp), reads=[B_cc], writes=[B_cc])
            S.op("dve", lambda E: E.tensor_scalar(out=A_bc[:], in0=A_bc[:], scalar1=-1.0, scalar2=None, op0=ALU.mult),
                 reads=[B_cc], writes=[B_cc])
            dtr = sb(c, "dtr", [128, 16, 64]); dtv = sb(c, "dtv", [128, 16, 64]); adt = sb(c, "adt", [128, 16, 64])
            tmpd = sb(c, "tmpd", [128, 16, 64])
            B_dt = Buf()
            for tile in range(16):
                S.dma("sp", dtr[:, tile, :], s_kv[tile * 128:(tile + 1) * 128, 576:640], reads=[B_scr], writes=[B_dt])
            bc64 = lambda t_: t_[:].unsqueeze(1).broadcast_to([128, 16, 64])
            S.op("dve", lambda E: E.tensor_tensor(out=dtr[:], in0=dtr[:], in1=bc64(dtb_bc), op=ALU.add), reads=[B_dt, B_cc], writes=[B_dt])
            S.op("act", lambda E: E.activation(out=tmpd[:], in_=dtr[:], func=AF.Abs), reads=[B_dt], writes=[B_dt])
            S.op("act", lambda E: E.activation(out=tmpd[:], in_=tmpd[:], func=AF.Exp, scale=-1.0), reads=[B_dt], writes=[B_dt])
            S.op("act", lambda E: E.activation(out=tmpd[:], in_=tmpd[:], func=AF.Ln, bias=1.0), reads=[B_dt], writes=[B_dt])
            S.op("dve", lambda E: E.scalar_tensor_tensor(out=dtv[:], in0=dtr[:], scalar=0.0, in1=tmpd[:], op0=ALU.max, op1=ALU.add),
                 reads=[B_dt], writes=[B_dt])
            S.op("dve", lambda E: E.tensor_tensor(out=adt[:], in0=dtv[:], in1=bc64(A_bc), op=ALU.mult), reads=[B_dt, B_cc], writes=[B_dt])

            import os as _os
            SUB = int(_os.environ.get("KSUB", "99"))
            cw_v = conv_w.rearrange("(n p) j -> n p j", p=128)
            cb_v = conv_b.rearrange("(n p o) -> n p o", p=128, o=1)
            cin = [sb(c, "cin%d" % i, [128, T + 4], BF16) for i in range(2)]; B_cin = [Buf(), Buf()]
            for i in range(2):
                S.op("pool", lambda E, i=i: E.memset(cin[i][:], 0.0), writes=[B_cin[i]])
            cacc = sb(c, "cacc", [128, T]); B_cacc = Buf()
            cwt = [sb(c, "cwt%d" % i, [128, 8]) for i in range(2)]; B_cwt = [Buf(), Buf()]
            cT = [sb(c, "cT%d" % i, [128, T], BF16) for i in range(4)]; B_cT = [Buf() for _ in range(4)]
            xtok = sb(c, "xtok", [128, 16, 256], BF16); B_xtok = Buf()
            Btok = sb(c, "Btok", [128, 16, 128], BF16); B_Btok = Buf()
            CBT = sb(c, "CBT", [128, 8, 128], BF16); B_CBT = Buf()
            yacc = sb(c, "yacc", [128, 8, 256]); B_yacc = Buf()
            stateL = [sb(c, "state%d" % i, [128, 256]) for i in range(2)]; B_stateL = [Buf(), Buf()]
            state_bfL = [sb(c, "state_bf%d" % i, [128, 256], BF16) for i in range(2)]; B_stbfL = [Buf(), Buf()]
            P_tp = ps(c, "P_tp", [128, 8, 128], BF16); B_Ptp = PB()
            P_ct = ps(c, "P_ct", [128, 512]); B_Pct = PB()
            P_cb = [ps(c, "P_cb%d" % i, [128, 4, 128]) for i in range(2)]; B_Pcb = [PB(), PB()]
            P_cbt = ps(c, "P_cbt", [128, 512]); B_Pcbt = PB()
            P_y = ps(c, "P_y", [128, 512]); B_Py = PB()
            P_yo = ps(c, "P_yo", [128, 512]); B_Pyo = PB()
            P_st = ps(c, "P_st", [128, 512]); B_Pst = PB()
            bankA = [P_ct, P_st]; B_bankA = [B_Pct, B_Pst]
            bankB = P_cb; B_bankB = B_Pcb
            bankC = [P_y, P_yo]; B_bankC = [B_Py, B_Pyo]
            ci_n = 0
            sm = [sb(c, "sm%d" % i, [128, 32]) for i in range(4)]; B_sm = [Buf() for _ in range(4)]
            arep = [sb(c, "arep%d" % i, [128, 4, 128]) for i in range(4)]; B_arep = [Buf() for _ in range(4)]
            LT = [sb(c, "LT%d" % i, [128, 4, 128]) for i in range(4)]; B_LT = [Buf() for _ in range(4)]
            MT = [sb(c, "MT%d" % i, [128, 4, 128], BF16) for i in range(4)]; B_MT = [Buf() for _ in range(4)]
            xdt = [sb(c, "xdt%d" % i, [128, 4, 64], BF16) for i in range(4)]; B_xdt = [Buf() for _ in range(4)]
            xdd = [sb(c, "xdd%d" % i, [128, 4, 64], BF16) for i in range(4)]; B_xdd = [Buf() for _ in range(4)]
            zt = [sb(c, "zt%d" % i, [128, 256]) for i in range(2)]; B_zt = [Buf(), Buf()]
            yf = [sb(c, "yf%d" % i, [128, 256]) for i in range(2)]; B_yf = [Buf(), Buf()]
            yb = [sb(c, "yb%d" % i, [128, 256], BF16) for i in range(2)]; B_yb = [Buf(), Buf()]
            jkC = sb(c, "jkC", [128, 256], BF16); B_jkC = Buf()
            it = 0
            for g in range(8 if SUB > 0 else 0):
                chans = [g * 256, g * 256 + 128, 2048 + g * 128, 3072 + g * 128]
                for qi, ch0 in enumerate(chans):
                    ci = ci_n % 2; ci_n += 1
                    S.dma("sp", cin[ci][:, 2:2 + T], s_xT[ch0:ch0 + 128, :], reads=[B_scr], writes=[B_cin[ci]])
                    S.dma("sp", cwt[ci][:, 0:5], cw_v[ch0 // 128], writes=[B_cwt[ci]])
                    S.dma("sp", cwt[ci][:, 5:6], cb_v[ch0 // 128], writes=[B_cwt[ci]])
                    S.op("dve", lambda E, ci=ci: E.tensor_scalar(out=cacc[:], in0=cin[ci][:, 0:T], scalar1=cwt[ci][:, 0:1], scalar2=None,
                                                                 op0=ALU.mult), reads=[B_cin[ci], B_cwt[ci]], writes=[B_cacc])
                    for j in range(1, 5):
                        S.op("dve", lambda E, ci=ci, j=j: E.scalar_tensor_tensor(out=cacc[:], in0=cin[ci][:, j:j + T], scalar=cwt[ci][:, j:j + 1],
                                                                                in1=cacc[:], op0=ALU.mult, op1=ALU.add),
                             reads=[B_cin[ci], B_cwt[ci]], writes=[B_cacc])
                    S.op("act", lambda E, ci=ci, qi=qi: E.activation(out=cT[qi][:], in_=cacc[:], func=AF.Silu, bias=cwt[ci][:, 5:6]),
                         reads=[B_cacc, B_cwt[ci]], writes=[B_cT[qi]])
                if SUB < 2:
                    continue
                for tile in range(16):
                    for qi in range(3):
                        S.op("pe", lambda E, qi=qi, tile=tile: E.transpose(out=P_tp[:, qi, :], in_=cT[qi][:, tile * 128:(tile + 1) * 128],
                                                                           identity=ident[:]), reads=[B_cT[qi], B_ident], writes=[B_Ptp])
                    S.op("dve", lambda E, tile=tile: E.tensor_copy(out=xtok[:, tile, :], in_=P_tp[:, 0:2, :]), reads=[B_Ptp], writes=[B_xtok])
                    S.op("act", lambda E, tile=tile: E.copy(out=Btok[:, tile, :], in_=P_tp[:, 2, :]), reads=[B_Ptp], writes=[B_Btok])
                if SUB < 3:
                    continue
                for ch in range(8):
                    S.op("pe", lambda E, ch=ch: E.matmul(P_cbt[:, 0:128], lhsT=cT[2][:, ch * 128:(ch + 1) * 128],
                                                         rhs=cT[3][:, ch * 128:(ch + 1) * 128], start=True, stop=True),
                         reads=[B_cT[2], B_cT[3]], writes=[B_Pcbt])
                    S.op("act", lambda E, ch=ch: E.copy(out=CBT[:, ch, :], in_=P_cbt[:, 0:128]), reads=[B_Pcbt], writes=[B_CBT])
                S.op("dve", lambda E, g=g: E.tensor_tensor(
                    out=yacc[:].rearrange("p c (r q) -> p c r q", r=4),
                    in0=xtok[:, 0:8, :].rearrange("p c (r q) -> p c r q", r=4),
                    in1=dsk_bc[:, g * 4:(g + 1) * 4].unsqueeze(1).unsqueeze(3).broadcast_to([128, 8, 4, 64]), op=ALU.mult),
                    reads=[B_xtok, B_cc], writes=[B_yacc])
                cnt_d = [0, 0]

                def scan_chunk(di, ch):
                    colX = 127 if di == 0 else 0
                    triX = tri[:, di, :]
                    negX = tri[:, 2 + di, :]
                    own = ch < 8
                    k_ = di * 2 + cnt_d[di] % 2; cnt_d[di] += 1
                    s_ = sm[k_]; Bs = B_sm[k_]
                    h0 = di * 32 + g * 4
                    adt4 = adt[:, ch, h0:h0 + 4]
                    bA = bankA[di]; BbA = B_bankA[di]; pc = bankB[di]; Bpc = B_bankB[di]; bC = bankC[di]; BbC = B_bankC[di]
                    st_ = stateL[di]; Bst = B_stateL[di]; sbf = state_bfL[di]; Bsbf = B_stbfL[di]
                    S.op("pe", lambda E, triX=triX, adt4=adt4, bA=bA: E.matmul(bA[:, 0:4], lhsT=triX, rhs=adt4, start=True, stop=True),
                         reads=[B_cc, B_dt], writes=[BbA])
                    yield
                    S.op("dve", lambda E, k_=k_, adt4=adt4, triX=triX: E.tensor_tensor(
                        out=arep[k_][:], in0=triX.unsqueeze(1).broadcast_to([128, 4, 128]),
                        in1=adt4.unsqueeze(2).broadcast_to([128, 4, 128]), op=ALU.mult), reads=[B_dt, B_cc], writes=[B_arep[k_]])
                    yield
                    S.op("pe", lambda E, pc=pc, k_=k_: E.matmul(pc[:].rearrange("p r l -> p (r l)"), lhsT=onesF[:],
                                                                rhs=arep[k_][:].rearrange("p r l -> p (r l)"), start=True, stop=False),
                         reads=[B_arep[k_], B_cc], writes=[Bpc])
                    yield
                    S.op("pe", lambda E, pc=pc, di=di: E.matmul(pc[:].rearrange("p r l -> p (r l)"), lhsT=ident[:],
                                                                rhs=neg4[di][:].rearrange("p r l -> p (r l)"), start=False, stop=True),
                         reads=[B_cc, B_ident], writes=[Bpc])
                    yield
                    S.op("dve", lambda E, s_=s_, bA=bA: E.tensor_scalar(out=s_[:, 0:4], in0=bA[:, 0:4], scalar1=-1.0, scalar2=None, op0=ALU.mult),
                         reads=[BbA], writes=[Bs])
                    yield
                    S.op("act", lambda E, s_=s_, bA=bA: E.activation(out=s_[:, 4:8], in_=bA[:, 0:4], func=AF.Exp), reads=[BbA], writes=[Bs])
                    yield
                    S.op("dve", lambda E, s_=s_, pc=pc, colX=colX: E.tensor_tensor(out=s_[:, 16:20], in0=pc[:, :, colX], in1=s_[:, 0:4], op=ALU.add),
                         reads=[Bpc, Bs], writes=[Bs])
                    yield
                    S.op("act", lambda E, s_=s_: E.activation(out=s_[:, 8:12], in_=s_[:, 16:20], func=AF.Exp), reads=[Bs], writes=[Bs])
                    yield
                    S.op("act", lambda E, s_=s_, pc=pc, colX=colX: E.activation(out=s_[:, 12:16], in_=pc[:, :, colX], func=AF.Exp),
                         reads=[Bpc], writes=[Bs])
                    yield
                    dtc = dtv[:, ch, h0:h0 + 4]
                    S.op("dve", lambda E, k_=k_, ch=ch, dtc=dtc: E.tensor_tensor(
                        out=xdt[k_][:], in0=xtok[:, ch, :].rearrange("p (r q) -> p r q", r=4),
                        in1=dtc.unsqueeze(2).broadcast_to([128, 4, 64]), op=ALU.mult), reads=[B_xtok, B_dt], writes=[B_xdt[k_]])
                    yield
                    if own:
                        for r in range(4):
                            S.op("act", lambda E, k_=k_, r=r, pc=pc, s_=s_: E.activation(out=LT[k_][:, r, :], in_=pc[:, r, :], func=AF.Exp,
                                                                                         bias=s_[:, r:r + 1]), reads=[Bpc, Bs], writes=[B_LT[k_]])
                            yield
                        S.op("dve", lambda E, k_=k_, ch=ch: E.tensor_tensor(out=MT[k_][:], in0=LT[k_][:],
                                                                            in1=CBT[:, ch, :].unsqueeze(1).broadcast_to([128, 4, 128]), op=ALU.mult),
                             reads=[B_LT[k_], B_CBT], writes=[B_MT[k_]])
                        yield
                        for r in range(4):
                            S.op("pe", lambda E, k_=k_, r=r, bC=bC: E.matmul(bC[:, r * 64:(r + 1) * 64], lhsT=MT[k_][:, r, :], rhs=xdt[k_][:, r, :],
                                                                             start=True, stop=True), reads=[B_MT[k_], B_xdt[k_]], writes=[BbC])
                            yield
                        S.op("pe", lambda E, ch=ch, bC=bC, sbf=sbf: E.matmul(bC[:, 256:512], lhsT=cT[3][:, ch * 128:(ch + 1) * 128], rhs=sbf[:],
                                                                             start=True, stop=True), reads=[B_cT[3], Bsbf], writes=[BbC])
                        yield
                        S.op("dve", lambda E, ch=ch, bC=bC: E.tensor_tensor(out=yacc[:, ch, :], in0=yacc[:, ch, :], in1=bC[:, 0:256], op=ALU.add),
                             reads=[BbC], writes=[B_yacc])
                        yield
                        for r in range(4):
                            S.op("dve", lambda E, ch=ch, r=r, s_=s_, bC=bC: E.scalar_tensor_tensor(
                                out=yacc[:, ch, r * 64:(r + 1) * 64], in0=bC[:, 256 + r * 64:256 + (r + 1) * 64], scalar=s_[:, 4 + r:5 + r],
                                in1=yacc[:, ch, r * 64:(r + 1) * 64], op0=ALU.mult, op1=ALU.add), reads=[BbC, Bs], writes=[B_yacc])
                            yield
                    S.op("dve", lambda E, k_=k_, s_=s_: E.tensor_tensor(out=xdd[k_][:], in0=xdt[k_][:],
                                                                        in1=s_[:, 8:12].unsqueeze(2).broadcast_to([128, 4, 64]), op=ALU.mult),
                         reads=[B_xdt[k_], Bs], writes=[B_xdd[k_]])
                    yield
                    S.op("pe", lambda E, k_=k_, ch=ch, bA=bA: E.matmul(bA[:, 256:512], lhsT=Btok[:, ch, :],
                                                                       rhs=xdd[k_][:].rearrange("p r q -> p (r q)"), start=True, stop=True),
                         reads=[B_Btok, B_xdd[k_]], writes=[BbA])
                    yield
                    for r in range(4):
                        S.op("dve", lambda E, r=r, s_=s_, bA=bA, st_=st_: E.scalar_tensor_tensor(
                            out=st_[:, r * 64:(r + 1) * 64], in0=st_[:, r * 64:(r + 1) * 64], scalar=s_[:, 12 + r:13 + r],
                            in1=bA[:, 256 + r * 64:256 + (r + 1) * 64], op0=ALU.mult, op1=ALU.add), reads=[BbA, Bs], writes=[Bst])
                        yield
                    S.op("dve", lambda E, st_=st_, sbf=sbf: E.tensor_copy(out=sbf[:], in_=st_[:]), reads=[Bst], writes=[Bsbf])
                    yield

                if SUB > 3:
                    for di in range(2):
                        S.op("dve", lambda E, di=di: E.memset(stateL[di][:], 0.0), writes=[B_stateL[di]])
                        S.op("dve", lambda E, di=di: E.memset(state_bfL[di][:], 0.0), writes=[B_stbfL[di]])
                    for step in range(16):
                        if step < 8:
                            gens = [scan_chunk(1, 15 - step)]
                        else:
                            gens = [scan_chunk(1, 15 - step), scan_chunk(0, step - 8)]
                        while gens:
                            for gen in list(gens):
                                try:
                                    next(gen)
                                except StopIteration:
                                    gens.remove(gen)
                for ch in range(8 if SUB > 4 else 0):
                    k_ = ch % 2
                    S.dma("sp", zt[k_][:], s_z[ch * 128:(ch + 1) * 128, g * 256:(g + 1) * 256], reads=[B_scr], writes=[B_zt[k_]])
                    S.op("act", lambda E, k_=k_: E.activation(out=zt[k_][:], in_=zt[k_][:], func=AF.Silu), reads=[B_zt[k_]], writes=[B_zt[k_]])
                    S.op("dve", lambda E, k_=k_, ch=ch: E.tensor_tensor(out=yf[k_][:], in0=yacc[:, ch, :], in1=zt[k_][:], op=ALU.mult),
                         reads=[B_yacc, B_zt[k_]], writes=[B_yf[k_]])
                    s_ = sm[k_]; Bs = B_sm[k_]
                    S.op("act", lambda E, k_=k_, s_=s_: E.activation(out=jkC[:], in_=yf[k_][:], func=AF.Square, accum_out=s_[:, 20:21]),
                         reads=[B_yf[k_]], writes=[B_jkC, Bs])
                    S.op("dve", lambda E, s_=s_: E.tensor_scalar(out=s_[:, 21:22], in0=s_[:, 20:21], scalar1=1.0 / 256, scalar2=EPS,
                                                                 op0=ALU.mult, op1=ALU.add), reads=[Bs], writes=[Bs])
                    S.op("act", lambda E, s_=s_: E.activation(out=s_[:, 22:23], in_=s_[:, 21:22], func=AF.Sqrt), reads=[Bs], writes=[Bs])
                    S.op("dve", lambda E, s_=s_: E.reciprocal(out=s_[:, 23:24], in_=s_[:, 22:23]), reads=[Bs], writes=[Bs])
                    S.op("dve", lambda E, k_=k_, s_=s_, g=g: E.scalar_tensor_tensor(
                        out=yb[k_][:], in0=yf[k_][:], scalar=s_[:, 23:24], in1=son_bc[:, g * 256:(g + 1) * 256], op0=ALU.mult, op1=ALU.mult),
                        reads=[B_yf[k_], Bs, B_cc], writes=[B_yb[k_]])
                    S.dma("sp", s_mix[ch * 128:(ch + 1) * 128, 2048 + g * 256:2048 + (g + 1) * 256], yb[k_][:], reads=[B_yb[k_]], writes=[B_mix])
            S.barrier()

        B_out = Buf(); B_hnT = Buf()
        if stage >= 4 and "D" not in SKIP:
          hdst = dbg["d_h"] if stage == 4 else out
          with ExitStack() as c:
            nf_bc = sb(c, "nf_bc", [128, D]); B_nf = Buf()
            S.dma("sp", nf_bc[:], norm_ffn.partition_broadcast(128), writes=[B_nf])
            mt = [sb(c, "mt%d" % i, [128, D], BF16) for i in range(2)]; B_mt = [Buf(), Buf()]
            mixT = sb(c, "mixT", [128, 32, 512], BF16); B_mixT = [Buf() for _ in range(4)]
            wb_ = [sb(c, "wbD%d" % i, [128, 32, 256], BF16) for i in range(2)]; B_wb = [Buf(), Buf()]
            ht = [sb(c, "ht%d" % i, [128, D]) for i in range(4)]; B_ht = [Buf() for _ in range(4)]
            xr = [sb(c, "xr%d" % i, [128, 256]) for i in range(3)]; B_xr = [Buf() for _ in range(3)]
            hb = sb(c, "hb", [128, D], BF16); B_hb = Buf()
            jkD = sb(c, "jkD", [128, D], BF16); B_jkD = Buf()
            hst = [sb(c, "hst%d" % i, [128, 8, 128], BF16) for i in range(2)]; B_hst = [Buf(), Buf()]
            sD = sb(c, "sD", [128, 8]); B_sD = Buf()
            tpD = [ps(c, "tpD%d" % i, [128, 8, 128], BF16) for i in range(2)]; B_tpD = [PB(), PB()]
            accD = [ps(c, "accD%d" % i, [128, 512]) for i in range(4)]; B_accD = [PB() for _ in range(4)]
            wo_v = w_out.rearrange("(k p) c -> p k c", p=128)
            hn_v = s_hnT.rearrange("(k p) t -> p k t", p=128)
            wi = 0; ai = 0; xi = 0; gi = 0
            for tb in range(2):
                for tt in range(4):
                    tile = tb * 4 + tt
                    mi = tile % 2
                    S.dma("sp", mt[mi][:], s_mix[tile * 128:(tile + 1) * 128, :], reads=[B_mix], writes=[B_mt[mi]])
                    for g in range(4):
                        ti = gi % 2; gi += 1
                        for j in range(8):
                            kk = g * 8 + j
                            S.op("pe", lambda E, ti=ti, j=j, kk=kk, mi=mi: E.transpose(out=tpD[ti][:, j, :], in_=mt[mi][:, kk * 128:(kk + 1) * 128],
                                                                                      identity=ident[:]), reads=[B_mt[mi], B_ident], writes=[B_tpD[ti]])
                        if g % 2 == 0:
                            S.op("act", lambda E, ti=ti, g=g, tt=tt: E.copy(out=mixT[:, g * 8:(g + 1) * 8, tt * 128:(tt + 1) * 128], in_=tpD[ti][:]),
                                 reads=[B_tpD[ti]], writes=[B_mixT[tt]])
                        else:
                            S.op("dve", lambda E, ti=ti, g=g, tt=tt: E.tensor_copy(out=mixT[:, g * 8:(g + 1) * 8, tt * 128:(tt + 1) * 128], in_=tpD[ti][:]),
                                 reads=[B_tpD[ti]], writes=[B_mixT[tt]])
                for cb in range(16):
                    w_ = wb_[wi % 2]; Bw = B_wb[wi % 2]; wi += 1
                    S.dma("pool", w_[:], wo_v[:, :, cb * 256:(cb + 1) * 256], writes=[Bw])
                    for tt in range(4):
                        tile = tb * 4 + tt
                        a = accD[ai % 4]; Ba = B_accD[ai % 4]; ai += 1
                        x_ = xr[xi % 3]; Bx = B_xr[xi % 3]; xi += 1
                        S.dma("sp", x_[:], x[tile * 128:(tile + 1) * 128, cb * 256:(cb + 1) * 256], writes=[Bx])
                        for kk in range(32):
                            S.op("pe", lambda E, a=a, w_=w_, kk=kk, tt=tt: E.matmul(a[:, 0:256], lhsT=mixT[:, kk, tt * 128:(tt + 1) * 128],
                                                                                   rhs=w_[:, kk, :], start=(kk == 0), stop=(kk == 31)),
                                 reads=[Bw, B_mixT[tt]], writes=[Ba])
                        S.op("dve", lambda E, a=a, x_=x_, tt=tt, cb=cb: E.tensor_tensor(out=ht[tt][:, cb * 256:(cb + 1) * 256], in0=a[:, 0:256],
                                                                                       in1=x_[:], op=ALU.add), reads=[Ba, Bx], writes=[B_ht[tt]])
                for tt in range(4):
                    tile = tb * 4 + tt
                    S.dma("sp", hdst[tile * 128:(tile + 1) * 128, :], ht[tt][:], reads=[B_ht[tt]], writes=[B_out])
                    S.op("act", lambda E, tt=tt: E.activation(out=jkD[:], in_=ht[tt][:], func=AF.Square, accum_out=sD[:, 0:1]),
                         reads=[B_ht[tt]], writes=[B_jkD, B_sD])
                    S.op("dve", lambda E: E.tensor_scalar(out=sD[:, 1:2], in0=sD[:, 0:1], scalar1=1.0 / D, scalar2=EPS, op0=ALU.mult, op1=ALU.add),
                         reads=[B_sD], writes=[B_sD])
                    S.op("act", lambda E: E.activation(out=sD[:, 2:3], in_=sD[:, 1:2], func=AF.Sqrt), reads=[B_sD], writes=[B_sD])
                    S.op("dve", lambda E: E.reciprocal(out=sD[:, 3:4], in_=sD[:, 2:3]), reads=[B_sD], writes=[B_sD])
                    S.op("dve", lambda E, tt=tt: E.scalar_tensor_tensor(out=hb[:], in0=ht[tt][:], scalar=sD[:, 3:4], in1=nf_bc[:],
                                                                        op0=ALU.mult, op1=ALU.mult), reads=[B_ht[tt], B_sD, B_nf], writes=[B_hb])
                    for g in range(4):
                        ti = gi % 2; gi += 1
                        for j in range(8):
                            kk = g * 8 + j
                            S.op("pe", lambda E, ti=ti, j=j, kk=kk: E.transpose(out=tpD[ti][:, j, :], in_=hb[:, kk * 128:(kk + 1) * 128],
                                                                               identity=ident[:]), reads=[B_hb, B_ident], writes=[B_tpD[ti]])
                        hs = hst[ti]; Bh = B_hst[ti]
                        if g % 2 == 0:
                            S.op("act", lambda E, ti=ti, hs=hs: E.copy(out=hs[:], in_=tpD[ti][:]), reads=[B_tpD[ti]], writes=[Bh])
                        else:
                            S.op("dve", lambda E, ti=ti, hs=hs: E.tensor_copy(out=hs[:], in_=tpD[ti][:]), reads=[B_tpD[ti]], writes=[Bh])
                        S.dma("sp", hn_v[:, g * 8:(g + 1) * 8, tile * 128:(tile + 1) * 128], hs[:], reads=[Bh], writes=[B_hnT])
            S.barrier()

        if stage >= 5:
          with ExitStack() as c:
            s2all = sb(c, "s2all", [128, 8, 8, 128]); A1all = sb(c, "A1all", [128, 8, 8, 128])
            wAll = sb(c, "wAll", [128, 8, 8])
            B_gin = Buf()
            hnT = sb(c, "hnT", [128, 32, TO], BF16); B_hn = Buf()
            hn_v = s_hnT.rearrange("(k p) t -> p k t", p=128)
            for g in range(4):
                S.dma("sp", hnT[:, g * 8:(g + 1) * 8, :], hn_v[:, g * 8:(g + 1) * 8, :], reads=[B_hnT], writes=[B_hn])
            identF2 = sb(c, "identF2", [128, 128]); B_idf = Buf()
            S.dma("sp", identF2[:], ident_in, writes=[B_idf])
            with ExitStack() as c2:
                qT = sb(c2, "qT", [128, 16, TO], BF16); B_qT = Buf()
                skT = sb(c2, "skT", [128, 16, 128], BF16); B_skT = Buf()
                S.dma("pool", skT[:], skT_in.rearrange("h d k -> d h k"), writes=[B_skT])
                wqb = [sb(c2, "wqb%d" % i, [128, 32, 128], BF16) for i in range(2)]; B_wqb = [Buf(), Buf()]
                pq_ = [ps(c2, "pqE%d" % i, [128, 512]) for i in range(4)]; B_pq_ = [PB() for _ in range(4)]
                pi_ = 0
                wq_v = w_query.rearrange("(k p) c -> p k c", p=128)
                for cb in range(16):
                    w_ = wqb[cb % 2]; Bw = B_wqb[cb % 2]
                    S.dma("pool", w_[:], wq_v[:, :, cb * 128:(cb + 1) * 128], writes=[Bw])
                    for hh in range(1):
                        hc = cb
                        for th in range(2):
                            p_ = pq_[pi_ % 4]; Bp = B_pq_[pi_ % 4]; pi_ += 1
                            for kk in range(32):
                                S.op("pe", lambda E, p_=p_, w_=w_, kk=kk, hh=hh, th=th: E.matmul(
                                    p_[:], lhsT=w_[:, kk, hh * 128:(hh + 1) * 128], rhs=hnT[:, kk, th * 512:(th + 1) * 512],
                                    start=(kk == 0), stop=(kk == 31)), reads=[Bw, B_hn], writes=[Bp])
                            if pi_ % 2 == 0:
                                S.op("act", lambda E, p_=p_, hc=hc, th=th: E.copy(out=qT[:, hc, th * 512:(th + 1) * 512], in_=p_[:]),
                                     reads=[Bp], writes=[B_qT])
                            else:
                                S.op("dve", lambda E, p_=p_, hc=hc, th=th: E.tensor_copy(out=qT[:, hc, th * 512:(th + 1) * 512], in_=p_[:]),
                                     reads=[Bp], writes=[B_qT])
                sc = sb(c2, "sc", [128, 16, 128]); B_sc = Buf()
                wk = sb(c2, "wk", [128, 256]); B_wk = Buf()
                v16 = sb(c2, "v16", [128, 16, 16]); B_v16 = Buf()
                cand = sb(c2, "cand", [128, 8, 256]); B_cand = Buf()
                t24 = sb(c2, "t24", [128, 8, 24]); B_t24 = Buf()
                sE = sb(c2, "sE", [128, 8, 8]); B_sE = Buf()
                jkE = sb(c2, "jkE", [128, 16]); B_jkE = Buf()
                for tt in range(8):
                    for q4 in range(4):
                        p_ = pq_[pi_ % 4]; Bp = B_pq_[pi_ % 4]; pi_ += 1
                        for u in range(4):
                            hc = q4 * 4 + u
                            S.op("pe", lambda E, p_=p_, u=u, hc=hc, tt=tt: E.matmul(
                                p_[:, u * 128:(u + 1) * 128], lhsT=qT[:, hc, tt * 128:(tt + 1) * 128], rhs=skT[:, hc, :],
                                start=True, stop=True), reads=[B_qT, B_skT], writes=[Bp])
                        S.op("act", lambda E, p_=p_, q4=q4: E.copy(out=sc[:, q4 * 4:(q4 + 1) * 4, :], in_=p_[:]), reads=[Bp], writes=[B_sc])
                    for hc in range(16):
                        S.op("dve", lambda E, hc=hc: E.max(out=v16[:, hc, 0:8], in_=sc[:, hc, :]), reads=[B_sc], writes=[B_v16])
                        S.op("dve", lambda E, hc=hc: E.match_replace(out=wk[:, 0:128], in_to_replace=v16[:, hc, 0:8], in_values=sc[:, hc, :],
                                                                     imm_value=-1e30), reads=[B_sc, B_v16], writes=[B_wk])
                        S.op("dve", lambda E, hc=hc: E.max(out=v16[:, hc, 8:16], in_=wk[:, 0:128]), reads=[B_wk], writes=[B_v16])
                    v4 = v16[:].rearrange("p (h c) k -> p h c k", c=2)
                    S.op("dve", lambda E, v4=v4: E.tensor_tensor(
                        out=cand[:].rearrange("p h (a b) -> p h a b", a=16),
                        in0=v4[:, :, 0, :].unsqueeze(3).broadcast_to([128, 8, 16, 16]),
                        in1=v4[:, :, 1, :].unsqueeze(2).broadcast_to([128, 8, 16, 16]), op=ALU.add), reads=[B_v16], writes=[B_cand])
                    for h in range(8):
                        S.op("dve", lambda E, h=h: E.max(out=t24[:, h, 0:8], in_=cand[:, h, :]), reads=[B_cand], writes=[B_t24])
                        S.op("dve", lambda E, h=h: E.match_replace(out=wk[:], in_to_replace=t24[:, h, 0:8], in_values=cand[:, h, :],
                                                                   imm_value=-1e30), reads=[B_cand, B_t24], writes=[B_wk])
                        S.op("dve", lambda E, h=h: E.max(out=t24[:, h, 8:16], in_=wk[:]), reads=[B_wk], writes=[B_t24])
                        S.op("dve", lambda E, h=h: E.match_replace(out=wk[:], in_to_replace=t24[:, h, 8:16], in_values=wk[:],
                                                                   imm_value=-1e30), reads=[B_t24], writes=[B_wk])
                        S.op("dve", lambda E, h=h: E.max(out=t24[:, h, 16:24], in_=wk[:]), reads=[B_wk], writes=[B_t24])
                    S.op("dve", lambda E: E.tensor_tensor(out=sE[:, :, 0], in0=t24[:, :, 15], in1=t24[:, :, 16], op=ALU.add), reads=[B_t24], writes=[B_sE])
                    S.op("dve", lambda E: E.tensor_scalar(out=sE[:, :, 0], in0=sE[:, :, 0], scalar1=0.5, scalar2=None, op0=ALU.mult), reads=[B_sE], writes=[B_sE])
                    S.op("dve", lambda E: E.tensor_scalar(out=sE[:, :, 6], in0=t24[:, :, 0], scalar1=-1.0, scalar2=None, op0=ALU.mult), reads=[B_t24], writes=[B_sE])
                    for h in range(8):
                        S.op("act", lambda E, h=h: E.activation(out=jkE[:], in_=t24[:, h, 0:16], func=AF.Exp, bias=sE[:, h, 6:7],
                                                                accum_out=sE[:, h, 2:3]), reads=[B_t24, B_sE], writes=[B_jkE, B_sE])
                    S.op("act", lambda E: E.activation(out=sE[:, :, 3], in_=sE[:, :, 2], func=AF.Ln), reads=[B_sE], writes=[B_sE])
                    S.op("dve", lambda E: E.tensor_tensor(out=sE[:, :, 4], in0=sE[:, :, 0], in1=sE[:, :, 6], op=ALU.add), reads=[B_sE], writes=[B_sE])
                    S.op("dve", lambda E: E.tensor_tensor(out=sE[:, :, 4], in0=sE[:, :, 4], in1=sE[:, :, 3], op=ALU.subtract), reads=[B_sE], writes=[B_sE])
                    S.op("act", lambda E: E.activation(out=sE[:, :, 5], in_=sE[:, :, 4], func=AF.Exp), reads=[B_sE], writes=[B_sE])
                    sc4 = sc[:].rearrange("p (h c) k -> p h c k", c=2)
                    S.op("dve", lambda E, sc4=sc4, tt=tt: E.tensor_tensor(out=A1all[:, tt, :, :], in0=sc4[:, :, 0, :],
                                                                          in1=sE[:, :, 0:1].broadcast_to([128, 8, 128]), op=ALU.subtract),
                         reads=[B_sc, B_sE], writes=[B_gin])
                    S.op("act", lambda E, sc4=sc4, tt=tt: E.copy(out=s2all[:, tt, :, :], in_=sc4[:, :, 1, :]), reads=[B_sc], writes=[B_gin])
                    S.op("dve", lambda E, tt=tt: E.tensor_copy(out=wAll[:, tt, :], in_=sE[:, :, 5]), reads=[B_sE], writes=[B_gin])
                S.barrier()
            with ExitStack() as c2:
                ut = [sb(c2, "ut%d" % i, [128, 32, 128], BF16) for i in range(3)]; B_ut = [Buf() for _ in range(3)]
                gact = [sb(c2, "gact%d" % i, [128, TO], BF16) for i in range(2)]; B_gact = [Buf(), Buf()]
                NYB = 4
                Yb = [sb(c2, "Yb%d" % i, [128, 8, 128]) for i in range(NYB)]; B_Yb = [Buf() for _ in range(NYB)]
                Eb = [sb(c2, "Eb%d" % i, [128, 8, 128], BF16) for i in range(NYB)]; B_Eb = [Buf() for _ in range(NYB)]
                Gb = [sb(c2, "Gb%d" % i, [128, 8, 128], BF16) for i in range(3)]; B_Gb = [Buf() for _ in range(3)]
                wt = [sb(c2, "wt%d" % i, [128, TO], BF16) for i in range(2)]; B_wt = [Buf(), Buf()]
                P_a = [ps(c2, "P_a%d" % i, [128, 512]) for i in range(4)]; B_Pa = [PB() for _ in range(4)]
                P_g = [ps(c2, "P_g%d" % i, [128, 512]) for i in range(4)]; B_Pg = [PB() for _ in range(4)]
                NCH = int(_os0.environ.get("KNCH", "128"))
                yi = 0; gi = 0
                POOL_SET = [int(v) for v in _os0.environ.get("KPOOLSET", "1,3,5,7").split(",") if v != ""]
                STT_SET = [int(v) for v in _os0.environ.get("KSTTSET", "0,1,2,3,4,5,6,7").split(",") if v != ""]
                Dg = sb(c2, "Dg", [128, 8, 8, 128], BF16)
                for tt in range(8):
                    for h in range(8):
                        S.op("dve", lambda E, tt=tt, h=h: E.tensor_scalar(out=Dg[:, tt, h, :], in0=identF2[:], scalar1=wAll[:, tt, h:h + 1], scalar2=None,
                                                                          op0=ALU.mult), reads=[B_idf, B_gin], writes=[B_gin])
                gbuf = {}

                def load_u(i):
                    S.dma("pool", ut[i % 3][:], UTt[i], writes=[B_ut[i % 3]])

                def stage1_mm(i, grp):
                    u_ = ut[i % 3]; Bu = B_ut[i % 3]
                    th = grp // 4
                    pa = P_a[(i % 2) * 2 + th]; Bpa = B_Pa[(i % 2) * 2 + th]
                    for kk in range((grp % 4) * 8, (grp % 4) * 8 + 8):
                        S.op("pe", lambda E, pa=pa, th=th, kk=kk, u_=u_: E.matmul(pa[:], lhsT=u_[:, kk, :], rhs=hnT[:, kk, th * 512:(th + 1) * 512],
                                                                                 start=(kk == 0), stop=(kk == 31)), reads=[Bu, B_hn], writes=[Bpa])

                def unit_pre(i, tt):
                    k = i * 8 + tt
                    y_ = Yb[k % NYB]; By = B_Yb[k % NYB]; e_ = Eb[k % NYB]; Be = B_Eb[k % NYB]
                    g_ = Gb[k % 3]; Bg = B_Gb[k % 3]
                    gbuf[(i, tt)] = (g_, Bg)
                    S.op("pool" if tt in POOL_SET else "dve", lambda E, y_=y_, tt=tt, i=i: E.tensor_tensor(
                        out=y_[:], in0=s2all[:, tt, :, :], in1=A1all[:, tt, :, i:i + 1].broadcast_to([128, 8, 128]), op=ALU.add),
                        reads=[B_gin], writes=[By])
                    if tt in STT_SET:
                        S.op("act", lambda E, y_=y_, e_=e_: E.activation(out=e_[:], in_=y_[:], func=AF.Exp), reads=[By], writes=[Be])
                        S.op("dve", lambda E, y_=y_, e_=e_, g_=g_: E.scalar_tensor_tensor(out=g_[:], in0=y_[:], scalar=0.0, in1=e_[:],
                                                                                          op0=ALU.is_ge, op1=ALU.mult), reads=[By, Be], writes=[Bg])
                    else:
                        S.op("act", lambda E, y_=y_: E.activation(out=y_[:], in_=y_[:], func=AF.Prelu, alpha=1e30), reads=[By], writes=[By])
                        S.op("act", lambda E, y_=y_, g_=g_: E.activation(out=g_[:], in_=y_[:], func=AF.Exp), reads=[By], writes=[Bg])

                def unit_mm(i, tt):
                    g_, Bg = gbuf.pop((i, tt))
                    pb = (i % 2) * 2
                    pg = P_g[pb + tt // 4]; Bpg = B_Pg[pb + tt // 4]
                    for h in range(8):
                        S.op("pe", lambda E, pg=pg, g_=g_, h=h, tt=tt: E.matmul(pg[:, (tt % 4) * 128:(tt % 4 + 1) * 128], lhsT=g_[:, h, :],
                                                                                rhs=Dg[:, tt, h, :], start=(h == 0), stop=(h == 7)),
                             reads=[Bg, B_gin], writes=[Bpg])

                def stage3(i):
                    pb = (i % 2) * 2
                    ga = gact[i % 2]; Bga = B_gact[i % 2]
                    w_ = wt[i % 2]; Bwt = B_wt[i % 2]
                    for th in range(2):
                        S.op("act", lambda E, pb=pb, th=th, ga=ga: E.activation(out=ga[:, th * 512:(th + 1) * 512], in_=P_a[pb + th][:],
                                                                                func=AF.Gelu_apprx_tanh), reads=[B_Pa[pb + th]], writes=[Bga])
                    for th in range(2):
                        S.op("dve", lambda E, w_=w_, th=th, pb=pb, ga=ga: E.tensor_tensor(out=w_[:, th * 512:(th + 1) * 512], in0=P_g[pb + th][:],
                                                                                         in1=ga[:, th * 512:(th + 1) * 512], op=ALU.mult),
                             reads=[B_Pg[pb + th], Bga], writes=[Bwt])
                    S.dma("sp", s_WT[i], w_[:], reads=[Bwt], writes=[B_swt])

                load_u(0)
                if NCH > 1:
                    load_u(1)
                for grp in range(8):
                    stage1_mm(0, grp)
                for i in range(NCH):
                    if i + 2 < NCH:
                        load_u(i + 2)
                    for tt in range(8):
                        unit_pre(i, tt)
                        if i + 1 < NCH:
                            stage1_mm(i + 1, tt)
                        if tt >= 1:
                            unit_mm(i, tt - 1)
                    unit_mm(i, 7)
                    stage3(i)
                S.barrier()
          with ExitStack() as c:
            WTr = sb(c, "WTr", [128, 128, 512], BF16); B_WTr = Buf()
            vt = [sb(c, "vt%d" % i, [128, 4, 1024], BF16) for i in range(3)]; B_vt = [Buf() for _ in range(3)]
            hres = [sb(c, "hres%d" % i, [128, 512]) for i in range(3)]; B_hres = [Buf() for _ in range(3)]
            P_o = [ps(c, "P_o%d" % i, [128, 512]) for i in range(8)]; B_Po = [PB() for _ in range(8)]
            V_v = Vexp.rearrange("(i j) d -> j i d", j=128)
            wt_v = s_WT.rearrange("i j t -> j i t")
            vi = 0; hi_ = 0
            for tb in range(0 if "3" in SKIP else 2):
                for g in range(8):
                    S.dma("sp", WTr[:, g * 16:(g + 1) * 16, :], wt_v[:, g * 16:(g + 1) * 16, tb * 512:(tb + 1) * 512],
                          reads=[B_swt], writes=[B_WTr])
                for dq in range(4):
                    for ig in range(NCH // 4):
                        v_ = vt[vi % 3]; Bv = B_vt[vi % 3]; vi += 1
                        S.dma("pool", v_[:], V_v[:, ig * 4:(ig + 1) * 4, dq * 1024:(dq + 1) * 1024], writes=[Bv])
                        for ii in range(4):
                            i = ig * 4 + ii
                            for tt in range(4):
                                for hf in range(2):
                                    S.op("pe", lambda E, tt=tt, hf=hf, i=i, ii=ii, v_=v_: E.matmul(
                                        P_o[tt * 2 + hf][:], lhsT=WTr[:, i, tt * 128:(tt + 1) * 128], rhs=v_[:, ii, hf * 512:(hf + 1) * 512],
                                        start=(i == 0), stop=(i == NCH - 1)), reads=[B_WTr, Bv], writes=[B_Po[tt * 2 + hf]])
                    for tt in range(4):
                        for hf in range(2):
                            t0 = tb * 512 + tt * 128
                            c0 = dq * 1024 + hf * 512
                            h_ = hres[hi_ % 3]; Bh = B_hres[hi_ % 3]; hi_ += 1
                            S.dma("sp", h_[:], out[t0:t0 + 128, c0:c0 + 512], reads=[B_out], writes=[Bh])
                            S.op("dve", lambda E, h_=h_, tt=tt, hf=hf: E.tensor_tensor(out=h_[:], in0=P_o[tt * 2 + hf][:], in1=h_[:], op=ALU.add),
                                 reads=[B_Po[tt * 2 + hf], Bh], writes=[Bh])
                            S.dma("sp", out[t0:t0 + 128, c0:c0 + 512], h_[:], reads=[Bh], writes=[B_fin])
            S.barrier()

        S.barrier()
        with nc.Block() as block:
            S.emit(block)
    return nc


def _prep_inputs(inputs):
    g = {k: np.asarray(v) for k, v in inputs.items()}
    w_in = g["w_in"][0]
    sp = np.cumsum([0, 1024, 512, 64, 2048, 4096, 32, 32])
    c_q, c_kv, k_rope, z, xbc, dtf, dtb = [w_in[:, sp[i]:sp[i + 1]] for i in range(7)]
    xs, bs, cs = xbc[:, :2048], xbc[:, 2048:3072], xbc[:, 3072:4096]
    w_perm = [np.ascontiguousarray(np.concatenate([c_kv, k_rope, a, b, xs, bs, cs, c_q, z], axis=1))
              for (a, b) in ((dtf, dtb), (dtb, dtf))]
    ident = np.eye(128, dtype=np.float32)
    invf = (10000.0 ** (-(np.arange(32, dtype=np.float32) / np.float32(32)))).astype(np.float32)
    cwm = g["conv_w"][0][:, 0, :]
    cw = [np.ascontiguousarray(cwm.T), np.ascontiguousarray(cwm[::-1].T)]
    kk, ll = np.meshgrid(np.arange(128), np.arange(128), indexing="ij")
    tri = np.stack([(kk <= ll), (kk >= ll), np.where(kk <= ll, 0.0, -30000.0), np.where(kk >= ll, 0.0, -30000.0)]).astype(np.float32)
    skT = np.ascontiguousarray(g["sub_keys"][0].reshape(16, 128, 128).transpose(0, 2, 1))
    U = g["expert_u"][0]
    UTt = np.ascontiguousarray(U.reshape(128, 128, 32, 128).transpose(0, 3, 2, 1))
    maps = []
    for c in range(8):
        b, half = c // 2, c % 2
        xl = g["x"][b]
        if half:
            xl = xl[::-1]
        pl = g["positions"][b].astype(np.int32)
        if half:
            pl = pl[::-1]
        m = {"x": np.ascontiguousarray(xl), "norm_mix": g["norm_mix"][0], "w_in": w_perm[half], "ident": ident,
             "pos": np.ascontiguousarray(pl), "invf": invf, "q_a_norm": g["q_a_norm"][0], "w_uq": g["w_uq"][0],
             "kv_a_norm": g["kv_a_norm"][0], "w_ukv": g["w_ukv"][0], "q_norm": g["q_norm"][0], "k_norm": g["k_norm"][0],
             "aon": np.ascontiguousarray(g["attn_out_norm"][0].reshape(-1)),
             "conv_w": cw[half], "conv_b": g["conv_b"][0],
             "a_log": np.concatenate([g["a_log_fwd"][0], g["a_log_bwd"][0]][::(-1 if half else 1)]),
             "dt_bias": np.concatenate([g["dt_bias_fwd"][0], g["dt_bias_bwd"][0]][::(-1 if half else 1)]),
             "d_skip": g["d_skip"][0], "son": g["ssm_out_norm"][0], "tri": tri,
             "w_out": g["w_out"][0], "norm_ffn": g["norm_ffn"][0], "w_query": g["w_query"][0],
             "skT": skT, "UTt": UTt, "Vexp": g["expert_v"][0]}
        maps.append(m)
    return maps


def kernel(**inputs):
    maps = _prep_inputs(inputs)
    nc = build()
    res = run_bass_kernel_spmd(nc, maps, core_ids=list(range(8)))
    outp = np.zeros((4, 2048, D), np.float32)
    for c in range(8):
        b, half = c // 2, c % 2
        o = res.results[c]["out"]
        if half:
            outp[b, 1024:] = o[::-1]
        else:
            outp[b, :1024] = o
    return outp
```
